# Optimizing a Trainium2 kernel written in Bass

```python
import math
import jax, jax.numpy as jnp
from jax import lax
import numpy as np

D_MODEL = 1024
BATCH = 8
SEQ = 2048
DEPTH = 4

CTX_LEN = 256
GRID_W = 64
N_EVEN = (DEPTH + 1) // 2
N_ODD = DEPTH // 2
N_MOD = 6
EPS = 1e-6
NEG_INF = -1e30

FOURIER_GROUPS = 4
FOURIER_GROUP_DIM = 128
FOURIER_WIDTH = FOURIER_GROUPS * FOURIER_GROUP_DIM

N_HEADS = 8
N_KV_HEADS = 2
HEAD_GROUP = N_HEADS // N_KV_HEADS
HEAD_DIM = 64
ATTN_WIDTH = N_HEADS * HEAD_DIM
KV_WIDTH = N_KV_HEADS * HEAD_DIM
WINDOW = 128
BLOCK = WINDOW
ROPE_AXIS_DIM = HEAD_DIM // 2
ROPE_BASE = 10000.0

IN_WIDTH = FOURIER_WIDTH + ATTN_WIDTH + 2 * KV_WIDTH
MIX_WIDTH = FOURIER_WIDTH + ATTN_WIDTH

SSM_GROUP_DIM = 16
SSM_GROUPS = D_MODEL // SSM_GROUP_DIM
SSM_STATE = 64
DT_MIN = 1e-3
DT_MAX = 1e-1

D_FF = 4 * D_MODEL

kernel_name = "hybrid_fourier_swa_s5_diffusion_trunk"


def rmsnorm(x, g):
    xf = x.astype(jnp.float32)
    y = xf * lax.rsqrt(jnp.mean(xf * xf, axis=-1, keepdims=True) + EPS)
    return (y * g.astype(jnp.float32)).astype(x.dtype)


def modulation(cond, w, b):
    m = jax.nn.silu(cond) @ w + b
    m = m.reshape(m.shape[:-1] + (1, N_MOD, D_MODEL))
    return tuple(m[..., i, :] for i in range(N_MOD))


def axial_rope_tables(n):
    rows = n // GRID_W
    row = jnp.repeat(jnp.arange(rows), GRID_W).astype(jnp.float32)
    col = jnp.tile(jnp.arange(GRID_W), rows).astype(jnp.float32)
    inv = ROPE_BASE ** (-jnp.arange(0, ROPE_AXIS_DIM, 2, dtype=jnp.float32) / ROPE_AXIS_DIM)
    ang_r = row[:, None] * inv
    ang_c = col[:, None] * inv
    return (jnp.cos(ang_r), jnp.sin(ang_r), jnp.cos(ang_c), jnp.sin(ang_c))


def _rotate(xh, cos, sin):
    x1, x2 = jnp.split(xh, 2, axis=-1)
    cos = cos[None, :, None, :]
    sin = sin[None, :, None, :]
    return jnp.concatenate([x1 * cos - x2 * sin, x2 * cos + x1 * sin], axis=-1)


def apply_axial_rope(x, rope):
    cos_r, sin_r, cos_c, sin_c = rope
    xf = x.astype(jnp.float32)
    out = jnp.concatenate([_rotate(xf[..., :ROPE_AXIS_DIM], cos_r, sin_r),
                           _rotate(xf[..., ROPE_AXIS_DIM:], cos_c, sin_c)], axis=-1)
    return out.astype(x.dtype)


def fourier_mix(f):
    b, l, _ = f.shape
    fg = f.astype(jnp.float32).reshape(b, l, FOURIER_GROUPS, FOURIER_GROUP_DIM)
    out = jnp.fft.fftn(fg, axes=(1, 3), norm="ortho").real
    return out.reshape(b, l, FOURIER_WIDTH).astype(f.dtype)


def split_projection(p):
    b, l, _ = p.shape
    f = p[..., :FOURIER_WIDTH]
    q = p[..., FOURIER_WIDTH:FOURIER_WIDTH + ATTN_WIDTH].reshape(b, l, N_HEADS, HEAD_DIM)
    k = p[..., FOURIER_WIDTH + ATTN_WIDTH:FOURIER_WIDTH + ATTN_WIDTH + KV_WIDTH].reshape(b, l, N_KV_HEADS, HEAD_DIM)
    v = p[..., FOURIER_WIDTH + ATTN_WIDTH + KV_WIDTH:].reshape(b, l, N_KV_HEADS, HEAD_DIM)
    return f, q, k, v


def window_attention(q, k, v, kc, vc, sink):
    b, s = q.shape[0], q.shape[1]
    nb = s // BLOCK
    n_ctx = kc.shape[1]
    scale = HEAD_DIM ** -0.5
    qb = q.reshape(b, nb, BLOCK, N_KV_HEADS, HEAD_GROUP, HEAD_DIM)

    def band(t):
        tp = jnp.pad(t, ((0, 0), (WINDOW, WINDOW), (0, 0), (0, 0)))
        tp = tp.reshape(b, nb + 2, BLOCK, N_KV_HEADS, HEAD_DIM)
        return jnp.concatenate([tp[:, :-2], tp[:, 1:-1], tp[:, 2:]], axis=2)

    kw = band(k)
    vw = band(v)
    qi = jnp.arange(BLOCK)[:, None]
    kj = jnp.arange(3 * BLOCK)[None, :]
    in_band = (kj - qi >= 0) & (kj - qi <= 2 * WINDOW)
    kpos = jnp.arange(nb)[:, None] * BLOCK - WINDOW + jnp.arange(3 * BLOCK)[None, :]
    in_seq = (kpos >= 0) & (kpos < s)
    mask = in_band[None] & in_seq[:, None, :]

    s_loc = jnp.einsum('bnqhgd,bnkhd->bnhgqk', qb, kw).astype(jnp.float32) * scale
    s_loc = jnp.where(mask[None, :, None, None], s_loc, NEG_INF)
    s_ctx = jnp.einsum('bnqhgd,bchd->bnhgqc', qb, kc).astype(jnp.float32) * scale
    s_sink = jnp.broadcast_to(sink.astype(jnp.float32).reshape(1, 1, N_KV_HEADS, HEAD_GROUP, 1, 1),
                              s_loc.shape[:-1] + (1,))
    p = jax.nn.softmax(jnp.concatenate([s_loc, s_ctx, s_sink], axis=-1), axis=-1).astype(q.dtype)
    p_loc = p[..., :3 * BLOCK]
    p_ctx = p[..., 3 * BLOCK:3 * BLOCK + n_ctx]
    out = (jnp.einsum('bnhgqk,bnkhd->bnqhgd', p_loc, vw)
           + jnp.einsum('bnhgqc,bchd->bnqhgd', p_ctx, vc))
    return out.reshape(b, s, ATTN_WIDTH)


def context_attention(qc, kc, vc, sink):
    b, n_ctx = qc.shape[0], qc.shape[1]
    scale = HEAD_DIM ** -0.5
    qg = qc.reshape(b, n_ctx, N_KV_HEADS, HEAD_GROUP, HEAD_DIM)
    sc = jnp.einsum('bqhgd,bkhd->bhgqk', qg, kc).astype(jnp.float32) * scale
    s_sink = jnp.broadcast_to(sink.astype(jnp.float32).reshape(1, N_KV_HEADS, HEAD_GROUP, 1, 1),
                              sc.shape[:-1] + (1,))
    p = jax.nn.softmax(jnp.concatenate([sc, s_sink], axis=-1), axis=-1)[..., :n_ctx].astype(qc.dtype)
    out = jnp.einsum('bhgqk,bkhd->bqhgd', p, vc)
    return out.reshape(b, n_ctx, ATTN_WIDTH)


def fourier_attention_mixer(h, hc, w_in, w_out, sink, rope, need_ctx):
    f, q, k, v = split_projection(h @ w_in)
    fc, qc, kc, vc = split_projection(hc @ w_in)
    q = apply_axial_rope(q, rope)
    k = apply_axial_rope(k, rope)
    y = jnp.concatenate([fourier_mix(f), window_attention(q, k, v, kc, vc, sink)], axis=-1) @ w_out
    if not need_ctx:
        return y, None
    yc = jnp.concatenate([fourier_mix(fc), context_attention(qc, kc, vc, sink)], axis=-1) @ w_out
    return y, yc


def _linear_recurrence(e1, e2):
    a1, b1 = e1
    a2, b2 = e2
    return a1 * a2, a2 * b1 + b2


def s5_scan(u_seq, a_re, a_im, log_dt, b_re, b_im):
    lam = lax.complex(a_re.astype(jnp.float32), a_im.astype(jnp.float32))
    dt = jnp.exp(log_dt.astype(jnp.float32))[:, None]
    a_bar = jnp.exp(lam * dt)
    b_bar = ((a_bar - 1.0) / lam)[..., None] * lax.complex(b_re.astype(jnp.float32), b_im.astype(jnp.float32))
    b, l, _ = u_seq.shape
    ug = u_seq.reshape(b, l, SSM_GROUPS, SSM_GROUP_DIM)
    bu = lax.complex(jnp.einsum('blgc,gpc->lbgp', ug, b_bar.real),
                     jnp.einsum('blgc,gpc->lbgp', ug, b_bar.imag))
    a_seq = jnp.broadcast_to(a_bar[None, None], (l, 1, SSM_GROUPS, SSM_STATE))
    _, states = lax.associative_scan(_linear_recurrence, (a_seq, bu), axis=0)
    return states


def s5_readout(states, c_re, c_im):
    y = (jnp.einsum('lbgp,gcp->blgc', states.real, c_re.astype(jnp.float32))
         - jnp.einsum('lbgp,gcp->blgc', states.imag, c_im.astype(jnp.float32)))
    return y.reshape(y.shape[0], y.shape[1], D_MODEL)


def _gelu_glu(y, glu_w, dtype):
    z = jax.nn.gelu(y).astype(dtype) @ glu_w
    return z[..., :D_MODEL] * jax.nn.sigmoid(z[..., D_MODEL:])


def s5_mixer(h, hc, a_re, a_im, log_dt, b_re, b_im, c_re, c_im, d_skip, glu_w, need_ctx):
    n_ctx = hc.shape[1]
    u = h.astype(jnp.float32)
    uc = hc.astype(jnp.float32)
    d = d_skip.astype(jnp.float32)
    st_f = s5_scan(jnp.concatenate([uc, u], axis=1), a_re[0], a_im[0], log_dt[0], b_re[0], b_im[0])
    st_b = s5_scan(jnp.flip(jnp.concatenate([u, uc], axis=1), axis=1), a_re[1], a_im[1], log_dt[1], b_re[1], b_im[1])
    y = (s5_readout(st_f[n_ctx:], c_re[0], c_im[0])
         + jnp.flip(s5_readout(st_b[n_ctx:], c_re[1], c_im[1]), axis=1) + d * u)
    out = _gelu_glu(y, glu_w, h.dtype)
    if not need_ctx:
        return out, None
    yc = (s5_readout(st_f[:n_ctx], c_re[0], c_im[0])
          + jnp.flip(s5_readout(st_b[:n_ctx], c_re[1], c_im[1]), axis=1) + d * uc)
    return out, _gelu_glu(yc, glu_w, hc.dtype)


def sq_relu_mlp(h, w1, w2):
    a = jax.nn.relu(h @ w1)
    return (a * a) @ w2


def setup_inputs(seed: int = 0) -> dict:
    key = jax.random.key(seed)
    ks = jax.random.split(key, 24)
    f32 = jnp.float32
    nrm = lambda k, shape, s: jax.random.normal(k, shape, f32) * s
    x = nrm(ks[0], (BATCH, SEQ, D_MODEL), 1.0)
    c = nrm(ks[1], (BATCH, D_MODEL), 1.0)
    ctx = nrm(ks[2], (BATCH, CTX_LEN, D_MODEL), 1.0)
    c_ctx = nrm(ks[3], (D_MODEL,), 1.0)
    mod_w = nrm(ks[4], (DEPTH, D_MODEL, N_MOD * D_MODEL), 0.3 * D_MODEL ** -0.5)
    mod_b = nrm(ks[5], (DEPTH, N_MOD * D_MODEL), 0.01)
    mix_pre_g = 1.0 + nrm(ks[6], (DEPTH, D_MODEL), 0.05)
    mix_post_g = 1.0 + nrm(ks[7], (DEPTH, D_MODEL), 0.05)
    ffn_pre_g = 1.0 + nrm(ks[8], (DEPTH, D_MODEL), 0.05)
    ffn_post_g = 1.0 + nrm(ks[9], (DEPTH, D_MODEL), 0.05)
    ffn_w1 = nrm(ks[10], (DEPTH, D_MODEL, D_FF), D_MODEL ** -0.5)
    ffn_w2 = nrm(ks[11], (DEPTH, D_FF, D_MODEL), D_FF ** -0.5)
    even_w_in = nrm(ks[12], (N_EVEN, D_MODEL, IN_WIDTH), D_MODEL ** -0.5)
    even_w_out = nrm(ks[13], (N_EVEN, MIX_WIDTH, D_MODEL), MIX_WIDTH ** -0.5)
    even_sink = nrm(ks[14], (N_EVEN, N_HEADS), 0.5)
    ssm_shape = (N_ODD, 2, SSM_GROUPS, SSM_STATE)
    ssm_a_re = -0.5 * (1.0 + 0.05 * jax.random.uniform(ks[15], ssm_shape, f32, -1.0, 1.0))
    ssm_a_im = (math.pi * jnp.arange(SSM_STATE, dtype=f32)) + nrm(ks[16], ssm_shape, 0.01)
    ssm_log_dt = jax.random.uniform(ks[17], (N_ODD, 2, SSM_GROUPS), f32, math.log(DT_MIN), math.log(DT_MAX))
    ssm_b_re = nrm(ks[18], (N_ODD, 2, SSM_GROUPS, SSM_STATE, SSM_GROUP_DIM), (2 * SSM_GROUP_DIM) ** -0.5)
    ssm_b_im = nrm(ks[19], (N_ODD, 2, SSM_GROUPS, SSM_STATE, SSM_GROUP_DIM), (2 * SSM_GROUP_DIM) ** -0.5)
    ssm_c_re = nrm(ks[20], (N_ODD, 2, SSM_GROUPS, SSM_GROUP_DIM, SSM_STATE), SSM_STATE ** -0.5)
    ssm_c_im = nrm(ks[21], (N_ODD, 2, SSM_GROUPS, SSM_GROUP_DIM, SSM_STATE), SSM_STATE ** -0.5)
    ssm_d = nrm(ks[22], (N_ODD, D_MODEL), 1.0)
    ssm_glu_w = nrm(ks[23], (N_ODD, D_MODEL, 2 * D_MODEL), D_MODEL ** -0.5)
    return {"x": x, "c": c, "ctx": ctx, "c_ctx": c_ctx,
            "mod_w": mod_w, "mod_b": mod_b,
            "mix_pre_g": mix_pre_g, "mix_post_g": mix_post_g,
            "ffn_pre_g": ffn_pre_g, "ffn_post_g": ffn_post_g,
            "ffn_w1": ffn_w1, "ffn_w2": ffn_w2,
            "even_w_in": even_w_in, "even_w_out": even_w_out, "even_sink": even_sink,
            "ssm_a_re": ssm_a_re, "ssm_a_im": ssm_a_im, "ssm_log_dt": ssm_log_dt,
            "ssm_b_re": ssm_b_re, "ssm_b_im": ssm_b_im,
            "ssm_c_re": ssm_c_re, "ssm_c_im": ssm_c_im,
            "ssm_d": ssm_d, "ssm_glu_w": ssm_glu_w}


def reference(x, c, ctx, c_ctx, mod_w, mod_b, mix_pre_g, mix_post_g, ffn_pre_g, ffn_post_g,
              ffn_w1, ffn_w2, even_w_in, even_w_out, even_sink,
              ssm_a_re, ssm_a_im, ssm_log_dt, ssm_b_re, ssm_b_im, ssm_c_re, ssm_c_im,
              ssm_d, ssm_glu_w):
    rope = axial_rope_tables(x.shape[1])
    xc = ctx.astype(x.dtype)
    for layer in range(DEPTH):
        need_ctx = layer < DEPTH - 1
        sh1, sc1, g1, sh2, sc2, g2 = modulation(c, mod_w[layer], mod_b[layer])
        sh1c, sc1c, g1c, sh2c, sc2c, g2c = modulation(c_ctx, mod_w[layer], mod_b[layer])
        h = rmsnorm(x, mix_pre_g[layer]) * (1.0 + sc1) + sh1
        hc = rmsnorm(xc, mix_pre_g[layer]) * (1.0 + sc1c) + sh1c
        i = layer // 2
        if layer % 2 == 0:
            y, yc = fourier_attention_mixer(h, hc, even_w_in[i], even_w_out[i], even_sink[i], rope, need_ctx)
        else:
            y, yc = s5_mixer(h, hc, ssm_a_re[i], ssm_a_im[i], ssm_log_dt[i], ssm_b_re[i], ssm_b_im[i],
                             ssm_c_re[i], ssm_c_im[i], ssm_d[i], ssm_glu_w[i], need_ctx)
        x = x + g1 * rmsnorm(y, mix_post_g[layer])
        h2 = rmsnorm(x, ffn_pre_g[layer]) * (1.0 + sc2) + sh2
        x = x + g2 * rmsnorm(sq_relu_mlp(h2, ffn_w1[layer], ffn_w2[layer]), ffn_post_g[layer])
        if need_ctx:
            xc = xc + g1c * rmsnorm(yc, mix_post_g[layer])
            hc2 = rmsnorm(xc, ffn_pre_g[layer]) * (1.0 + sc2c) + sh2c
            xc = xc + g2c * rmsnorm(sq_relu_mlp(hc2, ffn_w1[layer], ffn_w2[layer]), ffn_post_g[layer])
    return x
```

```python
import contextlib
import numpy as np
import concourse.bass as bass
import concourse.mybir as mybir
from concourse.bass_utils import run_bass_kernel_spmd

F32 = mybir.dt.float32
BF16 = mybir.dt.bfloat16
I32 = mybir.dt.int32
ALU = mybir.AluOpType
AF = mybir.ActivationFunctionType
AX = mybir.AxisListType

S5_STAGE = 2
S5_SUB = 2
S5_NW = None
MIXER_ENABLED = True
SAME_ENGINE_SYNC = {"dve": True, "act": True, "pool": False, "pe": False}
DMA_RING = 8


class KB:
    def __init__(self, nc, es):
        self.nc = nc
        self.es = es
        self.raw = {"pe": nc.tensor, "dve": nc.vector, "act": nc.scalar, "pool": nc.gpsimd, "sp": nc.sync}
        self.sem = {}
        self.cnt = {}
        for e in ("pe", "dve", "act", "pool"):
            self.sem[e] = es.enter_context(nc.semaphore("s_" + e))
            self.cnt[e] = 0
        self.dring = {}
        self.dcnt = {}
        for q in ("sp", "act", "pool"):
            self.dring[q] = [es.enter_context(nc.semaphore("d_%s%d" % (q, i))) for i in range(DMA_RING)]
            self.dcnt[q] = 0
        self.seen = {e: {} for e in self.raw}
        self.lastw = {}
        self.readers = {}
        self.nwaits = 0
        self.ninst = 0

    def sb(self, es, name, shape, dt):
        self.uid = getattr(self, "uid", 0) + 1
        return es.enter_context(self.nc.sbuf_tensor("%s_u%d" % (name, self.uid), list(shape), dt))

    def ps(self, es, name, shape, dt=F32):
        return es.enter_context(self.nc.psum_tensor(name, list(shape), dt))

    def _collect(self, eng, reads, writes):
        need = {}

        def add(tok):
            if tok is None:
                return
            s, v, src = tok
            if src == eng and (not SAME_ENGINE_SYNC.get(eng, True) or v > self.cnt[eng]):
                return
            if need.get(s, (0,))[0] < v:
                need[s] = (v, src)

        for r in reads:
            add(self.lastw.get(r))
        for w in writes:
            add(self.lastw.get(w))
            for tok in self.readers.get(w, ()):
                add(tok)
        return need

    def _emit_waits(self, eng, need):
        seen = self.seen[eng]
        for s, (v, src) in need.items():
            if seen.get(s, 0) >= v:
                continue
            self.raw[eng].wait_ge(s, v)
            seen[s] = v
            self.nwaits += 1

    def _record(self, tok, reads, writes):
        for w in writes:
            self.lastw[w] = tok
            self.readers[w] = []
        for r in reads:
            if r in writes:
                continue
            self.readers.setdefault(r, []).append(tok)

    def op(self, eng, fn, reads=(), writes=(), inc=True):
        need = self._collect(eng, reads, writes)
        self._emit_waits(eng, need)
        inst = fn()
        self.ninst += 1
        if inc:
            self.cnt[eng] += 1
            inst.then_inc(self.sem[eng], 1)
            tok = (self.sem[eng], self.cnt[eng], eng)
        else:
            tok = (self.sem[eng], self.cnt[eng] + 1, eng)
        self._record(tok, reads, writes)
        return inst

    def dma(self, q, out, in_, reads=(), writes=(), **kw):
        need = self._collect(q, reads, writes)
        self._emit_waits(q, need)
        i = self.dcnt[q]
        self.dcnt[q] += 1
        s = self.dring[q][i % DMA_RING]
        v = 16 * (i // DMA_RING + 1)
        inst = self.raw[q].dma_start(out=out, in_=in_, **kw)
        inst.then_inc(s, 16)
        self.ninst += 1
        self._record((s, v, "dma_" + q), reads, writes)
        return inst

    def barrier(self):
        need = {}
        for e in ("pe", "dve", "act", "pool"):
            if self.cnt[e] > 0:
                need[self.sem[e]] = (self.cnt[e], "x")
        for q in ("sp", "act", "pool"):
            n = self.dcnt[q]
            for r in range(DMA_RING):
                cntr = (n - r + DMA_RING - 1) // DMA_RING if n > r else 0
                if cntr > 0:
                    need[self.dring[q][r]] = (16 * cntr, "x")
        for e in ("pe", "dve", "act", "pool", "sp"):
            self._emit_waits(e, need)
        self.lastw = {}
        self.readers = {}


def bc_mid(ap2, n):
    a = ap2.ap
    return bass.AP(ap2.tensor, ap2.offset, [list(a[0]), [0, n]] + [list(x) for x in a[1:]])


D = 1024
NT = 2304
NCTX = 256
TB = 256
NBLK = NT // TB
DFF = 4096
EPS = 1e-6


def build(nlayers=4, mixer=True, layers=None, debug=False):
    nc = bass.Bass("TRN2", target_bir_lowering=False)
    dt_in = lambda n, s: nc.dram_tensor(n, list(s), F32, kind="ExternalInput").ap()
    x_d = dt_in("x", [2048, D])
    ctx_d = dt_in("ctx", [NCTX, D])
    cc_d = dt_in("cc", [2, D])
    mod_w_d = dt_in("mod_w", [4, D, 6 * D])
    mod_b_d = dt_in("mod_b", [4, 6 * D])
    gains_d = dt_in("gains", [4, 4, D])
    w1_d = dt_in("ffn_w1", [4, D, DFF])
    w2_d = dt_in("ffn_w2", [4, DFF, D])
    w_in_d = dt_in("even_w_in", [2, D, 1280])
    w_out_d = dt_in("even_w_out", [2, D, D])
    even_sink_d = dt_in("even_sink", [2, 8])
    a_re_d = dt_in("ssm_a_re", [2, 2, 64, 64])
    a_im_d = dt_in("ssm_a_im", [2, 2, 64, 64])
    ldt_d = dt_in("ssm_log_dt", [2, 2, 64])
    b_re_d = dt_in("ssm_b_re", [2, 2, 64, 64, 16])
    b_im_d = dt_in("ssm_b_im", [2, 2, 64, 64, 16])
    c_re_d = dt_in("ssm_c_re", [2, 2, 64, 16, 64])
    c_im_d = dt_in("ssm_c_im", [2, 2, 64, 16, 64])
    dsk_d = dt_in("ssm_d", [2, D])
    glu_d = dt_in("ssm_glu_w", [2, D, 2 * D])
    out_d = nc.dram_tensor("out", [2048, D], F32, kind="ExternalOutput").ap()
    YF = nc.dram_tensor("YF", [2, D, NT], F32, kind=("ExternalOutput" if debug else "Internal")).ap()
    XT = nc.dram_tensor("XT", [D, NT], F32).ap()
    YT = nc.dram_tensor("YT", [D, NT], F32, kind=("ExternalOutput" if debug else "Internal")).ap()
    CL = nc.dram_tensor("CLtab", [2048, 2048], BF16, kind=("ExternalOutput" if debug else "Internal")).ap()
    SLn = nc.dram_tensor("SLtab", [2048, 2048], BF16, kind=("ExternalOutput" if debug else "Internal")).ap()
    ROPC = nc.dram_tensor("ROPC", [128, 2048], F32, kind=("ExternalOutput" if debug else "Internal")).ap()
    ROPS = nc.dram_tensor("ROPS", [128, 2048], F32, kind=("ExternalOutput" if debug else "Internal")).ap()
    XTv = XT.rearrange("(ct p) t -> p ct t", p=128)
    YTv = YT.rearrange("(ct p) t -> p ct t", p=128)

    with contextlib.ExitStack() as es:
        k = KB(nc, es)
        V, S, P, G = nc.vector, nc.scalar, nc.tensor, nc.gpsimd

        def dump(name, ap, keys):
            if not debug:
                return
            dt_ = nc.dram_tensor(name, list(ap.shape), ap.dtype, kind="ExternalOutput").ap()
            k.dma("sp", dt_, ap, reads=keys, writes=["dbg_" + name])
        ident = k.sb(es, "ident", [128, 128], F32)
        onesm = k.sb(es, "onesm", [128, 128], F32)
        k.op("pool", lambda: G.memset(ident[:], 0.0), writes=["ident"])
        k.op("pool", lambda: G.affine_select(out=ident[:], in_=ident[:], compare_op=ALU.not_equal, fill=1.0,
                                             base=0, pattern=[[-1, 128]], channel_multiplier=1),
             reads=["ident"], writes=["ident"])
        k.op("pool", lambda: G.memset(onesm[:], 1.0 / D), writes=["onesm"])
        psb = [k.ps(es, "psb%d" % i, [128, 512], F32) for i in range(8)]
        MV = k.sb(es, "MV", [128, 6, 8, 2], F32)
        GN = k.sb(es, "GN", [128, 4, 4, 8], F32)
        SC = k.sb(es, "SC", [128, 8, 2], F32)
        SCb = k.sb(es, "SCb", [128, 8, 2], BF16)
        PRM = k.sb(es, "PRM", [128, 6, 8, 2], F32)
        k.dma("sp", GN[:].rearrange("p a l c -> p (a l) c"),
              gains_d.rearrange("a l (c p) -> p (a l) c", p=128), writes=["GN"], allow_slow_non_contiguous=True)
        for j in range(2):
            k.dma("sp", SC[:, :, j], cc_d[j].rearrange("(c p) -> p c", p=128), writes=["SC"], allow_slow_non_contiguous=True)
        k.op("act", lambda: S.activation(SCb[:], SC[:], AF.Silu), reads=["SC"], writes=["SCb"])


        MAGIC = 12582912.0
        TWO_PI = float(2 * np.pi)
        if mixer and any(l % 2 == 0 for l in (layers if layers is not None else range(nlayers))):
            with contextlib.ExitStack() as ph:
                cidx = k.sb(ph, "cidx", [128, 2048], F32)
                prow = k.sb(ph, "prow", [128, 1], F32)
                tcol = k.sb(ph, "tcol", [128, 1], F32)
                k.op("pool", lambda: G.iota(cidx[:], pattern=[[1, 2048]], base=0, channel_multiplier=0, allow_small_or_imprecise_dtypes=True), writes=["cidx"])
                k.op("pool", lambda: G.iota(prow[:], pattern=[[0, 1]], base=0, channel_multiplier=1, allow_small_or_imprecise_dtypes=True), writes=["prow"])
                uu = [k.sb(ph, "uu%d" % z, [128, 2048], F32) for z in range(2)]
                nn = [k.sb(ph, "nn%d" % z, [128, 2048], F32) for z in range(2)]
                tb = [k.sb(ph, "tb%d" % z, [128, 2048], BF16) for z in range(2)]
                for tt in range(16):
                    k.op("dve", lambda: V.tensor_scalar(tcol[:], prow[:], float(tt * 128), None, ALU.add), reads=["prow", "tcol"], writes=["tcol"])
                    for z, (tab, shift, scl, eng, E) in enumerate(((CL, 0.25, TWO_PI, "dve", V), (SLn, 0.0, -TWO_PI, "pool", G))):
                        u_, n_, ku, kn = uu[z], nn[z], "uu%d" % z, "nn%d" % z
                        k.op(eng, lambda: E.tensor_scalar(u_[:], cidx[:], tcol[:, 0:1], 1.0 / 2048, ALU.mult, ALU.mult), reads=["cidx", "tcol", ku], writes=[ku])
                        if shift != 0.0:
                            k.op(eng, lambda: E.tensor_scalar(u_[:], u_[:], shift, None, ALU.add), reads=[ku], writes=[ku])
                        k.op(eng, lambda: E.tensor_scalar(n_[:], u_[:], MAGIC, None, ALU.add), reads=[ku, kn], writes=[kn])
                        k.op(eng, lambda: E.tensor_scalar(n_[:], n_[:], -MAGIC, None, ALU.add), reads=[kn], writes=[kn])
                        k.op(eng, lambda: E.tensor_tensor(u_[:], u_[:], n_[:], ALU.subtract), reads=[ku, kn], writes=[ku])
                        k.op(eng, lambda: E.tensor_scalar(u_[:], u_[:], -0.49999, 0.49999, ALU.max, ALU.min), reads=[ku], writes=[ku])
                        k.op("act", lambda: S.activation(tb[z][:], u_[:], AF.Sin, scale=scl), reads=[ku, "tb%d" % z], writes=["tb%d" % z])
                        k.dma("sp", tab[tt * 128:(tt + 1) * 128, :], tb[z][:], reads=["tb%d" % z], writes=["TAB"])
                k.barrier()
            with contextlib.ExitStack() as ph:
                pidx = k.sb(ph, "rpidx", [128, 1], I32)
                pi2 = k.sb(ph, "rpi2", [128, 1], I32)
                fi = k.sb(ph, "rfi", [128, 1], F32)
                invp = k.sb(ph, "rinvp", [128, 1], F32)
                axs = k.sb(ph, "raxs", [128, 1], F32)
                sgn = k.sb(ph, "rsgn", [128, 1], F32)
                k.op("pool", lambda: G.iota(pidx[:], pattern=[[0, 1]], base=0, channel_multiplier=1), writes=["pidx"])
                k.op("dve", lambda: V.tensor_scalar(pi2[:], pidx[:], 15, None, ALU.bitwise_and), reads=["pidx"], writes=["pi2"])
                k.op("dve", lambda: V.tensor_copy(fi[:], pi2[:]), reads=["pi2"], writes=["fi"])
                k.op("act", lambda: S.activation(invp[:], fi[:], AF.Exp, scale=-float(np.log(10000.0)) / 16.0), reads=["fi"], writes=["invp"])
                k.op("dve", lambda: V.tensor_scalar(pi2[:], pidx[:], 5, 1, ALU.arith_shift_right, ALU.bitwise_and), reads=["pidx", "fi"], writes=["pi2"])
                k.op("dve", lambda: V.tensor_copy(axs[:], pi2[:]), reads=["pi2"], writes=["axs"])
                k.op("dve", lambda: V.tensor_scalar(pi2[:], pidx[:], 4, 1, ALU.arith_shift_right, ALU.bitwise_and), reads=["pidx", "axs"], writes=["pi2"])
                k.op("dve", lambda: V.tensor_copy(sgn[:], pi2[:]), reads=["pi2"], writes=["sgn"])
                k.op("dve", lambda: V.tensor_scalar(sgn[:], sgn[:], 2.0, -1.0, ALU.mult, ALU.add), reads=["sgn"], writes=["sgn"])
                rowp = k.sb(ph, "rowp", [128, 2048], F32)
                colp = k.sb(ph, "colp", [128, 2048], F32)
                ang = k.sb(ph, "rang", [128, 2048], F32)
                u_ = k.sb(ph, "ru", [128, 2048], F32)
                n_ = k.sb(ph, "rn", [128, 2048], F32)
                k.op("pool", lambda: G.iota(rowp[:], pattern=[[1, 32], [0, 64]], base=0, channel_multiplier=0, allow_small_or_imprecise_dtypes=True), writes=["rowp"])
                k.op("pool", lambda: G.iota(colp[:], pattern=[[0, 32], [1, 64]], base=0, channel_multiplier=0, allow_small_or_imprecise_dtypes=True), writes=["colp"])
                k.op("dve", lambda: V.tensor_tensor(colp[:], colp[:], rowp[:], ALU.subtract), reads=["colp", "rowp"], writes=["colp"])
                k.op("dve", lambda: V.scalar_tensor_tensor(out=ang[:], in0=colp[:], scalar=axs[:, 0:1], in1=rowp[:], op0=ALU.mult, op1=ALU.add), reads=["colp", "rowp", "axs"], writes=["ang"])
                k.op("dve", lambda: V.tensor_scalar(ang[:], ang[:], invp[:, 0:1], 1.0 / TWO_PI, ALU.mult, ALU.mult), reads=["ang", "invp"], writes=["ang"])
                for (tab, shift) in ((ROPC, 0.25), (ROPS, 0.0)):
                    k.op("dve", lambda: V.tensor_scalar(u_[:], ang[:], shift, None, ALU.add), reads=["ang", "ru"], writes=["ru"])
                    k.op("dve", lambda: V.tensor_scalar(n_[:], u_[:], MAGIC, None, ALU.add), reads=["ru", "rn"], writes=["rn"])
                    k.op("dve", lambda: V.tensor_scalar(n_[:], n_[:], -MAGIC, None, ALU.add), reads=["rn"], writes=["rn"])
                    k.op("dve", lambda: V.tensor_tensor(u_[:], u_[:], n_[:], ALU.subtract), reads=["rn", "ru"], writes=["ru"])
                    k.op("dve", lambda: V.tensor_scalar(u_[:], u_[:], -0.49999, 0.49999, ALU.max, ALU.min), reads=["ru"], writes=["ru"])
                    k.op("act", lambda: S.activation(n_[:], u_[:], AF.Sin, scale=TWO_PI), reads=["ru", "rn"], writes=["rn"])
                    if shift == 0.0:
                        k.op("dve", lambda: V.tensor_scalar(n_[:], n_[:], sgn[:, 0:1], None, ALU.mult), reads=["rn", "sgn"], writes=["rn"])
                    k.dma("sp", tab[:, :], n_[:], reads=["rn"], writes=["ROP"])
                k.barrier()
        mask_ge = k.sb(es, "mask_ge", [128, 128], BF16)
        mask_le = k.sb(es, "mask_le", [128, 128], BF16)
        ones64 = k.sb(es, "ones64", [128, 64], BF16)
        k.op("pool", lambda: G.memset(mask_ge[:], 1.0), writes=["mask_ge"])
        k.op("pool", lambda: G.affine_select(out=mask_ge[:], in_=mask_ge[:], compare_op=ALU.is_ge, fill=0.0, base=0, pattern=[[-1, 128]], channel_multiplier=1), reads=["mask_ge"], writes=["mask_ge"])
        k.op("pool", lambda: G.memset(mask_le[:], 1.0), writes=["mask_le"])
        k.op("pool", lambda: G.affine_select(out=mask_le[:], in_=mask_le[:], compare_op=ALU.is_ge, fill=0.0, base=0, pattern=[[1, 128]], channel_multiplier=-1), reads=["mask_le"], writes=["mask_le"])
        k.op("pool", lambda: G.memset(ones64[:], 1.0), writes=["ones64"])
        with contextlib.ExitStack() as ph:
            xin = [k.sb(ph, "xin%d" % i, [128, D], F32) for i in range(2)]
            xst = [k.sb(ph, "xst%d" % i, [128, 8, 128], F32) for i in range(2)]
            for tt in range(NT // 128):
                b = tt % 2
                src = ctx_d[tt * 128:(tt + 1) * 128, :] if tt < 2 else x_d[(tt - 2) * 128:(tt - 1) * 128, :]
                k.dma("sp", xin[b][:], src, writes=["xin%d" % b])
                for half in range(2):
                    pb = psb[(tt * 2 + half) % 8]
                    for c4 in range(4):
                        ct = half * 4 + c4
                        k.op("pe", lambda: P.transpose(pb[:, c4 * 128:(c4 + 1) * 128], xin[b][:, ct * 128:(ct + 1) * 128], ident[:]),
                             reads=["xin%d" % b, "ident"], writes=["psb%d" % ((tt * 2 + half) % 8)], inc=(c4 == 3))
                    eng = "act" if half == 0 else "dve"
                    if eng == "act":
                        k.op("act", lambda: S.copy(xst[b][:, half * 4:(half + 1) * 4, :].rearrange("p c t -> p (c t)"), pb[:]),
                             reads=["psb%d" % ((tt * 2 + half) % 8)], writes=["xst%d_%d" % (b, half)])
                    else:
                        k.op("dve", lambda: V.tensor_copy(xst[b][:, half * 4:(half + 1) * 4, :].rearrange("p c t -> p (c t)"), pb[:]),
                             reads=["psb%d" % ((tt * 2 + half) % 8)], writes=["xst%d_%d" % (b, half)])
                k.dma("sp", XTv[:, :, tt * 128:(tt + 1) * 128], xst[b][:], reads=["xst%d_0" % b, "xst%d_1" % b], writes=["XT"])
            k.barrier()

        for l in (layers if layers is not None else range(nlayers)):
            with contextlib.ExitStack() as ph:
                mw = [k.sb(ph, "mw%d" % i, [128, 8, 512], BF16) for i in range(2)]
                mbias = k.sb(ph, "mbias", [128, 48], F32)
                k.dma("sp", mbias[:], mod_b_d[l].rearrange("(c p) -> p c", p=128), writes=["mbias"], allow_slow_non_contiguous=True)
                for ch in range(12):
                    b = ch % 2
                    k.dma("pool", mw[b][:], mod_w_d[l][:, ch * 512:(ch + 1) * 512].rearrange("(kt p) n -> p kt n", p=128),
                          writes=["mw%d" % b])
                    for s4 in range(4):
                        col = ch * 4 + s4
                        pb = psb[col % 8]
                        for kt in range(8):
                            k.op("pe", lambda: P.matmul(pb[:, 0:2], mw[b][:, kt, s4 * 128:(s4 + 1) * 128], SCb[:, kt, :],
                                                        start=(kt == 0), stop=(kt == 7)),
                                 reads=["mw%d" % b, "SCb"], writes=["psb%d" % (col % 8)], inc=(kt == 7))
                        k.op("dve", lambda: V.tensor_scalar(MV[:, col // 8, col % 8, :], pb[:, 0:2], mbias[:, col:col + 1], None, ALU.add),
                             reads=["psb%d" % (col % 8), "mbias"], writes=["MV"])
                for (o, isc, ish, ig, gpre, gpost) in ((0, 1, 0, 2, 0, 1), (3, 4, 3, 5, 2, 3)):
                    for j in range(2):
                        k.op("dve", lambda: V.scalar_tensor_tensor(out=PRM[:, o, :, j], in0=MV[:, isc, :, j], scalar=1.0, in1=GN[:, gpre, l, :],
                                                                   op0=ALU.add, op1=ALU.mult), reads=["MV", "GN"], writes=["PRM"])
                        k.op("dve", lambda: V.tensor_copy(PRM[:, o + 1, :, j], MV[:, ish, :, j]), reads=["MV"], writes=["PRM"])
                        k.op("dve", lambda: V.tensor_tensor(PRM[:, o + 2, :, j], MV[:, ig, :, j], GN[:, gpost, l, :], ALU.mult),
                             reads=["MV", "GN"], writes=["PRM"])
                k.barrier()

            def rms_bc(ph_tiles, src3, key_src, tag):
                sq, rs, pbank, pkey = ph_tiles
                if src3 is not None:
                    k.op("act", lambda: S.activation(sq[:], src3, AF.Square), reads=[key_src], writes=["sq"])
                for ct in range(8):
                    k.op("pe", lambda: P.matmul(pbank[:, 0:TB], onesm[:], sq[:, ct, :], start=(ct == 0), stop=(ct == 7)),
                         reads=["sq", "onesm"], writes=[pkey], inc=(ct == 7))
                k.op("dve", lambda: V.tensor_scalar(rs[:], pbank[:, 0:TB], EPS, None, ALU.add), reads=[pkey], writes=["rs"])
                k.op("act", lambda: S.activation(rs[:], rs[:], AF.Sqrt), reads=["rs"], writes=["rs"])
                k.op("dve", lambda: V.reciprocal(rs[:], rs[:]), reads=["rs"], writes=["rs"])
                return rs


            def prenorm_to_hT(ph, hT):
                xb = k.sb(ph, "pxb", [128, 8, TB], F32)
                sq = k.sb(ph, "psq", [128, 8, TB], F32)
                tmp = k.sb(ph, "ptmp", [128, 8, TB], F32)
                rs = k.sb(ph, "prs", [128, TB], F32)
                tiles = (sq, rs, psb[6], "psb6")
                for blk in range(NBLK):
                    j = 1 if blk == 0 else 0
                    t0 = blk * TB
                    k.dma("sp", xb[:], XTv[:, :, t0:t0 + TB], reads=["XT"], writes=["xb"])
                    rms_bc(tiles, xb[:], "xb", "p")
                    k.op("dve", lambda: V.tensor_tensor(tmp[:], xb[:], bc_mid(rs[:], 8), ALU.mult), reads=["xb", "rs"], writes=["tmp"])
                    for ct in range(8):
                        k.op("act", lambda: S.activation(hT[:, ct, t0:t0 + TB], tmp[:, ct, :], AF.Identity, bias=PRM[:, 1, ct, j:j + 1],
                                                         scale=PRM[:, 0, ct, j:j + 1]), reads=["tmp", "PRM"], writes=["hT"])

            def s5_mixer(l):
                i = l // 2
                PI = float(np.pi)
                W = 64
                NW = NT // W
                with contextlib.ExitStack() as ph:
                    hT = k.sb(ph, "hT", [128, 8, NT], BF16)
                    with contextlib.ExitStack() as ph2:
                        prenorm_to_hT(ph2, hT)
                        k.barrier()
                    dsk = k.sb(ph, "dsk", [128, 8], F32)
                    k.dma("sp", dsk[:], dsk_d[i].rearrange("(c p) -> p c", p=128), writes=["dsk"], allow_slow_non_contiguous=True)
                    sc = contextlib.ExitStack()
                    WinT = [[k.sb(sc, "WinT%d%d" % (d, ri), [128, 8, 128], BF16) for ri in range(2)] for d in range(2)]
                    CwQ = [[k.sb(sc, "CwQ%d%d" % (d, ri), [128, 32, 128], BF16) for ri in range(2)] for d in range(2)]
                    Gt = [k.sb(sc, "Gt%d" % d, [128, 2, 64], F32) for d in range(2)]
                    with contextlib.ExitStack() as pp:
                        def t32(n):
                            return k.sb(pp, n, [128, 32], F32)
                        twopi = t32("twopi")
                        k.op("pool", lambda: G.memset(twopi[:], 2 * PI), writes=["twopi"])
                        pidx = k.sb(pp, "pidx", [128, 1], I32)
                        modd = k.sb(pp, "modd", [128, 1], F32)
                        mevn = k.sb(pp, "mevn", [128, 1], F32)
                        nodd = k.sb(pp, "nodd", [128, 1], F32)
                        nevn = k.sb(pp, "nevn", [128, 1], F32)
                        k.op("pool", lambda: G.iota(pidx[:], pattern=[[0, 1]], base=0, channel_multiplier=1), writes=["pidx"])
                        k.op("dve", lambda: V.tensor_scalar(pidx[:], pidx[:], 4, 1, ALU.arith_shift_right, ALU.bitwise_and), reads=["pidx"], writes=["pidx"])
                        k.op("dve", lambda: V.tensor_copy(modd[:], pidx[:]), reads=["pidx"], writes=["modd"])
                        k.op("dve", lambda: V.tensor_scalar(mevn[:], modd[:], -1.0, 1.0, ALU.mult, ALU.add), reads=["modd"], writes=["mevn"])
                        k.op("dve", lambda: V.tensor_scalar(nodd[:], modd[:], -1.0, None, ALU.mult), reads=["modd"], writes=["nodd"])
                        k.op("dve", lambda: V.tensor_scalar(nevn[:], mevn[:], -1.0, None, ALU.mult), reads=["mevn"], writes=["nevn"])
                        Br = k.sb(pp, "Br", [128, 32, 32], F32)
                        Bi = k.sb(pp, "Bi", [128, 32, 32], F32)
                        BbR = k.sb(pp, "BbR", [128, 32, 32], F32)
                        BbI = k.sb(pp, "BbI", [128, 32, 32], F32)
                        T1 = k.sb(pp, "T1", [128, 32, 32], F32)
                        Cn = k.sb(pp, "Cn", [128, 8, 64], F32)
                        Cblk = k.sb(pp, "Cblk", [128, 8, 128], F32)
                        lr, li, dtt, tq, mag, ang, sa, sinv, cosv, Ar, Ai, am1, n2, kr, ki, u1 = [t32("p%d" % z) for z in range(16)]
                        for d in range(2):
                            def dve(fn, r, w):
                                k.op("dve", fn, reads=r, writes=w)
                            def act(fn, r, w):
                                k.op("act", fn, reads=r, writes=w)
                            k.dma("sp", lr[:], a_re_d[i, d].rearrange("(q a) p -> (a p) q", a=2), writes=["lr"], allow_slow_non_contiguous=True)
                            k.dma("sp", li[:], a_im_d[i, d].rearrange("(q a) p -> (a p) q", a=2), writes=["li"], allow_slow_non_contiguous=True)
                            for g2 in range(2):
                                base = ldt_d[i, d]
                                src = bass.AP(base.tensor, base.offset + g2, [[0, 64], [2, 32]])
                                k.dma("sp", dtt[g2 * 64:(g2 + 1) * 64, :], src, writes=["dtt"], allow_slow_non_contiguous=True)
                            act(lambda: S.activation(dtt[:], dtt[:], AF.Exp), ["dtt"], ["dtt"])
                            dve(lambda: V.tensor_tensor(tq[:], lr[:], dtt[:], ALU.mult), ["lr", "dtt"], ["tq"])
                            act(lambda: S.activation(mag[:], tq[:], AF.Exp), ["tq"], ["mag"])
                            dve(lambda: V.tensor_tensor(ang[:], li[:], dtt[:], ALU.mult), ["li", "dtt"], ["ang"])
                            MAGIC = 12582912.0
                            PIC = 3.1415925
                            for (dst, shift, tag) in ((sinv, 0.0, "s"), (cosv, 0.5 * PI, "c")):
                                dve(lambda: V.tensor_scalar(u1[:], ang[:], shift, None, ALU.add), ["ang", "sa", "u1"], ["u1"])
                                dve(lambda: V.tensor_scalar(sa[:], u1[:], 1.0 / (2 * PI), None, ALU.mult), ["u1", "sa"], ["sa"])
                                dve(lambda: V.tensor_scalar(sa[:], sa[:], MAGIC, None, ALU.add), ["sa"], ["sa"])
                                dve(lambda: V.tensor_scalar(sa[:], sa[:], -MAGIC, None, ALU.add), ["sa"], ["sa"])
                                dve(lambda: V.scalar_tensor_tensor(out=sa[:], in0=sa[:], scalar=-2 * PI, in1=u1[:], op0=ALU.mult, op1=ALU.add), ["sa", "u1"], ["sa"])
                                dve(lambda: V.tensor_scalar(sa[:], sa[:], -PIC, PIC, ALU.max, ALU.min), ["sa"], ["sa"])
                                act(lambda: S.activation(dst[:], sa[:], AF.Sin), ["sa"], ["sinv" if tag == "s" else "cosv"])
                            dve(lambda: V.tensor_tensor(Ar[:], mag[:], cosv[:], ALU.mult), ["mag", "cosv"], ["Ar"])
                            dve(lambda: V.tensor_tensor(Ai[:], mag[:], sinv[:], ALU.mult), ["mag", "sinv"], ["Ai"])
                            gk = "Gt%d" % d
                            Arp = Ar[:].rearrange("p (c r) -> p r c", r=4)
                            Aip = Ai[:].rearrange("p (c r) -> p r c", r=4)
                            gv = lambda a, lo: Gt[d][:, a, lo:lo + 32].rearrange("p (r c) -> p r c", r=4)
                            dve(lambda: V.tensor_copy(gv(0, 0), Arp), ["Ar"], [gk])
                            dve(lambda: V.tensor_scalar(gv(0, 32), Aip, -1.0, None, ALU.mult), ["Ai", gk], [gk])
                            dve(lambda: V.tensor_copy(gv(1, 0), Aip), ["Ai", gk], [gk])
                            dve(lambda: V.tensor_copy(gv(1, 32), Arp), ["Ar", gk], [gk])
                            dve(lambda: V.tensor_scalar(am1[:], Ar[:], -1.0, None, ALU.add), ["Ar"], ["am1"])
                            dve(lambda: V.tensor_tensor(n2[:], lr[:], lr[:], ALU.mult), ["lr"], ["n2"])
                            dve(lambda: V.tensor_tensor(u1[:], li[:], li[:], ALU.mult), ["li"], ["u1"])
                            dve(lambda: V.tensor_tensor(n2[:], n2[:], u1[:], ALU.add), ["n2", "u1"], ["n2"])
                            dve(lambda: V.reciprocal(n2[:], n2[:]), ["n2"], ["n2"])
                            dve(lambda: V.tensor_tensor(kr[:], am1[:], lr[:], ALU.mult), ["am1", "lr"], ["kr"])
                            dve(lambda: V.tensor_tensor(u1[:], Ai[:], li[:], ALU.mult), ["Ai", "li", "n2"], ["u1"])
                            dve(lambda: V.tensor_tensor(kr[:], kr[:], u1[:], ALU.add), ["kr", "u1"], ["kr"])
                            dve(lambda: V.tensor_tensor(kr[:], kr[:], n2[:], ALU.mult), ["kr", "n2"], ["kr"])
                            dve(lambda: V.tensor_tensor(ki[:], Ai[:], lr[:], ALU.mult), ["Ai", "lr"], ["ki"])
                            dve(lambda: V.tensor_tensor(u1[:], am1[:], li[:], ALU.mult), ["am1", "li", "kr"], ["u1"])
                            dve(lambda: V.tensor_tensor(ki[:], ki[:], u1[:], ALU.subtract), ["ki", "u1"], ["ki"])
                            dve(lambda: V.tensor_tensor(ki[:], ki[:], n2[:], ALU.mult), ["ki", "n2"], ["ki"])
                            k.op("pool", lambda: G.memset(Br[:], 0.0), reads=["BbR", "BbI"], writes=["Br"])
                            k.op("pool", lambda: G.memset(Bi[:], 0.0), reads=["BbR", "BbI"], writes=["Bi"])
                            for g2 in range(2):
                                for (dst, srcd, key) in ((Br, b_re_d, "Br"), (Bi, b_im_d, "Bi")):
                                    base = srcd[i, d]
                                    src = bass.AP(base.tensor, base.offset + g2 * 1024, [[16, 64], [2048, 32], [1, 16]])
                                    k.dma("sp", dst[g2 * 64:(g2 + 1) * 64, :, g2 * 16:(g2 + 1) * 16], src, reads=[key], writes=[key])
                            def bc_last(a2, n):
                                a = a2.ap
                                return bass.AP(a2.tensor, a2.offset, [list(a[0]), list(a[1]), [0, n]])
                            krb, kib = bc_last(kr[:], 32), bc_last(ki[:], 32)
                            dve(lambda: V.tensor_tensor(BbR[:], Br[:], krb, ALU.mult), ["Br", "kr"], ["BbR"])
                            dve(lambda: V.tensor_tensor(T1[:], Bi[:], kib, ALU.mult), ["Bi", "ki"], ["T1"])
                            dve(lambda: V.tensor_tensor(BbR[:], BbR[:], T1[:], ALU.subtract), ["BbR", "T1"], ["BbR"])
                            dve(lambda: V.tensor_tensor(BbI[:], Bi[:], krb, ALU.mult), ["Bi", "kr"], ["BbI"])
                            dve(lambda: V.tensor_tensor(T1[:], Br[:], kib, ALU.mult), ["Br", "ki", "BbR"], ["T1"])
                            dve(lambda: V.tensor_tensor(BbI[:], BbI[:], T1[:], ALU.add), ["BbI", "T1"], ["BbI"])
                            for ri, srcb, key in ((0, BbR, "BbR"), (1, BbI, "BbI")):
                                for ct in range(8):
                                    pk = "psb%d" % (ct % 4)
                                    k.op("pe", lambda: P.transpose(psb[ct % 4][:, 0:128], srcb[:, 4 * ct:4 * ct + 4, :].rearrange("p a b -> p (a b)"), ident[:]),
                                         reads=[key, "ident"], writes=[pk])
                                    act(lambda: S.copy(WinT[d][ri][:, ct, :], psb[ct % 4][:, 0:128]), [pk], ["WinT%d%d" % (d, ri)])
                            for ri, srcd, mo, me in ((0, c_re_d, modd, mevn), (1, c_im_d, nodd, nevn)):
                                ck = "CwQ%d%d" % (d, ri)
                                k.op("pool", lambda: G.memset(CwQ[d][ri][:], 0.0), writes=[ck])
                                k.dma("sp", Cn[:], srcd[i, d].rearrange("(ct g) c p -> (g c) ct p", g=8), reads=["Cn"], writes=["Cn"])
                                dve(lambda: V.tensor_scalar(Cblk[:, :, 0:64], Cn[:], me[:, 0:1], None, ALU.mult), ["Cn", "mevn", "nevn", "Cblk"], ["Cblk"])
                                dve(lambda: V.tensor_scalar(Cblk[:, :, 64:128], Cn[:], mo[:, 0:1], None, ALU.mult), ["Cn", "modd", "nodd", "Cblk"], ["Cblk"])
                                for ct in range(8):
                                    pk = "psb%d" % (4 + ct % 4)
                                    k.op("pe", lambda: P.transpose(psb[4 + ct % 4][:, 0:128], Cblk[:, ct, :], ident[:]), reads=["Cblk", "ident"], writes=[pk])
                                    for q4 in range(4):
                                        act(lambda: S.copy(CwQ[d][ri][:, ct * 4 + q4, 32 * q4:32 * q4 + 32], psb[4 + ct % 4][:, 32 * q4:32 * q4 + 32]), [pk, ck], [ck])
                        k.barrier()
                    if S5_STAGE < 1:
                        sc.close()
                        return
                    Bw = [k.sb(sc, "Bw%d" % d, [128, 64, W], F32) for d in range(2)]
                    H = [k.sb(sc, "H%d" % d, [128, 64, W + 1], F32) for d in range(2)]
                    Sb = [k.sb(sc, "Sb%d" % d, [128, 64, W], BF16) for d in range(2)]
                    XY = [k.sb(sc, "XY%d" % d, [128, 2, 64], F32) for d in range(2)]
                    Nn = [k.sb(sc, "Nn%d" % d, [128, 2, 32], F32) for d in range(2)]
                    yo = [k.sb(sc, "yo%d" % d, [128, 8, W], F32) for d in range(2)]
                    k.op("pool", lambda: G.memset(H[0][:], 0.0), writes=["H0"])
                    k.op("pool", lambda: G.memset(H[1][:], 0.0), writes=["H1"])
                    nwc = NCTX // W
                    order_f = list(range(NW))
                    order_b = list(range(nwc - 1, -1, -1)) + list(range(NW - 1, nwc - 1, -1))
                    for step_w in range(NW if S5_NW is None else S5_NW):
                        wins = (order_f[step_w], order_b[step_w])
                        for d in range(2):
                            t0 = wins[d] * W
                            for half in range(2):
                                for c4 in range(4):
                                    ct = half * 4 + c4
                                    for ri in range(2):
                                        for r in range(4):
                                            slot = c4 * 2 + ri
                                            last = (c4 == 3 and ri == 1)
                                            k.op("pe", lambda: P.matmul(psb[r][:, slot * W:(slot + 1) * W],
                                                                        WinT[d][ri][32 * r:32 * r + 32, ct, :],
                                                                        hT[32 * r:32 * r + 32, ct, t0:t0 + W],
                                                                        start=True, stop=True, tile_position=(32 * r, 0)),
                                                 reads=["hT", "WinT%d%d" % (d, ri)], writes=["psb%d" % r], inc=(last and r == 3))
                                for r in range(4):
                                    for ri in range(2):
                                        src = psb[r][:].rearrange("p (c i w) -> p c i w", i=2, w=W)[:, :, ri, :]
                                        lo = ri * 32 + r * 8 + half * 4
                                        eng = "act" if ri == 0 else "dve"
                                        if eng == "act":
                                            k.op("act", lambda: S.copy(Bw[d][:, lo:lo + 4, :], src), reads=["psb%d" % r, "Bw%d" % d], writes=["Bw%d" % d])
                                        else:
                                            k.op("dve", lambda: V.tensor_copy(Bw[d][:, lo:lo + 4, :], src), reads=["psb%d" % r, "Bw%d" % d], writes=["Bw%d" % d])
                        for jj in range(W if S5_SUB >= 1 else 0):
                            cols = ((jj, jj + 1, jj), (W - jj, W - 1 - jj, W - 1 - jj))
                            for d in range(2):
                                pc, ncol, bj = cols[d]
                                k.op("dve", lambda: V.tensor_tensor(XY[d][:], bc_mid(H[d][:, :, pc], 2), Gt[d][:], ALU.mult),
                                     reads=["H%d" % d, "Gt%d" % d], writes=["XY%d" % d])
                            for d in range(2):
                                xv = XY[d][:].rearrange("p a (b q) -> p a b q", b=2)
                                k.op("dve", lambda: V.tensor_tensor(Nn[d][:], xv[:, :, 0, :], xv[:, :, 1, :], ALU.add),
                                     reads=["XY%d" % d], writes=["Nn%d" % d])
                            for d in range(2):
                                pc, ncol, bj = cols[d]
                                k.op("dve", lambda: V.tensor_tensor(H[d][:, :, ncol], Nn[d][:].rearrange("p a q -> p (a q)"), Bw[d][:, :, bj], ALU.add),
                                     reads=["Nn%d" % d, "Bw%d" % d], writes=["H%d" % d])
                        for d in range(2 if S5_SUB >= 2 else 0):
                            t0 = wins[d] * W
                            if d == 0:
                                k.op("act", lambda: S.copy(Sb[0][:], H[0][:, :, 1:W + 1]), reads=["H0"], writes=["Sb0"])
                                k.op("dve", lambda: V.tensor_copy(H[0][:, :, 0], H[0][:, :, W]), reads=["H0"], writes=["H0"])
                            else:
                                k.op("act", lambda: S.copy(Sb[1][:], H[1][:, :, 0:W]), reads=["H1"], writes=["Sb1"])
                                k.op("dve", lambda: V.tensor_copy(H[1][:, :, W], H[1][:, :, 0]), reads=["H1"], writes=["H1"])
                            for ct in range(8):
                                pk = "psb%d" % (4 + ct % 4)
                                n = 0
                                for q4 in range(4):
                                    for ri in range(2):
                                        q = ct * 4 + q4
                                        k.op("pe", lambda: P.matmul(psb[4 + ct % 4][:, 0:W], CwQ[d][ri][:, q, :], Sb[d][:, ri * 32 + q4 * 8 + ct, :],
                                                                    start=(n == 0), stop=(n == 7)),
                                             reads=["CwQ%d%d" % (d, ri), "Sb%d" % d], writes=[pk], inc=(n == 7))
                                        n += 1
                                k.op("act", lambda: S.copy(yo[d][:, ct, :], psb[4 + ct % 4][:, 0:W]), reads=[pk, "yo%d" % d], writes=["yo%d" % d])
                            k.dma("sp", YF[d].rearrange("(ct p) t -> p ct t", p=128)[:, :, t0:t0 + W], yo[d][:], reads=["yo%d" % d], writes=["YF"])
                    k.barrier()
                    sc.close()
                    if S5_STAGE < 2:
                        return
                    with contextlib.ExitStack() as pg:
                        gw = k.sb(pg, "gw", [128, 8, 2 * D], BF16)
                        for kt in range(8):
                            k.dma("pool", gw[:, kt, :], glu_d[i][kt * 128:(kt + 1) * 128, :], writes=["gw"])
                        ya = k.sb(pg, "ya", [128, 8, TB], F32)
                        yb2 = k.sb(pg, "yb2", [128, 8, TB], F32)
                        y2 = k.sb(pg, "y2", [128, 8, TB], F32)
                        gl = k.sb(pg, "gl", [128, 8, TB], BF16)
                        sg = k.sb(pg, "sg", [128, TB], F32)
                        zo = k.sb(pg, "zo", [128, 8, TB], F32)
                        for blk in range(NBLK):
                            t0 = blk * TB
                            k.dma("sp", ya[:], YF[0].rearrange("(ct p) t -> p ct t", p=128)[:, :, t0:t0 + TB], reads=["YF"], writes=["ya"])
                            k.dma("sp", yb2[:], YF[1].rearrange("(ct p) t -> p ct t", p=128)[:, :, t0:t0 + TB], reads=["YF"], writes=["yb2"])
                            k.op("dve", lambda: V.tensor_tensor(ya[:], ya[:], yb2[:], ALU.add), reads=["ya", "yb2"], writes=["ya"])
                            for ct in range(8):
                                k.op("dve", lambda: V.scalar_tensor_tensor(out=ya[:, ct, :], in0=hT[:, ct, t0:t0 + TB], scalar=dsk[:, ct:ct + 1],
                                                                           in1=ya[:, ct, :], op0=ALU.mult, op1=ALU.add), reads=["ya", "hT", "dsk"], writes=["ya"])
                            k.op("dve", lambda: V.tensor_tensor(y2[:], ya[:], ya[:], ALU.mult), reads=["ya"], writes=["y2"])
                            k.op("dve", lambda: V.tensor_scalar(y2[:], y2[:], 0.044715, 1.0, ALU.mult, ALU.add), reads=["y2"], writes=["y2"])
                            k.op("dve", lambda: V.tensor_tensor(y2[:], y2[:], ya[:], ALU.mult), reads=["y2", "ya"], writes=["y2"])
                            k.op("act", lambda: S.activation(y2[:], y2[:], AF.Sigmoid, scale=1.5957691216057308), reads=["y2"], writes=["y2"])
                            k.op("dve", lambda: V.tensor_tensor(gl[:], y2[:], ya[:], ALU.mult), reads=["y2", "ya"], writes=["gl"])
                            for ct in range(8):
                                pa, pb_ = psb[(2 * ct) % 4], psb[(2 * ct + 1) % 4]
                                ka, kb_ = "psb%d" % ((2 * ct) % 4), "psb%d" % ((2 * ct + 1) % 4)
                                for kt in range(8):
                                    k.op("pe", lambda: P.matmul(pa[:, 0:TB], gw[:, kt, ct * 128:(ct + 1) * 128], gl[:, kt, :], start=(kt == 0), stop=(kt == 7)),
                                         reads=["gw", "gl"], writes=[ka], inc=(kt == 7))
                                for kt in range(8):
                                    k.op("pe", lambda: P.matmul(pb_[:, 0:TB], gw[:, kt, D + ct * 128:D + (ct + 1) * 128], gl[:, kt, :], start=(kt == 0), stop=(kt == 7)),
                                         reads=["gw", "gl"], writes=[kb_], inc=(kt == 7))
                                k.op("act", lambda: S.activation(sg[:], pb_[:, 0:TB], AF.Sigmoid), reads=[kb_], writes=["sg"])
                                k.op("dve", lambda: V.tensor_tensor(zo[:, ct, :], pa[:, 0:TB], sg[:], ALU.mult), reads=[ka, "sg", "zo"], writes=["zo"])
                            k.dma("sp", YTv[:, :, t0:t0 + TB], zo[:], reads=["zo"], writes=["YT"])
                        k.barrier()

            def even_mixer(l):
                i = l // 2
                NLAT = 2048
                with contextlib.ExitStack() as ph:
                    fT = k.sb(ph, "fT", [128, 4, NT], BF16)
                    QT = k.sb(ph, "QT", [128, 4, NT], BF16)
                    KT = k.sb(ph, "KT", [128, 2, NT], BF16)
                    Vtm = k.sb(ph, "Vtm", [128, NT // 128, 128], BF16)
                    mixT = k.sb(ph, "mixT", [128, 8, NT], BF16)
                    SEall = k.sb(ph, "SEall", [128, 8], F32)
                    SE = k.sb(ph, "SE", [128, 2, 2], F32)
                    sk = even_sink_d[i]
                    k.dma("sp", SEall[:], bass.AP(sk.tensor, sk.offset, [[0, 128], [1, 8]]), writes=["SEall"], allow_slow_non_contiguous=True)
                    k.op("act", lambda: S.activation(SEall[:], SEall[:], AF.Exp), reads=["SEall"], writes=["SEall"])
                    for kh in range(2):
                        for tl in range(2):
                            k.op("dve", lambda: V.tensor_copy(SE[0:64, kh, tl:tl + 1], SEall[0:64, 4 * kh + 2 * tl:4 * kh + 2 * tl + 1]), reads=["SEall", "SE"], writes=["SE"])
                            k.op("dve", lambda: V.tensor_copy(SE[64:128, kh, tl:tl + 1], SEall[64:128, 4 * kh + 2 * tl + 1:4 * kh + 2 * tl + 2]), reads=["SEall", "SE"], writes=["SE"])
                    with contextlib.ExitStack() as pa:
                        hT = k.sb(pa, "hT", [128, 8, NT], BF16)
                        with contextlib.ExitStack() as ph2:
                            prenorm_to_hT(ph2, hT)
                            k.barrier()
                        wb = k.sb(pa, "wb", [128, 8, 1280], BF16)
                        for kt in range(8):
                            k.dma("pool", wb[:, kt, :], w_in_d[i][kt * 128:(kt + 1) * 128, :], writes=["wb"])
                        wsw = k.sb(pa, "wsw", [128, 8, 640], BF16)
                        wv = wb[:, :, 512:1152].rearrange("p k (h two e) -> p k h two e", two=2, e=16)
                        wsv = wsw[:].rearrange("p k (h two e) -> p k h two e", two=2, e=16)
                        for kt in range(8):
                            k.op("pool", lambda: G.tensor_copy(wsv[:, kt, :, 0, :], wv[:, kt, :, 1, :]), reads=["wb", "wsw"], writes=["wsw"])
                            k.op("pool", lambda: G.tensor_copy(wsv[:, kt, :, 1, :], wv[:, kt, :, 0, :]), reads=["wb", "wsw"], writes=["wsw"])
                        wkd = k.sb(pa, "wkd", [128, 8, 2, 128], BF16)
                        wkds = k.sb(pa, "wkds", [128, 8, 2, 128], BF16)
                        for dup in range(2):
                            k.op("pool", lambda: G.tensor_copy(wkd[:, :, :, dup * 64:(dup + 1) * 64], wb[:, :, 1024:1152].rearrange("p k (h d) -> p k h d", d=64)), reads=["wb", "wkd"], writes=["wkd"])
                            k.op("pool", lambda: G.tensor_copy(wkds[:, :, :, dup * 64:(dup + 1) * 64], wsw[:, :, 512:640].rearrange("p k (h d) -> p k h d", d=64)), reads=["wsw", "wkds"], writes=["wkds"])
                        ropc = k.sb(pa, "ropc", [128, NLAT], F32)
                        rops = k.sb(pa, "rops", [128, NLAT], F32)
                        k.dma("sp", ropc[:], ROPC[:, :], reads=["ROP"], writes=["ropc"])
                        k.dma("sp", rops[:], ROPS[:, :], reads=["ROP"], writes=["rops"])
                        t1 = k.sb(pa, "rt1", [128, 512], F32)
                        t2 = k.sb(pa, "rt2", [128, 512], F32)
                        blocks = [(0, 256)] + [(256 + 512 * b, 512) for b in range(4)]
                        nb = 0
                        for (t0, n) in blocks:
                            lat = t0 >= NCTX
                            for g in range(4):
                                pb, pk = psb[nb % 4], "psb%d" % (nb % 4); nb += 1
                                for kt in range(8):
                                    k.op("pe", lambda: P.matmul(pb[:, 0:n], wb[:, kt, g * 128:(g + 1) * 128], hT[:, kt, t0:t0 + n], start=(kt == 0), stop=(kt == 7)),
                                         reads=["wb", "hT"], writes=[pk], inc=(kt == 7))
                                k.op("act", lambda: S.copy(fT[:, g, t0:t0 + n], pb[:, 0:n]), reads=[pk, "fT"], writes=["fT"])
                            for j in range(6):
                                if j < 4:
                                    lw = lambda kt: wb[:, kt, 512 + j * 128:512 + (j + 1) * 128]
                                    lws = lambda kt: wsw[:, kt, j * 128:(j + 1) * 128]
                                    dst = QT[:, j, t0:t0 + n]
                                    dk = "QT"
                                else:
                                    lw = lambda kt: wkd[:, kt, j - 4, :]
                                    lws = lambda kt: wkds[:, kt, j - 4, :]
                                    dst = KT[:, j - 4, t0:t0 + n]
                                    dk = "KT"
                                pb, pk = psb[nb % 4], "psb%d" % (nb % 4); nb += 1
                                for kt in range(8):
                                    k.op("pe", lambda: P.matmul(pb[:, 0:n], lw(kt), hT[:, kt, t0:t0 + n], start=(kt == 0), stop=(kt == 7)),
                                         reads=["wb", "wkd", "hT"], writes=[pk], inc=(kt == 7))
                                if not lat:
                                    k.op("act", lambda: S.copy(dst, pb[:, 0:n]), reads=[pk, dk], writes=[dk])
                                else:
                                    pb2, pk2 = psb[4 + nb % 4], "psb%d" % (4 + nb % 4)
                                    for kt in range(8):
                                        k.op("pe", lambda: P.matmul(pb2[:, 0:n], lws(kt), hT[:, kt, t0:t0 + n], start=(kt == 0), stop=(kt == 7)),
                                             reads=["wsw", "wkds", "hT"], writes=[pk2], inc=(kt == 7))
                                    r0 = t0 - NCTX
                                    k.op("dve", lambda: V.tensor_tensor(t1[:, 0:n], pb[:, 0:n], ropc[:, r0:r0 + n], ALU.mult), reads=[pk, "ropc", "rt1"], writes=["rt1"])
                                    k.op("dve", lambda: V.tensor_tensor(t2[:, 0:n], pb2[:, 0:n], rops[:, r0:r0 + n], ALU.mult), reads=[pk2, "rops", "rt2"], writes=["rt2"])
                                    k.op("pool", lambda: G.tensor_tensor(dst, t1[:, 0:n], t2[:, 0:n], ALU.add), reads=["rt1", "rt2", dk], writes=[dk])
                            for s in range(n // 128):
                                tt = (t0 + s * 128) // 128
                                pb, pk = psb[nb % 4], "psb%d" % (nb % 4); nb += 1
                                for kt in range(8):
                                    k.op("pe", lambda: P.matmul(pb[:, 0:128], hT[:, kt, tt * 128:(tt + 1) * 128], wb[:, kt, 1152:1280], start=(kt == 0), stop=(kt == 7)),
                                         reads=["wb", "hT"], writes=[pk], inc=(kt == 7))
                                k.op("act", lambda: S.copy(Vtm[:, tt, :], pb[:, 0:128]), reads=[pk, "Vtm"], writes=["Vtm"])
                        dump("dbg_fT", fT[:], ["fT"]); dump("dbg_QT", QT[:], ["QT"]); dump("dbg_KT", KT[:], ["KT"]); dump("dbg_Vtm", Vtm[:], ["Vtm"])
                        k.barrier()
                    with contextlib.ExitStack() as pf:
                        Gtm = k.sb(pf, "Gtm", [128, NT // 128, 4, 256], BF16)
                        csc = k.sb(pf, "csc", [128, 256], BF16)
                        k.dma("sp", csc[:, 0:128], CL.rearrange("(t e) c -> t e c", e=16)[:, 0, 0:128], reads=["TAB"], writes=["csc"])
                        k.dma("sp", csc[:, 128:256], SLn.rearrange("(t e) c -> t e c", e=16)[:, 0, 0:128], reads=["TAB"], writes=["csc"])
                        k.op("dve", lambda: V.tensor_scalar(csc[:, 128:256], csc[:, 128:256], -1.0, None, ALU.mult), reads=["csc"], writes=["csc"])
                        sc_lat = float(1.0 / np.sqrt(2048.0 * 128.0))
                        sc_ctx = float(1.0 / np.sqrt(256.0 * 128.0))
                        nb = 0
                        for tt in range(NT // 128):
                            for g in range(4):
                                pb, pk = psb[nb % 4], "psb%d" % (nb % 4); nb += 1
                                k.op("pe", lambda: P.matmul(pb[:, 0:256], fT[:, g, tt * 128:(tt + 1) * 128], csc[:], start=True, stop=True),
                                     reads=["fT", "csc"], writes=[pk])
                                k.op("act", lambda: S.activation(Gtm[:, tt, g, :], pb[:, 0:256], AF.Copy, scale=(sc_ctx if tt < 2 else sc_lat)), reads=[pk, "Gtm"], writes=["Gtm"])
                        cl = k.sb(pf, "cl", [128, 16, 512], BF16)
                        sl = k.sb(pf, "sl", [128, 16, 512], BF16)
                        c8 = CL.rearrange("(t e) c -> t e c", e=8)[:, 0, 0:256].rearrange("(tt p) c -> p tt c", p=128)
                        s8 = SLn.rearrange("(t e) c -> t e c", e=8)[:, 0, 0:256].rearrange("(tt p) c -> p tt c", p=128)
                        k.dma("sp", cl[:, 0:2, 0:256], c8, reads=["TAB"], writes=["cl"])
                        k.dma("sp", sl[:, 0:2, 0:256], s8, reads=["TAB"], writes=["sl"])
                        for g in range(4):
                            pb, pk = psb[4 + g % 4], "psb%d" % (4 + g % 4)
                            n_ = 0
                            for tt in range(2):
                                for (half, tabl, tk) in ((0, cl, "cl"), (1, sl, "sl")):
                                    k.op("pe", lambda: P.matmul(pb[:, 0:256], Gtm[:, tt, g, half * 128:(half + 1) * 128], tabl[:, tt, 0:256], start=(n_ == 0), stop=(n_ == 3)),
                                         reads=["Gtm", tk], writes=[pk], inc=(n_ == 3))
                                    n_ += 1
                            k.op("act", lambda: S.copy(mixT[:, g, 0:256], pb[:, 0:256]), reads=[pk, "mixT"], writes=["mixT"])
                        for pbk in range(4):
                            k.dma("sp", cl[:], CL[:, pbk * 512:(pbk + 1) * 512].rearrange("(tt p) c -> p tt c", p=128), reads=["TAB", "cl"], writes=["cl"])
                            k.dma("sp", sl[:], SLn[:, pbk * 512:(pbk + 1) * 512].rearrange("(tt p) c -> p tt c", p=128), reads=["TAB", "sl"], writes=["sl"])
                            for g in range(4):
                                pb, pk = psb[4 + g % 4], "psb%d" % (4 + g % 4)
                                n_ = 0
                                for tt in range(16):
                                    for (half, tabl, tk) in ((0, cl, "cl"), (1, sl, "sl")):
                                        k.op("pe", lambda: P.matmul(pb[:, 0:512], Gtm[:, 2 + tt, g, half * 128:(half + 1) * 128], tabl[:, tt, :], start=(n_ == 0), stop=(n_ == 31)),
                                             reads=["Gtm", tk], writes=[pk], inc=(n_ == 31))
                                        n_ += 1
                                k.op("act", lambda: S.copy(mixT[:, g, NCTX + pbk * 512:NCTX + (pbk + 1) * 512], pb[:, 0:512]), reads=[pk, "mixT"], writes=["mixT"])
                        k.barrier()
                    with contextlib.ExitStack() as pt:
                        PT = [k.sb(pt, "PT%d" % z, [128, 2, 2, 128], BF16) for z in range(2)]
                        rden = k.sb(pt, "rden", [128, 2, 128], F32)
                        scale = 0.125
                        it = 0
                        qblocks = [("c", 0), ("c", 1)] + [("l", n) for n in range(16)]
                        for (kind, n) in qblocks:
                            q0 = n * 128 if kind == "c" else NCTX + n * 128
                            for kh in range(2):
                                chunks = []
                                if kind == "l":
                                    for dlt in (-1, 0, 1):
                                        if 0 <= n + dlt < 16:
                                            chunks.append((NCTX + (n + dlt) * 128, dlt))
                                chunks += [(0, 0), (128, 0)]
                                for ci, (k0, dlt) in enumerate(chunks):
                                    z = it % 2
                                    it += 1
                                    pS = (psb[0 + 2 * z], psb[1 + 2 * z])
                                    kS = ("psb%d" % (2 * z), "psb%d" % (1 + 2 * z))
                                    for par in range(2):
                                        k.op("pe", lambda: P.matmul(pS[par][:, 0:256].rearrange("p (t q) -> p t q", t=2),
                                                                    KT[64 * par:64 * par + 64, kh, k0:k0 + 128],
                                                                    QT[64 * par:64 * par + 64, 2 * kh:2 * kh + 2, q0:q0 + 128],
                                                                    start=True, stop=True, tile_position=(64 * par, 0)),
                                             reads=["KT", "QT"], writes=[kS[par]])
                                    for par in range(2):
                                        k.op("act", lambda: S.activation(PT[z][:, par, :, :].rearrange("p t q -> p (t q)"), pS[par][:, 0:256], AF.Exp, scale=scale),
                                             reads=[kS[par], "PT%d" % z], writes=["PT%d" % z])
                                    if dlt != 0:
                                        msk = mask_ge if dlt == -1 else mask_le
                                        mb = bass.AP(msk[:].tensor, msk[:].offset, [list(msk[:].ap[0]), [0, 4], list(msk[:].ap[1])])
                                        k.op("dve", lambda: V.tensor_tensor(PT[z][:].rearrange("p a t q -> p (a t) q"), PT[z][:].rearrange("p a t q -> p (a t) q"), mb, ALU.mult),
                                             reads=["PT%d" % z, "mask_ge", "mask_le"], writes=["PT%d" % z])
                                    tt = k0 // 128
                                    first, last = (ci == 0), (ci == len(chunks) - 1)
                                    for par in range(2):
                                        k.op("pe", lambda: P.matmul(psb[4 + par][64 * par:64 * par + 64, 0:256], Vtm[:, tt, kh * 64:(kh + 1) * 64],
                                                                    PT[z][:, par, :, :].rearrange("p t q -> p (t q)"), start=first, stop=last,
                                                                    tile_position=(0, 64 * par)),
                                             reads=["Vtm", "PT%d" % z], writes=["psb%d" % (4 + par)], inc=last)
                                        k.op("pe", lambda: P.matmul(psb[6 + par][64 * par:64 * par + 64, 0:256], ones64[:],
                                                                    PT[z][:, par, :, :].rearrange("p t q -> p (t q)"), start=first, stop=last,
                                                                    tile_position=(0, 64 * par)),
                                             reads=["ones64", "PT%d" % z], writes=["psb%d" % (6 + par)], inc=last)
                                for par in range(2):
                                    lo, hi = 64 * par, 64 * par + 64
                                    seb = bass.AP(SE[:].tensor, SE[lo:hi, kh, :].offset, [list(SE[lo:hi, kh, :].ap[0]), list(SE[lo:hi, kh, :].ap[1]), [0, 128]])
                                    k.op("dve", lambda: V.tensor_tensor(rden[lo:hi, :, :], psb[6 + par][lo:hi, 0:256].rearrange("p (t q) -> p t q", t=2), seb, ALU.add),
                                         reads=["psb%d" % (6 + par), "SE", "rden%d" % par], writes=["rden%d" % par])
                                    k.op("dve", lambda: V.reciprocal(rden[lo:hi, :, :], rden[lo:hi, :, :]), reads=["rden%d" % par], writes=["rden%d" % par])
                                    k.op("dve", lambda: V.tensor_tensor(mixT[lo:hi, 4 + 2 * kh:6 + 2 * kh, q0:q0 + 128], psb[4 + par][lo:hi, 0:256].rearrange("p (t q) -> p t q", t=2),
                                                                        rden[lo:hi, :, :], ALU.mult),
                                         reads=["psb%d" % (4 + par), "rden%d" % par, "mixT"], writes=["mixT"])
                        dump("dbg_mixT", mixT[:], ["mixT"])
                        k.barrier()
                    with contextlib.ExitStack() as po:
                        wo = k.sb(po, "wo", [128, 8, D], BF16)
                        for kt in range(8):
                            k.dma("pool", wo[:, kt, :], w_out_d[i][kt * 128:(kt + 1) * 128, :], writes=["wo"])
                        yo = [k.sb(po, "eyo%d" % z, [128, 8, 256], F32) for z in range(2)]
                        for blk in range(NBLK):
                            t0 = blk * TB
                            z = blk % 2
                            for ct in range(8):
                                pb, pk = psb[ct % 4], "psb%d" % (ct % 4)
                                for mt in range(8):
                                    k.op("pe", lambda: P.matmul(pb[:, 0:TB], wo[:, mt, ct * 128:(ct + 1) * 128], mixT[:, mt, t0:t0 + TB], start=(mt == 0), stop=(mt == 7)),
                                         reads=["wo", "mixT"], writes=[pk], inc=(mt == 7))
                                k.op("act", lambda: S.copy(yo[z][:, ct, :], pb[:, 0:TB]), reads=[pk, "eyo%d" % z], writes=["eyo%d" % z])
                            k.dma("sp", YTv[:, :, t0:t0 + TB], yo[z][:], reads=["eyo%d" % z], writes=["YT"])
                        k.barrier()
            if mixer and l % 2 == 1:
                s5_mixer(l)
            elif mixer:
                even_mixer(l)

            with contextlib.ExitStack() as ph:
                w1b = k.sb(ph, "w1b", [128, 8, DFF], BF16)
                w2b = k.sb(ph, "w2b", [128, 32, D], BF16)
                for kt in range(8):
                    k.dma("pool", w1b[:, kt, :], w1_d[l][kt * 128:(kt + 1) * 128, :], writes=["w1b"])
                for j4 in range(8):
                    k.dma("pool", w2b[:, j4 * 4:(j4 + 1) * 4, :],
                          w2_d[l][j4 * 512:(j4 + 1) * 512, :].rearrange("(j p) n -> p j n", p=128), writes=["w2b"])
                xb = k.sb(ph, "xb", [128, 8, TB], F32)
                yb = k.sb(ph, "yb", [128, 8, TB], F32)
                sq = k.sb(ph, "sq", [128, 8, TB], F32)
                tmp = k.sb(ph, "tmp", [128, 8, TB], F32)
                rs = k.sb(ph, "rs", [128, TB], F32)
                h2 = k.sb(ph, "h2", [128, 8, TB], BF16)
                ob = k.sb(ph, "ob", [128, 8, TB], F32)
                ar = [k.sb(ph, "ar%d" % i, [128, TB], F32) for i in range(2)]
                a2all = k.sb(ph, "a2all", [128, 32, TB], BF16)
                tiles = (sq, rs, psb[6], "psb6")
                for blk in range(NBLK):
                    j = 1 if blk == 0 else 0
                    t0 = blk * TB
                    k.dma("sp", xb[:], XTv[:, :, t0:t0 + TB], reads=["XT"], writes=["xb"])
                    if mixer:
                        k.dma("sp", yb[:], YTv[:, :, t0:t0 + TB], reads=["YT"], writes=["yb"])
                        rms_bc(tiles, yb[:], "yb", "m")
                        k.op("dve", lambda: V.tensor_tensor(tmp[:], yb[:], bc_mid(rs[:], 8), ALU.mult), reads=["yb", "rs"], writes=["tmp"])
                        for ct in range(8):
                            k.op("dve", lambda: V.scalar_tensor_tensor(out=xb[:, ct, :], in0=tmp[:, ct, :], scalar=PRM[:, 2, ct, j:j + 1],
                                                                       in1=xb[:, ct, :], op0=ALU.mult, op1=ALU.add),
                                 reads=["tmp", "xb", "PRM"], writes=["xb"])
                    rms_bc(tiles, xb[:], "xb", "f")
                    k.op("dve", lambda: V.tensor_tensor(tmp[:], xb[:], bc_mid(rs[:], 8), ALU.mult), reads=["xb", "rs"], writes=["tmp"])
                    for ct in range(8):
                        k.op("act", lambda: S.activation(h2[:, ct, :], tmp[:, ct, :], AF.Identity, bias=PRM[:, 4, ct, j:j + 1],
                                                         scale=PRM[:, 3, ct, j:j + 1]), reads=["tmp", "PRM"], writes=["h2"])
                    for jf in range(32):
                        pa = psb[4 + jf % 2]
                        pak = "psb%d" % (4 + jf % 2)
                        for kt in range(8):
                            k.op("pe", lambda: P.matmul(pa[:, 0:TB], w1b[:, kt, jf * 128:(jf + 1) * 128], h2[:, kt, :],
                                                        start=(kt == 0), stop=(kt == 7)),
                                 reads=["w1b", "h2"], writes=[pak], inc=(kt == 7))
                        k.op("act", lambda: S.activation(ar[jf % 2][:], pa[:, 0:TB], AF.Relu), reads=[pak], writes=["ar%d" % (jf % 2)])
                        k.op("pool", lambda: G.tensor_tensor(a2all[:, jf, :], ar[jf % 2][:], ar[jf % 2][:], ALU.mult),
                             reads=["ar%d" % (jf % 2)], writes=["a2all"])
                    for ft in range(8):
                        po = psb[ft % 4]
                        pok = "psb%d" % (ft % 4)
                        for jf in range(32):
                            k.op("pe", lambda: P.matmul(po[:, 0:TB], w2b[:, jf, ft * 128:(ft + 1) * 128], a2all[:, jf, :],
                                                        start=(jf == 0), stop=(jf == 31)),
                                 reads=["w2b", "a2all"], writes=[pok], inc=(jf == 31))
                        if ft % 2 == 0:
                            k.op("act", lambda: S.copy(ob[:, ft, :], po[:, 0:TB]), reads=[pok], writes=["ob%d" % ft])
                        else:
                            k.op("dve", lambda: V.tensor_copy(ob[:, ft, :], po[:, 0:TB]), reads=[pok], writes=["ob%d" % ft])
                    obk = ["ob%d" % i for i in range(8)]
                    k.op("act", lambda: S.activation(sq[:], ob[:], AF.Square), reads=obk, writes=["sq"])
                    rms_bc(tiles, None, "sq", "o")
                    k.op("dve", lambda: V.tensor_tensor(tmp[:], ob[:], bc_mid(rs[:], 8), ALU.mult), reads=obk + ["rs"], writes=["tmp"])
                    for ct in range(8):
                        k.op("dve", lambda: V.scalar_tensor_tensor(out=xb[:, ct, :], in0=tmp[:, ct, :], scalar=PRM[:, 5, ct, j:j + 1],
                                                                   in1=xb[:, ct, :], op0=ALU.mult, op1=ALU.add),
                             reads=["tmp", "xb", "PRM"], writes=["xb"])
                    k.dma("sp", XTv[:, :, t0:t0 + TB], xb[:], reads=["xb"], writes=["XT"])
                k.barrier()

        with contextlib.ExitStack() as ph:
            xf = [k.sb(ph, "xf%d" % i, [128, 8, 128], F32) for i in range(2)]
            xo = [k.sb(ph, "xo%d" % i, [128, D], F32) for i in range(2)]
            for tt in range(16):
                b = tt % 2
                k.dma("sp", xf[b][:], XTv[:, :, NCTX + tt * 128:NCTX + (tt + 1) * 128], reads=["XT"], writes=["xf%d" % b])
                for half in range(2):
                    pkey = "psb%d" % ((tt * 2 + half) % 8)
                    pb = psb[(tt * 2 + half) % 8]
                    for c4 in range(4):
                        ct = half * 4 + c4
                        k.op("pe", lambda: P.transpose(pb[:, c4 * 128:(c4 + 1) * 128], xf[b][:, ct, :], ident[:]),
                             reads=["xf%d" % b, "ident"], writes=[pkey], inc=(c4 == 3))
                    if half == 0:
                        k.op("act", lambda: S.copy(xo[b][:, 0:512], pb[:]), reads=[pkey], writes=["xo%d_0" % b])
                    else:
                        k.op("dve", lambda: V.tensor_copy(xo[b][:, 512:1024], pb[:]), reads=[pkey], writes=["xo%d_1" % b])
                k.dma("sp", out_d[tt * 128:(tt + 1) * 128, :], xo[b][:], reads=["xo%d_0" % b, "xo%d_1" % b], writes=["out"])
            k.barrier()
        print("ninst", k.ninst, "nwaits", k.nwaits)
    return nc


def make_in_maps(inp):
    gains = np.stack([inp["mix_pre_g"], inp["mix_post_g"], inp["ffn_pre_g"], inp["ffn_post_g"]], 0)
    maps = []
    for b in range(8):
        m = {
            "x": np.ascontiguousarray(inp["x"][b]), "ctx": np.ascontiguousarray(inp["ctx"][b]),
            "cc": np.ascontiguousarray(np.stack([inp["c"][b], inp["c_ctx"]], 0)),
            "mod_w": inp["mod_w"], "mod_b": inp["mod_b"], "gains": np.ascontiguousarray(gains),
            "ffn_w1": inp["ffn_w1"], "ffn_w2": inp["ffn_w2"],
            "ssm_a_re": inp["ssm_a_re"], "ssm_a_im": inp["ssm_a_im"], "ssm_log_dt": inp["ssm_log_dt"],
            "ssm_b_re": inp["ssm_b_re"], "ssm_b_im": inp["ssm_b_im"], "ssm_c_re": inp["ssm_c_re"], "ssm_c_im": inp["ssm_c_im"],
            "ssm_d": inp["ssm_d"], "ssm_glu_w": inp["ssm_glu_w"],
            "even_w_in": inp["even_w_in"], "even_w_out": inp["even_w_out"], "even_sink": inp["even_sink"],
        }
        maps.append(m)
    return maps


def kernel(**inp):
    inp = {k_: np.asarray(v) for k_, v in inp.items()}
    nc = build(mixer=MIXER_ENABLED)
    res = run_bass_kernel_spmd(nc, make_in_maps(inp), core_ids=list(range(8)))
    return np.stack([r["out"] for r in res.results], 0)
```

```python
import contextlib
import numpy as np
import concourse.bass as bass
import concourse.mybir as mybir
from concourse.bass_utils import run_bass_kernel_spmd

F32 = mybir.dt.float32
BF16 = mybir.dt.bfloat16
I32 = mybir.dt.int32
ALU = mybir.AluOpType
AF = mybir.ActivationFunctionType
AX = mybir.AxisListType

S5_STAGE = 2
S5_SUB = 2
S5_NW = None
MIXER_ENABLED = True
SAME_ENGINE_SYNC = {"dve": True, "act": True, "pool": False, "pe": False}
DMA_RING = 8


class KB:
    def __init__(self, nc, es):
        self.nc = nc
        self.es = es
        self.raw = {"pe": nc.tensor, "dve": nc.vector, "act": nc.scalar, "pool": nc.gpsimd, "sp": nc.sync}
        self.sem = {}
        self.cnt = {}
        for e in ("pe", "dve", "act", "pool"):
            self.sem[e] = es.enter_context(nc.semaphore("s_" + e))
            self.cnt[e] = 0
        self.dring = {}
        self.dcnt = {}
        for q in ("sp", "act", "pool"):
            self.dring[q] = [es.enter_context(nc.semaphore("d_%s%d" % (q, i))) for i in range(DMA_RING)]
            self.dcnt[q] = 0
        self.seen = {e: {} for e in self.raw}
        self.lastw = {}
        self.readers = {}
        self.nwaits = 0
        self.ninst = 0

    def sb(self, es, name, shape, dt):
        self.uid = getattr(self, "uid", 0) + 1
        return es.enter_context(self.nc.sbuf_tensor("%s_u%d" % (name, self.uid), list(shape), dt))

    def ps(self, es, name, shape, dt=F32):
        return es.enter_context(self.nc.psum_tensor(name, list(shape), dt))

    def _collect(self, eng, reads, writes):
        need = {}

        def add(tok):
            if tok is None:
                return
            s, v, src = tok
            if src == eng and (not SAME_ENGINE_SYNC.get(eng, True) or v > self.cnt[eng]):
                return
            if need.get(s, (0,))[0] < v:
                need[s] = (v, src)

        for r in reads:
            add(self.lastw.get(r))
        for w in writes:
            add(self.lastw.get(w))
            for tok in self.readers.get(w, ()):
                add(tok)
        return need

    def _emit_waits(self, eng, need):
        seen = self.seen[eng]
        for s, (v, src) in need.items():
            if seen.get(s, 0) >= v:
                continue
            self.raw[eng].wait_ge(s, v)
            seen[s] = v
            self.nwaits += 1

    def _record(self, tok, reads, writes):
        for w in writes:
            self.lastw[w] = tok
            self.readers[w] = []
        for r in reads:
            if r in writes:
                continue
            self.readers.setdefault(r, []).append(tok)

    def op(self, eng, fn, reads=(), writes=(), inc=True):
        need = self._collect(eng, reads, writes)
        self._emit_waits(eng, need)
        inst = fn()
        self.ninst += 1
        if inc:
            self.cnt[eng] += 1
            inst.then_inc(self.sem[eng], 1)
            tok = (self.sem[eng], self.cnt[eng], eng)
        else:
            tok = (self.sem[eng], self.cnt[eng] + 1, eng)
        self._record(tok, reads, writes)
        return inst

    def dma(self, q, out, in_, reads=(), writes=(), **kw):
        need = self._collect(q, reads, writes)
        self._emit_waits(q, need)
        i = self.dcnt[q]
        self.dcnt[q] += 1
        s = self.dring[q][i % DMA_RING]
        v = 16 * (i // DMA_RING + 1)
        inst = self.raw[q].dma_start(out=out, in_=in_, **kw)
        inst.then_inc(s, 16)
        self.ninst += 1
        self._record((s, v, "dma_" + q), reads, writes)
        return inst

    def barrier(self):
        need = {}
        for e in ("pe", "dve", "act", "pool"):
            if self.cnt[e] > 0:
                need[self.sem[e]] = (self.cnt[e], "x")
        for q in ("sp", "act", "pool"):
            n = self.dcnt[q]
            for r in range(DMA_RING):
                cntr = (n - r + DMA_RING - 1) // DMA_RING if n > r else 0
                if cntr > 0:
                    need[self.dring[q][r]] = (16 * cntr, "x")
        for e in ("pe", "dve", "act", "pool", "sp"):
            self._emit_waits(e, need)
        self.lastw = {}
        self.readers = {}


def bc_mid(ap2, n):
    a = ap2.ap
    return bass.AP(ap2.tensor, ap2.offset, [list(a[0]), [0, n]] + [list(x) for x in a[1:]])


D = 1024
NT = 2304
NCTX = 256
TB = 256
NBLK = NT // TB
DFF = 4096
EPS = 1e-6


def build(nlayers=4, mixer=True, layers=None, debug=False):
    nc = bass.Bass("TRN2", target_bir_lowering=False)
    dt_in = lambda n, s: nc.dram_tensor(n, list(s), F32, kind="ExternalInput").ap()
    x_d = dt_in("x", [2048, D])
    ctx_d = dt_in("ctx", [NCTX, D])
    cc_d = dt_in("cc", [2, D])
    mod_w_d = dt_in("mod_w", [4, D, 6 * D])
    mod_b_d = dt_in("mod_b", [4, 6 * D])
    gains_d = dt_in("gains", [4, 4, D])
    w1_d = dt_in("ffn_w1", [4, D, DFF])
    w2_d = dt_in("ffn_w2", [4, DFF, D])
    w_in_d = dt_in("even_w_in", [2, D, 1280])
    w_out_d = dt_in("even_w_out", [2, D, D])
    even_sink_d = dt_in("even_sink", [2, 8])
    a_re_d = dt_in("ssm_a_re", [2, 2, 64, 64])
    a_im_d = dt_in("ssm_a_im", [2, 2, 64, 64])
    ldt_d = dt_in("ssm_log_dt", [2, 2, 64])
    b_re_d = dt_in("ssm_b_re", [2, 2, 64, 64, 16])
    b_im_d = dt_in("ssm_b_im", [2, 2, 64, 64, 16])
    c_re_d = dt_in("ssm_c_re", [2, 2, 64, 16, 64])
    c_im_d = dt_in("ssm_c_im", [2, 2, 64, 16, 64])
    dsk_d = dt_in("ssm_d", [2, D])
    glu_d = dt_in("ssm_glu_w", [2, D, 2 * D])
    out_d = nc.dram_tensor("out", [2048, D], F32, kind="ExternalOutput").ap()
    YF = nc.dram_tensor("YF", [2, D, NT], F32, kind=("ExternalOutput" if debug else "Internal")).ap()
    XT = nc.dram_tensor("XT", [D, NT], F32).ap()
    YT = nc.dram_tensor("YT", [D, NT], F32, kind=("ExternalOutput" if debug else "Internal")).ap()
    CL = nc.dram_tensor("CLtab", [2048, 2048], BF16, kind=("ExternalOutput" if debug else "Internal")).ap()
    SLn = nc.dram_tensor("SLtab", [2048, 2048], BF16, kind=("ExternalOutput" if debug else "Internal")).ap()
    ROPC = nc.dram_tensor("ROPC", [128, 2048], F32, kind=("ExternalOutput" if debug else "Internal")).ap()
    ROPS = nc.dram_tensor("ROPS", [128, 2048], F32, kind=("ExternalOutput" if debug else "Internal")).ap()
    XTv = XT.rearrange("(ct p) t -> p ct t", p=128)
    YTv = YT.rearrange("(ct p) t -> p ct t", p=128)

    with contextlib.ExitStack() as es:
        k = KB(nc, es)
        V, S, P, G = nc.vector, nc.scalar, nc.tensor, nc.gpsimd

        def dump(name, ap, keys):
            if not debug:
                return
            dt_ = nc.dram_tensor(name, list(ap.shape), ap.dtype, kind="ExternalOutput").ap()
            k.dma("sp", dt_, ap, reads=keys, writes=["dbg_" + name])
        ident = k.sb(es, "ident", [128, 128], F32)
        onesm = k.sb(es, "onesm", [128, 128], F32)
        k.op("pool", lambda: G.memset(ident[:], 0.0), writes=["ident"])
        k.op("pool", lambda: G.affine_select(out=ident[:], in_=ident[:], compare_op=ALU.not_equal, fill=1.0,
                                             base=0, pattern=[[-1, 128]], channel_multiplier=1),
             reads=["ident"], writes=["ident"])
        k.op("pool", lambda: G.memset(onesm[:], 1.0 / D), writes=["onesm"])
        psb = [k.ps(es, "psb%d" % i, [128, 512], F32) for i in range(8)]
        MV = k.sb(es, "MV", [128, 6, 8, 2], F32)
        GN = k.sb(es, "GN", [128, 4, 4, 8], F32)
        SC = k.sb(es, "SC", [128, 8, 2], F32)
        SCb = k.sb(es, "SCb", [128, 8, 2], BF16)
        PRM = k.sb(es, "PRM", [128, 6, 8, 2], F32)
        k.dma("sp", GN[:].rearrange("p a l c -> p (a l) c"),
              gains_d.rearrange("a l (c p) -> p (a l) c", p=128), writes=["GN"], allow_slow_non_contiguous=True)
        for j in range(2):
            k.dma("sp", SC[:, :, j], cc_d[j].rearrange("(c p) -> p c", p=128), writes=["SC"], allow_slow_non_contiguous=True)
        k.op("act", lambda: S.activation(SCb[:], SC[:], AF.Silu), reads=["SC"], writes=["SCb"])


        MAGIC = 12582912.0
        TWO_PI = float(2 * np.pi)
        if mixer and any(l % 2 == 0 for l in (layers if layers is not None else range(nlayers))):
            with contextlib.ExitStack() as ph:
                cidx = k.sb(ph, "cidx", [128, 2048], F32)
                prow = k.sb(ph, "prow", [128, 1], F32)
                tcol = k.sb(ph, "tcol", [128, 1], F32)
                k.op("pool", lambda: G.iota(cidx[:], pattern=[[1, 2048]], base=0, channel_multiplier=0, allow_small_or_imprecise_dtypes=True), writes=["cidx"])
                k.op("pool", lambda: G.iota(prow[:], pattern=[[0, 1]], base=0, channel_multiplier=1, allow_small_or_imprecise_dtypes=True), writes=["prow"])
                uu = [k.sb(ph, "uu%d" % z, [128, 2048], F32) for z in range(2)]
                nn = [k.sb(ph, "nn%d" % z, [128, 2048], F32) for z in range(2)]
                n2 = [k.sb(ph, "nq%d" % z, [128, 2048], F32) for z in range(2)]
                ff = [k.sb(ph, "ff%d" % z, [128, 2048], F32) for z in range(2)]
                tb = [k.sb(ph, "tb%d" % z, [128, 2048], BF16) for z in range(2)]
                tcs = [k.sb(ph, "tcs%d" % z, [128, 1], F32) for z in range(2)]
                SCL = TWO_PI * (1.0 - 2e-6)
                for tt in range(16):
                    tc_ = tcs[tt % 2]
                    k.op("dve", lambda: V.tensor_scalar(tc_[:], prow[:], float(tt * 128), 1.0 / 2048, ALU.add, ALU.mult), reads=["prow", "tcs%d" % (tt % 2)], writes=["tcs%d" % (tt % 2)])
                    for z, (tab, shift, scl) in enumerate(((CL, 0.25, SCL), (SLn, 0.0, -SCL))):
                        k.op("act", lambda: S.activation(uu[z][:], cidx[:], AF.Identity, bias=shift, scale=tc_[:, 0:1]), reads=["cidx", "tcs%d" % (tt % 2), "uu%d" % z], writes=["uu%d" % z])
                        k.op("act", lambda: S.activation(nn[z][:], uu[z][:], AF.Identity, bias=MAGIC, scale=1.0), reads=["uu%d" % z, "nn%d" % z], writes=["nn%d" % z])
                        k.op("act", lambda: S.activation(n2[z][:], nn[z][:], AF.Identity, bias=-MAGIC, scale=1.0), reads=["nn%d" % z, "nq%d" % z], writes=["nq%d" % z])
                        k.op("dve", lambda: V.tensor_tensor(ff[z][:], uu[z][:], n2[z][:], ALU.subtract), reads=["uu%d" % z, "nq%d" % z, "ff%d" % z], writes=["ff%d" % z])
                        k.op("act", lambda: S.activation(tb[z][:], ff[z][:], AF.Sin, scale=scl), reads=["ff%d" % z, "tb%d" % z], writes=["tb%d" % z])
                        k.dma("sp", tab[tt * 128:(tt + 1) * 128, :], tb[z][:], reads=["tb%d" % z], writes=["TAB"])
                k.barrier()
            with contextlib.ExitStack() as ph:
                pidx = k.sb(ph, "rpidx", [128, 1], I32)
                pi2 = k.sb(ph, "rpi2", [128, 1], I32)
                fi = k.sb(ph, "rfi", [128, 1], F32)
                invp = k.sb(ph, "rinvp", [128, 1], F32)
                axs = k.sb(ph, "raxs", [128, 1], F32)
                sgn = k.sb(ph, "rsgn", [128, 1], F32)
                k.op("pool", lambda: G.iota(pidx[:], pattern=[[0, 1]], base=0, channel_multiplier=1), writes=["pidx"])
                k.op("dve", lambda: V.tensor_scalar(pi2[:], pidx[:], 15, None, ALU.bitwise_and), reads=["pidx"], writes=["pi2"])
                k.op("dve", lambda: V.tensor_copy(fi[:], pi2[:]), reads=["pi2"], writes=["fi"])
                k.op("act", lambda: S.activation(invp[:], fi[:], AF.Exp, scale=-float(np.log(10000.0)) / 16.0), reads=["fi"], writes=["invp"])
                k.op("dve", lambda: V.tensor_scalar(pi2[:], pidx[:], 5, 1, ALU.arith_shift_right, ALU.bitwise_and), reads=["pidx", "fi"], writes=["pi2"])
                k.op("dve", lambda: V.tensor_copy(axs[:], pi2[:]), reads=["pi2"], writes=["axs"])
                k.op("dve", lambda: V.tensor_scalar(pi2[:], pidx[:], 4, 1, ALU.arith_shift_right, ALU.bitwise_and), reads=["pidx", "axs"], writes=["pi2"])
                k.op("dve", lambda: V.tensor_copy(sgn[:], pi2[:]), reads=["pi2"], writes=["sgn"])
                k.op("dve", lambda: V.tensor_scalar(sgn[:], sgn[:], 2.0, -1.0, ALU.mult, ALU.add), reads=["sgn"], writes=["sgn"])
                rowp = k.sb(ph, "rowp", [128, 2048], F32)
                colp = k.sb(ph, "colp", [128, 2048], F32)
                ang = k.sb(ph, "rang", [128, 2048], F32)
                u_ = k.sb(ph, "ru", [128, 2048], F32)
                n_ = k.sb(ph, "rn", [128, 2048], F32)
                k.op("pool", lambda: G.iota(rowp[:], pattern=[[1, 32], [0, 64]], base=0, channel_multiplier=0, allow_small_or_imprecise_dtypes=True), writes=["rowp"])
                k.op("pool", lambda: G.iota(colp[:], pattern=[[0, 32], [1, 64]], base=0, channel_multiplier=0, allow_small_or_imprecise_dtypes=True), writes=["colp"])
                k.op("dve", lambda: V.tensor_tensor(colp[:], colp[:], rowp[:], ALU.subtract), reads=["colp", "rowp"], writes=["colp"])
                k.op("dve", lambda: V.scalar_tensor_tensor(out=ang[:], in0=colp[:], scalar=axs[:, 0:1], in1=rowp[:], op0=ALU.mult, op1=ALU.add), reads=["colp", "rowp", "axs"], writes=["ang"])
                k.op("dve", lambda: V.tensor_scalar(ang[:], ang[:], invp[:, 0:1], 1.0 / TWO_PI, ALU.mult, ALU.mult), reads=["ang", "invp"], writes=["ang"])
                for (tab, shift) in ((ROPC, 0.25), (ROPS, 0.0)):
                    k.op("dve", lambda: V.tensor_scalar(u_[:], ang[:], shift, None, ALU.add), reads=["ang", "ru"], writes=["ru"])
                    k.op("dve", lambda: V.tensor_scalar(n_[:], u_[:], MAGIC, None, ALU.add), reads=["ru", "rn"], writes=["rn"])
                    k.op("dve", lambda: V.tensor_scalar(n_[:], n_[:], -MAGIC, None, ALU.add), reads=["rn"], writes=["rn"])
                    k.op("dve", lambda: V.tensor_tensor(u_[:], u_[:], n_[:], ALU.subtract), reads=["rn", "ru"], writes=["ru"])
                    k.op("dve", lambda: V.tensor_scalar(u_[:], u_[:], -0.49999, 0.49999, ALU.max, ALU.min), reads=["ru"], writes=["ru"])
                    k.op("act", lambda: S.activation(n_[:], u_[:], AF.Sin, scale=TWO_PI), reads=["ru", "rn"], writes=["rn"])
                    if shift == 0.0:
                        k.op("dve", lambda: V.tensor_scalar(n_[:], n_[:], sgn[:, 0:1], None, ALU.mult), reads=["rn", "sgn"], writes=["rn"])
                    k.dma("sp", tab[:, :], n_[:], reads=["rn"], writes=["ROP"])
                k.barrier()
        mask_ge = k.sb(es, "mask_ge", [128, 128], BF16)
        mask_le = k.sb(es, "mask_le", [128, 128], BF16)
        ones64 = k.sb(es, "ones64", [128, 64], BF16)
        k.op("pool", lambda: G.memset(mask_ge[:], 1.0), writes=["mask_ge"])
        k.op("pool", lambda: G.affine_select(out=mask_ge[:], in_=mask_ge[:], compare_op=ALU.is_ge, fill=0.0, base=0, pattern=[[-1, 128]], channel_multiplier=1), reads=["mask_ge"], writes=["mask_ge"])
        k.op("pool", lambda: G.memset(mask_le[:], 1.0), writes=["mask_le"])
        k.op("pool", lambda: G.affine_select(out=mask_le[:], in_=mask_le[:], compare_op=ALU.is_ge, fill=0.0, base=0, pattern=[[1, 128]], channel_multiplier=-1), reads=["mask_le"], writes=["mask_le"])
        k.op("pool", lambda: G.memset(ones64[:], 1.0), writes=["ones64"])
        with contextlib.ExitStack() as ph:
            xin = [k.sb(ph, "xin%d" % i, [128, D], F32) for i in range(2)]
            xst = [k.sb(ph, "xst%d" % i, [128, 8, 128], F32) for i in range(2)]
            for tt in range(NT // 128):
                b = tt % 2
                src = ctx_d[tt * 128:(tt + 1) * 128, :] if tt < 2 else x_d[(tt - 2) * 128:(tt - 1) * 128, :]
                k.dma("sp", xin[b][:], src, writes=["xin%d" % b])
                for half in range(2):
                    pb = psb[(tt * 2 + half) % 8]
                    for c4 in range(4):
                        ct = half * 4 + c4
                        k.op("pe", lambda: P.transpose(pb[:, c4 * 128:(c4 + 1) * 128], xin[b][:, ct * 128:(ct + 1) * 128], ident[:]),
                             reads=["xin%d" % b, "ident"], writes=["psb%d" % ((tt * 2 + half) % 8)], inc=(c4 == 3))
                    eng = "act" if half == 0 else "dve"
                    if eng == "act":
                        k.op("act", lambda: S.copy(xst[b][:, half * 4:(half + 1) * 4, :].rearrange("p c t -> p (c t)"), pb[:]),
                             reads=["psb%d" % ((tt * 2 + half) % 8)], writes=["xst%d_%d" % (b, half)])
                    else:
                        k.op("dve", lambda: V.tensor_copy(xst[b][:, half * 4:(half + 1) * 4, :].rearrange("p c t -> p (c t)"), pb[:]),
                             reads=["psb%d" % ((tt * 2 + half) % 8)], writes=["xst%d_%d" % (b, half)])
                k.dma("sp", XTv[:, :, tt * 128:(tt + 1) * 128], xst[b][:], reads=["xst%d_0" % b, "xst%d_1" % b], writes=["XT"])
            k.barrier()

        for l in (layers if layers is not None else range(nlayers)):
            with contextlib.ExitStack() as ph:
                mw = [k.sb(ph, "mw%d" % i, [128, 8, 512], BF16) for i in range(2)]
                mbias = k.sb(ph, "mbias", [128, 48], F32)
                k.dma("sp", mbias[:], mod_b_d[l].rearrange("(c p) -> p c", p=128), writes=["mbias"], allow_slow_non_contiguous=True)
                for ch in range(12):
                    b = ch % 2
                    k.dma("pool", mw[b][:], mod_w_d[l][:, ch * 512:(ch + 1) * 512].rearrange("(kt p) n -> p kt n", p=128),
                          writes=["mw%d" % b])
                    for s4 in range(4):
                        col = ch * 4 + s4
                        pb = psb[col % 8]
                        for kt in range(8):
                            k.op("pe", lambda: P.matmul(pb[:, 0:2], mw[b][:, kt, s4 * 128:(s4 + 1) * 128], SCb[:, kt, :],
                                                        start=(kt == 0), stop=(kt == 7)),
                                 reads=["mw%d" % b, "SCb"], writes=["psb%d" % (col % 8)], inc=(kt == 7))
                        k.op("dve", lambda: V.tensor_scalar(MV[:, col // 8, col % 8, :], pb[:, 0:2], mbias[:, col:col + 1], None, ALU.add),
                             reads=["psb%d" % (col % 8), "mbias"], writes=["MV"])
                for (o, isc, ish, ig, gpre, gpost) in ((0, 1, 0, 2, 0, 1), (3, 4, 3, 5, 2, 3)):
                    for j in range(2):
                        k.op("dve", lambda: V.scalar_tensor_tensor(out=PRM[:, o, :, j], in0=MV[:, isc, :, j], scalar=1.0, in1=GN[:, gpre, l, :],
                                                                   op0=ALU.add, op1=ALU.mult), reads=["MV", "GN"], writes=["PRM"])
                        k.op("dve", lambda: V.tensor_copy(PRM[:, o + 1, :, j], MV[:, ish, :, j]), reads=["MV"], writes=["PRM"])
                        k.op("dve", lambda: V.tensor_tensor(PRM[:, o + 2, :, j], MV[:, ig, :, j], GN[:, gpost, l, :], ALU.mult),
                             reads=["MV", "GN"], writes=["PRM"])
                k.barrier()

            def rms_bc(ph_tiles, src3, key_src, tag):
                sq, rs, pbank, pkey = ph_tiles
                if src3 is not None:
                    k.op("act", lambda: S.activation(sq[:], src3, AF.Square), reads=[key_src], writes=["sq"])
                for ct in range(8):
                    k.op("pe", lambda: P.matmul(pbank[:, 0:TB], onesm[:], sq[:, ct, :], start=(ct == 0), stop=(ct == 7)),
                         reads=["sq", "onesm"], writes=[pkey], inc=(ct == 7))
                k.op("dve", lambda: V.tensor_scalar(rs[:], pbank[:, 0:TB], EPS, None, ALU.add), reads=[pkey], writes=["rs"])
                k.op("act", lambda: S.activation(rs[:], rs[:], AF.Sqrt), reads=["rs"], writes=["rs"])
                k.op("dve", lambda: V.reciprocal(rs[:], rs[:]), reads=["rs"], writes=["rs"])
                return rs


            def prenorm_to_hT(ph, hT):
                xb = k.sb(ph, "pxb", [128, 8, TB], F32)
                sq = k.sb(ph, "psq", [128, 8, TB], F32)
                tmp = k.sb(ph, "ptmp", [128, 8, TB], F32)
                rs = k.sb(ph, "prs", [128, TB], F32)
                tiles = (sq, rs, psb[6], "psb6")
                for blk in range(NBLK):
                    j = 1 if blk == 0 else 0
                    t0 = blk * TB
                    k.dma("sp", xb[:], XTv[:, :, t0:t0 + TB], reads=["XT"], writes=["xb"])
                    rms_bc(tiles, xb[:], "xb", "p")
                    k.op("dve", lambda: V.tensor_tensor(tmp[:], xb[:], bc_mid(rs[:], 8), ALU.mult), reads=["xb", "rs"], writes=["tmp"])
                    for ct in range(8):
                        k.op("act", lambda: S.activation(hT[:, ct, t0:t0 + TB], tmp[:, ct, :], AF.Identity, bias=PRM[:, 1, ct, j:j + 1],
                                                         scale=PRM[:, 0, ct, j:j + 1]), reads=["tmp", "PRM"], writes=["hT"])

            def s5_mixer(l):
                i = l // 2
                PI = float(np.pi)
                W = 64
                NW = NT // W
                with contextlib.ExitStack() as ph:
                    hT = k.sb(ph, "hT", [128, 8, NT], BF16)
                    with contextlib.ExitStack() as ph2:
                        prenorm_to_hT(ph2, hT)
                        k.barrier()
                    dsk = k.sb(ph, "dsk", [128, 8], F32)
                    k.dma("sp", dsk[:], dsk_d[i].rearrange("(c p) -> p c", p=128), writes=["dsk"], allow_slow_non_contiguous=True)
                    sc = contextlib.ExitStack()
                    WinT = [[k.sb(sc, "WinT%d%d" % (d, ri), [128, 8, 128], BF16) for ri in range(2)] for d in range(2)]
                    CwQ = [[k.sb(sc, "CwQ%d%d" % (d, ri), [128, 32, 128], BF16) for ri in range(2)] for d in range(2)]
                    Gt = [k.sb(sc, "Gt%d" % d, [128, 2, 64], F32) for d in range(2)]
                    with contextlib.ExitStack() as pp:
                        def t32(n):
                            return k.sb(pp, n, [128, 32], F32)
                        twopi = t32("twopi")
                        k.op("pool", lambda: G.memset(twopi[:], 2 * PI), writes=["twopi"])
                        pidx = k.sb(pp, "pidx", [128, 1], I32)
                        modd = k.sb(pp, "modd", [128, 1], F32)
                        mevn = k.sb(pp, "mevn", [128, 1], F32)
                        nodd = k.sb(pp, "nodd", [128, 1], F32)
                        nevn = k.sb(pp, "nevn", [128, 1], F32)
                        k.op("pool", lambda: G.iota(pidx[:], pattern=[[0, 1]], base=0, channel_multiplier=1), writes=["pidx"])
                        k.op("dve", lambda: V.tensor_scalar(pidx[:], pidx[:], 4, 1, ALU.arith_shift_right, ALU.bitwise_and), reads=["pidx"], writes=["pidx"])
                        k.op("dve", lambda: V.tensor_copy(modd[:], pidx[:]), reads=["pidx"], writes=["modd"])
                        k.op("dve", lambda: V.tensor_scalar(mevn[:], modd[:], -1.0, 1.0, ALU.mult, ALU.add), reads=["modd"], writes=["mevn"])
                        k.op("dve", lambda: V.tensor_scalar(nodd[:], modd[:], -1.0, None, ALU.mult), reads=["modd"], writes=["nodd"])
                        k.op("dve", lambda: V.tensor_scalar(nevn[:], mevn[:], -1.0, None, ALU.mult), reads=["mevn"], writes=["nevn"])
                        Br = k.sb(pp, "Br", [128, 32, 32], F32)
                        Bi = k.sb(pp, "Bi", [128, 32, 32], F32)
                        BbR = k.sb(pp, "BbR", [128, 32, 32], F32)
                        BbI = k.sb(pp, "BbI", [128, 32, 32], F32)
                        T1 = k.sb(pp, "T1", [128, 32, 32], F32)
                        Cn = k.sb(pp, "Cn", [128, 8, 64], F32)
                        Cblk = k.sb(pp, "Cblk", [128, 8, 128], F32)
                        lr, li, dtt, tq, mag, ang, sa, sinv, cosv, Ar, Ai, am1, n2, kr, ki, u1 = [t32("p%d" % z) for z in range(16)]
                        for d in range(2):
                            def dve(fn, r, w):
                                k.op("dve", fn, reads=r, writes=w)
                            def act(fn, r, w):
                                k.op("act", fn, reads=r, writes=w)
                            k.dma("sp", lr[:], a_re_d[i, d].rearrange("(q a) p -> (a p) q", a=2), writes=["lr"], allow_slow_non_contiguous=True)
                            k.dma("sp", li[:], a_im_d[i, d].rearrange("(q a) p -> (a p) q", a=2), writes=["li"], allow_slow_non_contiguous=True)
                            for g2 in range(2):
                                base = ldt_d[i, d]
                                src = bass.AP(base.tensor, base.offset + g2, [[0, 64], [2, 32]])
                                k.dma("sp", dtt[g2 * 64:(g2 + 1) * 64, :], src, writes=["dtt"], allow_slow_non_contiguous=True)
                            act(lambda: S.activation(dtt[:], dtt[:], AF.Exp), ["dtt"], ["dtt"])
                            dve(lambda: V.tensor_tensor(tq[:], lr[:], dtt[:], ALU.mult), ["lr", "dtt"], ["tq"])
                            act(lambda: S.activation(mag[:], tq[:], AF.Exp), ["tq"], ["mag"])
                            dve(lambda: V.tensor_tensor(ang[:], li[:], dtt[:], ALU.mult), ["li", "dtt"], ["ang"])
                            MAGIC = 12582912.0
                            PIC = 3.1415925
                            for (dst, shift, tag) in ((sinv, 0.0, "s"), (cosv, 0.5 * PI, "c")):
                                dve(lambda: V.tensor_scalar(u1[:], ang[:], shift, None, ALU.add), ["ang", "sa", "u1"], ["u1"])
                                dve(lambda: V.tensor_scalar(sa[:], u1[:], 1.0 / (2 * PI), None, ALU.mult), ["u1", "sa"], ["sa"])
                                dve(lambda: V.tensor_scalar(sa[:], sa[:], MAGIC, None, ALU.add), ["sa"], ["sa"])
                                dve(lambda: V.tensor_scalar(sa[:], sa[:], -MAGIC, None, ALU.add), ["sa"], ["sa"])
                                dve(lambda: V.scalar_tensor_tensor(out=sa[:], in0=sa[:], scalar=-2 * PI, in1=u1[:], op0=ALU.mult, op1=ALU.add), ["sa", "u1"], ["sa"])
                                dve(lambda: V.tensor_scalar(sa[:], sa[:], -PIC, PIC, ALU.max, ALU.min), ["sa"], ["sa"])
                                act(lambda: S.activation(dst[:], sa[:], AF.Sin), ["sa"], ["sinv" if tag == "s" else "cosv"])
                            dve(lambda: V.tensor_tensor(Ar[:], mag[:], cosv[:], ALU.mult), ["mag", "cosv"], ["Ar"])
                            dve(lambda: V.tensor_tensor(Ai[:], mag[:], sinv[:], ALU.mult), ["mag", "sinv"], ["Ai"])
                            gk = "Gt%d" % d
                            Arp = Ar[:].rearrange("p (c r) -> p r c", r=4)
                            Aip = Ai[:].rearrange("p (c r) -> p r c", r=4)
                            gv = lambda a, lo: Gt[d][:, a, lo:lo + 32].rearrange("p (r c) -> p r c", r=4)
                            dve(lambda: V.tensor_copy(gv(0, 0), Arp), ["Ar"], [gk])
                            dve(lambda: V.tensor_scalar(gv(0, 32), Aip, -1.0, None, ALU.mult), ["Ai", gk], [gk])
                            dve(lambda: V.tensor_copy(gv(1, 0), Aip), ["Ai", gk], [gk])
                            dve(lambda: V.tensor_copy(gv(1, 32), Arp), ["Ar", gk], [gk])
                            dve(lambda: V.tensor_scalar(am1[:], Ar[:], -1.0, None, ALU.add), ["Ar"], ["am1"])
                            dve(lambda: V.tensor_tensor(n2[:], lr[:], lr[:], ALU.mult), ["lr"], ["n2"])
                            dve(lambda: V.tensor_tensor(u1[:], li[:], li[:], ALU.mult), ["li"], ["u1"])
                            dve(lambda: V.tensor_tensor(n2[:], n2[:], u1[:], ALU.add), ["n2", "u1"], ["n2"])
                            dve(lambda: V.reciprocal(n2[:], n2[:]), ["n2"], ["n2"])
                            dve(lambda: V.tensor_tensor(kr[:], am1[:], lr[:], ALU.mult), ["am1", "lr"], ["kr"])
                            dve(lambda: V.tensor_tensor(u1[:], Ai[:], li[:], ALU.mult), ["Ai", "li", "n2"], ["u1"])
                            dve(lambda: V.tensor_tensor(kr[:], kr[:], u1[:], ALU.add), ["kr", "u1"], ["kr"])
                            dve(lambda: V.tensor_tensor(kr[:], kr[:], n2[:], ALU.mult), ["kr", "n2"], ["kr"])
                            dve(lambda: V.tensor_tensor(ki[:], Ai[:], lr[:], ALU.mult), ["Ai", "lr"], ["ki"])
                            dve(lambda: V.tensor_tensor(u1[:], am1[:], li[:], ALU.mult), ["am1", "li", "kr"], ["u1"])
                            dve(lambda: V.tensor_tensor(ki[:], ki[:], u1[:], ALU.subtract), ["ki", "u1"], ["ki"])
                            dve(lambda: V.tensor_tensor(ki[:], ki[:], n2[:], ALU.mult), ["ki", "n2"], ["ki"])
                            k.op("pool", lambda: G.memset(Br[:], 0.0), reads=["BbR", "BbI"], writes=["Br"])
                            k.op("pool", lambda: G.memset(Bi[:], 0.0), reads=["BbR", "BbI"], writes=["Bi"])
                            for g2 in range(2):
                                for (dst, srcd, key) in ((Br, b_re_d, "Br"), (Bi, b_im_d, "Bi")):
                                    base = srcd[i, d]
                                    src = bass.AP(base.tensor, base.offset + g2 * 1024, [[16, 64], [2048, 32], [1, 16]])
                                    k.dma("sp", dst[g2 * 64:(g2 + 1) * 64, :, g2 * 16:(g2 + 1) * 16], src, reads=[key], writes=[key])
                            def bc_last(a2, n):
                                a = a2.ap
                                return bass.AP(a2.tensor, a2.offset, [list(a[0]), list(a[1]), [0, n]])
                            krb, kib = bc_last(kr[:], 32), bc_last(ki[:], 32)
                            dve(lambda: V.tensor_tensor(BbR[:], Br[:], krb, ALU.mult), ["Br", "kr"], ["BbR"])
                            dve(lambda: V.tensor_tensor(T1[:], Bi[:], kib, ALU.mult), ["Bi", "ki"], ["T1"])
                            dve(lambda: V.tensor_tensor(BbR[:], BbR[:], T1[:], ALU.subtract), ["BbR", "T1"], ["BbR"])
                            dve(lambda: V.tensor_tensor(BbI[:], Bi[:], krb, ALU.mult), ["Bi", "kr"], ["BbI"])
                            dve(lambda: V.tensor_tensor(T1[:], Br[:], kib, ALU.mult), ["Br", "ki", "BbR"], ["T1"])
                            dve(lambda: V.tensor_tensor(BbI[:], BbI[:], T1[:], ALU.add), ["BbI", "T1"], ["BbI"])
                            for ri, srcb, key in ((0, BbR, "BbR"), (1, BbI, "BbI")):
                                for ct in range(8):
                                    pk = "psb%d" % (ct % 4)
                                    k.op("pe", lambda: P.transpose(psb[ct % 4][:, 0:128], srcb[:, 4 * ct:4 * ct + 4, :].rearrange("p a b -> p (a b)"), ident[:]),
                                         reads=[key, "ident"], writes=[pk])
                                    act(lambda: S.copy(WinT[d][ri][:, ct, :], psb[ct % 4][:, 0:128]), [pk], ["WinT%d%d" % (d, ri)])
                            for ri, srcd, mo, me in ((0, c_re_d, modd, mevn), (1, c_im_d, nodd, nevn)):
                                ck = "CwQ%d%d" % (d, ri)
                                k.op("pool", lambda: G.memset(CwQ[d][ri][:], 0.0), writes=[ck])
                                k.dma("sp", Cn[:], srcd[i, d].rearrange("(ct g) c p -> (g c) ct p", g=8), reads=["Cn"], writes=["Cn"])
                                dve(lambda: V.tensor_scalar(Cblk[:, :, 0:64], Cn[:], me[:, 0:1], None, ALU.mult), ["Cn", "mevn", "nevn", "Cblk"], ["Cblk"])
                                dve(lambda: V.tensor_scalar(Cblk[:, :, 64:128], Cn[:], mo[:, 0:1], None, ALU.mult), ["Cn", "modd", "nodd", "Cblk"], ["Cblk"])
                                for ct in range(8):
                                    pk = "psb%d" % (4 + ct % 4)
                                    k.op("pe", lambda: P.transpose(psb[4 + ct % 4][:, 0:128], Cblk[:, ct, :], ident[:]), reads=["Cblk", "ident"], writes=[pk])
                                    for q4 in range(4):
                                        act(lambda: S.copy(CwQ[d][ri][:, ct * 4 + q4, 32 * q4:32 * q4 + 32], psb[4 + ct % 4][:, 32 * q4:32 * q4 + 32]), [pk, ck], [ck])
                        k.barrier()
                    if S5_STAGE < 1:
                        sc.close()
                        return
                    Bw = [k.sb(sc, "Bw%d" % d, [128, 64, W], F32) for d in range(2)]
                    H = [k.sb(sc, "H%d" % d, [128, 64, W + 1], F32) for d in range(2)]
                    Sb = [k.sb(sc, "Sb%d" % d, [128, 64, W], BF16) for d in range(2)]
                    XY = [k.sb(sc, "XY%d" % d, [128, 2, 64], F32) for d in range(2)]
                    Nn = [k.sb(sc, "Nn%d" % d, [128, 2, 32], F32) for d in range(2)]
                    yo = [k.sb(sc, "yo%d" % d, [128, 8, W], F32) for d in range(2)]
                    k.op("pool", lambda: G.memset(H[0][:], 0.0), writes=["H0"])
                    k.op("pool", lambda: G.memset(H[1][:], 0.0), writes=["H1"])
                    nwc = NCTX // W
                    order_f = list(range(NW))
                    order_b = list(range(nwc - 1, -1, -1)) + list(range(NW - 1, nwc - 1, -1))
                    for step_w in range(NW if S5_NW is None else S5_NW):
                        wins = (order_f[step_w], order_b[step_w])
                        for d in range(2):
                            t0 = wins[d] * W
                            for half in range(2):
                                for c4 in range(4):
                                    ct = half * 4 + c4
                                    for ri in range(2):
                                        for r in range(4):
                                            slot = c4 * 2 + ri
                                            last = (c4 == 3 and ri == 1)
                                            k.op("pe", lambda: P.matmul(psb[r][:, slot * W:(slot + 1) * W],
                                                                        WinT[d][ri][32 * r:32 * r + 32, ct, :],
                                                                        hT[32 * r:32 * r + 32, ct, t0:t0 + W],
                                                                        start=True, stop=True, tile_position=(32 * r, 0)),
                                                 reads=["hT", "WinT%d%d" % (d, ri)], writes=["psb%d" % r], inc=(last and r == 3))
                                for r in range(4):
                                    for ri in range(2):
                                        src = psb[r][:].rearrange("p (c i w) -> p c i w", i=2, w=W)[:, :, ri, :]
                                        lo = ri * 32 + r * 8 + half * 4
                                        eng = "act" if ri == 0 else "dve"
                                        if eng == "act":
                                            k.op("act", lambda: S.copy(Bw[d][:, lo:lo + 4, :], src), reads=["psb%d" % r, "Bw%d" % d], writes=["Bw%d" % d])
                                        else:
                                            k.op("dve", lambda: V.tensor_copy(Bw[d][:, lo:lo + 4, :], src), reads=["psb%d" % r, "Bw%d" % d], writes=["Bw%d" % d])
                        for jj in range(W if S5_SUB >= 1 else 0):
                            cols = ((jj, jj + 1, jj), (W - jj, W - 1 - jj, W - 1 - jj))
                            for d in range(2):
                                pc, ncol, bj = cols[d]
                                k.op("dve", lambda: V.tensor_tensor(XY[d][:], bc_mid(H[d][:, :, pc], 2), Gt[d][:], ALU.mult),
                                     reads=["H%d" % d, "Gt%d" % d], writes=["XY%d" % d])
                            for d in range(2):
                                xv = XY[d][:].rearrange("p a (b q) -> p a b q", b=2)
                                k.op("dve", lambda: V.tensor_tensor(Nn[d][:], xv[:, :, 0, :], xv[:, :, 1, :], ALU.add),
                                     reads=["XY%d" % d], writes=["Nn%d" % d])
                            for d in range(2):
                                pc, ncol, bj = cols[d]
                                k.op("dve", lambda: V.tensor_tensor(H[d][:, :, ncol], Nn[d][:].rearrange("p a q -> p (a q)"), Bw[d][:, :, bj], ALU.add),
                                     reads=["Nn%d" % d, "Bw%d" % d], writes=["H%d" % d])
                        for d in range(2 if S5_SUB >= 2 else 0):
                            t0 = wins[d] * W
                            if d == 0:
                                k.op("act", lambda: S.copy(Sb[0][:], H[0][:, :, 1:W + 1]), reads=["H0"], writes=["Sb0"])
                                k.op("dve", lambda: V.tensor_copy(H[0][:, :, 0], H[0][:, :, W]), reads=["H0"], writes=["H0"])
                            else:
                                k.op("act", lambda: S.copy(Sb[1][:], H[1][:, :, 0:W]), reads=["H1"], writes=["Sb1"])
                                k.op("dve", lambda: V.tensor_copy(H[1][:, :, W], H[1][:, :, 0]), reads=["H1"], writes=["H1"])
                            for ct in range(8):
                                pk = "psb%d" % (4 + ct % 4)
                                n = 0
                                for q4 in range(4):
                                    for ri in range(2):
                                        q = ct * 4 + q4
                                        k.op("pe", lambda: P.matmul(psb[4 + ct % 4][:, 0:W], CwQ[d][ri][:, q, :], Sb[d][:, ri * 32 + q4 * 8 + ct, :],
                                                                    start=(n == 0), stop=(n == 7)),
                                             reads=["CwQ%d%d" % (d, ri), "Sb%d" % d], writes=[pk], inc=(n == 7))
                                        n += 1
                                k.op("act", lambda: S.copy(yo[d][:, ct, :], psb[4 + ct % 4][:, 0:W]), reads=[pk, "yo%d" % d], writes=["yo%d" % d])
                            k.dma("sp", YF[d].rearrange("(ct p) t -> p ct t", p=128)[:, :, t0:t0 + W], yo[d][:], reads=["yo%d" % d], writes=["YF"])
                    k.barrier()
                    sc.close()
                    if S5_STAGE < 2:
                        return
                    with contextlib.ExitStack() as pg:
                        gw = k.sb(pg, "gw", [128, 8, 2 * D], BF16)
                        for kt in range(8):
                            k.dma("pool", gw[:, kt, :], glu_d[i][kt * 128:(kt + 1) * 128, :], writes=["gw"])
                        ya = k.sb(pg, "ya", [128, 8, TB], F32)
                        yb2 = k.sb(pg, "yb2", [128, 8, TB], F32)
                        y2 = k.sb(pg, "y2", [128, 8, TB], F32)
                        gl = k.sb(pg, "gl", [128, 8, TB], BF16)
                        sg = k.sb(pg, "sg", [128, TB], F32)
                        zo = k.sb(pg, "zo", [128, 8, TB], F32)
                        for blk in range(NBLK):
                            t0 = blk * TB
                            k.dma("sp", ya[:], YF[0].rearrange("(ct p) t -> p ct t", p=128)[:, :, t0:t0 + TB], reads=["YF"], writes=["ya"])
                            k.dma("sp", yb2[:], YF[1].rearrange("(ct p) t -> p ct t", p=128)[:, :, t0:t0 + TB], reads=["YF"], writes=["yb2"])
                            k.op("dve", lambda: V.tensor_tensor(ya[:], ya[:], yb2[:], ALU.add), reads=["ya", "yb2"], writes=["ya"])
                            for ct in range(8):
                                k.op("dve", lambda: V.scalar_tensor_tensor(out=ya[:, ct, :], in0=hT[:, ct, t0:t0 + TB], scalar=dsk[:, ct:ct + 1],
                                                                           in1=ya[:, ct, :], op0=ALU.mult, op1=ALU.add), reads=["ya", "hT", "dsk"], writes=["ya"])
                            k.op("dve", lambda: V.tensor_tensor(y2[:], ya[:], ya[:], ALU.mult), reads=["ya"], writes=["y2"])
                            k.op("dve", lambda: V.tensor_scalar(y2[:], y2[:], 0.044715, 1.0, ALU.mult, ALU.add), reads=["y2"], writes=["y2"])
                            k.op("dve", lambda: V.tensor_tensor(y2[:], y2[:], ya[:], ALU.mult), reads=["y2", "ya"], writes=["y2"])
                            k.op("act", lambda: S.activation(y2[:], y2[:], AF.Sigmoid, scale=1.5957691216057308), reads=["y2"], writes=["y2"])
                            k.op("dve", lambda: V.tensor_tensor(gl[:], y2[:], ya[:], ALU.mult), reads=["y2", "ya"], writes=["gl"])
                            for ct in range(8):
                                pa, pb_ = psb[(2 * ct) % 4], psb[(2 * ct + 1) % 4]
                                ka, kb_ = "psb%d" % ((2 * ct) % 4), "psb%d" % ((2 * ct + 1) % 4)
                                for kt in range(8):
                                    k.op("pe", lambda: P.matmul(pa[:, 0:TB], gw[:, kt, ct * 128:(ct + 1) * 128], gl[:, kt, :], start=(kt == 0), stop=(kt == 7)),
                                         reads=["gw", "gl"], writes=[ka], inc=(kt == 7))
                                for kt in range(8):
                                    k.op("pe", lambda: P.matmul(pb_[:, 0:TB], gw[:, kt, D + ct * 128:D + (ct + 1) * 128], gl[:, kt, :], start=(kt == 0), stop=(kt == 7)),
                                         reads=["gw", "gl"], writes=[kb_], inc=(kt == 7))
                                k.op("act", lambda: S.activation(sg[:], pb_[:, 0:TB], AF.Sigmoid), reads=[kb_], writes=["sg"])
                                k.op("dve", lambda: V.tensor_tensor(zo[:, ct, :], pa[:, 0:TB], sg[:], ALU.mult), reads=[ka, "sg", "zo"], writes=["zo"])
                            k.dma("sp", YTv[:, :, t0:t0 + TB], zo[:], reads=["zo"], writes=["YT"])
                        k.barrier()

            def even_mixer(l):
                i = l // 2
                NLAT = 2048
                with contextlib.ExitStack() as ph:
                    fT = k.sb(ph, "fT", [128, 4, NT], BF16)
                    QT = k.sb(ph, "QT", [128, 4, NT], BF16)
                    KT = k.sb(ph, "KT", [128, 2, NT], BF16)
                    Vtm = k.sb(ph, "Vtm", [128, NT // 128, 128], BF16)
                    mixT = k.sb(ph, "mixT", [128, 8, NT], BF16)
                    SEall = k.sb(ph, "SEall", [128, 8], F32)
                    SE = k.sb(ph, "SE", [128, 2, 2], F32)
                    sk = even_sink_d[i]
                    k.dma("sp", SEall[:], bass.AP(sk.tensor, sk.offset, [[0, 128], [1, 8]]), writes=["SEall"], allow_slow_non_contiguous=True)
                    k.op("act", lambda: S.activation(SEall[:], SEall[:], AF.Exp), reads=["SEall"], writes=["SEall"])
                    for kh in range(2):
                        for tl in range(2):
                            k.op("dve", lambda: V.tensor_copy(SE[0:64, kh, tl:tl + 1], SEall[0:64, 4 * kh + 2 * tl:4 * kh + 2 * tl + 1]), reads=["SEall", "SE"], writes=["SE"])
                            k.op("dve", lambda: V.tensor_copy(SE[64:128, kh, tl:tl + 1], SEall[64:128, 4 * kh + 2 * tl + 1:4 * kh + 2 * tl + 2]), reads=["SEall", "SE"], writes=["SE"])
                    with contextlib.ExitStack() as pa:
                        hT = k.sb(pa, "hT", [128, 8, NT], BF16)
                        with contextlib.ExitStack() as ph2:
                            prenorm_to_hT(ph2, hT)
                            k.barrier()
                        wb = k.sb(pa, "wb", [128, 8, 1280], BF16)
                        for kt in range(8):
                            k.dma("pool", wb[:, kt, :], w_in_d[i][kt * 128:(kt + 1) * 128, :], writes=["wb"])
                        wsw = k.sb(pa, "wsw", [128, 8, 640], BF16)
                        wv = wb[:, :, 512:1152].rearrange("p k (h two e) -> p k h two e", two=2, e=16)
                        wsv = wsw[:].rearrange("p k (h two e) -> p k h two e", two=2, e=16)
                        for kt in range(8):
                            k.op("pool", lambda: G.tensor_copy(wsv[:, kt, :, 0, :], wv[:, kt, :, 1, :]), reads=["wb", "wsw"], writes=["wsw"])
                            k.op("pool", lambda: G.tensor_copy(wsv[:, kt, :, 1, :], wv[:, kt, :, 0, :]), reads=["wb", "wsw"], writes=["wsw"])
                        wkd = k.sb(pa, "wkd", [128, 8, 2, 128], BF16)
                        wkds = k.sb(pa, "wkds", [128, 8, 2, 128], BF16)
                        for dup in range(2):
                            k.op("pool", lambda: G.tensor_copy(wkd[:, :, :, dup * 64:(dup + 1) * 64], wb[:, :, 1024:1152].rearrange("p k (h d) -> p k h d", d=64)), reads=["wb", "wkd"], writes=["wkd"])
                            k.op("pool", lambda: G.tensor_copy(wkds[:, :, :, dup * 64:(dup + 1) * 64], wsw[:, :, 512:640].rearrange("p k (h d) -> p k h d", d=64)), reads=["wsw", "wkds"], writes=["wkds"])
                        ropc = k.sb(pa, "ropc", [128, NLAT], F32)
                        rops = k.sb(pa, "rops", [128, NLAT], F32)
                        k.dma("sp", ropc[:], ROPC[:, :], reads=["ROP"], writes=["ropc"])
                        k.dma("sp", rops[:], ROPS[:, :], reads=["ROP"], writes=["rops"])
                        t1 = k.sb(pa, "rt1", [128, 512], F32)
                        t2 = k.sb(pa, "rt2", [128, 512], F32)
                        blocks = [(0, 256)] + [(256 + 512 * b, 512) for b in range(4)]
                        nb = 0
                        for (t0, n) in blocks:
                            lat = t0 >= NCTX
                            for g in range(4):
                                pb, pk = psb[nb % 4], "psb%d" % (nb % 4); nb += 1
                                for kt in range(8):
                                    k.op("pe", lambda: P.matmul(pb[:, 0:n], wb[:, kt, g * 128:(g + 1) * 128], hT[:, kt, t0:t0 + n], start=(kt == 0), stop=(kt == 7)),
                                         reads=["wb", "hT"], writes=[pk], inc=(kt == 7))
                                k.op("act", lambda: S.copy(fT[:, g, t0:t0 + n], pb[:, 0:n]), reads=[pk, "fT"], writes=["fT"])
                            for j in range(6):
                                if j < 4:
                                    lw = lambda kt: wb[:, kt, 512 + j * 128:512 + (j + 1) * 128]
                                    lws = lambda kt: wsw[:, kt, j * 128:(j + 1) * 128]
                                    dst = QT[:, j, t0:t0 + n]
                                    dk = "QT"
                                else:
                                    lw = lambda kt: wkd[:, kt, j - 4, :]
                                    lws = lambda kt: wkds[:, kt, j - 4, :]
                                    dst = KT[:, j - 4, t0:t0 + n]
                                    dk = "KT"
                                pb, pk = psb[nb % 4], "psb%d" % (nb % 4); nb += 1
                                for kt in range(8):
                                    k.op("pe", lambda: P.matmul(pb[:, 0:n], lw(kt), hT[:, kt, t0:t0 + n], start=(kt == 0), stop=(kt == 7)),
                                         reads=["wb", "wkd", "hT"], writes=[pk], inc=(kt == 7))
                                if not lat:
                                    k.op("act", lambda: S.copy(dst, pb[:, 0:n]), reads=[pk, dk], writes=[dk])
                                else:
                                    pb2, pk2 = psb[4 + nb % 4], "psb%d" % (4 + nb % 4)
                                    for kt in range(8):
                                        k.op("pe", lambda: P.matmul(pb2[:, 0:n], lws(kt), hT[:, kt, t0:t0 + n], start=(kt == 0), stop=(kt == 7)),
                                             reads=["wsw", "wkds", "hT"], writes=[pk2], inc=(kt == 7))
                                    r0 = t0 - NCTX
                                    k.op("dve", lambda: V.tensor_tensor(t1[:, 0:n], pb[:, 0:n], ropc[:, r0:r0 + n], ALU.mult), reads=[pk, "ropc", "rt1"], writes=["rt1"])
                                    k.op("dve", lambda: V.tensor_tensor(t2[:, 0:n], pb2[:, 0:n], rops[:, r0:r0 + n], ALU.mult), reads=[pk2, "rops", "rt2"], writes=["rt2"])
                                    k.op("pool", lambda: G.tensor_tensor(dst, t1[:, 0:n], t2[:, 0:n], ALU.add), reads=["rt1", "rt2", dk], writes=[dk])
                            for s in range(n // 128):
                                tt = (t0 + s * 128) // 128
                                pb, pk = psb[nb % 4], "psb%d" % (nb % 4); nb += 1
                                for kt in range(8):
                                    k.op("pe", lambda: P.matmul(pb[:, 0:128], hT[:, kt, tt * 128:(tt + 1) * 128], wb[:, kt, 1152:1280], start=(kt == 0), stop=(kt == 7)),
                                         reads=["wb", "hT"], writes=[pk], inc=(kt == 7))
                                k.op("act", lambda: S.copy(Vtm[:, tt, :], pb[:, 0:128]), reads=[pk, "Vtm"], writes=["Vtm"])
                        dump("dbg_fT", fT[:], ["fT"]); dump("dbg_QT", QT[:], ["QT"]); dump("dbg_KT", KT[:], ["KT"]); dump("dbg_Vtm", Vtm[:], ["Vtm"])
                        k.barrier()
                    with contextlib.ExitStack() as pf:
                        Gtm = k.sb(pf, "Gtm", [128, NT // 128, 4, 256], BF16)
                        csc = k.sb(pf, "csc", [128, 256], BF16)
                        k.dma("sp", csc[:, 0:128], CL.rearrange("(t e) c -> t e c", e=16)[:, 0, 0:128], reads=["TAB"], writes=["csc"])
                        k.dma("sp", csc[:, 128:256], SLn.rearrange("(t e) c -> t e c", e=16)[:, 0, 0:128], reads=["TAB"], writes=["csc"])
                        k.op("dve", lambda: V.tensor_scalar(csc[:, 128:256], csc[:, 128:256], -1.0, None, ALU.mult), reads=["csc"], writes=["csc"])
                        sc_lat = float(1.0 / np.sqrt(2048.0 * 128.0))
                        sc_ctx = float(1.0 / np.sqrt(256.0 * 128.0))
                        nb = 0
                        for tt in range(NT // 128):
                            for g in range(4):
                                pb, pk = psb[nb % 4], "psb%d" % (nb % 4); nb += 1
                                k.op("pe", lambda: P.matmul(pb[:, 0:256], fT[:, g, tt * 128:(tt + 1) * 128], csc[:], start=True, stop=True),
                                     reads=["fT", "csc"], writes=[pk])
                                k.op("act", lambda: S.activation(Gtm[:, tt, g, :], pb[:, 0:256], AF.Copy, scale=(sc_ctx if tt < 2 else sc_lat)), reads=[pk, "Gtm"], writes=["Gtm"])
                        cl = k.sb(pf, "cl", [128, 16, 512], BF16)
                        sl = k.sb(pf, "sl", [128, 16, 512], BF16)
                        c8 = CL.rearrange("(t e) c -> t e c", e=8)[:, 0, 0:256].rearrange("(tt p) c -> p tt c", p=128)
                        s8 = SLn.rearrange("(t e) c -> t e c", e=8)[:, 0, 0:256].rearrange("(tt p) c -> p tt c", p=128)
                        k.dma("sp", cl[:, 0:2, 0:256], c8, reads=["TAB"], writes=["cl"])
                        k.dma("sp", sl[:, 0:2, 0:256], s8, reads=["TAB"], writes=["sl"])
                        for g in range(4):
                            pb, pk = psb[4 + g % 4], "psb%d" % (4 + g % 4)
                            n_ = 0
                            for tt in range(2):
                                for (half, tabl, tk) in ((0, cl, "cl"), (1, sl, "sl")):
                                    k.op("pe", lambda: P.matmul(pb[:, 0:256], Gtm[:, tt, g, half * 128:(half + 1) * 128], tabl[:, tt, 0:256], start=(n_ == 0), stop=(n_ == 3)),
                                         reads=["Gtm", tk], writes=[pk], inc=(n_ == 3))
                                    n_ += 1
                            k.op("act", lambda: S.copy(mixT[:, g, 0:256], pb[:, 0:256]), reads=[pk, "mixT"], writes=["mixT"])
                        for pbk in range(4):
                            k.dma("sp", cl[:], CL[:, pbk * 512:(pbk + 1) * 512].rearrange("(tt p) c -> p tt c", p=128), reads=["TAB", "cl"], writes=["cl"])
                            k.dma("sp", sl[:], SLn[:, pbk * 512:(pbk + 1) * 512].rearrange("(tt p) c -> p tt c", p=128), reads=["TAB", "sl"], writes=["sl"])
                            for g in range(4):
                                pb, pk = psb[4 + g % 4], "psb%d" % (4 + g % 4)
                                n_ = 0
                                for tt in range(16):
                                    for (half, tabl, tk) in ((0, cl, "cl"), (1, sl, "sl")):
                                        k.op("pe", lambda: P.matmul(pb[:, 0:512], Gtm[:, 2 + tt, g, half * 128:(half + 1) * 128], tabl[:, tt, :], start=(n_ == 0), stop=(n_ == 31)),
                                             reads=["Gtm", tk], writes=[pk], inc=(n_ == 31))
                                        n_ += 1
                                k.op("act", lambda: S.copy(mixT[:, g, NCTX + pbk * 512:NCTX + (pbk + 1) * 512], pb[:, 0:512]), reads=[pk, "mixT"], writes=["mixT"])
                        k.barrier()
                    with contextlib.ExitStack() as pt:
                        PT = [k.sb(pt, "PT%d" % z, [128, 2, 2, 128], BF16) for z in range(2)]
                        rden = k.sb(pt, "rden", [128, 2, 128], F32)
                        scale = 0.125
                        it = 0
                        qblocks = [("c", 0), ("c", 1)] + [("l", n) for n in range(16)]
                        for (kind, n) in qblocks:
                            q0 = n * 128 if kind == "c" else NCTX + n * 128
                            for kh in range(2):
                                chunks = []
                                if kind == "l":
                                    for dlt in (-1, 0, 1):
                                        if 0 <= n + dlt < 16:
                                            chunks.append((NCTX + (n + dlt) * 128, dlt))
                                chunks += [(0, 0), (128, 0)]
                                for ci, (k0, dlt) in enumerate(chunks):
                                    z = it % 2
                                    it += 1
                                    pS = (psb[0 + 2 * z], psb[1 + 2 * z])
                                    kS = ("psb%d" % (2 * z), "psb%d" % (1 + 2 * z))
                                    for par in range(2):
                                        k.op("pe", lambda: P.matmul(pS[par][:, 0:256].rearrange("p (t q) -> p t q", t=2),
                                                                    KT[64 * par:64 * par + 64, kh, k0:k0 + 128],
                                                                    QT[64 * par:64 * par + 64, 2 * kh:2 * kh + 2, q0:q0 + 128],
                                                                    start=True, stop=True, tile_position=(64 * par, 0)),
                                             reads=["KT", "QT"], writes=[kS[par]])
                                    for par in range(2):
                                        k.op("act", lambda: S.activation(PT[z][:, par, :, :].rearrange("p t q -> p (t q)"), pS[par][:, 0:256], AF.Exp, scale=scale),
                                             reads=[kS[par], "PT%d" % z], writes=["PT%d" % z])
                                    if dlt != 0:
                                        msk = mask_ge if dlt == -1 else mask_le
                                        mb = bass.AP(msk[:].tensor, msk[:].offset, [list(msk[:].ap[0]), [0, 4], list(msk[:].ap[1])])
                                        k.op("dve", lambda: V.tensor_tensor(PT[z][:].rearrange("p a t q -> p (a t) q"), PT[z][:].rearrange("p a t q -> p (a t) q"), mb, ALU.mult),
                                             reads=["PT%d" % z, "mask_ge", "mask_le"], writes=["PT%d" % z])
                                    tt = k0 // 128
                                    first, last = (ci == 0), (ci == len(chunks) - 1)
                                    for par in range(2):
                                        k.op("pe", lambda: P.matmul(psb[4 + par][64 * par:64 * par + 64, 0:256], Vtm[:, tt, kh * 64:(kh + 1) * 64],
                                                                    PT[z][:, par, :, :].rearrange("p t q -> p (t q)"), start=first, stop=last,
                                                                    tile_position=(0, 64 * par)),
                                             reads=["Vtm", "PT%d" % z], writes=["psb%d" % (4 + par)], inc=last)
                                        k.op("pe", lambda: P.matmul(psb[6 + par][64 * par:64 * par + 64, 0:256], ones64[:],
                                                                    PT[z][:, par, :, :].rearrange("p t q -> p (t q)"), start=first, stop=last,
                                                                    tile_position=(0, 64 * par)),
                                             reads=["ones64", "PT%d" % z], writes=["psb%d" % (6 + par)], inc=last)
                                for par in range(2):
                                    lo, hi = 64 * par, 64 * par + 64
                                    seb = bass.AP(SE[:].tensor, SE[lo:hi, kh, :].offset, [list(SE[lo:hi, kh, :].ap[0]), list(SE[lo:hi, kh, :].ap[1]), [0, 128]])
                                    k.op("dve", lambda: V.tensor_tensor(rden[lo:hi, :, :], psb[6 + par][lo:hi, 0:256].rearrange("p (t q) -> p t q", t=2), seb, ALU.add),
                                         reads=["psb%d" % (6 + par), "SE", "rden%d" % par], writes=["rden%d" % par])
                                    k.op("dve", lambda: V.reciprocal(rden[lo:hi, :, :], rden[lo:hi, :, :]), reads=["rden%d" % par], writes=["rden%d" % par])
                                    k.op("dve", lambda: V.tensor_tensor(mixT[lo:hi, 4 + 2 * kh:6 + 2 * kh, q0:q0 + 128], psb[4 + par][lo:hi, 0:256].rearrange("p (t q) -> p t q", t=2),
                                                                        rden[lo:hi, :, :], ALU.mult),
                                         reads=["psb%d" % (4 + par), "rden%d" % par, "mixT"], writes=["mixT"])
                        dump("dbg_mixT", mixT[:], ["mixT"])
                        k.barrier()
                    with contextlib.ExitStack() as po:
                        wo = k.sb(po, "wo", [128, 8, D], BF16)
                        for kt in range(8):
                            k.dma("pool", wo[:, kt, :], w_out_d[i][kt * 128:(kt + 1) * 128, :], writes=["wo"])
                        yo = [k.sb(po, "eyo%d" % z, [128, 8, 256], F32) for z in range(2)]
                        for blk in range(NBLK):
                            t0 = blk * TB
                            z = blk % 2
                            for ct in range(8):
                                pb, pk = psb[ct % 4], "psb%d" % (ct % 4)
                                for mt in range(8):
                                    k.op("pe", lambda: P.matmul(pb[:, 0:TB], wo[:, mt, ct * 128:(ct + 1) * 128], mixT[:, mt, t0:t0 + TB], start=(mt == 0), stop=(mt == 7)),
                                         reads=["wo", "mixT"], writes=[pk], inc=(mt == 7))
                                k.op("act", lambda: S.copy(yo[z][:, ct, :], pb[:, 0:TB]), reads=[pk, "eyo%d" % z], writes=["eyo%d" % z])
                            k.dma("sp", YTv[:, :, t0:t0 + TB], yo[z][:], reads=["eyo%d" % z], writes=["YT"])
                        k.barrier()
            if mixer and l % 2 == 1:
                s5_mixer(l)
            elif mixer:
                even_mixer(l)

            with contextlib.ExitStack() as ph:
                w1b = k.sb(ph, "w1b", [128, 8, DFF], BF16)
                w2b = k.sb(ph, "w2b", [128, 32, D], BF16)
                for kt in range(8):
                    k.dma("pool", w1b[:, kt, :], w1_d[l][kt * 128:(kt + 1) * 128, :], writes=["w1b"])
                for j4 in range(8):
                    k.dma("pool", w2b[:, j4 * 4:(j4 + 1) * 4, :],
                          w2_d[l][j4 * 512:(j4 + 1) * 512, :].rearrange("(j p) n -> p j n", p=128), writes=["w2b"])
                NXB = 3
                xbs = [k.sb(ph, "xb%d" % z, [128, 8, TB], F32) for z in range(NXB)]
                yb = k.sb(ph, "yb", [128, 8, TB], F32)
                sq = k.sb(ph, "sq", [128, 8, TB], F32)
                tmp = sq
                rs = k.sb(ph, "rs", [128, TB], F32)
                h2s = [k.sb(ph, "h2%d" % z, [128, 8, TB], BF16) for z in range(2)]
                ob = k.sb(ph, "ob", [128, 8, TB], F32)
                ar = [k.sb(ph, "ar%d" % z, [128, TB], F32) for z in range(2)]
                a2all = k.sb(ph, "a2all", [128, 32, TB], BF16)
                tiles = (sq, rs, psb[3], "psb3")

                def P_load(blk):
                    z = blk % NXB
                    t0 = blk * TB
                    k.dma("sp", xbs[z][:], XTv[:, :, t0:t0 + TB], reads=["XT"], writes=["xb%d" % z])

                def P_stage(blk):
                    z = blk % NXB
                    xb, h2, xk, hk = xbs[z], h2s[blk % 2], "xb%d" % z, "h2%d" % (blk % 2)
                    j = 1 if blk == 0 else 0
                    t0 = blk * TB
                    if mixer:
                        k.dma("sp", yb[:], YTv[:, :, t0:t0 + TB], reads=["YT"], writes=["yb"])
                        rms_bc(tiles, yb[:], "yb", "m")
                        k.op("dve", lambda: V.tensor_tensor(tmp[:], yb[:], bc_mid(rs[:], 8), ALU.mult), reads=["yb", "rs"], writes=["sq"])
                        for ct in range(8):
                            k.op("dve", lambda: V.scalar_tensor_tensor(out=xb[:, ct, :], in0=tmp[:, ct, :], scalar=PRM[:, 2, ct, j:j + 1],
                                                                       in1=xb[:, ct, :], op0=ALU.mult, op1=ALU.add),
                                 reads=["sq", xk, "PRM"], writes=[xk])
                    rms_bc(tiles, xb[:], xk, "f")
                    k.op("dve", lambda: V.tensor_tensor(tmp[:], xb[:], bc_mid(rs[:], 8), ALU.mult), reads=[xk, "rs"], writes=["sq"])
                    for ct in range(8):
                        k.op("act", lambda: S.activation(h2[:, ct, :], tmp[:, ct, :], AF.Identity, bias=PRM[:, 4, ct, j:j + 1],
                                                         scale=PRM[:, 3, ct, j:j + 1]), reads=["sq", "PRM"], writes=[hk])

                def M1_stage(blk, j0, j1):
                    h2, hk = h2s[blk % 2], "h2%d" % (blk % 2)
                    for jf in range(j0, j1):
                        pa = psb[4 + jf % 4]
                        pak = "psb%d" % (4 + jf % 4)
                        for kt in range(8):
                            k.op("pe", lambda: P.matmul(pa[:, 0:TB], w1b[:, kt, jf * 128:(jf + 1) * 128], h2[:, kt, :],
                                                        start=(kt == 0), stop=(kt == 7)),
                                 reads=["w1b", hk], writes=[pak], inc=(kt == 7))
                        k.op("act", lambda: S.activation(ar[jf % 2][:], pa[:, 0:TB], AF.Relu), reads=[pak], writes=["ar%d" % (jf % 2)])
                        k.op("dve", lambda: V.tensor_tensor(a2all[:, jf, :], ar[jf % 2][:], ar[jf % 2][:], ALU.mult),
                             reads=["ar%d" % (jf % 2)], writes=["a2all"])

                def M2_stage(blk):
                    for ft in range(8):
                        po = psb[ft % 3]
                        pok = "psb%d" % (ft % 3)
                        for jf in range(32):
                            k.op("pe", lambda: P.matmul(po[:, 0:TB], w2b[:, jf, ft * 128:(ft + 1) * 128], a2all[:, jf, :],
                                                        start=(jf == 0), stop=(jf == 31)),
                                 reads=["w2b", "a2all"], writes=[pok], inc=(jf == 31))
                        k.op("act", lambda: S.copy(ob[:, ft, :], po[:, 0:TB]), reads=[pok], writes=["ob_%d" % ft])

                def E_stage(blk):
                    z = blk % NXB
                    xb, xk = xbs[z], "xb%d" % z
                    j = 1 if blk == 0 else 0
                    t0 = blk * TB
                    obk = ["ob_%d" % i_ for i_ in range(8)]
                    k.op("act", lambda: S.activation(sq[:], ob[:], AF.Square), reads=obk, writes=["sq"])
                    rms_bc(tiles, None, "sq", "o")
                    k.op("dve", lambda: V.tensor_tensor(tmp[:], ob[:], bc_mid(rs[:], 8), ALU.mult), reads=obk + ["rs"], writes=["sq"])
                    for ct in range(8):
                        k.op("dve", lambda: V.scalar_tensor_tensor(out=xb[:, ct, :], in0=tmp[:, ct, :], scalar=PRM[:, 5, ct, j:j + 1],
                                                                   in1=xb[:, ct, :], op0=ALU.mult, op1=ALU.add),
                             reads=["sq", xk, "PRM"], writes=[xk])
                    k.dma("pool", XTv[:, :, t0:t0 + TB], xb[:], reads=[xk], writes=["XT_st"])

                P_load(0)
                P_load(1)
                P_stage(0)
                for blk in range(NBLK):
                    M1_stage(blk, 0, 32)
                    if blk >= 1:
                        E_stage(blk - 1)
                    if blk + 2 < NBLK:
                        P_load(blk + 2)
                    if blk + 1 < NBLK:
                        P_stage(blk + 1)
                    M2_stage(blk)
                E_stage(NBLK - 1)
                k.barrier()

        with contextlib.ExitStack() as ph:
            xf = [k.sb(ph, "xf%d" % i, [128, 8, 128], F32) for i in range(2)]
            xo = [k.sb(ph, "xo%d" % i, [128, D], F32) for i in range(2)]
            for tt in range(16):
                b = tt % 2
                k.dma("sp", xf[b][:], XTv[:, :, NCTX + tt * 128:NCTX + (tt + 1) * 128], reads=["XT"], writes=["xf%d" % b])
                for half in range(2):
                    pkey = "psb%d" % ((tt * 2 + half) % 8)
                    pb = psb[(tt * 2 + half) % 8]
                    for c4 in range(4):
                        ct = half * 4 + c4
                        k.op("pe", lambda: P.transpose(pb[:, c4 * 128:(c4 + 1) * 128], xf[b][:, ct, :], ident[:]),
                             reads=["xf%d" % b, "ident"], writes=[pkey], inc=(c4 == 3))
                    if half == 0:
                        k.op("act", lambda: S.copy(xo[b][:, 0:512], pb[:]), reads=[pkey], writes=["xo%d_0" % b])
                    else:
                        k.op("dve", lambda: V.tensor_copy(xo[b][:, 512:1024], pb[:]), reads=[pkey], writes=["xo%d_1" % b])
                k.dma("sp", out_d[tt * 128:(tt + 1) * 128, :], xo[b][:], reads=["xo%d_0" % b, "xo%d_1" % b], writes=["out"])
            k.barrier()
        print("ninst", k.ninst, "nwaits", k.nwaits)
    return nc


def make_in_maps(inp):
    gains = np.stack([inp["mix_pre_g"], inp["mix_post_g"], inp["ffn_pre_g"], inp["ffn_post_g"]], 0)
    maps = []
    for b in range(8):
        m = {
            "x": np.ascontiguousarray(inp["x"][b]), "ctx": np.ascontiguousarray(inp["ctx"][b]),
            "cc": np.ascontiguousarray(np.stack([inp["c"][b], inp["c_ctx"]], 0)),
            "mod_w": inp["mod_w"], "mod_b": inp["mod_b"], "gains": np.ascontiguousarray(gains),
            "ffn_w1": inp["ffn_w1"], "ffn_w2": inp["ffn_w2"],
            "ssm_a_re": inp["ssm_a_re"], "ssm_a_im": inp["ssm_a_im"], "ssm_log_dt": inp["ssm_log_dt"],
            "ssm_b_re": inp["ssm_b_re"], "ssm_b_im": inp["ssm_b_im"], "ssm_c_re": inp["ssm_c_re"], "ssm_c_im": inp["ssm_c_im"],
            "ssm_d": inp["ssm_d"], "ssm_glu_w": inp["ssm_glu_w"],
            "even_w_in": inp["even_w_in"], "even_w_out": inp["even_w_out"], "even_sink": inp["even_sink"],
        }
        maps.append(m)
    return maps


def kernel(**inp):
    inp = {k_: np.asarray(v) for k_, v in inp.items()}
    nc = build(mixer=MIXER_ENABLED)
    res = run_bass_kernel_spmd(nc, make_in_maps(inp), core_ids=list(range(8)))
    return np.stack([r["out"] for r in res.results], 0)
```

```python
import contextlib
import numpy as np
import concourse.bass as bass
import concourse.mybir as mybir
from concourse.bass_utils import run_bass_kernel_spmd

F32 = mybir.dt.float32
BF16 = mybir.dt.bfloat16
I32 = mybir.dt.int32
ALU = mybir.AluOpType
AF = mybir.ActivationFunctionType
AX = mybir.AxisListType

S5_STAGE = 2
S5_SUB = 2
S5_NW = None
MIXER_ENABLED = True
SAME_ENGINE_SYNC = {"dve": True, "act": True, "pool": False, "pe": False}
DMA_RING = 8


class KB:
    def __init__(self, nc, es):
        self.nc = nc
        self.es = es
        self.raw = {"pe": nc.tensor, "dve": nc.vector, "act": nc.scalar, "pool": nc.gpsimd, "sp": nc.sync}
        self.sem = {}
        self.cnt = {}
        for e in ("pe", "dve", "act", "pool"):
            self.sem[e] = es.enter_context(nc.semaphore("s_" + e))
            self.cnt[e] = 0
        self.dring = {}
        self.dcnt = {}
        for q in ("sp", "act", "pool"):
            self.dring[q] = [es.enter_context(nc.semaphore("d_%s%d" % (q, i))) for i in range(DMA_RING)]
            self.dcnt[q] = 0
        self.seen = {e: {} for e in self.raw}
        self.lastw = {}
        self.readers = {}
        self.nwaits = 0
        self.ninst = 0

    def sb(self, es, name, shape, dt):
        self.uid = getattr(self, "uid", 0) + 1
        return es.enter_context(self.nc.sbuf_tensor("%s_u%d" % (name, self.uid), list(shape), dt))

    def ps(self, es, name, shape, dt=F32):
        return es.enter_context(self.nc.psum_tensor(name, list(shape), dt))

    def _collect(self, eng, reads, writes):
        need = {}

        def add(tok):
            if tok is None:
                return
            s, v, src = tok
            if src == eng and (not SAME_ENGINE_SYNC.get(eng, True) or v > self.cnt[eng]):
                return
            if need.get(s, (0,))[0] < v:
                need[s] = (v, src)

        for r in reads:
            add(self.lastw.get(r))
        for w in writes:
            add(self.lastw.get(w))
            for tok in self.readers.get(w, ()):
                add(tok)
        return need

    def _emit_waits(self, eng, need):
        seen = self.seen[eng]
        for s, (v, src) in need.items():
            if seen.get(s, 0) >= v:
                continue
            self.raw[eng].wait_ge(s, v)
            seen[s] = v
            self.nwaits += 1

    def _record(self, tok, reads, writes):
        for w in writes:
            self.lastw[w] = tok
            self.readers[w] = []
        for r in reads:
            if r in writes:
                continue
            self.readers.setdefault(r, []).append(tok)

    def op(self, eng, fn, reads=(), writes=(), inc=True):
        need = self._collect(eng, reads, writes)
        self._emit_waits(eng, need)
        inst = fn()
        self.ninst += 1
        if inc:
            self.cnt[eng] += 1
            inst.then_inc(self.sem[eng], 1)
            tok = (self.sem[eng], self.cnt[eng], eng)
        else:
            tok = (self.sem[eng], self.cnt[eng] + 1, eng)
        self._record(tok, reads, writes)
        return inst

    def dma(self, q, out, in_, reads=(), writes=(), **kw):
        need = self._collect(q, reads, writes)
        self._emit_waits(q, need)
        i = self.dcnt[q]
        self.dcnt[q] += 1
        s = self.dring[q][i % DMA_RING]
        v = 16 * (i // DMA_RING + 1)
        inst = self.raw[q].dma_start(out=out, in_=in_, **kw)
        inst.then_inc(s, 16)
        self.ninst += 1
        self._record((s, v, "dma_" + q), reads, writes)
        return inst

    def barrier(self):
        need = {}
        for e in ("pe", "dve", "act", "pool"):
            if self.cnt[e] > 0:
                need[self.sem[e]] = (self.cnt[e], "x")
        for q in ("sp", "act", "pool"):
            n = self.dcnt[q]
            for r in range(DMA_RING):
                cntr = (n - r + DMA_RING - 1) // DMA_RING if n > r else 0
                if cntr > 0:
                    need[self.dring[q][r]] = (16 * cntr, "x")
        for e in ("pe", "dve", "act", "pool", "sp"):
            self._emit_waits(e, need)
        self.lastw = {}
        self.readers = {}


def bc_mid(ap2, n):
    a = ap2.ap
    return bass.AP(ap2.tensor, ap2.offset, [list(a[0]), [0, n]] + [list(x) for x in a[1:]])


D = 1024
NT = 2304
NCTX = 256
TB = 256
NBLK = NT // TB
DFF = 4096
EPS = 1e-6


def build(nlayers=4, mixer=True, layers=None, debug=False):
    nc = bass.Bass("TRN2", target_bir_lowering=False)
    dt_in = lambda n, s: nc.dram_tensor(n, list(s), F32, kind="ExternalInput").ap()
    x_d = dt_in("x", [2048, D])
    ctx_d = dt_in("ctx", [NCTX, D])
    cc_d = dt_in("cc", [2, D])
    mod_w_d = dt_in("mod_w", [4, D, 6 * D])
    mod_b_d = dt_in("mod_b", [4, 6 * D])
    gains_d = dt_in("gains", [4, 4, D])
    w1_d = dt_in("ffn_w1", [4, D, DFF])
    w2_d = dt_in("ffn_w2", [4, DFF, D])
    w_in_d = dt_in("even_w_in", [2, D, 1280])
    w_out_d = dt_in("even_w_out", [2, D, D])
    even_sink_d = dt_in("even_sink", [2, 8])
    a_re_d = dt_in("ssm_a_re", [2, 2, 64, 64])
    a_im_d = dt_in("ssm_a_im", [2, 2, 64, 64])
    ldt_d = dt_in("ssm_log_dt", [2, 2, 64])
    b_re_d = dt_in("ssm_b_re", [2, 2, 64, 64, 16])
    b_im_d = dt_in("ssm_b_im", [2, 2, 64, 64, 16])
    c_re_d = dt_in("ssm_c_re", [2, 2, 64, 16, 64])
    c_im_d = dt_in("ssm_c_im", [2, 2, 64, 16, 64])
    dsk_d = dt_in("ssm_d", [2, D])
    glu_d = dt_in("ssm_glu_w", [2, D, 2 * D])
    out_d = nc.dram_tensor("out", [2048, D], F32, kind="ExternalOutput").ap()
    YF = nc.dram_tensor("YF", [2, D, NT], F32, kind=("ExternalOutput" if debug else "Internal")).ap()
    XT = nc.dram_tensor("XT", [D, NT], F32).ap()
    YT = nc.dram_tensor("YT", [D, NT], F32, kind=("ExternalOutput" if debug else "Internal")).ap()
    CL = nc.dram_tensor("CLtab", [2048, 2048], BF16, kind=("ExternalOutput" if debug else "Internal")).ap()
    SLn = nc.dram_tensor("SLtab", [2048, 2048], BF16, kind=("ExternalOutput" if debug else "Internal")).ap()
    ROPC = nc.dram_tensor("ROPC", [128, 2048], F32, kind=("ExternalOutput" if debug else "Internal")).ap()
    ROPS = nc.dram_tensor("ROPS", [128, 2048], F32, kind=("ExternalOutput" if debug else "Internal")).ap()
    XTv = XT.rearrange("(ct p) t -> p ct t", p=128)
    YTv = YT.rearrange("(ct p) t -> p ct t", p=128)

    with contextlib.ExitStack() as es:
        k = KB(nc, es)
        V, S, P, G = nc.vector, nc.scalar, nc.tensor, nc.gpsimd

        def dump(name, ap, keys):
            if not debug:
                return
            dt_ = nc.dram_tensor(name, list(ap.shape), ap.dtype, kind="ExternalOutput").ap()
            k.dma("sp", dt_, ap, reads=keys, writes=["dbg_" + name])
        ident = k.sb(es, "ident", [128, 128], F32)
        onesm = k.sb(es, "onesm", [128, 128], F32)
        k.op("pool", lambda: G.memset(ident[:], 0.0), writes=["ident"])
        k.op("pool", lambda: G.affine_select(out=ident[:], in_=ident[:], compare_op=ALU.not_equal, fill=1.0,
                                             base=0, pattern=[[-1, 128]], channel_multiplier=1),
             reads=["ident"], writes=["ident"])
        k.op("pool", lambda: G.memset(onesm[:], 1.0 / D), writes=["onesm"])
        psb = [k.ps(es, "psb%d" % i, [128, 512], F32) for i in range(8)]
        MV = k.sb(es, "MV", [128, 6, 8, 2], F32)
        GN = k.sb(es, "GN", [128, 4, 4, 8], F32)
        SC = k.sb(es, "SC", [128, 8, 2], F32)
        SCb = k.sb(es, "SCb", [128, 8, 2], BF16)
        PRM = k.sb(es, "PRM", [128, 6, 8, 2], F32)
        k.dma("sp", GN[:].rearrange("p a l c -> p (a l) c"),
              gains_d.rearrange("a l (c p) -> p (a l) c", p=128), writes=["GN"], allow_slow_non_contiguous=True)
        for j in range(2):
            k.dma("sp", SC[:, :, j], cc_d[j].rearrange("(c p) -> p c", p=128), writes=["SC"], allow_slow_non_contiguous=True)
        k.op("act", lambda: S.activation(SCb[:], SC[:], AF.Silu), reads=["SC"], writes=["SCb"])


        MAGIC = 12582912.0
        TWO_PI = float(2 * np.pi)
        if mixer and any(l % 2 == 0 for l in (layers if layers is not None else range(nlayers))):
            with contextlib.ExitStack() as ph:
                cidx = k.sb(ph, "cidx", [128, 2048], F32)
                prow = k.sb(ph, "prow", [128, 1], F32)
                tcol = k.sb(ph, "tcol", [128, 1], F32)
                k.op("pool", lambda: G.iota(cidx[:], pattern=[[1, 2048]], base=0, channel_multiplier=0, allow_small_or_imprecise_dtypes=True), writes=["cidx"])
                k.op("pool", lambda: G.iota(prow[:], pattern=[[0, 1]], base=0, channel_multiplier=1, allow_small_or_imprecise_dtypes=True), writes=["prow"])
                uu = [k.sb(ph, "uu%d" % z, [128, 2048], F32) for z in range(2)]
                nn = [k.sb(ph, "nn%d" % z, [128, 2048], F32) for z in range(2)]
                n2 = [k.sb(ph, "nq%d" % z, [128, 2048], F32) for z in range(2)]
                ff = [k.sb(ph, "ff%d" % z, [128, 2048], F32) for z in range(2)]
                tb = [k.sb(ph, "tb%d" % z, [128, 2048], BF16) for z in range(2)]
                tcs = [k.sb(ph, "tcs%d" % z, [128, 1], F32) for z in range(2)]
                SCL = TWO_PI * (1.0 - 2e-6)
                for tt in range(16):
                    tc_ = tcs[tt % 2]
                    k.op("dve", lambda: V.tensor_scalar(tc_[:], prow[:], float(tt * 128), 1.0 / 2048, ALU.add, ALU.mult), reads=["prow", "tcs%d" % (tt % 2)], writes=["tcs%d" % (tt % 2)])
                    for z, (tab, shift, scl) in enumerate(((CL, 0.25, SCL), (SLn, 0.0, -SCL))):
                        k.op("act", lambda: S.activation(uu[z][:], cidx[:], AF.Identity, bias=shift, scale=tc_[:, 0:1]), reads=["cidx", "tcs%d" % (tt % 2), "uu%d" % z], writes=["uu%d" % z])
                        k.op("act", lambda: S.activation(nn[z][:], uu[z][:], AF.Identity, bias=MAGIC, scale=1.0), reads=["uu%d" % z, "nn%d" % z], writes=["nn%d" % z])
                        k.op("act", lambda: S.activation(n2[z][:], nn[z][:], AF.Identity, bias=-MAGIC, scale=1.0), reads=["nn%d" % z, "nq%d" % z], writes=["nq%d" % z])
                        k.op("dve", lambda: V.tensor_tensor(ff[z][:], uu[z][:], n2[z][:], ALU.subtract), reads=["uu%d" % z, "nq%d" % z, "ff%d" % z], writes=["ff%d" % z])
                        k.op("act", lambda: S.activation(tb[z][:], ff[z][:], AF.Sin, scale=scl), reads=["ff%d" % z, "tb%d" % z], writes=["tb%d" % z])
                        k.dma("sp", tab[tt * 128:(tt + 1) * 128, :], tb[z][:], reads=["tb%d" % z], writes=["TAB"])
                k.barrier()
            with contextlib.ExitStack() as ph:
                pidx = k.sb(ph, "rpidx", [128, 1], I32)
                pi2 = k.sb(ph, "rpi2", [128, 1], I32)
                fi = k.sb(ph, "rfi", [128, 1], F32)
                invp = k.sb(ph, "rinvp", [128, 1], F32)
                axs = k.sb(ph, "raxs", [128, 1], F32)
                sgn = k.sb(ph, "rsgn", [128, 1], F32)
                k.op("pool", lambda: G.iota(pidx[:], pattern=[[0, 1]], base=0, channel_multiplier=1), writes=["pidx"])
                k.op("dve", lambda: V.tensor_scalar(pi2[:], pidx[:], 15, None, ALU.bitwise_and), reads=["pidx"], writes=["pi2"])
                k.op("dve", lambda: V.tensor_copy(fi[:], pi2[:]), reads=["pi2"], writes=["fi"])
                k.op("act", lambda: S.activation(invp[:], fi[:], AF.Exp, scale=-float(np.log(10000.0)) / 16.0), reads=["fi"], writes=["invp"])
                k.op("dve", lambda: V.tensor_scalar(pi2[:], pidx[:], 5, 1, ALU.arith_shift_right, ALU.bitwise_and), reads=["pidx", "fi"], writes=["pi2"])
                k.op("dve", lambda: V.tensor_copy(axs[:], pi2[:]), reads=["pi2"], writes=["axs"])
                k.op("dve", lambda: V.tensor_scalar(pi2[:], pidx[:], 4, 1, ALU.arith_shift_right, ALU.bitwise_and), reads=["pidx", "axs"], writes=["pi2"])
                k.op("dve", lambda: V.tensor_copy(sgn[:], pi2[:]), reads=["pi2"], writes=["sgn"])
                k.op("dve", lambda: V.tensor_scalar(sgn[:], sgn[:], 2.0, -1.0, ALU.mult, ALU.add), reads=["sgn"], writes=["sgn"])
                rowp = k.sb(ph, "rowp", [128, 2048], F32)
                colp = k.sb(ph, "colp", [128, 2048], F32)
                ang = k.sb(ph, "rang", [128, 2048], F32)
                u_ = k.sb(ph, "ru", [128, 2048], F32)
                n_ = k.sb(ph, "rn", [128, 2048], F32)
                k.op("pool", lambda: G.iota(rowp[:], pattern=[[1, 32], [0, 64]], base=0, channel_multiplier=0, allow_small_or_imprecise_dtypes=True), writes=["rowp"])
                k.op("pool", lambda: G.iota(colp[:], pattern=[[0, 32], [1, 64]], base=0, channel_multiplier=0, allow_small_or_imprecise_dtypes=True), writes=["colp"])
                k.op("dve", lambda: V.tensor_tensor(colp[:], colp[:], rowp[:], ALU.subtract), reads=["colp", "rowp"], writes=["colp"])
                k.op("dve", lambda: V.scalar_tensor_tensor(out=ang[:], in0=colp[:], scalar=axs[:, 0:1], in1=rowp[:], op0=ALU.mult, op1=ALU.add), reads=["colp", "rowp", "axs"], writes=["ang"])
                k.op("dve", lambda: V.tensor_scalar(ang[:], ang[:], invp[:, 0:1], 1.0 / TWO_PI, ALU.mult, ALU.mult), reads=["ang", "invp"], writes=["ang"])
                for (tab, shift) in ((ROPC, 0.25), (ROPS, 0.0)):
                    k.op("dve", lambda: V.tensor_scalar(u_[:], ang[:], shift, None, ALU.add), reads=["ang", "ru"], writes=["ru"])
                    k.op("dve", lambda: V.tensor_scalar(n_[:], u_[:], MAGIC, None, ALU.add), reads=["ru", "rn"], writes=["rn"])
                    k.op("dve", lambda: V.tensor_scalar(n_[:], n_[:], -MAGIC, None, ALU.add), reads=["rn"], writes=["rn"])
                    k.op("dve", lambda: V.tensor_tensor(u_[:], u_[:], n_[:], ALU.subtract), reads=["rn", "ru"], writes=["ru"])
                    k.op("dve", lambda: V.tensor_scalar(u_[:], u_[:], -0.49999, 0.49999, ALU.max, ALU.min), reads=["ru"], writes=["ru"])
                    k.op("act", lambda: S.activation(n_[:], u_[:], AF.Sin, scale=TWO_PI), reads=["ru", "rn"], writes=["rn"])
                    if shift == 0.0:
                        k.op("dve", lambda: V.tensor_scalar(n_[:], n_[:], sgn[:, 0:1], None, ALU.mult), reads=["rn", "sgn"], writes=["rn"])
                    k.dma("sp", tab[:, :], n_[:], reads=["rn"], writes=["ROP"])
                k.barrier()
        mask_ge = k.sb(es, "mask_ge", [128, 128], BF16)
        mask_le = k.sb(es, "mask_le", [128, 128], BF16)
        ones64 = k.sb(es, "ones64", [128, 64], BF16)
        k.op("pool", lambda: G.memset(mask_ge[:], 1.0), writes=["mask_ge"])
        k.op("pool", lambda: G.affine_select(out=mask_ge[:], in_=mask_ge[:], compare_op=ALU.is_ge, fill=0.0, base=0, pattern=[[-1, 128]], channel_multiplier=1), reads=["mask_ge"], writes=["mask_ge"])
        k.op("pool", lambda: G.memset(mask_le[:], 1.0), writes=["mask_le"])
        k.op("pool", lambda: G.affine_select(out=mask_le[:], in_=mask_le[:], compare_op=ALU.is_ge, fill=0.0, base=0, pattern=[[1, 128]], channel_multiplier=-1), reads=["mask_le"], writes=["mask_le"])
        k.op("pool", lambda: G.memset(ones64[:], 1.0), writes=["ones64"])
        with contextlib.ExitStack() as ph:
            xin = [k.sb(ph, "xin%d" % i, [128, D], F32) for i in range(2)]
            xst = [k.sb(ph, "xst%d" % i, [128, 8, 128], F32) for i in range(2)]
            for tt in range(NT // 128):
                b = tt % 2
                src = ctx_d[tt * 128:(tt + 1) * 128, :] if tt < 2 else x_d[(tt - 2) * 128:(tt - 1) * 128, :]
                k.dma("sp", xin[b][:], src, writes=["xin%d" % b])
                for half in range(2):
                    pb = psb[(tt * 2 + half) % 8]
                    for c4 in range(4):
                        ct = half * 4 + c4
                        k.op("pe", lambda: P.transpose(pb[:, c4 * 128:(c4 + 1) * 128], xin[b][:, ct * 128:(ct + 1) * 128], ident[:]),
                             reads=["xin%d" % b, "ident"], writes=["psb%d" % ((tt * 2 + half) % 8)], inc=(c4 == 3))
                    eng = "act" if half == 0 else "dve"
                    if eng == "act":
                        k.op("act", lambda: S.copy(xst[b][:, half * 4:(half + 1) * 4, :].rearrange("p c t -> p (c t)"), pb[:]),
                             reads=["psb%d" % ((tt * 2 + half) % 8)], writes=["xst%d_%d" % (b, half)])
                    else:
                        k.op("dve", lambda: V.tensor_copy(xst[b][:, half * 4:(half + 1) * 4, :].rearrange("p c t -> p (c t)"), pb[:]),
                             reads=["psb%d" % ((tt * 2 + half) % 8)], writes=["xst%d_%d" % (b, half)])
                k.dma("sp", XTv[:, :, tt * 128:(tt + 1) * 128], xst[b][:], reads=["xst%d_0" % b, "xst%d_1" % b], writes=["XT"])
            k.barrier()

        for l in (layers if layers is not None else range(nlayers)):
            with contextlib.ExitStack() as ph:
                mw = [k.sb(ph, "mw%d" % i, [128, 8, 512], BF16) for i in range(2)]
                mbias = k.sb(ph, "mbias", [128, 48], F32)
                k.dma("sp", mbias[:], mod_b_d[l].rearrange("(c p) -> p c", p=128), writes=["mbias"], allow_slow_non_contiguous=True)
                for ch in range(12):
                    b = ch % 2
                    k.dma("pool", mw[b][:], mod_w_d[l][:, ch * 512:(ch + 1) * 512].rearrange("(kt p) n -> p kt n", p=128),
                          writes=["mw%d" % b])
                    for s4 in range(4):
                        col = ch * 4 + s4
                        pb = psb[col % 8]
                        for kt in range(8):
                            k.op("pe", lambda: P.matmul(pb[:, 0:2], mw[b][:, kt, s4 * 128:(s4 + 1) * 128], SCb[:, kt, :],
                                                        start=(kt == 0), stop=(kt == 7)),
                                 reads=["mw%d" % b, "SCb"], writes=["psb%d" % (col % 8)], inc=(kt == 7))
                        k.op("dve", lambda: V.tensor_scalar(MV[:, col // 8, col % 8, :], pb[:, 0:2], mbias[:, col:col + 1], None, ALU.add),
                             reads=["psb%d" % (col % 8), "mbias"], writes=["MV"])
                for (o, isc, ish, ig, gpre, gpost) in ((0, 1, 0, 2, 0, 1), (3, 4, 3, 5, 2, 3)):
                    for j in range(2):
                        k.op("dve", lambda: V.scalar_tensor_tensor(out=PRM[:, o, :, j], in0=MV[:, isc, :, j], scalar=1.0, in1=GN[:, gpre, l, :],
                                                                   op0=ALU.add, op1=ALU.mult), reads=["MV", "GN"], writes=["PRM"])
                        k.op("dve", lambda: V.tensor_copy(PRM[:, o + 1, :, j], MV[:, ish, :, j]), reads=["MV"], writes=["PRM"])
                        k.op("dve", lambda: V.tensor_tensor(PRM[:, o + 2, :, j], MV[:, ig, :, j], GN[:, gpost, l, :], ALU.mult),
                             reads=["MV", "GN"], writes=["PRM"])
                k.barrier()

            def rms_bc(ph_tiles, src3, key_src, tag):
                sq, rs, pbank, pkey = ph_tiles
                if src3 is not None:
                    k.op("act", lambda: S.activation(sq[:], src3, AF.Square), reads=[key_src], writes=["sq"])
                for ct in range(8):
                    k.op("pe", lambda: P.matmul(pbank[:, 0:TB], onesm[:], sq[:, ct, :], start=(ct == 0), stop=(ct == 7)),
                         reads=["sq", "onesm"], writes=[pkey], inc=(ct == 7))
                k.op("dve", lambda: V.tensor_scalar(rs[:], pbank[:, 0:TB], EPS, None, ALU.add), reads=[pkey], writes=["rs"])
                k.op("act", lambda: S.activation(rs[:], rs[:], AF.Sqrt), reads=["rs"], writes=["rs"])
                k.op("dve", lambda: V.reciprocal(rs[:], rs[:]), reads=["rs"], writes=["rs"])
                return rs


            def prenorm_to_hT(ph, hT, off=0):
                xb = k.sb(ph, "pxb", [128, 8, TB], F32)
                sq = k.sb(ph, "psq", [128, 8, TB], F32)
                tmp = k.sb(ph, "ptmp", [128, 8, TB], F32)
                rs = k.sb(ph, "prs", [128, TB], F32)
                tiles = (sq, rs, psb[6], "psb6")
                for blk in range(NBLK):
                    j = 1 if blk == 0 else 0
                    t0 = blk * TB
                    k.dma("sp", xb[:], XTv[:, :, t0:t0 + TB], reads=["XT"], writes=["xb"])
                    rms_bc(tiles, xb[:], "xb", "p")
                    k.op("dve", lambda: V.tensor_tensor(tmp[:], xb[:], bc_mid(rs[:], 8), ALU.mult), reads=["xb", "rs"], writes=["tmp"])
                    for ct in range(8):
                        k.op("act", lambda: S.activation(hT[:, ct, off + t0:off + t0 + TB], tmp[:, ct, :], AF.Identity, bias=PRM[:, 1, ct, j:j + 1],
                                                         scale=PRM[:, 0, ct, j:j + 1]), reads=["tmp", "PRM"], writes=["hT"])

            def s5_mixer(l):
                i = l // 2
                PI = float(np.pi)
                W = 64
                NW = NT // W
                with contextlib.ExitStack() as ph:
                    T1 = 4
                    PAD = T1 - 1
                    hT = k.sb(ph, "hT", [128, 8, NT + NCTX + 2 * PAD], BF16)
                    k.op("pool", lambda: G.memset(hT[:, :, 0:PAD], 0.0), writes=["hT"])
                    k.op("pool", lambda: G.memset(hT[:, :, PAD + NT + NCTX:], 0.0), reads=["hT"], writes=["hT"])
                    with contextlib.ExitStack() as ph2:
                        prenorm_to_hT(ph2, hT, off=PAD)
                        k.barrier()
                    dsk = k.sb(ph, "dsk", [128, 8], F32)
                    k.dma("sp", dsk[:], dsk_d[i].rearrange("(c p) -> p c", p=128), writes=["dsk"], allow_slow_non_contiguous=True)
                    sc = contextlib.ExitStack()
                    k.op("act", lambda: S.copy(hT[:, :, PAD + NT:PAD + NT + NCTX], hT[:, :, PAD:PAD + NCTX]), reads=["hT"], writes=["hT"])
                    BOFF = PAD + NCTX
                    WinT = [[[k.sb(sc, "WinT%d%d%d" % (d, ri, ti), [128, 8, 128], BF16) for ti in range(T1)] for ri in range(2)] for d in range(2)]
                    CwQ = [[k.sb(sc, "CwQ%d%d" % (d, ri), [128, 32, 32], BF16) for ri in range(2)] for d in range(2)]
                    Gt = [k.sb(sc, "Gt%d" % d, [128, 2, 64], F32) for d in range(2)]
                    with contextlib.ExitStack() as pp:
                        def t32(n):
                            return k.sb(pp, n, [128, 32], F32)
                        twopi = t32("twopi")
                        k.op("pool", lambda: G.memset(twopi[:], 2 * PI), writes=["twopi"])
                        pidx = k.sb(pp, "pidx", [128, 1], I32)
                        modd = k.sb(pp, "modd", [128, 1], F32)
                        mevn = k.sb(pp, "mevn", [128, 1], F32)
                        nodd = k.sb(pp, "nodd", [128, 1], F32)
                        nevn = k.sb(pp, "nevn", [128, 1], F32)
                        k.op("pool", lambda: G.iota(pidx[:], pattern=[[0, 1]], base=0, channel_multiplier=1), writes=["pidx"])
                        k.op("dve", lambda: V.tensor_scalar(pidx[:], pidx[:], 4, 1, ALU.arith_shift_right, ALU.bitwise_and), reads=["pidx"], writes=["pidx"])
                        k.op("dve", lambda: V.tensor_copy(modd[:], pidx[:]), reads=["pidx"], writes=["modd"])
                        k.op("dve", lambda: V.tensor_scalar(mevn[:], modd[:], -1.0, 1.0, ALU.mult, ALU.add), reads=["modd"], writes=["mevn"])
                        k.op("dve", lambda: V.tensor_scalar(nodd[:], modd[:], -1.0, None, ALU.mult), reads=["modd"], writes=["nodd"])
                        k.op("dve", lambda: V.tensor_scalar(nevn[:], mevn[:], -1.0, None, ALU.mult), reads=["mevn"], writes=["nevn"])
                        Br = k.sb(pp, "Br", [128, 32, 32], F32)
                        Bi = k.sb(pp, "Bi", [128, 32, 32], F32)
                        BbR = k.sb(pp, "BbR", [128, 32, 32], F32)
                        BbI = k.sb(pp, "BbI", [128, 32, 32], F32)
                        T1t = k.sb(pp, "T1t", [128, 32, 32], F32)
                        TbR = k.sb(pp, "TbR", [128, 32, 32], F32)
                        TbI = k.sb(pp, "TbI", [128, 32, 32], F32)
                        Cn = k.sb(pp, "Cn", [128, 8, 64], F32)
                        Cblk = k.sb(pp, "Cblk", [128, 8, 128], F32)
                        lr, li, dtt, tq, mag, ang, sa, sinv, cosv, Ar, Ai, am1, n2, kr, ki, u1 = [t32("p%d" % z) for z in range(16)]
                        for d in range(2):
                            def dve(fn, r, w):
                                k.op("dve", fn, reads=r, writes=w)
                            def act(fn, r, w):
                                k.op("act", fn, reads=r, writes=w)
                            k.dma("sp", lr[:], a_re_d[i, d].rearrange("(q a) p -> (a p) q", a=2), writes=["lr"], allow_slow_non_contiguous=True)
                            k.dma("sp", li[:], a_im_d[i, d].rearrange("(q a) p -> (a p) q", a=2), writes=["li"], allow_slow_non_contiguous=True)
                            for g2 in range(2):
                                base = ldt_d[i, d]
                                src = bass.AP(base.tensor, base.offset + g2, [[0, 64], [2, 32]])
                                k.dma("sp", dtt[g2 * 64:(g2 + 1) * 64, :], src, writes=["dtt"], allow_slow_non_contiguous=True)
                            act(lambda: S.activation(dtt[:], dtt[:], AF.Exp), ["dtt"], ["dtt"])
                            dve(lambda: V.tensor_tensor(tq[:], lr[:], dtt[:], ALU.mult), ["lr", "dtt"], ["tq"])
                            act(lambda: S.activation(mag[:], tq[:], AF.Exp), ["tq"], ["mag"])
                            dve(lambda: V.tensor_tensor(ang[:], li[:], dtt[:], ALU.mult), ["li", "dtt"], ["ang"])
                            MAGIC = 12582912.0
                            PIC = 3.1415925
                            for (dst, shift, tag) in ((sinv, 0.0, "s"), (cosv, 0.5 * PI, "c")):
                                dve(lambda: V.tensor_scalar(u1[:], ang[:], shift, None, ALU.add), ["ang", "sa", "u1"], ["u1"])
                                dve(lambda: V.tensor_scalar(sa[:], u1[:], 1.0 / (2 * PI), None, ALU.mult), ["u1", "sa"], ["sa"])
                                dve(lambda: V.tensor_scalar(sa[:], sa[:], MAGIC, None, ALU.add), ["sa"], ["sa"])
                                dve(lambda: V.tensor_scalar(sa[:], sa[:], -MAGIC, None, ALU.add), ["sa"], ["sa"])
                                dve(lambda: V.scalar_tensor_tensor(out=sa[:], in0=sa[:], scalar=-2 * PI, in1=u1[:], op0=ALU.mult, op1=ALU.add), ["sa", "u1"], ["sa"])
                                dve(lambda: V.tensor_scalar(sa[:], sa[:], -PIC, PIC, ALU.max, ALU.min), ["sa"], ["sa"])
                                act(lambda: S.activation(dst[:], sa[:], AF.Sin), ["sa"], ["sinv" if tag == "s" else "cosv"])
                            dve(lambda: V.tensor_tensor(Ar[:], mag[:], cosv[:], ALU.mult), ["mag", "cosv"], ["Ar"])
                            dve(lambda: V.tensor_tensor(Ai[:], mag[:], sinv[:], ALU.mult), ["mag", "sinv"], ["Ai"])
                            gk = "Gt%d" % d
                            pws = [(None, None), (Ar, Ai)]
                            for pi_ in range(2, T1 + 1):
                                pr_ = t32("pwr%d_%d" % (d, pi_)); pim_ = t32("pwi%d_%d" % (d, pi_))
                                qr_, qi_ = pws[pi_ - 1]
                                kq = ["pw%d" % (pi_ - 1), "Ar", "Ai"]
                                dve(lambda: V.tensor_tensor(pr_[:], qr_[:], Ar[:], ALU.mult), kq, ["pw%d" % pi_])
                                dve(lambda: V.tensor_tensor(u1[:], qi_[:], Ai[:], ALU.mult), kq + ["u1"], ["u1"])
                                dve(lambda: V.tensor_tensor(pr_[:], pr_[:], u1[:], ALU.subtract), ["pw%d" % pi_, "u1"], ["pw%d" % pi_])
                                dve(lambda: V.tensor_tensor(pim_[:], qr_[:], Ai[:], ALU.mult), kq + ["pw%d" % pi_], ["pw%d" % pi_])
                                dve(lambda: V.tensor_tensor(u1[:], qi_[:], Ar[:], ALU.mult), kq + ["u1"], ["u1"])
                                dve(lambda: V.tensor_tensor(pim_[:], pim_[:], u1[:], ALU.add), ["pw%d" % pi_, "u1"], ["pw%d" % pi_])
                                pws.append((pr_, pim_))
                            AT_r, AT_i = pws[T1]
                            Arp = AT_r[:].rearrange("p (c r) -> p r c", r=4)
                            Aip = AT_i[:].rearrange("p (c r) -> p r c", r=4)
                            gv = lambda a, lo: Gt[d][:, a, lo:lo + 32].rearrange("p (r c) -> p r c", r=4)
                            dve(lambda: V.tensor_copy(gv(0, 0), Arp), ["pw%d" % T1], [gk])
                            dve(lambda: V.tensor_scalar(gv(0, 32), Aip, -1.0, None, ALU.mult), ["pw%d" % T1, gk], [gk])
                            dve(lambda: V.tensor_copy(gv(1, 0), Aip), ["pw%d" % T1, gk], [gk])
                            dve(lambda: V.tensor_copy(gv(1, 32), Arp), ["pw%d" % T1, gk], [gk])
                            dve(lambda: V.tensor_scalar(am1[:], Ar[:], -1.0, None, ALU.add), ["Ar"], ["am1"])
                            dve(lambda: V.tensor_tensor(n2[:], lr[:], lr[:], ALU.mult), ["lr"], ["n2"])
                            dve(lambda: V.tensor_tensor(u1[:], li[:], li[:], ALU.mult), ["li", "u1"], ["u1"])
                            dve(lambda: V.tensor_tensor(n2[:], n2[:], u1[:], ALU.add), ["n2", "u1"], ["n2"])
                            dve(lambda: V.reciprocal(n2[:], n2[:]), ["n2"], ["n2"])
                            dve(lambda: V.tensor_tensor(kr[:], am1[:], lr[:], ALU.mult), ["am1", "lr"], ["kr"])
                            dve(lambda: V.tensor_tensor(u1[:], Ai[:], li[:], ALU.mult), ["Ai", "li", "n2", "u1"], ["u1"])
                            dve(lambda: V.tensor_tensor(kr[:], kr[:], u1[:], ALU.add), ["kr", "u1"], ["kr"])
                            dve(lambda: V.tensor_tensor(kr[:], kr[:], n2[:], ALU.mult), ["kr", "n2"], ["kr"])
                            dve(lambda: V.tensor_tensor(ki[:], Ai[:], lr[:], ALU.mult), ["Ai", "lr"], ["ki"])
                            dve(lambda: V.tensor_tensor(u1[:], am1[:], li[:], ALU.mult), ["am1", "li", "kr", "u1"], ["u1"])
                            dve(lambda: V.tensor_tensor(ki[:], ki[:], u1[:], ALU.subtract), ["ki", "u1"], ["ki"])
                            dve(lambda: V.tensor_tensor(ki[:], ki[:], n2[:], ALU.mult), ["ki", "n2"], ["ki"])
                            k.op("pool", lambda: G.memset(Br[:], 0.0), reads=["BbR", "BbI"], writes=["Br"])
                            k.op("pool", lambda: G.memset(Bi[:], 0.0), reads=["BbR", "BbI"], writes=["Bi"])
                            for g2 in range(2):
                                for (dst, srcd, key) in ((Br, b_re_d, "Br"), (Bi, b_im_d, "Bi")):
                                    base = srcd[i, d]
                                    src = bass.AP(base.tensor, base.offset + g2 * 1024, [[16, 64], [2048, 32], [1, 16]])
                                    k.dma("sp", dst[g2 * 64:(g2 + 1) * 64, :, g2 * 16:(g2 + 1) * 16], src, reads=[key], writes=[key])
                            def bc_last(a2, n):
                                a = a2.ap
                                return bass.AP(a2.tensor, a2.offset, [list(a[0]), list(a[1]), [0, n]])
                            def cmul(outR, outI, xr, xi, inR, inI, kx, kin, kout):
                                xrb, xib = bc_last(xr[:], 32), bc_last(xi[:], 32)
                                dve(lambda: V.tensor_tensor(outR[:], inR[:], xrb, ALU.mult), kin + kx + kout, kout)
                                dve(lambda: V.tensor_tensor(T1t[:], inI[:], xib, ALU.mult), kin + kx + ["T1t"], ["T1t"])
                                dve(lambda: V.tensor_tensor(outR[:], outR[:], T1t[:], ALU.subtract), kout + ["T1t"], kout)
                                dve(lambda: V.tensor_tensor(outI[:], inI[:], xrb, ALU.mult), kin + kx + kout, kout)
                                dve(lambda: V.tensor_tensor(T1t[:], inR[:], xib, ALU.mult), kin + kx + ["T1t"], ["T1t"])
                                dve(lambda: V.tensor_tensor(outI[:], outI[:], T1t[:], ALU.add), kout + ["T1t"], kout)
                            cmul(BbR, BbI, kr, ki, Br, Bi, ["kr", "ki"], ["Br", "Bi"], ["Bb"])
                            for ti in range(T1):
                                if ti == 0:
                                    srcs = (BbR, BbI)
                                    skey = ["Bb"]
                                else:
                                    cmul(TbR, TbI, pws[ti][0], pws[ti][1], BbR, BbI, ["pw%d" % ti], ["Bb"], ["Tb"])
                                    srcs = (TbR, TbI)
                                    skey = ["Tb"]
                                for ri in range(2):
                                    for ct in range(8):
                                        pk = "psb%d" % (ct % 4)
                                        k.op("pe", lambda: P.transpose(psb[ct % 4][:, 0:128], srcs[ri][:, 4 * ct:4 * ct + 4, :].rearrange("p a b -> p (a b)"), ident[:]),
                                             reads=skey + ["ident"], writes=[pk])
                                        act(lambda: S.copy(WinT[d][ri][ti][:, ct, :], psb[ct % 4][:, 0:128]), [pk], ["WinT%d%d" % (d, ri)])
                            for ri, srcd, mo, me in ((0, c_re_d, modd, mevn), (1, c_im_d, nodd, nevn)):
                                ck = "CwQ%d%d" % (d, ri)
                                k.dma("sp", Cn[:], srcd[i, d].rearrange("(ct g) c p -> (g c) ct p", g=8), reads=["Cn"], writes=["Cn"])
                                dve(lambda: V.tensor_scalar(Cblk[:, :, 0:64], Cn[:], me[:, 0:1], None, ALU.mult), ["Cn", "mevn", "nevn", "Cblk"], ["Cblk"])
                                dve(lambda: V.tensor_scalar(Cblk[:, :, 64:128], Cn[:], mo[:, 0:1], None, ALU.mult), ["Cn", "modd", "nodd", "Cblk"], ["Cblk"])
                                for ct in range(8):
                                    pk = "psb%d" % (4 + ct % 4)
                                    k.op("pe", lambda: P.transpose(psb[4 + ct % 4][:, 0:128], Cblk[:, ct, :], ident[:]), reads=["Cblk", "ident"], writes=[pk])
                                    for q4 in range(4):
                                        act(lambda: S.copy(CwQ[d][ri][:, ct * 4 + q4, :], psb[4 + ct % 4][:, 32 * q4:32 * q4 + 32]), [pk, ck], [ck])
                        k.barrier()
                    if S5_STAGE < 1:
                        sc.close()
                        return
                    NG = W // T1
                    Bw = [[k.sb(sc, "Bw%d_%d" % (d, z_), [128, 64, W], BF16) for z_ in range(2)] for d in range(2)]
                    H = [k.sb(sc, "H%d" % d, [128, 64, W + T1], F32) for d in range(2)]
                    Sb = [k.sb(sc, "Sb%d" % d, [128, 64, W], BF16) for d in range(2)]
                    XY = [k.sb(sc, "XY%d" % d, [128, 2, 64, T1], F32) for d in range(2)]
                    Nn = [k.sb(sc, "Nn%d" % d, [128, 2, 32, T1], F32) for d in range(2)]
                    yo = [k.sb(sc, "yo%d" % d, [128, 8, W], F32) for d in range(2)]
                    k.op("pool", lambda: G.memset(H[0][:], 0.0), writes=["H0"])
                    k.op("pool", lambda: G.memset(H[1][:], 0.0), writes=["H1"])

                    def bc4(ap3):
                        a_ = ap3.ap
                        return bass.AP(ap3.tensor, ap3.offset, [list(a_[0]), [0, 2], list(a_[1]), list(a_[2])])

                    def gbc(d):
                        a_ = Gt[d][:].ap
                        return bass.AP(Gt[d][:].tensor, Gt[d][:].offset, [list(a_[0]), list(a_[1]), list(a_[2]), [0, T1]])

                    NWR = NW if S5_NW is None else S5_NW

                    def emit_bu(step_w):
                        zb = step_w % 2
                        wins = (step_w, NW - 1 - step_w)
                        for d in range(2):
                            p0 = wins[d] * W
                            bk = "Bw%d_%d" % (d, zb)
                            for half in range(2):
                                for c4 in range(4):
                                    ct = half * 4 + c4
                                    for ri in range(2):
                                        for r in range(4):
                                            slot = c4 * 2 + ri
                                            last = (c4 == 3 and ri == 1)
                                            for ti in range(T1):
                                                c0 = (PAD + p0 - ti) if d == 0 else (BOFF + p0 + ti)
                                                k.op("pe", lambda: P.matmul(psb[r][:, slot * W:(slot + 1) * W],
                                                                            WinT[d][ri][ti][32 * r:32 * r + 32, ct, :],
                                                                            hT[32 * r:32 * r + 32, ct, c0:c0 + W],
                                                                            start=(ti == 0), stop=(ti == T1 - 1), tile_position=(32 * r, 0)),
                                                     reads=["hT", "WinT%d%d" % (d, ri)], writes=["psb%d" % r], inc=(last and r == 3 and ti == T1 - 1))
                                for r in range(4):
                                    for ri in range(2):
                                        src = psb[r][:].rearrange("p (c i w) -> p c i w", i=2, w=W)[:, :, ri, :]
                                        lo = ri * 32 + r * 8 + half * 4
                                        k.op("act", lambda: S.copy(Bw[d][zb][:, lo:lo + 4, :], src), reads=["psb%d" % r, bk], writes=[bk])

                    emit_bu(0)
                    for step_w in range(NWR):
                        zb = step_w % 2
                        wins = (step_w, NW - 1 - step_w)
                        if step_w + 1 < NWR:
                            emit_bu(step_w + 1)
                        for g in range(NG if S5_SUB >= 1 else 0):
                            rd = (g * T1, W - g * T1)
                            wr = (T1 + g * T1, W - (g + 1) * T1)
                            bj = (g * T1, W - (g + 1) * T1)
                            for d in range(2):
                                k.op("dve", lambda: V.tensor_tensor(XY[d][:], bc4(H[d][:, :, rd[d]:rd[d] + T1]), gbc(d), ALU.mult),
                                     reads=["H%d" % d, "Gt%d" % d], writes=["XY%d" % d])
                            for d in range(2):
                                k.op("dve", lambda: V.tensor_tensor(Nn[d][:], XY[d][:, :, 0:32, :], XY[d][:, :, 32:64, :], ALU.add),
                                     reads=["XY%d" % d], writes=["Nn%d" % d])
                            for d in range(2):
                                k.op("dve", lambda: V.tensor_tensor(H[d][:, :, wr[d]:wr[d] + T1], Nn[d][:].rearrange("p a q t -> p (a q) t"),
                                                                    Bw[d][zb][:, :, bj[d]:bj[d] + T1], ALU.add),
                                     reads=["Nn%d" % d, "Bw%d_%d" % (d, zb)], writes=["H%d" % d])
                        for d in range(2 if S5_SUB >= 2 else 0):
                            p0 = wins[d] * W
                            if d == 0:
                                t0 = p0
                                k.op("act", lambda: S.copy(Sb[0][:], H[0][:, :, T1:W + T1]), reads=["H0"], writes=["Sb0"])
                                k.op("dve", lambda: V.tensor_copy(H[0][:, :, 0:T1], H[0][:, :, W:W + T1]), reads=["H0"], writes=["H0"])
                            else:
                                t0 = (NCTX + p0) if p0 < 2048 else (p0 - 2048)
                                k.op("act", lambda: S.copy(Sb[1][:], H[1][:, :, 0:W]), reads=["H1"], writes=["Sb1"])
                                k.op("dve", lambda: V.tensor_copy(H[1][:, :, W:W + T1], H[1][:, :, 0:T1]), reads=["H1"], writes=["H1"])
                            for ct in range(8):
                                for r in range(4):
                                    q = ct * 4 + r
                                    for ri in range(2):
                                        k.op("pe", lambda: P.matmul(psb[4 + r][32 * r:32 * r + 32, ct * W:(ct + 1) * W], CwQ[d][ri][:, q, :],
                                                                    Sb[d][:, ri * 32 + r * 8 + ct, :], start=(ri == 0), stop=(ri == 1),
                                                                    tile_position=(0, 32 * r)),
                                             reads=["CwQ%d%d" % (d, ri), "Sb%d" % d], writes=["psb%d" % (4 + r)], inc=(ct == 7 and ri == 1))
                            for r in range(4):
                                k.op("act", lambda: S.copy(yo[d][32 * r:32 * r + 32, :, :].rearrange("p c w -> p (c w)"), psb[4 + r][32 * r:32 * r + 32, 0:8 * W]),
                                     reads=["psb%d" % (4 + r), "yo%d" % d], writes=["yo%d" % d])
                            k.dma("sp", YF[d].rearrange("(ct p) t -> p ct t", p=128)[:, :, t0:t0 + W], yo[d][:], reads=["yo%d" % d], writes=["YF"])
                    k.barrier()
                    sc.close()
                    if S5_STAGE < 2:
                        return
                    with contextlib.ExitStack() as pg:
                        gw = k.sb(pg, "gw", [128, 8, 2 * D], BF16)
                        for kt in range(8):
                            k.dma("pool", gw[:, kt, :], glu_d[i][kt * 128:(kt + 1) * 128, :], writes=["gw"])
                        ya = k.sb(pg, "ya", [128, 8, TB], F32)
                        yb2 = k.sb(pg, "yb2", [128, 8, TB], F32)
                        y2 = k.sb(pg, "y2", [128, 8, TB], F32)
                        gl = k.sb(pg, "gl", [128, 8, TB], BF16)
                        sg = k.sb(pg, "sg", [128, TB], F32)
                        zo = k.sb(pg, "zo", [128, 8, TB], F32)
                        for blk in range(NBLK):
                            t0 = blk * TB
                            k.dma("sp", ya[:], YF[0].rearrange("(ct p) t -> p ct t", p=128)[:, :, t0:t0 + TB], reads=["YF"], writes=["ya"])
                            k.dma("sp", yb2[:], YF[1].rearrange("(ct p) t -> p ct t", p=128)[:, :, t0:t0 + TB], reads=["YF"], writes=["yb2"])
                            k.op("dve", lambda: V.tensor_tensor(ya[:], ya[:], yb2[:], ALU.add), reads=["ya", "yb2"], writes=["ya"])
                            for ct in range(8):
                                k.op("dve", lambda: V.scalar_tensor_tensor(out=ya[:, ct, :], in0=hT[:, ct, PAD + t0:PAD + t0 + TB], scalar=dsk[:, ct:ct + 1],
                                                                           in1=ya[:, ct, :], op0=ALU.mult, op1=ALU.add), reads=["ya", "hT", "dsk"], writes=["ya"])
                            k.op("dve", lambda: V.tensor_tensor(y2[:], ya[:], ya[:], ALU.mult), reads=["ya"], writes=["y2"])
                            k.op("dve", lambda: V.tensor_scalar(y2[:], y2[:], 0.044715, 1.0, ALU.mult, ALU.add), reads=["y2"], writes=["y2"])
                            k.op("dve", lambda: V.tensor_tensor(y2[:], y2[:], ya[:], ALU.mult), reads=["y2", "ya"], writes=["y2"])
                            k.op("act", lambda: S.activation(y2[:], y2[:], AF.Sigmoid, scale=1.5957691216057308), reads=["y2"], writes=["y2"])
                            k.op("dve", lambda: V.tensor_tensor(gl[:], y2[:], ya[:], ALU.mult), reads=["y2", "ya"], writes=["gl"])
                            for ct in range(8):
                                pa, pb_ = psb[(2 * ct) % 4], psb[(2 * ct + 1) % 4]
                                ka, kb_ = "psb%d" % ((2 * ct) % 4), "psb%d" % ((2 * ct + 1) % 4)
                                for kt in range(8):
                                    k.op("pe", lambda: P.matmul(pa[:, 0:TB], gw[:, kt, ct * 128:(ct + 1) * 128], gl[:, kt, :], start=(kt == 0), stop=(kt == 7)),
                                         reads=["gw", "gl"], writes=[ka], inc=(kt == 7))
                                for kt in range(8):
                                    k.op("pe", lambda: P.matmul(pb_[:, 0:TB], gw[:, kt, D + ct * 128:D + (ct + 1) * 128], gl[:, kt, :], start=(kt == 0), stop=(kt == 7)),
                                         reads=["gw", "gl"], writes=[kb_], inc=(kt == 7))
                                k.op("act", lambda: S.activation(sg[:], pb_[:, 0:TB], AF.Sigmoid), reads=[kb_], writes=["sg"])
                                k.op("dve", lambda: V.tensor_tensor(zo[:, ct, :], pa[:, 0:TB], sg[:], ALU.mult), reads=[ka, "sg", "zo"], writes=["zo"])
                            k.dma("sp", YTv[:, :, t0:t0 + TB], zo[:], reads=["zo"], writes=["YT"])
                        k.barrier()

            def even_mixer(l):
                i = l // 2
                NLAT = 2048
                with contextlib.ExitStack() as ph:
                    fT = k.sb(ph, "fT", [128, 4, NT], BF16)
                    QT = k.sb(ph, "QT", [128, 4, NT], BF16)
                    KT = k.sb(ph, "KT", [128, 2, NT], BF16)
                    Vtm = k.sb(ph, "Vtm", [128, NT // 128, 128], BF16)
                    mixT = k.sb(ph, "mixT", [128, 8, NT], BF16)
                    SEall = k.sb(ph, "SEall", [128, 8], F32)
                    SE = k.sb(ph, "SE", [128, 2, 2], F32)
                    sk = even_sink_d[i]
                    k.dma("sp", SEall[:], bass.AP(sk.tensor, sk.offset, [[0, 128], [1, 8]]), writes=["SEall"], allow_slow_non_contiguous=True)
                    k.op("act", lambda: S.activation(SEall[:], SEall[:], AF.Exp), reads=["SEall"], writes=["SEall"])
                    for kh in range(2):
                        for tl in range(2):
                            k.op("dve", lambda: V.tensor_copy(SE[0:64, kh, tl:tl + 1], SEall[0:64, 4 * kh + 2 * tl:4 * kh + 2 * tl + 1]), reads=["SEall", "SE"], writes=["SE"])
                            k.op("dve", lambda: V.tensor_copy(SE[64:128, kh, tl:tl + 1], SEall[64:128, 4 * kh + 2 * tl + 1:4 * kh + 2 * tl + 2]), reads=["SEall", "SE"], writes=["SE"])
                    with contextlib.ExitStack() as pa:
                        hT = k.sb(pa, "hT", [128, 8, NT], BF16)
                        with contextlib.ExitStack() as ph2:
                            prenorm_to_hT(ph2, hT)
                            k.barrier()
                        wb = k.sb(pa, "wb", [128, 8, 1280], BF16)
                        for kt in range(8):
                            k.dma("pool", wb[:, kt, :], w_in_d[i][kt * 128:(kt + 1) * 128, :], writes=["wb"])
                        wsw = k.sb(pa, "wsw", [128, 8, 640], BF16)
                        wv = wb[:, :, 512:1152].rearrange("p k (h two e) -> p k h two e", two=2, e=16)
                        wsv = wsw[:].rearrange("p k (h two e) -> p k h two e", two=2, e=16)
                        for kt in range(8):
                            k.op("pool", lambda: G.tensor_copy(wsv[:, kt, :, 0, :], wv[:, kt, :, 1, :]), reads=["wb", "wsw"], writes=["wsw"])
                            k.op("pool", lambda: G.tensor_copy(wsv[:, kt, :, 1, :], wv[:, kt, :, 0, :]), reads=["wb", "wsw"], writes=["wsw"])
                        wkd = k.sb(pa, "wkd", [128, 8, 2, 128], BF16)
                        wkds = k.sb(pa, "wkds", [128, 8, 2, 128], BF16)
                        for dup in range(2):
                            k.op("pool", lambda: G.tensor_copy(wkd[:, :, :, dup * 64:(dup + 1) * 64], wb[:, :, 1024:1152].rearrange("p k (h d) -> p k h d", d=64)), reads=["wb", "wkd"], writes=["wkd"])
                            k.op("pool", lambda: G.tensor_copy(wkds[:, :, :, dup * 64:(dup + 1) * 64], wsw[:, :, 512:640].rearrange("p k (h d) -> p k h d", d=64)), reads=["wsw", "wkds"], writes=["wkds"])
                        ropc = k.sb(pa, "ropc", [128, NLAT], F32)
                        rops = k.sb(pa, "rops", [128, NLAT], F32)
                        k.dma("sp", ropc[:], ROPC[:, :], reads=["ROP"], writes=["ropc"])
                        k.dma("sp", rops[:], ROPS[:, :], reads=["ROP"], writes=["rops"])
                        t1 = k.sb(pa, "rt1", [128, 512], F32)
                        t2 = k.sb(pa, "rt2", [128, 512], F32)
                        blocks = [(0, 256)] + [(256 + 512 * b, 512) for b in range(4)]
                        nb = 0
                        for (t0, n) in blocks:
                            lat = t0 >= NCTX
                            for g in range(4):
                                pb, pk = psb[nb % 4], "psb%d" % (nb % 4); nb += 1
                                for kt in range(8):
                                    k.op("pe", lambda: P.matmul(pb[:, 0:n], wb[:, kt, g * 128:(g + 1) * 128], hT[:, kt, t0:t0 + n], start=(kt == 0), stop=(kt == 7)),
                                         reads=["wb", "hT"], writes=[pk], inc=(kt == 7))
                                k.op("act", lambda: S.copy(fT[:, g, t0:t0 + n], pb[:, 0:n]), reads=[pk, "fT"], writes=["fT"])
                            for j in range(6):
                                if j < 4:
                                    lw = lambda kt: wb[:, kt, 512 + j * 128:512 + (j + 1) * 128]
                                    lws = lambda kt: wsw[:, kt, j * 128:(j + 1) * 128]
                                    dst = QT[:, j, t0:t0 + n]
                                    dk = "QT"
                                else:
                                    lw = lambda kt: wkd[:, kt, j - 4, :]
                                    lws = lambda kt: wkds[:, kt, j - 4, :]
                                    dst = KT[:, j - 4, t0:t0 + n]
                                    dk = "KT"
                                pb, pk = psb[nb % 4], "psb%d" % (nb % 4); nb += 1
                                for kt in range(8):
                                    k.op("pe", lambda: P.matmul(pb[:, 0:n], lw(kt), hT[:, kt, t0:t0 + n], start=(kt == 0), stop=(kt == 7)),
                                         reads=["wb", "wkd", "hT"], writes=[pk], inc=(kt == 7))
                                if not lat:
                                    k.op("act", lambda: S.copy(dst, pb[:, 0:n]), reads=[pk, dk], writes=[dk])
                                else:
                                    pb2, pk2 = psb[4 + nb % 4], "psb%d" % (4 + nb % 4)
                                    for kt in range(8):
                                        k.op("pe", lambda: P.matmul(pb2[:, 0:n], lws(kt), hT[:, kt, t0:t0 + n], start=(kt == 0), stop=(kt == 7)),
                                             reads=["wsw", "wkds", "hT"], writes=[pk2], inc=(kt == 7))
                                    r0 = t0 - NCTX
                                    k.op("dve", lambda: V.tensor_tensor(t1[:, 0:n], pb[:, 0:n], ropc[:, r0:r0 + n], ALU.mult), reads=[pk, "ropc", "rt1"], writes=["rt1"])
                                    k.op("dve", lambda: V.tensor_tensor(t2[:, 0:n], pb2[:, 0:n], rops[:, r0:r0 + n], ALU.mult), reads=[pk2, "rops", "rt2"], writes=["rt2"])
                                    k.op("pool", lambda: G.tensor_tensor(dst, t1[:, 0:n], t2[:, 0:n], ALU.add), reads=["rt1", "rt2", dk], writes=[dk])
                            for s in range(n // 128):
                                tt = (t0 + s * 128) // 128
                                pb, pk = psb[nb % 4], "psb%d" % (nb % 4); nb += 1
                                for kt in range(8):
                                    k.op("pe", lambda: P.matmul(pb[:, 0:128], hT[:, kt, tt * 128:(tt + 1) * 128], wb[:, kt, 1152:1280], start=(kt == 0), stop=(kt == 7)),
                                         reads=["wb", "hT"], writes=[pk], inc=(kt == 7))
                                k.op("act", lambda: S.copy(Vtm[:, tt, :], pb[:, 0:128]), reads=[pk, "Vtm"], writes=["Vtm"])
                        dump("dbg_fT", fT[:], ["fT"]); dump("dbg_QT", QT[:], ["QT"]); dump("dbg_KT", KT[:], ["KT"]); dump("dbg_Vtm", Vtm[:], ["Vtm"])
                        k.barrier()
                    with contextlib.ExitStack() as pf:
                        Gtm = k.sb(pf, "Gtm", [128, NT // 128, 4, 256], BF16)
                        csc = k.sb(pf, "csc", [128, 256], BF16)
                        k.dma("sp", csc[:, 0:128], CL.rearrange("(t e) c -> t e c", e=16)[:, 0, 0:128], reads=["TAB"], writes=["csc"])
                        k.dma("sp", csc[:, 128:256], SLn.rearrange("(t e) c -> t e c", e=16)[:, 0, 0:128], reads=["TAB"], writes=["csc"])
                        k.op("dve", lambda: V.tensor_scalar(csc[:, 128:256], csc[:, 128:256], -1.0, None, ALU.mult), reads=["csc"], writes=["csc"])
                        sc_lat = float(1.0 / np.sqrt(2048.0 * 128.0))
                        sc_ctx = float(1.0 / np.sqrt(256.0 * 128.0))
                        nb = 0
                        for tt in range(NT // 128):
                            for g in range(4):
                                pb, pk = psb[nb % 4], "psb%d" % (nb % 4); nb += 1
                                k.op("pe", lambda: P.matmul(pb[:, 0:256], fT[:, g, tt * 128:(tt + 1) * 128], csc[:], start=True, stop=True),
                                     reads=["fT", "csc"], writes=[pk])
                                k.op("act", lambda: S.activation(Gtm[:, tt, g, :], pb[:, 0:256], AF.Copy, scale=(sc_ctx if tt < 2 else sc_lat)), reads=[pk, "Gtm"], writes=["Gtm"])
                        cl = k.sb(pf, "cl", [128, 16, 512], BF16)
                        sl = k.sb(pf, "sl", [128, 16, 512], BF16)
                        c8 = CL.rearrange("(t e) c -> t e c", e=8)[:, 0, 0:256].rearrange("(tt p) c -> p tt c", p=128)
                        s8 = SLn.rearrange("(t e) c -> t e c", e=8)[:, 0, 0:256].rearrange("(tt p) c -> p tt c", p=128)
                        k.dma("sp", cl[:, 0:2, 0:256], c8, reads=["TAB"], writes=["cl"])
                        k.dma("sp", sl[:, 0:2, 0:256], s8, reads=["TAB"], writes=["sl"])
                        for g in range(4):
                            pb, pk = psb[4 + g % 4], "psb%d" % (4 + g % 4)
                            n_ = 0
                            for tt in range(2):
                                for (half, tabl, tk) in ((0, cl, "cl"), (1, sl, "sl")):
                                    k.op("pe", lambda: P.matmul(pb[:, 0:256], Gtm[:, tt, g, half * 128:(half + 1) * 128], tabl[:, tt, 0:256], start=(n_ == 0), stop=(n_ == 3)),
                                         reads=["Gtm", tk], writes=[pk], inc=(n_ == 3))
                                    n_ += 1
                            k.op("act", lambda: S.copy(mixT[:, g, 0:256], pb[:, 0:256]), reads=[pk, "mixT"], writes=["mixT"])
                        for pbk in range(4):
                            k.dma("sp", cl[:], CL[:, pbk * 512:(pbk + 1) * 512].rearrange("(tt p) c -> p tt c", p=128), reads=["TAB", "cl"], writes=["cl"])
                            k.dma("sp", sl[:], SLn[:, pbk * 512:(pbk + 1) * 512].rearrange("(tt p) c -> p tt c", p=128), reads=["TAB", "sl"], writes=["sl"])
                            for g in range(4):
                                pb, pk = psb[4 + g % 4], "psb%d" % (4 + g % 4)
                                n_ = 0
                                for tt in range(16):
                                    for (half, tabl, tk) in ((0, cl, "cl"), (1, sl, "sl")):
                                        k.op("pe", lambda: P.matmul(pb[:, 0:512], Gtm[:, 2 + tt, g, half * 128:(half + 1) * 128], tabl[:, tt, :], start=(n_ == 0), stop=(n_ == 31)),
                                             reads=["Gtm", tk], writes=[pk], inc=(n_ == 31))
                                        n_ += 1
                                k.op("act", lambda: S.copy(mixT[:, g, NCTX + pbk * 512:NCTX + (pbk + 1) * 512], pb[:, 0:512]), reads=[pk, "mixT"], writes=["mixT"])
                        k.barrier()
                    with contextlib.ExitStack() as pt:
                        PT = [k.sb(pt, "PT%d" % z, [128, 2, 2, 128], BF16) for z in range(2)]
                        rden = k.sb(pt, "rden", [128, 2, 128], F32)
                        scale = 0.125
                        it = 0
                        qblocks = [("c", 0), ("c", 1)] + [("l", n) for n in range(16)]
                        for (kind, n) in qblocks:
                            q0 = n * 128 if kind == "c" else NCTX + n * 128
                            for kh in range(2):
                                chunks = []
                                if kind == "l":
                                    for dlt in (-1, 0, 1):
                                        if 0 <= n + dlt < 16:
                                            chunks.append((NCTX + (n + dlt) * 128, dlt))
                                chunks += [(0, 0), (128, 0)]
                                for ci, (k0, dlt) in enumerate(chunks):
                                    z = it % 2
                                    it += 1
                                    pS = (psb[0 + 2 * z], psb[1 + 2 * z])
                                    kS = ("psb%d" % (2 * z), "psb%d" % (1 + 2 * z))
                                    for par in range(2):
                                        k.op("pe", lambda: P.matmul(pS[par][:, 0:256].rearrange("p (t q) -> p t q", t=2),
                                                                    KT[64 * par:64 * par + 64, kh, k0:k0 + 128],
                                                                    QT[64 * par:64 * par + 64, 2 * kh:2 * kh + 2, q0:q0 + 128],
                                                                    start=True, stop=True, tile_position=(64 * par, 0)),
                                             reads=["KT", "QT"], writes=[kS[par]])
                                    for par in range(2):
                                        k.op("act", lambda: S.activation(PT[z][:, par, :, :].rearrange("p t q -> p (t q)"), pS[par][:, 0:256], AF.Exp, scale=scale),
                                             reads=[kS[par], "PT%d" % z], writes=["PT%d" % z])
                                    if dlt != 0:
                                        msk = mask_ge if dlt == -1 else mask_le
                                        mb = bass.AP(msk[:].tensor, msk[:].offset, [list(msk[:].ap[0]), [0, 4], list(msk[:].ap[1])])
                                        k.op("dve", lambda: V.tensor_tensor(PT[z][:].rearrange("p a t q -> p (a t) q"), PT[z][:].rearrange("p a t q -> p (a t) q"), mb, ALU.mult),
                                             reads=["PT%d" % z, "mask_ge", "mask_le"], writes=["PT%d" % z])
                                    tt = k0 // 128
                                    first, last = (ci == 0), (ci == len(chunks) - 1)
                                    for par in range(2):
                                        k.op("pe", lambda: P.matmul(psb[4 + par][64 * par:64 * par + 64, 0:256], Vtm[:, tt, kh * 64:(kh + 1) * 64],
                                                                    PT[z][:, par, :, :].rearrange("p t q -> p (t q)"), start=first, stop=last,
                                                                    tile_position=(0, 64 * par)),
                                             reads=["Vtm", "PT%d" % z], writes=["psb%d" % (4 + par)], inc=last)
                                        k.op("pe", lambda: P.matmul(psb[6 + par][64 * par:64 * par + 64, 0:256], ones64[:],
                                                                    PT[z][:, par, :, :].rearrange("p t q -> p (t q)"), start=first, stop=last,
                                                                    tile_position=(0, 64 * par)),
                                             reads=["ones64", "PT%d" % z], writes=["psb%d" % (6 + par)], inc=last)
                                for par in range(2):
                                    lo, hi = 64 * par, 64 * par + 64
                                    seb = bass.AP(SE[:].tensor, SE[lo:hi, kh, :].offset, [list(SE[lo:hi, kh, :].ap[0]), list(SE[lo:hi, kh, :].ap[1]), [0, 128]])
                                    k.op("dve", lambda: V.tensor_tensor(rden[lo:hi, :, :], psb[6 + par][lo:hi, 0:256].rearrange("p (t q) -> p t q", t=2), seb, ALU.add),
                                         reads=["psb%d" % (6 + par), "SE", "rden%d" % par], writes=["rden%d" % par])
                                    k.op("dve", lambda: V.reciprocal(rden[lo:hi, :, :], rden[lo:hi, :, :]), reads=["rden%d" % par], writes=["rden%d" % par])
                                    k.op("dve", lambda: V.tensor_tensor(mixT[lo:hi, 4 + 2 * kh:6 + 2 * kh, q0:q0 + 128], psb[4 + par][lo:hi, 0:256].rearrange("p (t q) -> p t q", t=2),
                                                                        rden[lo:hi, :, :], ALU.mult),
                                         reads=["psb%d" % (4 + par), "rden%d" % par, "mixT"], writes=["mixT"])
                        dump("dbg_mixT", mixT[:], ["mixT"])
                        k.barrier()
                    with contextlib.ExitStack() as po:
                        wo = k.sb(po, "wo", [128, 8, D], BF16)
                        for kt in range(8):
                            k.dma("pool", wo[:, kt, :], w_out_d[i][kt * 128:(kt + 1) * 128, :], writes=["wo"])
                        yo = [k.sb(po, "eyo%d" % z, [128, 8, 256], F32) for z in range(2)]
                        for blk in range(NBLK):
                            t0 = blk * TB
                            z = blk % 2
                            for ct in range(8):
                                pb, pk = psb[ct % 4], "psb%d" % (ct % 4)
                                for mt in range(8):
                                    k.op("pe", lambda: P.matmul(pb[:, 0:TB], wo[:, mt, ct * 128:(ct + 1) * 128], mixT[:, mt, t0:t0 + TB], start=(mt == 0), stop=(mt == 7)),
                                         reads=["wo", "mixT"], writes=[pk], inc=(mt == 7))
                                k.op("act", lambda: S.copy(yo[z][:, ct, :], pb[:, 0:TB]), reads=[pk, "eyo%d" % z], writes=["eyo%d" % z])
                            k.dma("sp", YTv[:, :, t0:t0 + TB], yo[z][:], reads=["eyo%d" % z], writes=["YT"])
                        k.barrier()
            if mixer and l % 2 == 1:
                s5_mixer(l)
            elif mixer:
                even_mixer(l)

            with contextlib.ExitStack() as ph:
                w1b = k.sb(ph, "w1b", [128, 8, DFF], BF16)
                w2b = k.sb(ph, "w2b", [128, 32, D], BF16)
                for kt in range(8):
                    k.dma("pool", w1b[:, kt, :], w1_d[l][kt * 128:(kt + 1) * 128, :], writes=["w1b"])
                for j4 in range(8):
                    k.dma("pool", w2b[:, j4 * 4:(j4 + 1) * 4, :],
                          w2_d[l][j4 * 512:(j4 + 1) * 512, :].rearrange("(j p) n -> p j n", p=128), writes=["w2b"])
                NXB = 3
                xbs = [k.sb(ph, "xb%d" % z, [128, 8, TB], F32) for z in range(NXB)]
                yb = k.sb(ph, "yb", [128, 8, TB], F32)
                sq = k.sb(ph, "sq", [128, 8, TB], F32)
                tmp = sq
                rs = k.sb(ph, "rs", [128, TB], F32)
                h2s = [k.sb(ph, "h2%d" % z, [128, 8, TB], BF16) for z in range(2)]
                ob = k.sb(ph, "ob", [128, 8, TB], F32)
                ar = [k.sb(ph, "ar%d" % z, [128, TB], F32) for z in range(2)]
                a2all = k.sb(ph, "a2all", [128, 32, TB], BF16)
                tiles = (sq, rs, psb[3], "psb3")

                def P_load(blk):
                    z = blk % NXB
                    t0 = blk * TB
                    k.dma("sp", xbs[z][:], XTv[:, :, t0:t0 + TB], reads=["XT"], writes=["xb%d" % z])

                def P_stage(blk):
                    z = blk % NXB
                    xb, h2, xk, hk = xbs[z], h2s[blk % 2], "xb%d" % z, "h2%d" % (blk % 2)
                    j = 1 if blk == 0 else 0
                    t0 = blk * TB
                    if mixer:
                        k.dma("sp", yb[:], YTv[:, :, t0:t0 + TB], reads=["YT"], writes=["yb"])
                        rms_bc(tiles, yb[:], "yb", "m")
                        k.op("dve", lambda: V.tensor_tensor(tmp[:], yb[:], bc_mid(rs[:], 8), ALU.mult), reads=["yb", "rs"], writes=["sq"])
                        for ct in range(8):
                            k.op("dve", lambda: V.scalar_tensor_tensor(out=xb[:, ct, :], in0=tmp[:, ct, :], scalar=PRM[:, 2, ct, j:j + 1],
                                                                       in1=xb[:, ct, :], op0=ALU.mult, op1=ALU.add),
                                 reads=["sq", xk, "PRM"], writes=[xk])
                    rms_bc(tiles, xb[:], xk, "f")
                    k.op("dve", lambda: V.tensor_tensor(tmp[:], xb[:], bc_mid(rs[:], 8), ALU.mult), reads=[xk, "rs"], writes=["sq"])
                    for ct in range(8):
                        k.op("act", lambda: S.activation(h2[:, ct, :], tmp[:, ct, :], AF.Identity, bias=PRM[:, 4, ct, j:j + 1],
                                                         scale=PRM[:, 3, ct, j:j + 1]), reads=["sq", "PRM"], writes=[hk])

                def M1_stage(blk, j0, j1):
                    h2, hk = h2s[blk % 2], "h2%d" % (blk % 2)
                    for jf in range(j0, j1):
                        pa = psb[4 + jf % 4]
                        pak = "psb%d" % (4 + jf % 4)
                        for kt in range(8):
                            k.op("pe", lambda: P.matmul(pa[:, 0:TB], w1b[:, kt, jf * 128:(jf + 1) * 128], h2[:, kt, :],
                                                        start=(kt == 0), stop=(kt == 7)),
                                 reads=["w1b", hk], writes=[pak], inc=(kt == 7))
                        k.op("act", lambda: S.activation(ar[jf % 2][:], pa[:, 0:TB], AF.Relu), reads=[pak], writes=["ar%d" % (jf % 2)])
                        k.op("dve", lambda: V.tensor_tensor(a2all[:, jf, :], ar[jf % 2][:], ar[jf % 2][:], ALU.mult),
                             reads=["ar%d" % (jf % 2)], writes=["a2all"])

                def M2_stage(blk):
                    for ft in range(8):
                        po = psb[ft % 3]
                        pok = "psb%d" % (ft % 3)
                        for jf in range(32):
                            k.op("pe", lambda: P.matmul(po[:, 0:TB], w2b[:, jf, ft * 128:(ft + 1) * 128], a2all[:, jf, :],
                                                        start=(jf == 0), stop=(jf == 31)),
                                 reads=["w2b", "a2all"], writes=[pok], inc=(jf == 31))
                        k.op("act", lambda: S.copy(ob[:, ft, :], po[:, 0:TB]), reads=[pok], writes=["ob_%d" % ft])

                def E_stage(blk):
                    z = blk % NXB
                    xb, xk = xbs[z], "xb%d" % z
                    j = 1 if blk == 0 else 0
                    t0 = blk * TB
                    obk = ["ob_%d" % i_ for i_ in range(8)]
                    k.op("act", lambda: S.activation(sq[:], ob[:], AF.Square), reads=obk, writes=["sq"])
                    rms_bc(tiles, None, "sq", "o")
                    k.op("dve", lambda: V.tensor_tensor(tmp[:], ob[:], bc_mid(rs[:], 8), ALU.mult), reads=obk + ["rs"], writes=["sq"])
                    for ct in range(8):
                        k.op("dve", lambda: V.scalar_tensor_tensor(out=xb[:, ct, :], in0=tmp[:, ct, :], scalar=PRM[:, 5, ct, j:j + 1],
                                                                   in1=xb[:, ct, :], op0=ALU.mult, op1=ALU.add),
                             reads=["sq", xk, "PRM"], writes=[xk])
                    k.dma("pool", XTv[:, :, t0:t0 + TB], xb[:], reads=[xk], writes=["XT_st"])

                P_load(0)
                P_load(1)
                P_stage(0)
                for blk in range(NBLK):
                    M1_stage(blk, 0, 32)
                    if blk >= 1:
                        E_stage(blk - 1)
                    if blk + 2 < NBLK:
                        P_load(blk + 2)
                    if blk + 1 < NBLK:
                        P_stage(blk + 1)
                    M2_stage(blk)
                E_stage(NBLK - 1)
                k.barrier()

        with contextlib.ExitStack() as ph:
            xf = [k.sb(ph, "xf%d" % i, [128, 8, 128], F32) for i in range(2)]
            xo = [k.sb(ph, "xo%d" % i, [128, D], F32) for i in range(2)]
            for tt in range(16):
                b = tt % 2
                k.dma("sp", xf[b][:], XTv[:, :, NCTX + tt * 128:NCTX + (tt + 1) * 128], reads=["XT"], writes=["xf%d" % b])
                for half in range(2):
                    pkey = "psb%d" % ((tt * 2 + half) % 8)
                    pb = psb[(tt * 2 + half) % 8]
                    for c4 in range(4):
                        ct = half * 4 + c4
                        k.op("pe", lambda: P.transpose(pb[:, c4 * 128:(c4 + 1) * 128], xf[b][:, ct, :], ident[:]),
                             reads=["xf%d" % b, "ident"], writes=[pkey], inc=(c4 == 3))
                    if half == 0:
                        k.op("act", lambda: S.copy(xo[b][:, 0:512], pb[:]), reads=[pkey], writes=["xo%d_0" % b])
                    else:
                        k.op("dve", lambda: V.tensor_copy(xo[b][:, 512:1024], pb[:]), reads=[pkey], writes=["xo%d_1" % b])
                k.dma("sp", out_d[tt * 128:(tt + 1) * 128, :], xo[b][:], reads=["xo%d_0" % b, "xo%d_1" % b], writes=["out"])
            k.barrier()
        print("ninst", k.ninst, "nwaits", k.nwaits)
    return nc


def make_in_maps(inp):
    gains = np.stack([inp["mix_pre_g"], inp["mix_post_g"], inp["ffn_pre_g"], inp["ffn_post_g"]], 0)
    maps = []
    for b in range(8):
        m = {
            "x": np.ascontiguousarray(inp["x"][b]), "ctx": np.ascontiguousarray(inp["ctx"][b]),
            "cc": np.ascontiguousarray(np.stack([inp["c"][b], inp["c_ctx"]], 0)),
            "mod_w": inp["mod_w"], "mod_b": inp["mod_b"], "gains": np.ascontiguousarray(gains),
            "ffn_w1": inp["ffn_w1"], "ffn_w2": inp["ffn_w2"],
            "ssm_a_re": inp["ssm_a_re"], "ssm_a_im": inp["ssm_a_im"], "ssm_log_dt": inp["ssm_log_dt"],
            "ssm_b_re": inp["ssm_b_re"], "ssm_b_im": inp["ssm_b_im"], "ssm_c_re": inp["ssm_c_re"], "ssm_c_im": inp["ssm_c_im"],
            "ssm_d": inp["ssm_d"], "ssm_glu_w": inp["ssm_glu_w"],
            "even_w_in": inp["even_w_in"], "even_w_out": inp["even_w_out"], "even_sink": inp["even_sink"],
        }
        maps.append(m)
    return maps


def kernel(**inp):
    inp = {k_: np.asarray(v) for k_, v in inp.items()}
    nc = build(mixer=MIXER_ENABLED)
    res = run_bass_kernel_spmd(nc, make_in_maps(inp), core_ids=list(range(8)))
    return np.stack([r["out"] for r in res.results], 0)
```

```python
import contextlib
import numpy as np
import concourse.bass as bass
import concourse.mybir as mybir
from concourse.bass_utils import run_bass_kernel_spmd

F32 = mybir.dt.float32
BF16 = mybir.dt.bfloat16
I32 = mybir.dt.int32
ALU = mybir.AluOpType
AF = mybir.ActivationFunctionType
AX = mybir.AxisListType

S5_STAGE = 2
S5_SUB = 2
S5_NW = None
MIXER_ENABLED = True
SAME_ENGINE_SYNC = {"dve": True, "act": True, "pool": False, "pe": False}
DMA_RING = 8


class KB:
    def __init__(self, nc, es):
        self.nc = nc
        self.es = es
        self.raw = {"pe": nc.tensor, "dve": nc.vector, "act": nc.scalar, "pool": nc.gpsimd, "sp": nc.sync}
        self.sem = {}
        self.cnt = {}
        for e in ("pe", "dve", "act", "pool"):
            self.sem[e] = es.enter_context(nc.semaphore("s_" + e))
            self.cnt[e] = 0
        self.dring = {}
        self.dcnt = {}
        for q in ("sp", "act", "pool"):
            self.dring[q] = [es.enter_context(nc.semaphore("d_%s%d" % (q, i))) for i in range(DMA_RING)]
            self.dcnt[q] = 0
        self.seen = {e: {} for e in self.raw}
        self.lastw = {}
        self.readers = {}
        self.nwaits = 0
        self.ninst = 0

    def sb(self, es, name, shape, dt):
        self.uid = getattr(self, "uid", 0) + 1
        return es.enter_context(self.nc.sbuf_tensor("%s_u%d" % (name, self.uid), list(shape), dt))

    def ps(self, es, name, shape, dt=F32):
        return es.enter_context(self.nc.psum_tensor(name, list(shape), dt))

    def _collect(self, eng, reads, writes):
        need = {}

        def add(tok):
            if tok is None:
                return
            s, v, src = tok
            if src == eng and (not SAME_ENGINE_SYNC.get(eng, True) or v > self.cnt[eng]):
                return
            if need.get(s, (0,))[0] < v:
                need[s] = (v, src)

        for r in reads:
            add(self.lastw.get(r))
        for w in writes:
            add(self.lastw.get(w))
            for tok in self.readers.get(w, ()):
                add(tok)
        return need

    def _emit_waits(self, eng, need):
        seen = self.seen[eng]
        for s, (v, src) in need.items():
            if seen.get(s, 0) >= v:
                continue
            self.raw[eng].wait_ge(s, v)
            seen[s] = v
            self.nwaits += 1

    def _record(self, tok, reads, writes):
        for w in writes:
            self.lastw[w] = tok
            self.readers[w] = []
        for r in reads:
            if r in writes:
                continue
            self.readers.setdefault(r, []).append(tok)

    def op(self, eng, fn, reads=(), writes=(), inc=True):
        need = self._collect(eng, reads, writes)
        self._emit_waits(eng, need)
        inst = fn()
        self.ninst += 1
        if inc:
            self.cnt[eng] += 1
            inst.then_inc(self.sem[eng], 1)
            tok = (self.sem[eng], self.cnt[eng], eng)
        else:
            tok = (self.sem[eng], self.cnt[eng] + 1, eng)
        self._record(tok, reads, writes)
        return inst

    def dma(self, q, out, in_, reads=(), writes=(), **kw):
        need = self._collect(q, reads, writes)
        self._emit_waits(q, need)
        i = self.dcnt[q]
        self.dcnt[q] += 1
        s = self.dring[q][i % DMA_RING]
        v = 16 * (i // DMA_RING + 1)
        inst = self.raw[q].dma_start(out=out, in_=in_, **kw)
        inst.then_inc(s, 16)
        self.ninst += 1
        self._record((s, v, "dma_" + q), reads, writes)
        return inst

    def barrier(self):
        need = {}
        for e in ("pe", "dve", "act", "pool"):
            if self.cnt[e] > 0:
                need[self.sem[e]] = (self.cnt[e], "x")
        for q in ("sp", "act", "pool"):
            n = self.dcnt[q]
            for r in range(DMA_RING):
                cntr = (n - r + DMA_RING - 1) // DMA_RING if n > r else 0
                if cntr > 0:
                    need[self.dring[q][r]] = (16 * cntr, "x")
        for e in ("pe", "dve", "act", "pool", "sp"):
            self._emit_waits(e, need)
        self.lastw = {}
        self.readers = {}


def bc_mid(ap2, n):
    a = ap2.ap
    return bass.AP(ap2.tensor, ap2.offset, [list(a[0]), [0, n]] + [list(x) for x in a[1:]])


D = 1024
NT = 2304
NCTX = 256
TB = 256
NBLK = NT // TB
DFF = 4096
EPS = 1e-6


def build(nlayers=4, mixer=True, layers=None, debug=False):
    nc = bass.Bass("TRN2", target_bir_lowering=False)
    dt_in = lambda n, s: nc.dram_tensor(n, list(s), F32, kind="ExternalInput").ap()
    x_d = dt_in("x", [2048, D])
    ctx_d = dt_in("ctx", [NCTX, D])
    cc_d = dt_in("cc", [2, D])
    mod_w_d = dt_in("mod_w", [4, D, 6 * D])
    mod_b_d = dt_in("mod_b", [4, 6 * D])
    gains_d = dt_in("gains", [4, 4, D])
    w1_d = dt_in("ffn_w1", [4, D, DFF])
    w2_d = dt_in("ffn_w2", [4, DFF, D])
    w_in_d = dt_in("even_w_in", [2, D, 1280])
    w_out_d = dt_in("even_w_out", [2, D, D])
    even_sink_d = dt_in("even_sink", [2, 8])
    a_re_d = dt_in("ssm_a_re", [2, 2, 64, 64])
    a_im_d = dt_in("ssm_a_im", [2, 2, 64, 64])
    ldt_d = dt_in("ssm_log_dt", [2, 2, 64])
    b_re_d = dt_in("ssm_b_re", [2, 2, 64, 64, 16])
    b_im_d = dt_in("ssm_b_im", [2, 2, 64, 64, 16])
    c_re_d = dt_in("ssm_c_re", [2, 2, 64, 16, 64])
    c_im_d = dt_in("ssm_c_im", [2, 2, 64, 16, 64])
    dsk_d = dt_in("ssm_d", [2, D])
    glu_d = dt_in("ssm_glu_w", [2, D, 2 * D])
    out_d = nc.dram_tensor("out", [2048, D], F32, kind="ExternalOutput").ap()
    YF = nc.dram_tensor("YF", [2, D, NT], F32, kind=("ExternalOutput" if debug else "Internal")).ap()
    XT = nc.dram_tensor("XT", [D, NT], F32).ap()
    YT = nc.dram_tensor("YT", [D, NT], F32, kind=("ExternalOutput" if debug else "Internal")).ap()
    CL = nc.dram_tensor("CLtab", [2048, 2048], BF16, kind=("ExternalOutput" if debug else "Internal")).ap()
    SLn = nc.dram_tensor("SLtab", [2048, 2048], BF16, kind=("ExternalOutput" if debug else "Internal")).ap()
    ROPC = nc.dram_tensor("ROPC", [128, 2048], F32, kind=("ExternalOutput" if debug else "Internal")).ap()
    ROPS = nc.dram_tensor("ROPS", [128, 2048], F32, kind=("ExternalOutput" if debug else "Internal")).ap()
    XTv = XT.rearrange("(ct p) t -> p ct t", p=128)
    YTv = YT.rearrange("(ct p) t -> p ct t", p=128)

    with contextlib.ExitStack() as es:
        k = KB(nc, es)
        V, S, P, G = nc.vector, nc.scalar, nc.tensor, nc.gpsimd

        def dump(name, ap, keys):
            if not debug:
                return
            dt_ = nc.dram_tensor(name, list(ap.shape), ap.dtype, kind="ExternalOutput").ap()
            k.dma("sp", dt_, ap, reads=keys, writes=["dbg_" + name])
        ident = k.sb(es, "ident", [128, 128], F32)
        onesm = k.sb(es, "onesm", [128, 128], F32)
        k.op("pool", lambda: G.memset(ident[:], 0.0), writes=["ident"])
        k.op("pool", lambda: G.affine_select(out=ident[:], in_=ident[:], compare_op=ALU.not_equal, fill=1.0,
                                             base=0, pattern=[[-1, 128]], channel_multiplier=1),
             reads=["ident"], writes=["ident"])
        k.op("pool", lambda: G.memset(onesm[:], 1.0 / D), writes=["onesm"])
        psb = [k.ps(es, "psb%d" % i, [128, 512], F32) for i in range(8)]
        MV = k.sb(es, "MV", [128, 6, 8, 2], F32)
        GN = k.sb(es, "GN", [128, 4, 4, 8], F32)
        SC = k.sb(es, "SC", [128, 8, 2], F32)
        SCb = k.sb(es, "SCb", [128, 8, 2], BF16)
        PRM = k.sb(es, "PRM", [128, 6, 8, 2], F32)
        k.dma("sp", GN[:].rearrange("p a l c -> p (a l) c"),
              gains_d.rearrange("a l (c p) -> p (a l) c", p=128), writes=["GN"], allow_slow_non_contiguous=True)
        for j in range(2):
            k.dma("sp", SC[:, :, j], cc_d[j].rearrange("(c p) -> p c", p=128), writes=["SC"], allow_slow_non_contiguous=True)
        k.op("act", lambda: S.activation(SCb[:], SC[:], AF.Silu), reads=["SC"], writes=["SCb"])


        MAGIC = 12582912.0
        TWO_PI = float(2 * np.pi)
        if mixer and any(l % 2 == 0 for l in (layers if layers is not None else range(nlayers))):
            with contextlib.ExitStack() as ph:
                cidx = k.sb(ph, "cidx", [128, 2048], F32)
                prow = k.sb(ph, "prow", [128, 1], F32)
                tcol = k.sb(ph, "tcol", [128, 1], F32)
                k.op("pool", lambda: G.iota(cidx[:], pattern=[[1, 2048]], base=0, channel_multiplier=0, allow_small_or_imprecise_dtypes=True), writes=["cidx"])
                k.op("pool", lambda: G.iota(prow[:], pattern=[[0, 1]], base=0, channel_multiplier=1, allow_small_or_imprecise_dtypes=True), writes=["prow"])
                uu = [k.sb(ph, "uu%d" % z, [128, 2048], F32) for z in range(2)]
                nn = [k.sb(ph, "nn%d" % z, [128, 2048], F32) for z in range(2)]
                n2 = [k.sb(ph, "nq%d" % z, [128, 2048], F32) for z in range(2)]
                ff = [k.sb(ph, "ff%d" % z, [128, 2048], F32) for z in range(2)]
                tb = [k.sb(ph, "tb%d" % z, [128, 2048], BF16) for z in range(2)]
                tcs = [k.sb(ph, "tcs%d" % z, [128, 1], F32) for z in range(2)]
                SCL = TWO_PI * (1.0 - 2e-6)
                for tt in range(16):
                    tc_ = tcs[tt % 2]
                    k.op("dve", lambda: V.tensor_scalar(tc_[:], prow[:], float(tt * 128), 1.0 / 2048, ALU.add, ALU.mult), reads=["prow", "tcs%d" % (tt % 2)], writes=["tcs%d" % (tt % 2)])
                    for z, (tab, shift, scl) in enumerate(((CL, 0.25, SCL), (SLn, 0.0, -SCL))):
                        k.op("act", lambda: S.activation(uu[z][:], cidx[:], AF.Identity, bias=shift, scale=tc_[:, 0:1]), reads=["cidx", "tcs%d" % (tt % 2), "uu%d" % z], writes=["uu%d" % z])
                        k.op("act", lambda: S.activation(nn[z][:], uu[z][:], AF.Identity, bias=MAGIC, scale=1.0), reads=["uu%d" % z, "nn%d" % z], writes=["nn%d" % z])
                        k.op("act", lambda: S.activation(n2[z][:], nn[z][:], AF.Identity, bias=-MAGIC, scale=1.0), reads=["nn%d" % z, "nq%d" % z], writes=["nq%d" % z])
                        k.op("dve", lambda: V.tensor_tensor(ff[z][:], uu[z][:], n2[z][:], ALU.subtract), reads=["uu%d" % z, "nq%d" % z, "ff%d" % z], writes=["ff%d" % z])
                        k.op("act", lambda: S.activation(tb[z][:], ff[z][:], AF.Sin, scale=scl), reads=["ff%d" % z, "tb%d" % z], writes=["tb%d" % z])
                        k.dma("sp", tab[tt * 128:(tt + 1) * 128, :], tb[z][:], reads=["tb%d" % z], writes=["TAB"])
                k.barrier()
            with contextlib.ExitStack() as ph:
                pidx = k.sb(ph, "rpidx", [128, 1], I32)
                pi2 = k.sb(ph, "rpi2", [128, 1], I32)
                fi = k.sb(ph, "rfi", [128, 1], F32)
                invp = k.sb(ph, "rinvp", [128, 1], F32)
                axs = k.sb(ph, "raxs", [128, 1], F32)
                sgn = k.sb(ph, "rsgn", [128, 1], F32)
                k.op("pool", lambda: G.iota(pidx[:], pattern=[[0, 1]], base=0, channel_multiplier=1), writes=["pidx"])
                k.op("dve", lambda: V.tensor_scalar(pi2[:], pidx[:], 15, None, ALU.bitwise_and), reads=["pidx"], writes=["pi2"])
                k.op("dve", lambda: V.tensor_copy(fi[:], pi2[:]), reads=["pi2"], writes=["fi"])
                k.op("act", lambda: S.activation(invp[:], fi[:], AF.Exp, scale=-float(np.log(10000.0)) / 16.0), reads=["fi"], writes=["invp"])
                k.op("dve", lambda: V.tensor_scalar(pi2[:], pidx[:], 5, 1, ALU.arith_shift_right, ALU.bitwise_and), reads=["pidx", "fi"], writes=["pi2"])
                k.op("dve", lambda: V.tensor_copy(axs[:], pi2[:]), reads=["pi2"], writes=["axs"])
                k.op("dve", lambda: V.tensor_scalar(pi2[:], pidx[:], 4, 1, ALU.arith_shift_right, ALU.bitwise_and), reads=["pidx", "axs"], writes=["pi2"])
                k.op("dve", lambda: V.tensor_copy(sgn[:], pi2[:]), reads=["pi2"], writes=["sgn"])
                k.op("dve", lambda: V.tensor_scalar(sgn[:], sgn[:], 2.0, -1.0, ALU.mult, ALU.add), reads=["sgn"], writes=["sgn"])
                rowp = k.sb(ph, "rowp", [128, 2048], F32)
                colp = k.sb(ph, "colp", [128, 2048], F32)
                ang = k.sb(ph, "rang", [128, 2048], F32)
                u_ = k.sb(ph, "ru", [128, 2048], F32)
                n_ = k.sb(ph, "rn", [128, 2048], F32)
                k.op("pool", lambda: G.iota(rowp[:], pattern=[[1, 32], [0, 64]], base=0, channel_multiplier=0, allow_small_or_imprecise_dtypes=True), writes=["rowp"])
                k.op("pool", lambda: G.iota(colp[:], pattern=[[0, 32], [1, 64]], base=0, channel_multiplier=0, allow_small_or_imprecise_dtypes=True), writes=["colp"])
                k.op("dve", lambda: V.tensor_tensor(colp[:], colp[:], rowp[:], ALU.subtract), reads=["colp", "rowp"], writes=["colp"])
                k.op("dve", lambda: V.scalar_tensor_tensor(out=ang[:], in0=colp[:], scalar=axs[:, 0:1], in1=rowp[:], op0=ALU.mult, op1=ALU.add), reads=["colp", "rowp", "axs"], writes=["ang"])
                k.op("dve", lambda: V.tensor_scalar(ang[:], ang[:], invp[:, 0:1], 1.0 / TWO_PI, ALU.mult, ALU.mult), reads=["ang", "invp"], writes=["ang"])
                for (tab, shift) in ((ROPC, 0.25), (ROPS, 0.0)):
                    k.op("dve", lambda: V.tensor_scalar(u_[:], ang[:], shift, None, ALU.add), reads=["ang", "ru"], writes=["ru"])
                    k.op("dve", lambda: V.tensor_scalar(n_[:], u_[:], MAGIC, None, ALU.add), reads=["ru", "rn"], writes=["rn"])
                    k.op("dve", lambda: V.tensor_scalar(n_[:], n_[:], -MAGIC, None, ALU.add), reads=["rn"], writes=["rn"])
                    k.op("dve", lambda: V.tensor_tensor(u_[:], u_[:], n_[:], ALU.subtract), reads=["rn", "ru"], writes=["ru"])
                    k.op("dve", lambda: V.tensor_scalar(u_[:], u_[:], -0.49999, 0.49999, ALU.max, ALU.min), reads=["ru"], writes=["ru"])
                    k.op("act", lambda: S.activation(n_[:], u_[:], AF.Sin, scale=TWO_PI), reads=["ru", "rn"], writes=["rn"])
                    if shift == 0.0:
                        k.op("dve", lambda: V.tensor_scalar(n_[:], n_[:], sgn[:, 0:1], None, ALU.mult), reads=["rn", "sgn"], writes=["rn"])
                    k.dma("sp", tab[:, :], n_[:], reads=["rn"], writes=["ROP"])
                k.barrier()
        mask_ge = k.sb(es, "mask_ge", [128, 128], BF16)
        mask_le = k.sb(es, "mask_le", [128, 128], BF16)
        ones64 = k.sb(es, "ones64", [128, 64], BF16)
        k.op("pool", lambda: G.memset(mask_ge[:], 1.0), writes=["mask_ge"])
        k.op("pool", lambda: G.affine_select(out=mask_ge[:], in_=mask_ge[:], compare_op=ALU.is_ge, fill=0.0, base=0, pattern=[[-1, 128]], channel_multiplier=1), reads=["mask_ge"], writes=["mask_ge"])
        k.op("pool", lambda: G.memset(mask_le[:], 1.0), writes=["mask_le"])
        k.op("pool", lambda: G.affine_select(out=mask_le[:], in_=mask_le[:], compare_op=ALU.is_ge, fill=0.0, base=0, pattern=[[1, 128]], channel_multiplier=-1), reads=["mask_le"], writes=["mask_le"])
        k.op("pool", lambda: G.memset(ones64[:], 1.0), writes=["ones64"])
        with contextlib.ExitStack() as ph:
            xin = [k.sb(ph, "xin%d" % i, [128, D], F32) for i in range(2)]
            xst = [k.sb(ph, "xst%d" % i, [128, 8, 128], F32) for i in range(2)]
            for tt in range(NT // 128):
                b = tt % 2
                src = ctx_d[tt * 128:(tt + 1) * 128, :] if tt < 2 else x_d[(tt - 2) * 128:(tt - 1) * 128, :]
                k.dma("sp", xin[b][:], src, writes=["xin%d" % b])
                for half in range(2):
                    pb = psb[(tt * 2 + half) % 8]
                    for c4 in range(4):
                        ct = half * 4 + c4
                        k.op("pe", lambda: P.transpose(pb[:, c4 * 128:(c4 + 1) * 128], xin[b][:, ct * 128:(ct + 1) * 128], ident[:]),
                             reads=["xin%d" % b, "ident"], writes=["psb%d" % ((tt * 2 + half) % 8)], inc=(c4 == 3))
                    eng = "act" if half == 0 else "dve"
                    if eng == "act":
                        k.op("act", lambda: S.copy(xst[b][:, half * 4:(half + 1) * 4, :].rearrange("p c t -> p (c t)"), pb[:]),
                             reads=["psb%d" % ((tt * 2 + half) % 8)], writes=["xst%d_%d" % (b, half)])
                    else:
                        k.op("dve", lambda: V.tensor_copy(xst[b][:, half * 4:(half + 1) * 4, :].rearrange("p c t -> p (c t)"), pb[:]),
                             reads=["psb%d" % ((tt * 2 + half) % 8)], writes=["xst%d_%d" % (b, half)])
                k.dma("sp", XTv[:, :, tt * 128:(tt + 1) * 128], xst[b][:], reads=["xst%d_0" % b, "xst%d_1" % b], writes=["XT"])
            k.barrier()

        for l in (layers if layers is not None else range(nlayers)):
            with contextlib.ExitStack() as ph:
                mw = [k.sb(ph, "mw%d" % i, [128, 8, 512], BF16) for i in range(2)]
                mbias = k.sb(ph, "mbias", [128, 48], F32)
                k.dma("sp", mbias[:], mod_b_d[l].rearrange("(c p) -> p c", p=128), writes=["mbias"], allow_slow_non_contiguous=True)
                for ch in range(12):
                    b = ch % 2
                    k.dma("pool", mw[b][:], mod_w_d[l][:, ch * 512:(ch + 1) * 512].rearrange("(kt p) n -> p kt n", p=128),
                          writes=["mw%d" % b])
                    for s4 in range(4):
                        col = ch * 4 + s4
                        pb = psb[col % 8]
                        for kt in range(8):
                            k.op("pe", lambda: P.matmul(pb[:, 0:2], mw[b][:, kt, s4 * 128:(s4 + 1) * 128], SCb[:, kt, :],
                                                        start=(kt == 0), stop=(kt == 7)),
                                 reads=["mw%d" % b, "SCb"], writes=["psb%d" % (col % 8)], inc=(kt == 7))
                        k.op("dve", lambda: V.tensor_scalar(MV[:, col // 8, col % 8, :], pb[:, 0:2], mbias[:, col:col + 1], None, ALU.add),
                             reads=["psb%d" % (col % 8), "mbias"], writes=["MV"])
                for (o, isc, ish, ig, gpre, gpost) in ((0, 1, 0, 2, 0, 1), (3, 4, 3, 5, 2, 3)):
                    for j in range(2):
                        k.op("dve", lambda: V.scalar_tensor_tensor(out=PRM[:, o, :, j], in0=MV[:, isc, :, j], scalar=1.0, in1=GN[:, gpre, l, :],
                                                                   op0=ALU.add, op1=ALU.mult), reads=["MV", "GN"], writes=["PRM"])
                        k.op("dve", lambda: V.tensor_copy(PRM[:, o + 1, :, j], MV[:, ish, :, j]), reads=["MV"], writes=["PRM"])
                        k.op("dve", lambda: V.tensor_tensor(PRM[:, o + 2, :, j], MV[:, ig, :, j], GN[:, gpost, l, :], ALU.mult),
                             reads=["MV", "GN"], writes=["PRM"])
                k.barrier()

            def rms_bc(ph_tiles, src3, key_src, tag):
                sq, rs, pbank, pkey = ph_tiles
                if src3 is not None:
                    k.op("act", lambda: S.activation(sq[:], src3, AF.Square), reads=[key_src], writes=["sq"])
                for ct in range(8):
                    k.op("pe", lambda: P.matmul(pbank[:, 0:TB], onesm[:], sq[:, ct, :], start=(ct == 0), stop=(ct == 7)),
                         reads=["sq", "onesm"], writes=[pkey], inc=(ct == 7))
                k.op("dve", lambda: V.tensor_scalar(rs[:], pbank[:, 0:TB], EPS, None, ALU.add), reads=[pkey], writes=["rs"])
                k.op("act", lambda: S.activation(rs[:], rs[:], AF.Sqrt), reads=["rs"], writes=["rs"])
                k.op("dve", lambda: V.reciprocal(rs[:], rs[:]), reads=["rs"], writes=["rs"])
                return rs


            def prenorm_to_hT(ph, hT, off=0):
                xb = k.sb(ph, "pxb", [128, 8, TB], F32)
                sq = k.sb(ph, "psq", [128, 8, TB], F32)
                tmp = k.sb(ph, "ptmp", [128, 8, TB], F32)
                rs = k.sb(ph, "prs", [128, TB], F32)
                tiles = (sq, rs, psb[6], "psb6")
                for blk in range(NBLK):
                    j = 1 if blk == 0 else 0
                    t0 = blk * TB
                    k.dma("sp", xb[:], XTv[:, :, t0:t0 + TB], reads=["XT"], writes=["xb"])
                    rms_bc(tiles, xb[:], "xb", "p")
                    k.op("dve", lambda: V.tensor_tensor(tmp[:], xb[:], bc_mid(rs[:], 8), ALU.mult), reads=["xb", "rs"], writes=["tmp"])
                    for ct in range(8):
                        k.op("act", lambda: S.activation(hT[:, ct, off + t0:off + t0 + TB], tmp[:, ct, :], AF.Identity, bias=PRM[:, 1, ct, j:j + 1],
                                                         scale=PRM[:, 0, ct, j:j + 1]), reads=["tmp", "PRM"], writes=["hT"])

            def s5_mixer(l):
                i = l // 2
                PI = float(np.pi)
                W = 64
                NW = NT // W
                with contextlib.ExitStack() as ph:
                    T1 = 4
                    PAD = T1 - 1
                    hT = k.sb(ph, "hT", [128, 8, NT + NCTX + 2 * PAD], BF16)
                    k.op("pool", lambda: G.memset(hT[:, :, 0:PAD], 0.0), writes=["hT"])
                    k.op("pool", lambda: G.memset(hT[:, :, PAD + NT + NCTX:], 0.0), reads=["hT"], writes=["hT"])
                    with contextlib.ExitStack() as ph2:
                        prenorm_to_hT(ph2, hT, off=PAD)
                        k.barrier()
                    dsk = k.sb(ph, "dsk", [128, 8], F32)
                    k.dma("sp", dsk[:], dsk_d[i].rearrange("(c p) -> p c", p=128), writes=["dsk"], allow_slow_non_contiguous=True)
                    sc = contextlib.ExitStack()
                    k.op("act", lambda: S.copy(hT[:, :, PAD + NT:PAD + NT + NCTX], hT[:, :, PAD:PAD + NCTX]), reads=["hT"], writes=["hT"])
                    BOFF = PAD + NCTX
                    WinT = [[[k.sb(sc, "WinT%d%d%d" % (d, ri, ti), [128, 8, 128], BF16) for ti in range(T1)] for ri in range(2)] for d in range(2)]
                    CwQ = [[k.sb(sc, "CwQ%d%d" % (d, ri), [128, 32, 32], BF16) for ri in range(2)] for d in range(2)]
                    Gt = [k.sb(sc, "Gt%d" % d, [128, 2, 64], F32) for d in range(2)]
                    with contextlib.ExitStack() as pp:
                        def t32(n):
                            return k.sb(pp, n, [128, 32], F32)
                        twopi = t32("twopi")
                        k.op("pool", lambda: G.memset(twopi[:], 2 * PI), writes=["twopi"])
                        pidx = k.sb(pp, "pidx", [128, 1], I32)
                        modd = k.sb(pp, "modd", [128, 1], F32)
                        mevn = k.sb(pp, "mevn", [128, 1], F32)
                        nodd = k.sb(pp, "nodd", [128, 1], F32)
                        nevn = k.sb(pp, "nevn", [128, 1], F32)
                        k.op("pool", lambda: G.iota(pidx[:], pattern=[[0, 1]], base=0, channel_multiplier=1), writes=["pidx"])
                        k.op("dve", lambda: V.tensor_scalar(pidx[:], pidx[:], 4, 1, ALU.arith_shift_right, ALU.bitwise_and), reads=["pidx"], writes=["pidx"])
                        k.op("dve", lambda: V.tensor_copy(modd[:], pidx[:]), reads=["pidx"], writes=["modd"])
                        k.op("dve", lambda: V.tensor_scalar(mevn[:], modd[:], -1.0, 1.0, ALU.mult, ALU.add), reads=["modd"], writes=["mevn"])
                        k.op("dve", lambda: V.tensor_scalar(nodd[:], modd[:], -1.0, None, ALU.mult), reads=["modd"], writes=["nodd"])
                        k.op("dve", lambda: V.tensor_scalar(nevn[:], mevn[:], -1.0, None, ALU.mult), reads=["mevn"], writes=["nevn"])
                        Br = k.sb(pp, "Br", [128, 32, 32], F32)
                        Bi = k.sb(pp, "Bi", [128, 32, 32], F32)
                        BbR = k.sb(pp, "BbR", [128, 32, 32], F32)
                        BbI = k.sb(pp, "BbI", [128, 32, 32], F32)
                        T1t = k.sb(pp, "T1t", [128, 32, 32], F32)
                        TbR = k.sb(pp, "TbR", [128, 32, 32], F32)
                        TbI = k.sb(pp, "TbI", [128, 32, 32], F32)
                        Cn = k.sb(pp, "Cn", [128, 8, 64], F32)
                        Cblk = k.sb(pp, "Cblk", [128, 8, 128], F32)
                        lr, li, dtt, tq, mag, ang, sa, sinv, cosv, Ar, Ai, am1, n2, kr, ki, u1 = [t32("p%d" % z) for z in range(16)]
                        for d in range(2):
                            def dve(fn, r, w):
                                k.op("dve", fn, reads=r, writes=w)
                            def act(fn, r, w):
                                k.op("act", fn, reads=r, writes=w)
                            k.dma("sp", lr[:], a_re_d[i, d].rearrange("(q a) p -> (a p) q", a=2), writes=["lr"], allow_slow_non_contiguous=True)
                            k.dma("sp", li[:], a_im_d[i, d].rearrange("(q a) p -> (a p) q", a=2), writes=["li"], allow_slow_non_contiguous=True)
                            for g2 in range(2):
                                base = ldt_d[i, d]
                                src = bass.AP(base.tensor, base.offset + g2, [[0, 64], [2, 32]])
                                k.dma("sp", dtt[g2 * 64:(g2 + 1) * 64, :], src, writes=["dtt"], allow_slow_non_contiguous=True)
                            act(lambda: S.activation(dtt[:], dtt[:], AF.Exp), ["dtt"], ["dtt"])
                            dve(lambda: V.tensor_tensor(tq[:], lr[:], dtt[:], ALU.mult), ["lr", "dtt"], ["tq"])
                            act(lambda: S.activation(mag[:], tq[:], AF.Exp), ["tq"], ["mag"])
                            dve(lambda: V.tensor_tensor(ang[:], li[:], dtt[:], ALU.mult), ["li", "dtt"], ["ang"])
                            MAGIC = 12582912.0
                            PIC = 3.1415925
                            for (dst, shift, tag) in ((sinv, 0.0, "s"), (cosv, 0.5 * PI, "c")):
                                dve(lambda: V.tensor_scalar(u1[:], ang[:], shift, None, ALU.add), ["ang", "sa", "u1"], ["u1"])
                                dve(lambda: V.tensor_scalar(sa[:], u1[:], 1.0 / (2 * PI), None, ALU.mult), ["u1", "sa"], ["sa"])
                                dve(lambda: V.tensor_scalar(sa[:], sa[:], MAGIC, None, ALU.add), ["sa"], ["sa"])
                                dve(lambda: V.tensor_scalar(sa[:], sa[:], -MAGIC, None, ALU.add), ["sa"], ["sa"])
                                dve(lambda: V.scalar_tensor_tensor(out=sa[:], in0=sa[:], scalar=-2 * PI, in1=u1[:], op0=ALU.mult, op1=ALU.add), ["sa", "u1"], ["sa"])
                                dve(lambda: V.tensor_scalar(sa[:], sa[:], -PIC, PIC, ALU.max, ALU.min), ["sa"], ["sa"])
                                act(lambda: S.activation(dst[:], sa[:], AF.Sin), ["sa"], ["sinv" if tag == "s" else "cosv"])
                            dve(lambda: V.tensor_tensor(Ar[:], mag[:], cosv[:], ALU.mult), ["mag", "cosv"], ["Ar"])
                            dve(lambda: V.tensor_tensor(Ai[:], mag[:], sinv[:], ALU.mult), ["mag", "sinv"], ["Ai"])
                            gk = "Gt%d" % d
                            pws = [(None, None), (Ar, Ai)]
                            for pi_ in range(2, T1 + 1):
                                pr_ = t32("pwr%d_%d" % (d, pi_)); pim_ = t32("pwi%d_%d" % (d, pi_))
                                qr_, qi_ = pws[pi_ - 1]
                                kq = ["pw%d" % (pi_ - 1), "Ar", "Ai"]
                                dve(lambda: V.tensor_tensor(pr_[:], qr_[:], Ar[:], ALU.mult), kq, ["pw%d" % pi_])
                                dve(lambda: V.tensor_tensor(u1[:], qi_[:], Ai[:], ALU.mult), kq + ["u1"], ["u1"])
                                dve(lambda: V.tensor_tensor(pr_[:], pr_[:], u1[:], ALU.subtract), ["pw%d" % pi_, "u1"], ["pw%d" % pi_])
                                dve(lambda: V.tensor_tensor(pim_[:], qr_[:], Ai[:], ALU.mult), kq + ["pw%d" % pi_], ["pw%d" % pi_])
                                dve(lambda: V.tensor_tensor(u1[:], qi_[:], Ar[:], ALU.mult), kq + ["u1"], ["u1"])
                                dve(lambda: V.tensor_tensor(pim_[:], pim_[:], u1[:], ALU.add), ["pw%d" % pi_, "u1"], ["pw%d" % pi_])
                                pws.append((pr_, pim_))
                            AT_r, AT_i = pws[T1]
                            Arp = AT_r[:].rearrange("p (c r) -> p r c", r=4)
                            Aip = AT_i[:].rearrange("p (c r) -> p r c", r=4)
                            gv = lambda a, lo: Gt[d][:, a, lo:lo + 32].rearrange("p (r c) -> p r c", r=4)
                            dve(lambda: V.tensor_copy(gv(0, 0), Arp), ["pw%d" % T1], [gk])
                            dve(lambda: V.tensor_scalar(gv(0, 32), Aip, -1.0, None, ALU.mult), ["pw%d" % T1, gk], [gk])
                            dve(lambda: V.tensor_copy(gv(1, 0), Aip), ["pw%d" % T1, gk], [gk])
                            dve(lambda: V.tensor_copy(gv(1, 32), Arp), ["pw%d" % T1, gk], [gk])
                            dve(lambda: V.tensor_scalar(am1[:], Ar[:], -1.0, None, ALU.add), ["Ar"], ["am1"])
                            dve(lambda: V.tensor_tensor(n2[:], lr[:], lr[:], ALU.mult), ["lr"], ["n2"])
                            dve(lambda: V.tensor_tensor(u1[:], li[:], li[:], ALU.mult), ["li", "u1"], ["u1"])
                            dve(lambda: V.tensor_tensor(n2[:], n2[:], u1[:], ALU.add), ["n2", "u1"], ["n2"])
                            dve(lambda: V.reciprocal(n2[:], n2[:]), ["n2"], ["n2"])
                            dve(lambda: V.tensor_tensor(kr[:], am1[:], lr[:], ALU.mult), ["am1", "lr"], ["kr"])
                            dve(lambda: V.tensor_tensor(u1[:], Ai[:], li[:], ALU.mult), ["Ai", "li", "n2", "u1"], ["u1"])
                            dve(lambda: V.tensor_tensor(kr[:], kr[:], u1[:], ALU.add), ["kr", "u1"], ["kr"])
                            dve(lambda: V.tensor_tensor(kr[:], kr[:], n2[:], ALU.mult), ["kr", "n2"], ["kr"])
                            dve(lambda: V.tensor_tensor(ki[:], Ai[:], lr[:], ALU.mult), ["Ai", "lr"], ["ki"])
                            dve(lambda: V.tensor_tensor(u1[:], am1[:], li[:], ALU.mult), ["am1", "li", "kr", "u1"], ["u1"])
                            dve(lambda: V.tensor_tensor(ki[:], ki[:], u1[:], ALU.subtract), ["ki", "u1"], ["ki"])
                            dve(lambda: V.tensor_tensor(ki[:], ki[:], n2[:], ALU.mult), ["ki", "n2"], ["ki"])
                            k.op("pool", lambda: G.memset(Br[:], 0.0), reads=["BbR", "BbI"], writes=["Br"])
                            k.op("pool", lambda: G.memset(Bi[:], 0.0), reads=["BbR", "BbI"], writes=["Bi"])
                            for g2 in range(2):
                                for (dst, srcd, key) in ((Br, b_re_d, "Br"), (Bi, b_im_d, "Bi")):
                                    base = srcd[i, d]
                                    src = bass.AP(base.tensor, base.offset + g2 * 1024, [[16, 64], [2048, 32], [1, 16]])
                                    k.dma("sp", dst[g2 * 64:(g2 + 1) * 64, :, g2 * 16:(g2 + 1) * 16], src, reads=[key], writes=[key])
                            def bc_last(a2, n):
                                a = a2.ap
                                return bass.AP(a2.tensor, a2.offset, [list(a[0]), list(a[1]), [0, n]])
                            def cmul(outR, outI, xr, xi, inR, inI, kx, kin, kout):
                                xrb, xib = bc_last(xr[:], 32), bc_last(xi[:], 32)
                                dve(lambda: V.tensor_tensor(outR[:], inR[:], xrb, ALU.mult), kin + kx + kout, kout)
                                dve(lambda: V.tensor_tensor(T1t[:], inI[:], xib, ALU.mult), kin + kx + ["T1t"], ["T1t"])
                                dve(lambda: V.tensor_tensor(outR[:], outR[:], T1t[:], ALU.subtract), kout + ["T1t"], kout)
                                dve(lambda: V.tensor_tensor(outI[:], inI[:], xrb, ALU.mult), kin + kx + kout, kout)
                                dve(lambda: V.tensor_tensor(T1t[:], inR[:], xib, ALU.mult), kin + kx + ["T1t"], ["T1t"])
                                dve(lambda: V.tensor_tensor(outI[:], outI[:], T1t[:], ALU.add), kout + ["T1t"], kout)
                            cmul(BbR, BbI, kr, ki, Br, Bi, ["kr", "ki"], ["Br", "Bi"], ["Bb"])
                            for ti in range(T1):
                                if ti == 0:
                                    srcs = (BbR, BbI)
                                    skey = ["Bb"]
                                else:
                                    cmul(TbR, TbI, pws[ti][0], pws[ti][1], BbR, BbI, ["pw%d" % ti], ["Bb"], ["Tb"])
                                    srcs = (TbR, TbI)
                                    skey = ["Tb"]
                                for ri in range(2):
                                    for ct in range(8):
                                        pk = "psb%d" % (ct % 4)
                                        k.op("pe", lambda: P.transpose(psb[ct % 4][:, 0:128], srcs[ri][:, 4 * ct:4 * ct + 4, :].rearrange("p a b -> p (a b)"), ident[:]),
                                             reads=skey + ["ident"], writes=[pk])
                                        act(lambda: S.copy(WinT[d][ri][ti][:, ct, :], psb[ct % 4][:, 0:128]), [pk], ["WinT%d%d" % (d, ri)])
                            for ri, srcd, mo, me in ((0, c_re_d, modd, mevn), (1, c_im_d, nodd, nevn)):
                                ck = "CwQ%d%d" % (d, ri)
                                k.dma("sp", Cn[:], srcd[i, d].rearrange("(ct g) c p -> (g c) ct p", g=8), reads=["Cn"], writes=["Cn"])
                                dve(lambda: V.tensor_scalar(Cblk[:, :, 0:64], Cn[:], me[:, 0:1], None, ALU.mult), ["Cn", "mevn", "nevn", "Cblk"], ["Cblk"])
                                dve(lambda: V.tensor_scalar(Cblk[:, :, 64:128], Cn[:], mo[:, 0:1], None, ALU.mult), ["Cn", "modd", "nodd", "Cblk"], ["Cblk"])
                                for ct in range(8):
                                    pk = "psb%d" % (4 + ct % 4)
                                    k.op("pe", lambda: P.transpose(psb[4 + ct % 4][:, 0:128], Cblk[:, ct, :], ident[:]), reads=["Cblk", "ident"], writes=[pk])
                                    for q4 in range(4):
                                        act(lambda: S.copy(CwQ[d][ri][:, ct * 4 + q4, :], psb[4 + ct % 4][:, 32 * q4:32 * q4 + 32]), [pk, ck], [ck])
                        k.barrier()
                    if S5_STAGE < 1:
                        sc.close()
                        return
                    NG = W // T1
                    Bw = [[k.sb(sc, "Bw%d_%d" % (d, z_), [128, 64, W], BF16) for z_ in range(2)] for d in range(2)]
                    H = [k.sb(sc, "H%d" % d, [128, 64, W + T1], F32) for d in range(2)]
                    Sb = [k.sb(sc, "Sb%d" % d, [128, 64, W], BF16) for d in range(2)]
                    XY = [k.sb(sc, "XY%d" % d, [128, 2, 64, T1], F32) for d in range(2)]
                    Nn = [k.sb(sc, "Nn%d" % d, [128, 2, 32, T1], F32) for d in range(2)]
                    yo = [k.sb(sc, "yo%d" % d, [128, 8, W], F32) for d in range(2)]
                    k.op("pool", lambda: G.memset(H[0][:], 0.0), writes=["H0"])
                    k.op("pool", lambda: G.memset(H[1][:], 0.0), writes=["H1"])

                    def bc4(ap3):
                        a_ = ap3.ap
                        return bass.AP(ap3.tensor, ap3.offset, [list(a_[0]), [0, 2], list(a_[1]), list(a_[2])])

                    def gbc(d):
                        a_ = Gt[d][:].ap
                        return bass.AP(Gt[d][:].tensor, Gt[d][:].offset, [list(a_[0]), list(a_[1]), list(a_[2]), [0, T1]])

                    NWR = NW if S5_NW is None else S5_NW

                    def emit_bu(step_w):
                        zb = step_w % 2
                        wins = (step_w, NW - 1 - step_w)
                        for d in range(2):
                            p0 = wins[d] * W
                            bk = "Bw%d_%d" % (d, zb)
                            for half in range(2):
                                for c4 in range(4):
                                    ct = half * 4 + c4
                                    for ri in range(2):
                                        for r in range(4):
                                            slot = c4 * 2 + ri
                                            last = (c4 == 3 and ri == 1)
                                            for ti in range(T1):
                                                c0 = (PAD + p0 - ti) if d == 0 else (BOFF + p0 + ti)
                                                k.op("pe", lambda: P.matmul(psb[r][:, slot * W:(slot + 1) * W],
                                                                            WinT[d][ri][ti][32 * r:32 * r + 32, ct, :],
                                                                            hT[32 * r:32 * r + 32, ct, c0:c0 + W],
                                                                            start=(ti == 0), stop=(ti == T1 - 1), tile_position=(32 * r, 0)),
                                                     reads=["hT", "WinT%d%d" % (d, ri)], writes=["psb%d" % r], inc=(last and r == 3 and ti == T1 - 1))
                                for r in range(4):
                                    for ri in range(2):
                                        src = psb[r][:].rearrange("p (c i w) -> p c i w", i=2, w=W)[:, :, ri, :]
                                        lo = ri * 32 + r * 8 + half * 4
                                        k.op("act", lambda: S.copy(Bw[d][zb][:, lo:lo + 4, :], src), reads=["psb%d" % r, bk], writes=[bk])

                    emit_bu(0)
                    for step_w in range(NWR):
                        zb = step_w % 2
                        wins = (step_w, NW - 1 - step_w)
                        if step_w + 1 < NWR:
                            emit_bu(step_w + 1)
                        for g in range(NG if S5_SUB >= 1 else 0):
                            rd = (g * T1, W - g * T1)
                            wr = (T1 + g * T1, W - (g + 1) * T1)
                            bj = (g * T1, W - (g + 1) * T1)
                            for d in range(2):
                                k.op("dve", lambda: V.tensor_tensor(XY[d][:], bc4(H[d][:, :, rd[d]:rd[d] + T1]), gbc(d), ALU.mult),
                                     reads=["H%d" % d, "Gt%d" % d], writes=["XY%d" % d])
                            for d in range(2):
                                k.op("dve", lambda: V.tensor_tensor(Nn[d][:], XY[d][:, :, 0:32, :], XY[d][:, :, 32:64, :], ALU.add),
                                     reads=["XY%d" % d], writes=["Nn%d" % d])
                            for d in range(2):
                                k.op("dve", lambda: V.tensor_tensor(H[d][:, :, wr[d]:wr[d] + T1], Nn[d][:].rearrange("p a q t -> p (a q) t"),
                                                                    Bw[d][zb][:, :, bj[d]:bj[d] + T1], ALU.add),
                                     reads=["Nn%d" % d, "Bw%d_%d" % (d, zb)], writes=["H%d" % d])
                        for d in range(2 if S5_SUB >= 2 else 0):
                            p0 = wins[d] * W
                            if d == 0:
                                t0 = p0
                                k.op("act", lambda: S.copy(Sb[0][:], H[0][:, :, T1:W + T1]), reads=["H0"], writes=["Sb0"])
                                k.op("dve", lambda: V.tensor_copy(H[0][:, :, 0:T1], H[0][:, :, W:W + T1]), reads=["H0"], writes=["H0"])
                            else:
                                t0 = (NCTX + p0) if p0 < 2048 else (p0 - 2048)
                                k.op("act", lambda: S.copy(Sb[1][:], H[1][:, :, 0:W]), reads=["H1"], writes=["Sb1"])
                                k.op("dve", lambda: V.tensor_copy(H[1][:, :, W:W + T1], H[1][:, :, 0:T1]), reads=["H1"], writes=["H1"])
                            for ct in range(8):
                                for r in range(4):
                                    q = ct * 4 + r
                                    for ri in range(2):
                                        k.op("pe", lambda: P.matmul(psb[4 + r][32 * r:32 * r + 32, ct * W:(ct + 1) * W], CwQ[d][ri][:, q, :],
                                                                    Sb[d][:, ri * 32 + r * 8 + ct, :], start=(ri == 0), stop=(ri == 1),
                                                                    tile_position=(0, 32 * r)),
                                             reads=["CwQ%d%d" % (d, ri), "Sb%d" % d], writes=["psb%d" % (4 + r)], inc=(ct == 7 and ri == 1))
                            for r in range(4):
                                k.op("act", lambda: S.copy(yo[d][32 * r:32 * r + 32, :, :].rearrange("p c w -> p (c w)"), psb[4 + r][32 * r:32 * r + 32, 0:8 * W]),
                                     reads=["psb%d" % (4 + r), "yo%d" % d], writes=["yo%d" % d])
                            k.dma("sp", YF[d].rearrange("(ct p) t -> p ct t", p=128)[:, :, t0:t0 + W], yo[d][:], reads=["yo%d" % d], writes=["YF"])
                    k.barrier()
                    sc.close()
                    if S5_STAGE < 2:
                        return
                    with contextlib.ExitStack() as pg:
                        gw = k.sb(pg, "gw", [128, 8, 2 * D], BF16)
                        for kt in range(8):
                            k.dma("pool", gw[:, kt, :], glu_d[i][kt * 128:(kt + 1) * 128, :], writes=["gw"])
                        ya = k.sb(pg, "ya", [128, 8, TB], F32)
                        yb2 = k.sb(pg, "yb2", [128, 8, TB], F32)
                        y2 = k.sb(pg, "y2", [128, 8, TB], F32)
                        gl = k.sb(pg, "gl", [128, 8, TB], BF16)
                        sg = k.sb(pg, "sg", [128, TB], F32)
                        zo = k.sb(pg, "zo", [128, 8, TB], F32)
                        for blk in range(NBLK):
                            t0 = blk * TB
                            k.dma("sp", ya[:], YF[0].rearrange("(ct p) t -> p ct t", p=128)[:, :, t0:t0 + TB], reads=["YF"], writes=["ya"])
                            k.dma("sp", yb2[:], YF[1].rearrange("(ct p) t -> p ct t", p=128)[:, :, t0:t0 + TB], reads=["YF"], writes=["yb2"])
                            k.op("dve", lambda: V.tensor_tensor(ya[:], ya[:], yb2[:], ALU.add), reads=["ya", "yb2"], writes=["ya"])
                            for ct in range(8):
                                k.op("dve", lambda: V.scalar_tensor_tensor(out=ya[:, ct, :], in0=hT[:, ct, PAD + t0:PAD + t0 + TB], scalar=dsk[:, ct:ct + 1],
                                                                           in1=ya[:, ct, :], op0=ALU.mult, op1=ALU.add), reads=["ya", "hT", "dsk"], writes=["ya"])
                            k.op("dve", lambda: V.tensor_tensor(y2[:], ya[:], ya[:], ALU.mult), reads=["ya"], writes=["y2"])
                            k.op("dve", lambda: V.tensor_scalar(y2[:], y2[:], 0.044715, 1.0, ALU.mult, ALU.add), reads=["y2"], writes=["y2"])
                            k.op("dve", lambda: V.tensor_tensor(y2[:], y2[:], ya[:], ALU.mult), reads=["y2", "ya"], writes=["y2"])
                            k.op("act", lambda: S.activation(y2[:], y2[:], AF.Sigmoid, scale=1.5957691216057308), reads=["y2"], writes=["y2"])
                            k.op("dve", lambda: V.tensor_tensor(gl[:], y2[:], ya[:], ALU.mult), reads=["y2", "ya"], writes=["gl"])
                            for ct in range(8):
                                pa, pb_ = psb[(2 * ct) % 4], psb[(2 * ct + 1) % 4]
                                ka, kb_ = "psb%d" % ((2 * ct) % 4), "psb%d" % ((2 * ct + 1) % 4)
                                for kt in range(8):
                                    k.op("pe", lambda: P.matmul(pa[:, 0:TB], gw[:, kt, ct * 128:(ct + 1) * 128], gl[:, kt, :], start=(kt == 0), stop=(kt == 7)),
                                         reads=["gw", "gl"], writes=[ka], inc=(kt == 7))
                                for kt in range(8):
                                    k.op("pe", lambda: P.matmul(pb_[:, 0:TB], gw[:, kt, D + ct * 128:D + (ct + 1) * 128], gl[:, kt, :], start=(kt == 0), stop=(kt == 7)),
                                         reads=["gw", "gl"], writes=[kb_], inc=(kt == 7))
                                k.op("act", lambda: S.activation(sg[:], pb_[:, 0:TB], AF.Sigmoid), reads=[kb_], writes=["sg"])
                                k.op("dve", lambda: V.tensor_tensor(zo[:, ct, :], pa[:, 0:TB], sg[:], ALU.mult), reads=[ka, "sg", "zo"], writes=["zo"])
                            k.dma("sp", YTv[:, :, t0:t0 + TB], zo[:], reads=["zo"], writes=["YT"])
                        k.barrier()

            def even_mixer(l):
                i = l // 2
                NLAT = 2048
                with contextlib.ExitStack() as ph:
                    fT = k.sb(ph, "fT", [128, 4, NT], BF16)
                    QT = k.sb(ph, "QT", [128, 4, NT], BF16)
                    KT = k.sb(ph, "KT", [128, 2, NT], BF16)
                    Vtm = k.sb(ph, "Vtm", [128, NT // 128, 128], BF16)
                    mixT = k.sb(ph, "mixT", [128, 8, NT], BF16)
                    SEall = k.sb(ph, "SEall", [128, 8], F32)
                    SE = k.sb(ph, "SE", [128, 2, 2], F32)
                    sk = even_sink_d[i]
                    k.dma("sp", SEall[:], bass.AP(sk.tensor, sk.offset, [[0, 128], [1, 8]]), writes=["SEall"], allow_slow_non_contiguous=True)
                    k.op("act", lambda: S.activation(SEall[:], SEall[:], AF.Exp), reads=["SEall"], writes=["SEall"])
                    for kh in range(2):
                        for tl in range(2):
                            k.op("dve", lambda: V.tensor_copy(SE[0:64, kh, tl:tl + 1], SEall[0:64, 4 * kh + 2 * tl:4 * kh + 2 * tl + 1]), reads=["SEall", "SE"], writes=["SE"])
                            k.op("dve", lambda: V.tensor_copy(SE[64:128, kh, tl:tl + 1], SEall[64:128, 4 * kh + 2 * tl + 1:4 * kh + 2 * tl + 2]), reads=["SEall", "SE"], writes=["SE"])
                    with contextlib.ExitStack() as pa:
                        hT = k.sb(pa, "hT", [128, 8, NT], BF16)
                        with contextlib.ExitStack() as ph2:
                            prenorm_to_hT(ph2, hT)
                            k.barrier()
                        wb = k.sb(pa, "wb", [128, 8, 1280], BF16)
                        for kt in range(8):
                            k.dma("pool", wb[:, kt, :], w_in_d[i][kt * 128:(kt + 1) * 128, :], writes=["wb"])
                        wsw = k.sb(pa, "wsw", [128, 8, 640], BF16)
                        wv = wb[:, :, 512:1152].rearrange("p k (h two e) -> p k h two e", two=2, e=16)
                        wsv = wsw[:].rearrange("p k (h two e) -> p k h two e", two=2, e=16)
                        for kt in range(8):
                            k.op("pool", lambda: G.tensor_copy(wsv[:, kt, :, 0, :], wv[:, kt, :, 1, :]), reads=["wb", "wsw"], writes=["wsw"])
                            k.op("pool", lambda: G.tensor_copy(wsv[:, kt, :, 1, :], wv[:, kt, :, 0, :]), reads=["wb", "wsw"], writes=["wsw"])
                        wkd = k.sb(pa, "wkd", [128, 8, 2, 128], BF16)
                        wkds = k.sb(pa, "wkds", [128, 8, 2, 128], BF16)
                        for dup in range(2):
                            k.op("pool", lambda: G.tensor_copy(wkd[:, :, :, dup * 64:(dup + 1) * 64], wb[:, :, 1024:1152].rearrange("p k (h d) -> p k h d", d=64)), reads=["wb", "wkd"], writes=["wkd"])
                            k.op("pool", lambda: G.tensor_copy(wkds[:, :, :, dup * 64:(dup + 1) * 64], wsw[:, :, 512:640].rearrange("p k (h d) -> p k h d", d=64)), reads=["wsw", "wkds"], writes=["wkds"])
                        ropc = k.sb(pa, "ropc", [128, NLAT], F32)
                        rops = k.sb(pa, "rops", [128, NLAT], F32)
                        k.dma("sp", ropc[:], ROPC[:, :], reads=["ROP"], writes=["ropc"])
                        k.dma("sp", rops[:], ROPS[:, :], reads=["ROP"], writes=["rops"])
                        t1 = k.sb(pa, "rt1", [128, 512], F32)
                        t2 = k.sb(pa, "rt2", [128, 512], F32)
                        blocks = [(0, 256)] + [(256 + 512 * b, 512) for b in range(4)]
                        nb = 0
                        for (t0, n) in blocks:
                            lat = t0 >= NCTX
                            for g in range(4):
                                pb, pk = psb[nb % 4], "psb%d" % (nb % 4); nb += 1
                                for kt in range(8):
                                    k.op("pe", lambda: P.matmul(pb[:, 0:n], wb[:, kt, g * 128:(g + 1) * 128], hT[:, kt, t0:t0 + n], start=(kt == 0), stop=(kt == 7)),
                                         reads=["wb", "hT"], writes=[pk], inc=(kt == 7))
                                k.op("act", lambda: S.copy(fT[:, g, t0:t0 + n], pb[:, 0:n]), reads=[pk, "fT"], writes=["fT"])
                            for j in range(6):
                                if j < 4:
                                    lw = lambda kt: wb[:, kt, 512 + j * 128:512 + (j + 1) * 128]
                                    lws = lambda kt: wsw[:, kt, j * 128:(j + 1) * 128]
                                    dst = QT[:, j, t0:t0 + n]
                                    dk = "QT"
                                else:
                                    lw = lambda kt: wkd[:, kt, j - 4, :]
                                    lws = lambda kt: wkds[:, kt, j - 4, :]
                                    dst = KT[:, j - 4, t0:t0 + n]
                                    dk = "KT"
                                pb, pk = psb[nb % 4], "psb%d" % (nb % 4); nb += 1
                                for kt in range(8):
                                    k.op("pe", lambda: P.matmul(pb[:, 0:n], lw(kt), hT[:, kt, t0:t0 + n], start=(kt == 0), stop=(kt == 7)),
                                         reads=["wb", "wkd", "hT"], writes=[pk], inc=(kt == 7))
                                if not lat:
                                    k.op("act", lambda: S.copy(dst, pb[:, 0:n]), reads=[pk, dk], writes=[dk])
                                else:
                                    pb2, pk2 = psb[4 + nb % 4], "psb%d" % (4 + nb % 4)
                                    for kt in range(8):
                                        k.op("pe", lambda: P.matmul(pb2[:, 0:n], lws(kt), hT[:, kt, t0:t0 + n], start=(kt == 0), stop=(kt == 7)),
                                             reads=["wsw", "wkds", "hT"], writes=[pk2], inc=(kt == 7))
                                    r0 = t0 - NCTX
                                    k.op("dve", lambda: V.tensor_tensor(t1[:, 0:n], pb[:, 0:n], ropc[:, r0:r0 + n], ALU.mult), reads=[pk, "ropc", "rt1"], writes=["rt1"])
                                    k.op("dve", lambda: V.tensor_tensor(t2[:, 0:n], pb2[:, 0:n], rops[:, r0:r0 + n], ALU.mult), reads=[pk2, "rops", "rt2"], writes=["rt2"])
                                    k.op("pool", lambda: G.tensor_tensor(dst, t1[:, 0:n], t2[:, 0:n], ALU.add), reads=["rt1", "rt2", dk], writes=[dk])
                            for s in range(n // 128):
                                tt = (t0 + s * 128) // 128
                                pb, pk = psb[nb % 4], "psb%d" % (nb % 4); nb += 1
                                for kt in range(8):
                                    k.op("pe", lambda: P.matmul(pb[:, 0:128], hT[:, kt, tt * 128:(tt + 1) * 128], wb[:, kt, 1152:1280], start=(kt == 0), stop=(kt == 7)),
                                         reads=["wb", "hT"], writes=[pk], inc=(kt == 7))
                                k.op("act", lambda: S.copy(Vtm[:, tt, :], pb[:, 0:128]), reads=[pk, "Vtm"], writes=["Vtm"])
                        dump("dbg_fT", fT[:], ["fT"]); dump("dbg_QT", QT[:], ["QT"]); dump("dbg_KT", KT[:], ["KT"]); dump("dbg_Vtm", Vtm[:], ["Vtm"])
                        k.barrier()
                    with contextlib.ExitStack() as pf:
                        Gtm = k.sb(pf, "Gtm", [128, NT // 128, 4, 256], BF16)
                        csc = k.sb(pf, "csc", [128, 256], BF16)
                        k.dma("sp", csc[:, 0:128], CL.rearrange("(t e) c -> t e c", e=16)[:, 0, 0:128], reads=["TAB"], writes=["csc"])
                        k.dma("sp", csc[:, 128:256], SLn.rearrange("(t e) c -> t e c", e=16)[:, 0, 0:128], reads=["TAB"], writes=["csc"])
                        k.op("dve", lambda: V.tensor_scalar(csc[:, 128:256], csc[:, 128:256], -1.0, None, ALU.mult), reads=["csc"], writes=["csc"])
                        sc_lat = float(1.0 / np.sqrt(2048.0 * 128.0))
                        sc_ctx = float(1.0 / np.sqrt(256.0 * 128.0))
                        nb = 0
                        for tt in range(NT // 128):
                            for g in range(4):
                                pb, pk = psb[nb % 4], "psb%d" % (nb % 4); nb += 1
                                k.op("pe", lambda: P.matmul(pb[:, 0:256], fT[:, g, tt * 128:(tt + 1) * 128], csc[:], start=True, stop=True),
                                     reads=["fT", "csc"], writes=[pk])
                                k.op("act", lambda: S.activation(Gtm[:, tt, g, :], pb[:, 0:256], AF.Copy, scale=(sc_ctx if tt < 2 else sc_lat)), reads=[pk, "Gtm"], writes=["Gtm"])
                        cl = k.sb(pf, "cl", [128, 16, 512], BF16)
                        sl = k.sb(pf, "sl", [128, 16, 512], BF16)
                        c8 = CL.rearrange("(t e) c -> t e c", e=8)[:, 0, 0:256].rearrange("(tt p) c -> p tt c", p=128)
                        s8 = SLn.rearrange("(t e) c -> t e c", e=8)[:, 0, 0:256].rearrange("(tt p) c -> p tt c", p=128)
                        k.dma("sp", cl[:, 0:2, 0:256], c8, reads=["TAB"], writes=["cl"])
                        k.dma("sp", sl[:, 0:2, 0:256], s8, reads=["TAB"], writes=["sl"])
                        for g in range(4):
                            pb, pk = psb[4 + g % 4], "psb%d" % (4 + g % 4)
                            n_ = 0
                            for tt in range(2):
                                for (half, tabl, tk) in ((0, cl, "cl"), (1, sl, "sl")):
                                    k.op("pe", lambda: P.matmul(pb[:, 0:256], Gtm[:, tt, g, half * 128:(half + 1) * 128], tabl[:, tt, 0:256], start=(n_ == 0), stop=(n_ == 3)),
                                         reads=["Gtm", tk], writes=[pk], inc=(n_ == 3))
                                    n_ += 1
                            k.op("act", lambda: S.copy(mixT[:, g, 0:256], pb[:, 0:256]), reads=[pk, "mixT"], writes=["mixT"])
                        for pbk in range(4):
                            k.dma("sp", cl[:], CL[:, pbk * 512:(pbk + 1) * 512].rearrange("(tt p) c -> p tt c", p=128), reads=["TAB", "cl"], writes=["cl"])
                            k.dma("sp", sl[:], SLn[:, pbk * 512:(pbk + 1) * 512].rearrange("(tt p) c -> p tt c", p=128), reads=["TAB", "sl"], writes=["sl"])
                            for g in range(4):
                                pb, pk = psb[4 + g % 4], "psb%d" % (4 + g % 4)
                                n_ = 0
                                for tt in range(16):
                                    for (half, tabl, tk) in ((0, cl, "cl"), (1, sl, "sl")):
                                        k.op("pe", lambda: P.matmul(pb[:, 0:512], Gtm[:, 2 + tt, g, half * 128:(half + 1) * 128], tabl[:, tt, :], start=(n_ == 0), stop=(n_ == 31)),
                                             reads=["Gtm", tk], writes=[pk], inc=(n_ == 31))
                                        n_ += 1
                                k.op("act", lambda: S.copy(mixT[:, g, NCTX + pbk * 512:NCTX + (pbk + 1) * 512], pb[:, 0:512]), reads=[pk, "mixT"], writes=["mixT"])
                        k.barrier()
                    with contextlib.ExitStack() as pt:
                        PT = [k.sb(pt, "PT%d" % z, [128, 2, 2, 128], BF16) for z in range(2)]
                        rden = k.sb(pt, "rden", [128, 2, 128], F32)
                        scale = 0.125
                        qblocks = [("c", 0), ("c", 1)] + [("l", n) for n in range(16)]
                        items = []
                        for (kind, n) in qblocks:
                            q0 = n * 128 if kind == "c" else NCTX + n * 128
                            for kh in range(2):
                                chunks = []
                                if kind == "l":
                                    for dlt in (-1, 0, 1):
                                        if 0 <= n + dlt < 16:
                                            chunks.append((NCTX + (n + dlt) * 128, dlt))
                                chunks += [(0, 0), (128, 0)]
                                for ci, (k0, dlt) in enumerate(chunks):
                                    items.append((q0, kh, k0, dlt, ci == 0, ci == len(chunks) - 1))

                        def emit_S(i_):
                            q0, kh, k0, dlt, first, last = items[i_]
                            z = i_ % 2
                            for par in range(2):
                                k.op("pe", lambda: P.matmul(psb[par + 2 * z][:, 0:256].rearrange("p (t q) -> p t q", t=2),
                                                            KT[64 * par:64 * par + 64, kh, k0:k0 + 128],
                                                            QT[64 * par:64 * par + 64, 2 * kh:2 * kh + 2, q0:q0 + 128],
                                                            start=True, stop=True, tile_position=(64 * par, 0)),
                                     reads=["KT", "QT"], writes=["psb%d" % (par + 2 * z)])

                        emit_S(0)
                        for i_ in range(len(items)):
                            q0, kh, k0, dlt, first, last = items[i_]
                            z = i_ % 2
                            if i_ + 1 < len(items):
                                emit_S(i_ + 1)
                            for par in range(2):
                                k.op("act", lambda: S.activation(PT[z][:, par, :, :].rearrange("p t q -> p (t q)"), psb[par + 2 * z][:, 0:256], AF.Exp, scale=scale),
                                     reads=["psb%d" % (par + 2 * z), "PT%d" % z], writes=["PT%d" % z])
                            if dlt != 0:
                                msk = mask_ge if dlt == -1 else mask_le
                                mb = bass.AP(msk[:].tensor, msk[:].offset, [list(msk[:].ap[0]), [0, 4], list(msk[:].ap[1])])
                                k.op("dve", lambda: V.tensor_tensor(PT[z][:].rearrange("p a t q -> p (a t) q"), PT[z][:].rearrange("p a t q -> p (a t) q"), mb, ALU.mult),
                                     reads=["PT%d" % z, "mask_ge", "mask_le"], writes=["PT%d" % z])
                            tt = k0 // 128
                            for par in range(2):
                                k.op("pe", lambda: P.matmul(psb[4 + par][64 * par:64 * par + 64, 0:256], Vtm[:, tt, kh * 64:(kh + 1) * 64],
                                                            PT[z][:, par, :, :].rearrange("p t q -> p (t q)"), start=first, stop=last,
                                                            tile_position=(0, 64 * par)),
                                     reads=["Vtm", "PT%d" % z], writes=["psb%d" % (4 + par)], inc=last)
                                k.op("pe", lambda: P.matmul(psb[6 + par][64 * par:64 * par + 64, 0:256], ones64[:],
                                                            PT[z][:, par, :, :].rearrange("p t q -> p (t q)"), start=first, stop=last,
                                                            tile_position=(0, 64 * par)),
                                     reads=["ones64", "PT%d" % z], writes=["psb%d" % (6 + par)], inc=last)
                            if last:
                                for par in range(2):
                                    lo, hi = 64 * par, 64 * par + 64
                                    seb = bass.AP(SE[:].tensor, SE[lo:hi, kh, :].offset, [list(SE[lo:hi, kh, :].ap[0]), list(SE[lo:hi, kh, :].ap[1]), [0, 128]])
                                    k.op("dve", lambda: V.tensor_tensor(rden[lo:hi, :, :], psb[6 + par][lo:hi, 0:256].rearrange("p (t q) -> p t q", t=2), seb, ALU.add),
                                         reads=["psb%d" % (6 + par), "SE", "rden%d" % par], writes=["rden%d" % par])
                                    k.op("dve", lambda: V.reciprocal(rden[lo:hi, :, :], rden[lo:hi, :, :]), reads=["rden%d" % par], writes=["rden%d" % par])
                                    k.op("dve", lambda: V.tensor_tensor(mixT[lo:hi, 4 + 2 * kh:6 + 2 * kh, q0:q0 + 128], psb[4 + par][lo:hi, 0:256].rearrange("p (t q) -> p t q", t=2),
                                                                        rden[lo:hi, :, :], ALU.mult),
                                         reads=["psb%d" % (4 + par), "rden%d" % par, "mixT"], writes=["mixT"])
                        dump("dbg_mixT", mixT[:], ["mixT"])
                        k.barrier()
                    with contextlib.ExitStack() as po:
                        wo = k.sb(po, "wo", [128, 8, D], BF16)
                        for kt in range(8):
                            k.dma("pool", wo[:, kt, :], w_out_d[i][kt * 128:(kt + 1) * 128, :], writes=["wo"])
                        yo = [k.sb(po, "eyo%d" % z, [128, 8, 256], F32) for z in range(2)]
                        for blk in range(NBLK):
                            t0 = blk * TB
                            z = blk % 2
                            for ct in range(8):
                                pb, pk = psb[ct % 4], "psb%d" % (ct % 4)
                                for mt in range(8):
                                    k.op("pe", lambda: P.matmul(pb[:, 0:TB], wo[:, mt, ct * 128:(ct + 1) * 128], mixT[:, mt, t0:t0 + TB], start=(mt == 0), stop=(mt == 7)),
                                         reads=["wo", "mixT"], writes=[pk], inc=(mt == 7))
                                k.op("act", lambda: S.copy(yo[z][:, ct, :], pb[:, 0:TB]), reads=[pk, "eyo%d" % z], writes=["eyo%d" % z])
                            k.dma("sp", YTv[:, :, t0:t0 + TB], yo[z][:], reads=["eyo%d" % z], writes=["YT"])
                        k.barrier()
            if mixer and l % 2 == 1:
                s5_mixer(l)
            elif mixer:
                even_mixer(l)

            with contextlib.ExitStack() as ph:
                w1b = k.sb(ph, "w1b", [128, 8, DFF], BF16)
                w2b = k.sb(ph, "w2b", [128, 32, D], BF16)
                for kt in range(8):
                    k.dma("pool", w1b[:, kt, :], w1_d[l][kt * 128:(kt + 1) * 128, :], writes=["w1b"])
                for j4 in range(8):
                    k.dma("pool", w2b[:, j4 * 4:(j4 + 1) * 4, :],
                          w2_d[l][j4 * 512:(j4 + 1) * 512, :].rearrange("(j p) n -> p j n", p=128), writes=["w2b"])
                NXB = 3
                xbs = [k.sb(ph, "xb%d" % z, [128, 8, TB], F32) for z in range(NXB)]
                ybs = [k.sb(ph, "yb0", [128, 8, TB], F32)] * 2 if mixer else []
                sq = k.sb(ph, "sq", [128, 8, TB], F32)
                tmp = sq
                rs = k.sb(ph, "rs", [128, TB], F32)
                h2s = [k.sb(ph, "h2%d" % z, [128, 8, TB], BF16) for z in range(2)]
                ob = k.sb(ph, "ob", [128, 8, TB], F32)
                ar = [k.sb(ph, "ar%d" % z, [128, TB], F32) for z in range(2)]
                a2all = k.sb(ph, "a2all", [128, 32, TB], BF16)
                tiles = (sq, rs, psb[3], "psb3")

                def P_load(blk):
                    z = blk % NXB
                    t0 = blk * TB
                    k.dma("sp", xbs[z][:], XTv[:, :, t0:t0 + TB], reads=["XT"], writes=["xb%d" % z])

                def rms_thunks(src3, key_src):
                    pbank, pkey = psb[3], "psb3"
                    th = []
                    if src3 is not None:
                        th.append(lambda: k.op("act", lambda: S.activation(sq[:], src3, AF.Square), reads=[key_src], writes=["sq"]))
                    def mm():
                        for ct in range(8):
                            k.op("pe", lambda: P.matmul(pbank[:, 0:TB], onesm[:], sq[:, ct, :], start=(ct == 0), stop=(ct == 7)),
                                 reads=["sq", "onesm"], writes=[pkey], inc=(ct == 7))
                    th.append(mm)
                    th.append(lambda: k.op("dve", lambda: V.tensor_scalar(rs[:], pbank[:, 0:TB], EPS, None, ALU.add), reads=[pkey], writes=["rs"]))
                    th.append(lambda: k.op("act", lambda: S.activation(rs[:], rs[:], AF.Sqrt), reads=["rs"], writes=["rs"]))
                    th.append(lambda: k.op("dve", lambda: V.reciprocal(rs[:], rs[:]), reads=["rs"], writes=["rs"]))
                    return th

                def P_thunks(blk):
                    z = blk % NXB
                    xb, h2, xk, hk = xbs[z], h2s[blk % 2], "xb%d" % z, "h2%d" % (blk % 2)
                    j = 1 if blk == 0 else 0
                    th = []
                    if mixer:
                        yb, yk = ybs[0], "yb0"
                        t0 = blk * TB
                        th.append(lambda: k.dma("sp", yb[:], YTv[:, :, t0:t0 + TB], reads=["YT"], writes=[yk]))
                        th += rms_thunks(yb[:], yk)
                        th.append(lambda: k.op("dve", lambda: V.tensor_tensor(tmp[:], yb[:], bc_mid(rs[:], 8), ALU.mult), reads=[yk, "rs"], writes=["sq"]))
                        for ct in range(8):
                            th.append(lambda ct=ct: k.op("dve", lambda: V.scalar_tensor_tensor(out=xb[:, ct, :], in0=tmp[:, ct, :], scalar=PRM[:, 2, ct, j:j + 1],
                                                                                              in1=xb[:, ct, :], op0=ALU.mult, op1=ALU.add),
                                                         reads=["sq", xk, "PRM"], writes=[xk]))
                    th += rms_thunks(xb[:], xk)
                    th.append(lambda: k.op("dve", lambda: V.tensor_tensor(tmp[:], xb[:], bc_mid(rs[:], 8), ALU.mult), reads=[xk, "rs"], writes=["sq"]))
                    for ct in range(8):
                        th.append(lambda ct=ct: k.op("act", lambda: S.activation(h2[:, ct, :], tmp[:, ct, :], AF.Identity, bias=PRM[:, 4, ct, j:j + 1],
                                                                                 scale=PRM[:, 3, ct, j:j + 1]), reads=["sq", "PRM"], writes=[hk]))
                    return th

                def E_thunks(blk):
                    z = blk % NXB
                    xb, xk = xbs[z], "xb%d" % z
                    j = 1 if blk == 0 else 0
                    t0 = blk * TB
                    obk = ["ob_%d" % i_ for i_ in range(8)]
                    th = [lambda: k.op("act", lambda: S.activation(sq[:], ob[:], AF.Square), reads=obk, writes=["sq"])]
                    th += rms_thunks(None, "sq")
                    th.append(lambda: k.op("dve", lambda: V.tensor_tensor(tmp[:], ob[:], bc_mid(rs[:], 8), ALU.mult), reads=obk + ["rs"], writes=["sq"]))
                    for ct in range(8):
                        th.append(lambda ct=ct: k.op("dve", lambda: V.scalar_tensor_tensor(out=xb[:, ct, :], in0=tmp[:, ct, :], scalar=PRM[:, 5, ct, j:j + 1],
                                                                                          in1=xb[:, ct, :], op0=ALU.mult, op1=ALU.add),
                                                     reads=["sq", xk, "PRM"], writes=[xk]))
                    th.append(lambda: k.dma("pool", XTv[:, :, t0:t0 + TB], xb[:], reads=[xk], writes=["XT_st"]))
                    return th

                def M1_stage(blk, pending):
                    h2, hk = h2s[blk % 2], "h2%d" % (blk % 2)
                    for jf in range(32):
                        pa = psb[4 + jf % 4]
                        pak = "psb%d" % (4 + jf % 4)
                        for kt in range(8):
                            k.op("pe", lambda: P.matmul(pa[:, 0:TB], w1b[:, kt, jf * 128:(jf + 1) * 128], h2[:, kt, :],
                                                        start=(kt == 0), stop=(kt == 7)),
                                 reads=["w1b", hk], writes=[pak], inc=(kt == 7))
                        k.op("act", lambda: S.activation(ar[jf % 2][:], pa[:, 0:TB], AF.Relu), reads=[pak], writes=["ar%d" % (jf % 2)])
                        k.op("dve", lambda: V.tensor_tensor(a2all[:, jf, :], ar[jf % 2][:], ar[jf % 2][:], ALU.mult),
                             reads=["ar%d" % (jf % 2)], writes=["a2all"])
                        if pending and jf % 2 == 1:
                            pending.pop(0)()

                def M2_stage(blk, pending):
                    for ft in range(8):
                        po = psb[ft % 3]
                        pok = "psb%d" % (ft % 3)
                        for jf in range(32):
                            k.op("pe", lambda: P.matmul(po[:, 0:TB], w2b[:, jf, ft * 128:(ft + 1) * 128], a2all[:, jf, :],
                                                        start=(jf == 0), stop=(jf == 31)),
                                 reads=["w2b", "a2all"], writes=[pok], inc=(jf == 31))
                            if jf % 8 == 7 and pending:
                                pending.pop(0)()
                        k.op("act", lambda: S.copy(ob[:, ft, :], po[:, 0:TB]), reads=[pok], writes=["ob_%d" % ft])

                P_load(0)
                P_load(1)
                for t_ in P_thunks(0):
                    t_()
                for blk in range(NBLK):
                    pend1, pend2 = [], []
                    if blk >= 1:
                        pend1 += E_thunks(blk - 1)
                    if blk + 2 < NBLK:
                        pend1.append(lambda b_=blk + 2: P_load(b_))
                    if blk + 1 < NBLK:
                        pend2 += P_thunks(blk + 1)
                    M1_stage(blk, pend1)
                    pend2 = pend1 + pend2
                    M2_stage(blk, pend2)
                    while pend2:
                        pend2.pop(0)()
                for t_ in E_thunks(NBLK - 1):
                    t_()
                k.barrier()

        with contextlib.ExitStack() as ph:
            xf = [k.sb(ph, "xf%d" % i, [128, 8, 128], F32) for i in range(2)]
            xo = [k.sb(ph, "xo%d" % i, [128, D], F32) for i in range(2)]
            for tt in range(16):
                b = tt % 2
                k.dma("sp", xf[b][:], XTv[:, :, NCTX + tt * 128:NCTX + (tt + 1) * 128], reads=["XT"], writes=["xf%d" % b])
                for half in range(2):
                    pkey = "psb%d" % ((tt * 2 + half) % 8)
                    pb = psb[(tt * 2 + half) % 8]
                    for c4 in range(4):
                        ct = half * 4 + c4
                        k.op("pe", lambda: P.transpose(pb[:, c4 * 128:(c4 + 1) * 128], xf[b][:, ct, :], ident[:]),
                             reads=["xf%d" % b, "ident"], writes=[pkey], inc=(c4 == 3))
                    if half == 0:
                        k.op("act", lambda: S.copy(xo[b][:, 0:512], pb[:]), reads=[pkey], writes=["xo%d_0" % b])
                    else:
                        k.op("dve", lambda: V.tensor_copy(xo[b][:, 512:1024], pb[:]), reads=[pkey], writes=["xo%d_1" % b])
                k.dma("sp", out_d[tt * 128:(tt + 1) * 128, :], xo[b][:], reads=["xo%d_0" % b, "xo%d_1" % b], writes=["out"])
            k.barrier()
        print("ninst", k.ninst, "nwaits", k.nwaits)
    return nc


def make_in_maps(inp):
    gains = np.stack([inp["mix_pre_g"], inp["mix_post_g"], inp["ffn_pre_g"], inp["ffn_post_g"]], 0)
    maps = []
    for b in range(8):
        m = {
            "x": np.ascontiguousarray(inp["x"][b]), "ctx": np.ascontiguousarray(inp["ctx"][b]),
            "cc": np.ascontiguousarray(np.stack([inp["c"][b], inp["c_ctx"]], 0)),
            "mod_w": inp["mod_w"], "mod_b": inp["mod_b"], "gains": np.ascontiguousarray(gains),
            "ffn_w1": inp["ffn_w1"], "ffn_w2": inp["ffn_w2"],
            "ssm_a_re": inp["ssm_a_re"], "ssm_a_im": inp["ssm_a_im"], "ssm_log_dt": inp["ssm_log_dt"],
            "ssm_b_re": inp["ssm_b_re"], "ssm_b_im": inp["ssm_b_im"], "ssm_c_re": inp["ssm_c_re"], "ssm_c_im": inp["ssm_c_im"],
            "ssm_d": inp["ssm_d"], "ssm_glu_w": inp["ssm_glu_w"],
            "even_w_in": inp["even_w_in"], "even_w_out": inp["even_w_out"], "even_sink": inp["even_sink"],
        }
        maps.append(m)
    return maps


def kernel(**inp):
    inp = {k_: np.asarray(v) for k_, v in inp.items()}
    nc = build(mixer=MIXER_ENABLED)
    res = run_bass_kernel_spmd(nc, make_in_maps(inp), core_ids=list(range(8)))
    return np.stack([r["out"] for r in res.results], 0)
```

```python
import contextlib
import numpy as np
import concourse.bass as bass
import concourse.mybir as mybir
from concourse.bass_utils import run_bass_kernel_spmd

F32 = mybir.dt.float32
BF16 = mybir.dt.bfloat16
I32 = mybir.dt.int32
ALU = mybir.AluOpType
AF = mybir.ActivationFunctionType
AX = mybir.AxisListType

S5_STAGE = 2
S5_POOL = True
S5_SUB = 2
S5_NW = None
MIXER_ENABLED = True
SAME_ENGINE_SYNC = {"dve": True, "act": True, "pool": False, "pe": False}
DMA_RING = 8


class KB:
    def __init__(self, nc, es):
        self.nc = nc
        self.es = es
        self.raw = {"pe": nc.tensor, "dve": nc.vector, "act": nc.scalar, "pool": nc.gpsimd, "sp": nc.sync}
        self.sem = {}
        self.cnt = {}
        for e in ("pe", "dve", "act", "pool"):
            self.sem[e] = es.enter_context(nc.semaphore("s_" + e))
            self.cnt[e] = 0
        self.dring = {}
        self.dcnt = {}
        for q in ("sp", "act", "pool"):
            self.dring[q] = [es.enter_context(nc.semaphore("d_%s%d" % (q, i))) for i in range(DMA_RING)]
            self.dcnt[q] = 0
        self.seen = {e: {} for e in self.raw}
        self.lastw = {}
        self.readers = {}
        self.nwaits = 0
        self.ninst = 0

    def sb(self, es, name, shape, dt):
        self.uid = getattr(self, "uid", 0) + 1
        return es.enter_context(self.nc.sbuf_tensor("%s_u%d" % (name, self.uid), list(shape), dt))

    def ps(self, es, name, shape, dt=F32):
        return es.enter_context(self.nc.psum_tensor(name, list(shape), dt))

    def _collect(self, eng, reads, writes):
        need = {}

        def add(tok):
            if tok is None:
                return
            s, v, src = tok
            if src == eng and (not SAME_ENGINE_SYNC.get(eng, True) or v > self.cnt[eng]):
                return
            if need.get(s, (0,))[0] < v:
                need[s] = (v, src)

        for r in reads:
            add(self.lastw.get(r))
        for w in writes:
            add(self.lastw.get(w))
            for tok in self.readers.get(w, ()):
                add(tok)
        return need

    def _emit_waits(self, eng, need):
        seen = self.seen[eng]
        for s, (v, src) in need.items():
            if seen.get(s, 0) >= v:
                continue
            self.raw[eng].wait_ge(s, v)
            seen[s] = v
            self.nwaits += 1

    def _record(self, tok, reads, writes):
        for w in writes:
            self.lastw[w] = tok
            self.readers[w] = []
        for r in reads:
            if r in writes:
                continue
            self.readers.setdefault(r, []).append(tok)

    def op(self, eng, fn, reads=(), writes=(), inc=True):
        need = self._collect(eng, reads, writes)
        self._emit_waits(eng, need)
        inst = fn()
        self.ninst += 1
        if inc:
            self.cnt[eng] += 1
            inst.then_inc(self.sem[eng], 1)
            tok = (self.sem[eng], self.cnt[eng], eng)
        else:
            tok = (self.sem[eng], self.cnt[eng] + 1, eng)
        self._record(tok, reads, writes)
        return inst

    def dma(self, q, out, in_, reads=(), writes=(), **kw):
        need = self._collect(q, reads, writes)
        self._emit_waits(q, need)
        i = self.dcnt[q]
        self.dcnt[q] += 1
        s = self.dring[q][i % DMA_RING]
        v = 16 * (i // DMA_RING + 1)
        inst = self.raw[q].dma_start(out=out, in_=in_, **kw)
        inst.then_inc(s, 16)
        self.ninst += 1
        self._record((s, v, "dma_" + q), reads, writes)
        return inst

    def barrier(self):
        need = {}
        for e in ("pe", "dve", "act", "pool"):
            if self.cnt[e] > 0:
                need[self.sem[e]] = (self.cnt[e], "x")
        for q in ("sp", "act", "pool"):
            n = self.dcnt[q]
            for r in range(DMA_RING):
                cntr = (n - r + DMA_RING - 1) // DMA_RING if n > r else 0
                if cntr > 0:
                    need[self.dring[q][r]] = (16 * cntr, "x")
        for e in ("pe", "dve", "act", "pool", "sp"):
            self._emit_waits(e, need)
        self.lastw = {}
        self.readers = {}


def bc_mid(ap2, n):
    a = ap2.ap
    return bass.AP(ap2.tensor, ap2.offset, [list(a[0]), [0, n]] + [list(x) for x in a[1:]])


D = 1024
NT = 2304
NCTX = 256
TB = 256
NBLK = NT // TB
DFF = 4096
EPS = 1e-6


def build(nlayers=4, mixer=True, layers=None, debug=False):
    nc = bass.Bass("TRN2", target_bir_lowering=False)
    dt_in = lambda n, s: nc.dram_tensor(n, list(s), F32, kind="ExternalInput").ap()
    x_d = dt_in("x", [2048, D])
    ctx_d = dt_in("ctx", [NCTX, D])
    cc_d = dt_in("cc", [2, D])
    mod_w_d = dt_in("mod_w", [4, D, 6 * D])
    mod_b_d = dt_in("mod_b", [4, 6 * D])
    gains_d = dt_in("gains", [4, 4, D])
    w1_d = dt_in("ffn_w1", [4, D, DFF])
    w2_d = dt_in("ffn_w2", [4, DFF, D])
    w_in_d = dt_in("even_w_in", [2, D, 1280])
    w_out_d = dt_in("even_w_out", [2, D, D])
    even_sink_d = dt_in("even_sink", [2, 8])
    a_re_d = dt_in("ssm_a_re", [2, 2, 64, 64])
    a_im_d = dt_in("ssm_a_im", [2, 2, 64, 64])
    ldt_d = dt_in("ssm_log_dt", [2, 2, 64])
    b_re_d = dt_in("ssm_b_re", [2, 2, 64, 64, 16])
    b_im_d = dt_in("ssm_b_im", [2, 2, 64, 64, 16])
    c_re_d = dt_in("ssm_c_re", [2, 2, 64, 16, 64])
    c_im_d = dt_in("ssm_c_im", [2, 2, 64, 16, 64])
    dsk_d = dt_in("ssm_d", [2, D])
    glu_d = dt_in("ssm_glu_w", [2, D, 2 * D])
    out_d = nc.dram_tensor("out", [2048, D], F32, kind="ExternalOutput").ap()
    YF = nc.dram_tensor("YF", [2, D, NT], F32, kind=("ExternalOutput" if debug else "Internal")).ap()
    XT = nc.dram_tensor("XT", [D, NT], F32).ap()
    YT = nc.dram_tensor("YT", [D, NT], F32, kind=("ExternalOutput" if debug else "Internal")).ap()
    CL = nc.dram_tensor("CLtab", [2048, 2048], BF16, kind=("ExternalOutput" if debug else "Internal")).ap()
    SLn = nc.dram_tensor("SLtab", [2048, 2048], BF16, kind=("ExternalOutput" if debug else "Internal")).ap()
    ROPC = nc.dram_tensor("ROPC", [128, 2048], F32, kind=("ExternalOutput" if debug else "Internal")).ap()
    ROPS = nc.dram_tensor("ROPS", [128, 2048], F32, kind=("ExternalOutput" if debug else "Internal")).ap()
    XTv = XT.rearrange("(ct p) t -> p ct t", p=128)
    YTv = YT.rearrange("(ct p) t -> p ct t", p=128)

    with contextlib.ExitStack() as es:
        k = KB(nc, es)
        V, S, P, G = nc.vector, nc.scalar, nc.tensor, nc.gpsimd

        def dump(name, ap, keys):
            if not debug:
                return
            dt_ = nc.dram_tensor(name, list(ap.shape), ap.dtype, kind="ExternalOutput").ap()
            k.dma("sp", dt_, ap, reads=keys, writes=["dbg_" + name])
        ident = k.sb(es, "ident", [128, 128], F32)
        onesm = k.sb(es, "onesm", [128, 128], F32)
        k.op("pool", lambda: G.memset(ident[:], 0.0), writes=["ident"])
        k.op("pool", lambda: G.affine_select(out=ident[:], in_=ident[:], compare_op=ALU.not_equal, fill=1.0,
                                             base=0, pattern=[[-1, 128]], channel_multiplier=1),
             reads=["ident"], writes=["ident"])
        k.op("pool", lambda: G.memset(onesm[:], 1.0 / D), writes=["onesm"])
        psb = [k.ps(es, "psb%d" % i, [128, 512], F32) for i in range(8)]
        MV = k.sb(es, "MV", [128, 6, 8, 2], F32)
        GN = k.sb(es, "GN", [128, 4, 4, 8], F32)
        SC = k.sb(es, "SC", [128, 8, 2], F32)
        SCb = k.sb(es, "SCb", [128, 8, 2], BF16)
        PRM = k.sb(es, "PRM", [128, 6, 8, 2], F32)
        k.dma("sp", GN[:].rearrange("p a l c -> p (a l) c"),
              gains_d.rearrange("a l (c p) -> p (a l) c", p=128), writes=["GN"], allow_slow_non_contiguous=True)
        for j in range(2):
            k.dma("sp", SC[:, :, j], cc_d[j].rearrange("(c p) -> p c", p=128), writes=["SC"], allow_slow_non_contiguous=True)
        k.op("act", lambda: S.activation(SCb[:], SC[:], AF.Silu), reads=["SC"], writes=["SCb"])


        MAGIC = 12582912.0
        TWO_PI = float(2 * np.pi)
        if mixer and any(l % 2 == 0 for l in (layers if layers is not None else range(nlayers))):
            with contextlib.ExitStack() as ph:
                cidx = k.sb(ph, "cidx", [128, 2048], F32)
                prow = k.sb(ph, "prow", [128, 1], F32)
                tcol = k.sb(ph, "tcol", [128, 1], F32)
                k.op("pool", lambda: G.iota(cidx[:], pattern=[[1, 2048]], base=0, channel_multiplier=0, allow_small_or_imprecise_dtypes=True), writes=["cidx"])
                k.op("pool", lambda: G.iota(prow[:], pattern=[[0, 1]], base=0, channel_multiplier=1, allow_small_or_imprecise_dtypes=True), writes=["prow"])
                uu = [k.sb(ph, "uu%d" % z, [128, 2048], F32) for z in range(2)]
                nn = [k.sb(ph, "nn%d" % z, [128, 2048], F32) for z in range(2)]
                n2 = [k.sb(ph, "nq%d" % z, [128, 2048], F32) for z in range(2)]
                ff = [k.sb(ph, "ff%d" % z, [128, 2048], F32) for z in range(2)]
                tb = [k.sb(ph, "tb%d" % z, [128, 2048], BF16) for z in range(2)]
                tcs = [k.sb(ph, "tcs%d" % z, [128, 1], F32) for z in range(2)]
                SCL = TWO_PI * (1.0 - 2e-6)
                for tt in range(16):
                    tc_ = tcs[tt % 2]
                    k.op("dve", lambda: V.tensor_scalar(tc_[:], prow[:], float(tt * 128), 1.0 / 2048, ALU.add, ALU.mult), reads=["prow", "tcs%d" % (tt % 2)], writes=["tcs%d" % (tt % 2)])
                    for z, (tab, shift, scl) in enumerate(((CL, 0.25, SCL), (SLn, 0.0, -SCL))):
                        k.op("act", lambda: S.activation(uu[z][:], cidx[:], AF.Identity, bias=shift, scale=tc_[:, 0:1]), reads=["cidx", "tcs%d" % (tt % 2), "uu%d" % z], writes=["uu%d" % z])
                        k.op("act", lambda: S.activation(nn[z][:], uu[z][:], AF.Identity, bias=MAGIC, scale=1.0), reads=["uu%d" % z, "nn%d" % z], writes=["nn%d" % z])
                        k.op("act", lambda: S.activation(n2[z][:], nn[z][:], AF.Identity, bias=-MAGIC, scale=1.0), reads=["nn%d" % z, "nq%d" % z], writes=["nq%d" % z])
                        k.op("dve", lambda: V.tensor_tensor(ff[z][:], uu[z][:], n2[z][:], ALU.subtract), reads=["uu%d" % z, "nq%d" % z, "ff%d" % z], writes=["ff%d" % z])
                        k.op("act", lambda: S.activation(tb[z][:], ff[z][:], AF.Sin, scale=scl), reads=["ff%d" % z, "tb%d" % z], writes=["tb%d" % z])
                        k.dma("sp", tab[tt * 128:(tt + 1) * 128, :], tb[z][:], reads=["tb%d" % z], writes=["TAB"])
                k.barrier()
            with contextlib.ExitStack() as ph:
                pidx = k.sb(ph, "rpidx", [128, 1], I32)
                pi2 = k.sb(ph, "rpi2", [128, 1], I32)
                fi = k.sb(ph, "rfi", [128, 1], F32)
                invp = k.sb(ph, "rinvp", [128, 1], F32)
                axs = k.sb(ph, "raxs", [128, 1], F32)
                sgn = k.sb(ph, "rsgn", [128, 1], F32)
                k.op("pool", lambda: G.iota(pidx[:], pattern=[[0, 1]], base=0, channel_multiplier=1), writes=["pidx"])
                k.op("dve", lambda: V.tensor_scalar(pi2[:], pidx[:], 15, None, ALU.bitwise_and), reads=["pidx"], writes=["pi2"])
                k.op("dve", lambda: V.tensor_copy(fi[:], pi2[:]), reads=["pi2"], writes=["fi"])
                k.op("act", lambda: S.activation(invp[:], fi[:], AF.Exp, scale=-float(np.log(10000.0)) / 16.0), reads=["fi"], writes=["invp"])
                k.op("dve", lambda: V.tensor_scalar(pi2[:], pidx[:], 5, 1, ALU.arith_shift_right, ALU.bitwise_and), reads=["pidx", "fi"], writes=["pi2"])
                k.op("dve", lambda: V.tensor_copy(axs[:], pi2[:]), reads=["pi2"], writes=["axs"])
                k.op("dve", lambda: V.tensor_scalar(pi2[:], pidx[:], 4, 1, ALU.arith_shift_right, ALU.bitwise_and), reads=["pidx", "axs"], writes=["pi2"])
                k.op("dve", lambda: V.tensor_copy(sgn[:], pi2[:]), reads=["pi2"], writes=["sgn"])
                k.op("dve", lambda: V.tensor_scalar(sgn[:], sgn[:], 2.0, -1.0, ALU.mult, ALU.add), reads=["sgn"], writes=["sgn"])
                rowp = k.sb(ph, "rowp", [128, 2048], F32)
                colp = k.sb(ph, "colp", [128, 2048], F32)
                ang = k.sb(ph, "rang", [128, 2048], F32)
                u_ = k.sb(ph, "ru", [128, 2048], F32)
                n_ = k.sb(ph, "rn", [128, 2048], F32)
                k.op("pool", lambda: G.iota(rowp[:], pattern=[[1, 32], [0, 64]], base=0, channel_multiplier=0, allow_small_or_imprecise_dtypes=True), writes=["rowp"])
                k.op("pool", lambda: G.iota(colp[:], pattern=[[0, 32], [1, 64]], base=0, channel_multiplier=0, allow_small_or_imprecise_dtypes=True), writes=["colp"])
                k.op("dve", lambda: V.tensor_tensor(colp[:], colp[:], rowp[:], ALU.subtract), reads=["colp", "rowp"], writes=["colp"])
                k.op("dve", lambda: V.scalar_tensor_tensor(out=ang[:], in0=colp[:], scalar=axs[:, 0:1], in1=rowp[:], op0=ALU.mult, op1=ALU.add), reads=["colp", "rowp", "axs"], writes=["ang"])
                k.op("dve", lambda: V.tensor_scalar(ang[:], ang[:], invp[:, 0:1], 1.0 / TWO_PI, ALU.mult, ALU.mult), reads=["ang", "invp"], writes=["ang"])
                for (tab, shift) in ((ROPC, 0.25), (ROPS, 0.0)):
                    k.op("dve", lambda: V.tensor_scalar(u_[:], ang[:], shift, None, ALU.add), reads=["ang", "ru"], writes=["ru"])
                    k.op("dve", lambda: V.tensor_scalar(n_[:], u_[:], MAGIC, None, ALU.add), reads=["ru", "rn"], writes=["rn"])
                    k.op("dve", lambda: V.tensor_scalar(n_[:], n_[:], -MAGIC, None, ALU.add), reads=["rn"], writes=["rn"])
                    k.op("dve", lambda: V.tensor_tensor(u_[:], u_[:], n_[:], ALU.subtract), reads=["rn", "ru"], writes=["ru"])
                    k.op("dve", lambda: V.tensor_scalar(u_[:], u_[:], -0.49999, 0.49999, ALU.max, ALU.min), reads=["ru"], writes=["ru"])
                    k.op("act", lambda: S.activation(n_[:], u_[:], AF.Sin, scale=TWO_PI), reads=["ru", "rn"], writes=["rn"])
                    if shift == 0.0:
                        k.op("dve", lambda: V.tensor_scalar(n_[:], n_[:], sgn[:, 0:1], None, ALU.mult), reads=["rn", "sgn"], writes=["rn"])
                    k.dma("sp", tab[:, :], n_[:], reads=["rn"], writes=["ROP"])
                k.barrier()
        mask_ge = k.sb(es, "mask_ge", [128, 128], BF16)
        mask_le = k.sb(es, "mask_le", [128, 128], BF16)
        ones64 = k.sb(es, "ones64", [128, 64], BF16)
        k.op("pool", lambda: G.memset(mask_ge[:], 1.0), writes=["mask_ge"])
        k.op("pool", lambda: G.affine_select(out=mask_ge[:], in_=mask_ge[:], compare_op=ALU.is_ge, fill=0.0, base=0, pattern=[[-1, 128]], channel_multiplier=1), reads=["mask_ge"], writes=["mask_ge"])
        k.op("pool", lambda: G.memset(mask_le[:], 1.0), writes=["mask_le"])
        k.op("pool", lambda: G.affine_select(out=mask_le[:], in_=mask_le[:], compare_op=ALU.is_ge, fill=0.0, base=0, pattern=[[1, 128]], channel_multiplier=-1), reads=["mask_le"], writes=["mask_le"])
        k.op("pool", lambda: G.memset(ones64[:], 1.0), writes=["ones64"])
        with contextlib.ExitStack() as ph:
            xin = [k.sb(ph, "xin%d" % i, [128, D], F32) for i in range(2)]
            xst = [k.sb(ph, "xst%d" % i, [128, 8, 128], F32) for i in range(2)]
            for tt in range(NT // 128):
                b = tt % 2
                src = ctx_d[tt * 128:(tt + 1) * 128, :] if tt < 2 else x_d[(tt - 2) * 128:(tt - 1) * 128, :]
                k.dma("sp", xin[b][:], src, writes=["xin%d" % b])
                for half in range(2):
                    pb = psb[(tt * 2 + half) % 8]
                    for c4 in range(4):
                        ct = half * 4 + c4
                        k.op("pe", lambda: P.transpose(pb[:, c4 * 128:(c4 + 1) * 128], xin[b][:, ct * 128:(ct + 1) * 128], ident[:]),
                             reads=["xin%d" % b, "ident"], writes=["psb%d" % ((tt * 2 + half) % 8)], inc=(c4 == 3))
                    eng = "act" if half == 0 else "dve"
                    if eng == "act":
                        k.op("act", lambda: S.copy(xst[b][:, half * 4:(half + 1) * 4, :].rearrange("p c t -> p (c t)"), pb[:]),
                             reads=["psb%d" % ((tt * 2 + half) % 8)], writes=["xst%d_%d" % (b, half)])
                    else:
                        k.op("dve", lambda: V.tensor_copy(xst[b][:, half * 4:(half + 1) * 4, :].rearrange("p c t -> p (c t)"), pb[:]),
                             reads=["psb%d" % ((tt * 2 + half) % 8)], writes=["xst%d_%d" % (b, half)])
                k.dma("sp", XTv[:, :, tt * 128:(tt + 1) * 128], xst[b][:], reads=["xst%d_0" % b, "xst%d_1" % b], writes=["XT"])
            k.barrier()

        for l in (layers if layers is not None else range(nlayers)):
            with contextlib.ExitStack() as ph:
                mw = [k.sb(ph, "mw%d" % i, [128, 8, 512], BF16) for i in range(2)]
                mbias = k.sb(ph, "mbias", [128, 48], F32)
                k.dma("sp", mbias[:], mod_b_d[l].rearrange("(c p) -> p c", p=128), writes=["mbias"], allow_slow_non_contiguous=True)
                for ch in range(12):
                    b = ch % 2
                    k.dma("pool", mw[b][:], mod_w_d[l][:, ch * 512:(ch + 1) * 512].rearrange("(kt p) n -> p kt n", p=128),
                          writes=["mw%d" % b])
                    for s4 in range(4):
                        col = ch * 4 + s4
                        pb = psb[col % 8]
                        for kt in range(8):
                            k.op("pe", lambda: P.matmul(pb[:, 0:2], mw[b][:, kt, s4 * 128:(s4 + 1) * 128], SCb[:, kt, :],
                                                        start=(kt == 0), stop=(kt == 7)),
                                 reads=["mw%d" % b, "SCb"], writes=["psb%d" % (col % 8)], inc=(kt == 7))
                        k.op("dve", lambda: V.tensor_scalar(MV[:, col // 8, col % 8, :], pb[:, 0:2], mbias[:, col:col + 1], None, ALU.add),
                             reads=["psb%d" % (col % 8), "mbias"], writes=["MV"])
                for (o, isc, ish, ig, gpre, gpost) in ((0, 1, 0, 2, 0, 1), (3, 4, 3, 5, 2, 3)):
                    for j in range(2):
                        k.op("dve", lambda: V.scalar_tensor_tensor(out=PRM[:, o, :, j], in0=MV[:, isc, :, j], scalar=1.0, in1=GN[:, gpre, l, :],
                                                                   op0=ALU.add, op1=ALU.mult), reads=["MV", "GN"], writes=["PRM"])
                        k.op("dve", lambda: V.tensor_copy(PRM[:, o + 1, :, j], MV[:, ish, :, j]), reads=["MV"], writes=["PRM"])
                        k.op("dve", lambda: V.tensor_tensor(PRM[:, o + 2, :, j], MV[:, ig, :, j], GN[:, gpost, l, :], ALU.mult),
                             reads=["MV", "GN"], writes=["PRM"])
                k.barrier()

            def rms_bc(ph_tiles, src3, key_src, tag):
                sq, rs, pbank, pkey = ph_tiles
                if src3 is not None:
                    k.op("act", lambda: S.activation(sq[:], src3, AF.Square), reads=[key_src], writes=["sq"])
                for ct in range(8):
                    k.op("pe", lambda: P.matmul(pbank[:, 0:TB], onesm[:], sq[:, ct, :], start=(ct == 0), stop=(ct == 7)),
                         reads=["sq", "onesm"], writes=[pkey], inc=(ct == 7))
                k.op("dve", lambda: V.tensor_scalar(rs[:], pbank[:, 0:TB], EPS, None, ALU.add), reads=[pkey], writes=["rs"])
                k.op("act", lambda: S.activation(rs[:], rs[:], AF.Sqrt), reads=["rs"], writes=["rs"])
                k.op("dve", lambda: V.reciprocal(rs[:], rs[:]), reads=["rs"], writes=["rs"])
                return rs


            def prenorm_to_hT(ph, hT, off=0):
                xbb = [k.sb(ph, "pxb%d" % z_, [128, 8, TB], F32) for z_ in range(2)]
                sq = k.sb(ph, "psq", [128, 8, TB], F32)
                tmp = k.sb(ph, "ptmp", [128, 8, TB], F32)
                rs = k.sb(ph, "prs", [128, TB], F32)
                tiles = (sq, rs, psb[6], "psb6")
                k.dma("sp", xbb[0][:], XTv[:, :, 0:TB], reads=["XT"], writes=["pxb0"])
                for blk in range(NBLK):
                    j = 1 if blk == 0 else 0
                    t0 = blk * TB
                    xb, xkk = xbb[blk % 2], "pxb%d" % (blk % 2)
                    if blk + 1 < NBLK:
                        k.dma("sp", xbb[(blk + 1) % 2][:], XTv[:, :, t0 + TB:t0 + 2 * TB], reads=["XT"], writes=["pxb%d" % ((blk + 1) % 2)])
                    rms_bc(tiles, xb[:], xkk, "p")
                    k.op("dve", lambda: V.tensor_tensor(tmp[:], xb[:], bc_mid(rs[:], 8), ALU.mult), reads=[xkk, "rs"], writes=["tmp"])
                    for ct in range(8):
                        k.op("act", lambda: S.activation(hT[:, ct, off + t0:off + t0 + TB], tmp[:, ct, :], AF.Identity, bias=PRM[:, 1, ct, j:j + 1],
                                                         scale=PRM[:, 0, ct, j:j + 1]), reads=["tmp", "PRM"], writes=["hT"])

            def s5_mixer(l):
                i = l // 2
                PI = float(np.pi)
                W = 64
                NW = NT // W
                with contextlib.ExitStack() as ph:
                    T1 = 4
                    PAD = T1 - 1
                    hT = k.sb(ph, "hT", [128, 8, NT + NCTX + 2 * PAD], BF16)
                    k.op("pool", lambda: G.memset(hT[:, :, 0:PAD], 0.0), writes=["hT"])
                    k.op("pool", lambda: G.memset(hT[:, :, PAD + NT + NCTX:], 0.0), reads=["hT"], writes=["hT"])
                    with contextlib.ExitStack() as ph2:
                        prenorm_to_hT(ph2, hT, off=PAD)
                        k.barrier()
                    dsk = k.sb(ph, "dsk", [128, 8], F32)
                    k.dma("sp", dsk[:], dsk_d[i].rearrange("(c p) -> p c", p=128), writes=["dsk"], allow_slow_non_contiguous=True)
                    sc = contextlib.ExitStack()
                    k.op("act", lambda: S.copy(hT[:, :, PAD + NT:PAD + NT + NCTX], hT[:, :, PAD:PAD + NCTX]), reads=["hT"], writes=["hT"])
                    BOFF = PAD + NCTX
                    WinT = [[[k.sb(sc, "WinT%d%d%d" % (d, ri, ti), [128, 8, 128], BF16) for ti in range(T1)] for ri in range(2)] for d in range(2)]
                    CwQ = [[k.sb(sc, "CwQ%d%d" % (d, ri), [128, 32, 32], BF16) for ri in range(2)] for d in range(2)]
                    Gt = [k.sb(sc, "Gt%d" % d, [128, 2, 64], F32) for d in range(2)]
                    with contextlib.ExitStack() as pp:
                        def t32(n):
                            return k.sb(pp, n, [128, 32], F32)
                        twopi = t32("twopi")
                        k.op("pool", lambda: G.memset(twopi[:], 2 * PI), writes=["twopi"])
                        pidx = k.sb(pp, "pidx", [128, 1], I32)
                        modd = k.sb(pp, "modd", [128, 1], F32)
                        mevn = k.sb(pp, "mevn", [128, 1], F32)
                        nodd = k.sb(pp, "nodd", [128, 1], F32)
                        nevn = k.sb(pp, "nevn", [128, 1], F32)
                        k.op("pool", lambda: G.iota(pidx[:], pattern=[[0, 1]], base=0, channel_multiplier=1), writes=["pidx"])
                        k.op("dve", lambda: V.tensor_scalar(pidx[:], pidx[:], 4, 1, ALU.arith_shift_right, ALU.bitwise_and), reads=["pidx"], writes=["pidx"])
                        k.op("dve", lambda: V.tensor_copy(modd[:], pidx[:]), reads=["pidx"], writes=["modd"])
                        k.op("dve", lambda: V.tensor_scalar(mevn[:], modd[:], -1.0, 1.0, ALU.mult, ALU.add), reads=["modd"], writes=["mevn"])
                        k.op("dve", lambda: V.tensor_scalar(nodd[:], modd[:], -1.0, None, ALU.mult), reads=["modd"], writes=["nodd"])
                        k.op("dve", lambda: V.tensor_scalar(nevn[:], mevn[:], -1.0, None, ALU.mult), reads=["mevn"], writes=["nevn"])
                        Br = k.sb(pp, "Br", [128, 32, 32], F32)
                        Bi = k.sb(pp, "Bi", [128, 32, 32], F32)
                        BbR = k.sb(pp, "BbR", [128, 32, 32], F32)
                        BbI = k.sb(pp, "BbI", [128, 32, 32], F32)
                        T1t = k.sb(pp, "T1t", [128, 32, 32], F32)
                        TbR = k.sb(pp, "TbR", [128, 32, 32], F32)
                        TbI = k.sb(pp, "TbI", [128, 32, 32], F32)
                        Cn = k.sb(pp, "Cn", [128, 8, 64], F32)
                        Cblk = k.sb(pp, "Cblk", [128, 8, 128], F32)
                        lr, li, dtt, tq, mag, ang, sa, sinv, cosv, Ar, Ai, am1, n2, kr, ki, u1 = [t32("p%d" % z) for z in range(16)]
                        for d in range(2):
                            def dve(fn, r, w):
                                k.op("dve", fn, reads=r, writes=w)
                            def act(fn, r, w):
                                k.op("act", fn, reads=r, writes=w)
                            k.dma("sp", lr[:], a_re_d[i, d].rearrange("(q a) p -> (a p) q", a=2), writes=["lr"], allow_slow_non_contiguous=True)
                            k.dma("sp", li[:], a_im_d[i, d].rearrange("(q a) p -> (a p) q", a=2), writes=["li"], allow_slow_non_contiguous=True)
                            for g2 in range(2):
                                base = ldt_d[i, d]
                                src = bass.AP(base.tensor, base.offset + g2, [[0, 64], [2, 32]])
                                k.dma("sp", dtt[g2 * 64:(g2 + 1) * 64, :], src, writes=["dtt"], allow_slow_non_contiguous=True)
                            act(lambda: S.activation(dtt[:], dtt[:], AF.Exp), ["dtt"], ["dtt"])
                            dve(lambda: V.tensor_tensor(tq[:], lr[:], dtt[:], ALU.mult), ["lr", "dtt"], ["tq"])
                            act(lambda: S.activation(mag[:], tq[:], AF.Exp), ["tq"], ["mag"])
                            dve(lambda: V.tensor_tensor(ang[:], li[:], dtt[:], ALU.mult), ["li", "dtt"], ["ang"])
                            MAGIC = 12582912.0
                            PIC = 3.1415925
                            for (dst, shift, tag) in ((sinv, 0.0, "s"), (cosv, 0.5 * PI, "c")):
                                dve(lambda: V.tensor_scalar(u1[:], ang[:], shift, None, ALU.add), ["ang", "sa", "u1"], ["u1"])
                                dve(lambda: V.tensor_scalar(sa[:], u1[:], 1.0 / (2 * PI), None, ALU.mult), ["u1", "sa"], ["sa"])
                                dve(lambda: V.tensor_scalar(sa[:], sa[:], MAGIC, None, ALU.add), ["sa"], ["sa"])
                                dve(lambda: V.tensor_scalar(sa[:], sa[:], -MAGIC, None, ALU.add), ["sa"], ["sa"])
                                dve(lambda: V.scalar_tensor_tensor(out=sa[:], in0=sa[:], scalar=-2 * PI, in1=u1[:], op0=ALU.mult, op1=ALU.add), ["sa", "u1"], ["sa"])
                                dve(lambda: V.tensor_scalar(sa[:], sa[:], -PIC, PIC, ALU.max, ALU.min), ["sa"], ["sa"])
                                act(lambda: S.activation(dst[:], sa[:], AF.Sin), ["sa"], ["sinv" if tag == "s" else "cosv"])
                            dve(lambda: V.tensor_tensor(Ar[:], mag[:], cosv[:], ALU.mult), ["mag", "cosv"], ["Ar"])
                            dve(lambda: V.tensor_tensor(Ai[:], mag[:], sinv[:], ALU.mult), ["mag", "sinv"], ["Ai"])
                            gk = "Gt%d" % d
                            pws = [(None, None), (Ar, Ai)]
                            for pi_ in range(2, T1 + 1):
                                pr_ = t32("pwr%d_%d" % (d, pi_)); pim_ = t32("pwi%d_%d" % (d, pi_))
                                qr_, qi_ = pws[pi_ - 1]
                                kq = ["pw%d" % (pi_ - 1), "Ar", "Ai"]
                                dve(lambda: V.tensor_tensor(pr_[:], qr_[:], Ar[:], ALU.mult), kq, ["pw%d" % pi_])
                                dve(lambda: V.tensor_tensor(u1[:], qi_[:], Ai[:], ALU.mult), kq + ["u1"], ["u1"])
                                dve(lambda: V.tensor_tensor(pr_[:], pr_[:], u1[:], ALU.subtract), ["pw%d" % pi_, "u1"], ["pw%d" % pi_])
                                dve(lambda: V.tensor_tensor(pim_[:], qr_[:], Ai[:], ALU.mult), kq + ["pw%d" % pi_], ["pw%d" % pi_])
                                dve(lambda: V.tensor_tensor(u1[:], qi_[:], Ar[:], ALU.mult), kq + ["u1"], ["u1"])
                                dve(lambda: V.tensor_tensor(pim_[:], pim_[:], u1[:], ALU.add), ["pw%d" % pi_, "u1"], ["pw%d" % pi_])
                                pws.append((pr_, pim_))
                            AT_r, AT_i = pws[T1]
                            Arp = AT_r[:].rearrange("p (c r) -> p r c", r=4)
                            Aip = AT_i[:].rearrange("p (c r) -> p r c", r=4)
                            gv = lambda a, lo: Gt[d][:, a, lo:lo + 32].rearrange("p (r c) -> p r c", r=4)
                            dve(lambda: V.tensor_copy(gv(0, 0), Arp), ["pw%d" % T1], [gk])
                            dve(lambda: V.tensor_scalar(gv(0, 32), Aip, -1.0, None, ALU.mult), ["pw%d" % T1, gk], [gk])
                            dve(lambda: V.tensor_copy(gv(1, 0), Aip), ["pw%d" % T1, gk], [gk])
                            dve(lambda: V.tensor_copy(gv(1, 32), Arp), ["pw%d" % T1, gk], [gk])
                            dve(lambda: V.tensor_scalar(am1[:], Ar[:], -1.0, None, ALU.add), ["Ar"], ["am1"])
                            dve(lambda: V.tensor_tensor(n2[:], lr[:], lr[:], ALU.mult), ["lr"], ["n2"])
                            dve(lambda: V.tensor_tensor(u1[:], li[:], li[:], ALU.mult), ["li", "u1"], ["u1"])
                            dve(lambda: V.tensor_tensor(n2[:], n2[:], u1[:], ALU.add), ["n2", "u1"], ["n2"])
                            dve(lambda: V.reciprocal(n2[:], n2[:]), ["n2"], ["n2"])
                            dve(lambda: V.tensor_tensor(kr[:], am1[:], lr[:], ALU.mult), ["am1", "lr"], ["kr"])
                            dve(lambda: V.tensor_tensor(u1[:], Ai[:], li[:], ALU.mult), ["Ai", "li", "n2", "u1"], ["u1"])
                            dve(lambda: V.tensor_tensor(kr[:], kr[:], u1[:], ALU.add), ["kr", "u1"], ["kr"])
                            dve(lambda: V.tensor_tensor(kr[:], kr[:], n2[:], ALU.mult), ["kr", "n2"], ["kr"])
                            dve(lambda: V.tensor_tensor(ki[:], Ai[:], lr[:], ALU.mult), ["Ai", "lr"], ["ki"])
                            dve(lambda: V.tensor_tensor(u1[:], am1[:], li[:], ALU.mult), ["am1", "li", "kr", "u1"], ["u1"])
                            dve(lambda: V.tensor_tensor(ki[:], ki[:], u1[:], ALU.subtract), ["ki", "u1"], ["ki"])
                            dve(lambda: V.tensor_tensor(ki[:], ki[:], n2[:], ALU.mult), ["ki", "n2"], ["ki"])
                            k.op("pool", lambda: G.memset(Br[:], 0.0), reads=["BbR", "BbI"], writes=["Br"])
                            k.op("pool", lambda: G.memset(Bi[:], 0.0), reads=["BbR", "BbI"], writes=["Bi"])
                            for g2 in range(2):
                                for (dst, srcd, key) in ((Br, b_re_d, "Br"), (Bi, b_im_d, "Bi")):
                                    base = srcd[i, d]
                                    src = bass.AP(base.tensor, base.offset + g2 * 1024, [[16, 64], [2048, 32], [1, 16]])
                                    k.dma("sp", dst[g2 * 64:(g2 + 1) * 64, :, g2 * 16:(g2 + 1) * 16], src, reads=[key], writes=[key])
                            def bc_last(a2, n):
                                a = a2.ap
                                return bass.AP(a2.tensor, a2.offset, [list(a[0]), list(a[1]), [0, n]])
                            def cmul(outR, outI, xr, xi, inR, inI, kx, kin, kout):
                                xrb, xib = bc_last(xr[:], 32), bc_last(xi[:], 32)
                                dve(lambda: V.tensor_tensor(outR[:], inR[:], xrb, ALU.mult), kin + kx + kout, kout)
                                dve(lambda: V.tensor_tensor(T1t[:], inI[:], xib, ALU.mult), kin + kx + ["T1t"], ["T1t"])
                                dve(lambda: V.tensor_tensor(outR[:], outR[:], T1t[:], ALU.subtract), kout + ["T1t"], kout)
                                dve(lambda: V.tensor_tensor(outI[:], inI[:], xrb, ALU.mult), kin + kx + kout, kout)
                                dve(lambda: V.tensor_tensor(T1t[:], inR[:], xib, ALU.mult), kin + kx + ["T1t"], ["T1t"])
                                dve(lambda: V.tensor_tensor(outI[:], outI[:], T1t[:], ALU.add), kout + ["T1t"], kout)
                            cmul(BbR, BbI, kr, ki, Br, Bi, ["kr", "ki"], ["Br", "Bi"], ["Bb"])
                            for ti in range(T1):
                                if ti == 0:
                                    srcs = (BbR, BbI)
                                    skey = ["Bb"]
                                else:
                                    cmul(TbR, TbI, pws[ti][0], pws[ti][1], BbR, BbI, ["pw%d" % ti], ["Bb"], ["Tb"])
                                    srcs = (TbR, TbI)
                                    skey = ["Tb"]
                                for ri in range(2):
                                    for half in range(2):
                                        bi_ = (ti * 4 + ri * 2 + half) % 4
                                        pk = "psb%d" % bi_
                                        for c4 in range(4):
                                            ct = half * 4 + c4
                                            k.op("pe", lambda: P.transpose(psb[bi_][:, c4 * 128:(c4 + 1) * 128], srcs[ri][:, 4 * ct:4 * ct + 4, :].rearrange("p a b -> p (a b)"), ident[:]),
                                                 reads=skey + ["ident"], writes=[pk], inc=(c4 == 3))
                                        dstv = WinT[d][ri][ti][:, half * 4:(half + 1) * 4, :].rearrange("p c n -> p (c n)")
                                        if half == 0:
                                            act(lambda: S.copy(dstv, psb[bi_][:]), [pk], ["WinT%d%d" % (d, ri)])
                                        else:
                                            dve(lambda: V.tensor_copy(dstv, psb[bi_][:]), [pk], ["WinT%d%d" % (d, ri)])
                            for ri, srcd, mo, me in ((0, c_re_d, modd, mevn), (1, c_im_d, nodd, nevn)):
                                ck = "CwQ%d%d" % (d, ri)
                                k.dma("sp", Cn[:], srcd[i, d].rearrange("(ct g) c p -> (g c) ct p", g=8), reads=["Cn"], writes=["Cn"])
                                dve(lambda: V.tensor_scalar(Cblk[:, :, 0:64], Cn[:], me[:, 0:1], None, ALU.mult), ["Cn", "mevn", "nevn", "Cblk"], ["Cblk"])
                                dve(lambda: V.tensor_scalar(Cblk[:, :, 64:128], Cn[:], mo[:, 0:1], None, ALU.mult), ["Cn", "modd", "nodd", "Cblk"], ["Cblk"])
                                for half in range(2):
                                    bi_ = 4 + (ri * 2 + half) % 4
                                    pk = "psb%d" % bi_
                                    for c4 in range(4):
                                        ct = half * 4 + c4
                                        k.op("pe", lambda: P.transpose(psb[bi_][:, c4 * 128:(c4 + 1) * 128], Cblk[:, ct, :], ident[:]), reads=["Cblk", "ident"], writes=[pk], inc=(c4 == 3))
                                    act(lambda: S.copy(CwQ[d][ri][:, half * 16:(half + 1) * 16, :].rearrange("p q c -> p (q c)"), psb[bi_][:]), [pk, ck], [ck])
                        k.barrier()
                    if S5_STAGE < 1:
                        sc.close()
                        return
                    NG = W // T1
                    Bw = [[k.sb(sc, "Bw%d_%d" % (d, z_), [128, 64, W], BF16) for z_ in range(2)] for d in range(2)]
                    H = [k.sb(sc, "H%d" % d, [128, 64, W + T1], F32) for d in range(2)]
                    Sb = [k.sb(sc, "Sb%d" % d, [128, 64, W], BF16) for d in range(2)]
                    XY = [k.sb(sc, "XY%d" % d, [128, 2, 64, T1], F32) for d in range(2)]
                    Nn = [k.sb(sc, "Nn%d" % d, [128, 2, 32, T1], F32) for d in range(2)]
                    yo = [k.sb(sc, "yo%d" % d, [128, 8, W], F32) for d in range(2)]
                    k.op("pool", lambda: G.memset(H[0][:], 0.0), writes=["H0"])
                    k.op("pool", lambda: G.memset(H[1][:], 0.0), writes=["H1"])

                    def bc4(ap3):
                        a_ = ap3.ap
                        return bass.AP(ap3.tensor, ap3.offset, [list(a_[0]), [0, 2], list(a_[1]), list(a_[2])])

                    def gbc(d):
                        a_ = Gt[d][:].ap
                        return bass.AP(Gt[d][:].tensor, Gt[d][:].offset, [list(a_[0]), list(a_[1]), list(a_[2]), [0, T1]])

                    NWR = NW if S5_NW is None else S5_NW
                    CE = ("dve", "pool") if S5_POOL else ("dve", "dve")
                    CV = (V, G) if S5_POOL else (V, V)

                    def emit_bu(step_w):
                        zb = step_w % 2
                        wins = (step_w, NW - 1 - step_w)
                        for d in range(2):
                            p0 = wins[d] * W
                            bk = "Bw%d_%d" % (d, zb)
                            for half in range(2):
                                for c4 in range(4):
                                    ct = half * 4 + c4
                                    for ri in range(2):
                                        for r in range(4):
                                            slot = c4 * 2 + ri
                                            last = (c4 == 3 and ri == 1)
                                            for ti in range(T1):
                                                c0 = (PAD + p0 - ti) if d == 0 else (BOFF + p0 + ti)
                                                k.op("pe", lambda: P.matmul(psb[r][:, slot * W:(slot + 1) * W],
                                                                            WinT[d][ri][ti][32 * r:32 * r + 32, ct, :],
                                                                            hT[32 * r:32 * r + 32, ct, c0:c0 + W],
                                                                            start=(ti == 0), stop=(ti == T1 - 1), tile_position=(32 * r, 0)),
                                                     reads=["hT", "WinT%d%d" % (d, ri)], writes=["psb%d" % r], inc=(last and r == 3 and ti == T1 - 1))
                                for r in range(4):
                                    for ri in range(2):
                                        src = psb[r][:].rearrange("p (c i w) -> p c i w", i=2, w=W)[:, :, ri, :]
                                        lo = ri * 32 + r * 8 + half * 4
                                        k.op("act", lambda: S.copy(Bw[d][zb][:, lo:lo + 4, :], src), reads=["psb%d" % r, bk], writes=[bk])

                    emit_bu(0)
                    for step_w in range(NWR):
                        zb = step_w % 2
                        wins = (step_w, NW - 1 - step_w)
                        if step_w + 1 < NWR:
                            emit_bu(step_w + 1)
                        for g in range(NG if S5_SUB >= 1 else 0):
                            rd = (g * T1, W - g * T1)
                            wr = (T1 + g * T1, W - (g + 1) * T1)
                            bj = (g * T1, W - (g + 1) * T1)
                            for d in range(2):
                                k.op(CE[d], lambda: CV[d].tensor_tensor(XY[d][:], bc4(H[d][:, :, rd[d]:rd[d] + T1]), gbc(d), ALU.mult),
                                     reads=["H%d" % d, "Gt%d" % d], writes=["XY%d" % d])
                            for d in range(2):
                                k.op(CE[d], lambda: CV[d].tensor_tensor(Nn[d][:], XY[d][:, :, 0:32, :], XY[d][:, :, 32:64, :], ALU.add),
                                     reads=["XY%d" % d], writes=["Nn%d" % d])
                            for d in range(2):
                                k.op(CE[d], lambda: CV[d].tensor_tensor(H[d][:, :, wr[d]:wr[d] + T1], Nn[d][:].rearrange("p a q t -> p (a q) t"),
                                                                        Bw[d][zb][:, :, bj[d]:bj[d] + T1], ALU.add),
                                     reads=["Nn%d" % d, "Bw%d_%d" % (d, zb)], writes=["H%d" % d])
                        for d in range(2 if S5_SUB >= 2 else 0):
                            p0 = wins[d] * W
                            if d == 0:
                                t0 = p0
                                k.op("act", lambda: S.copy(Sb[0][:], H[0][:, :, T1:W + T1]), reads=["H0"], writes=["Sb0"])
                                k.op("dve", lambda: V.tensor_copy(H[0][:, :, 0:T1], H[0][:, :, W:W + T1]), reads=["H0"], writes=["H0"])
                            else:
                                t0 = (NCTX + p0) if p0 < 2048 else (p0 - 2048)
                                k.op("act", lambda: S.copy(Sb[1][:], H[1][:, :, 0:W]), reads=["H1"], writes=["Sb1"])
                                k.op(CE[1], lambda: CV[1].tensor_copy(H[1][:, :, W:W + T1], H[1][:, :, 0:T1]), reads=["H1"], writes=["H1"])
                            for ct in range(8):
                                for r in range(4):
                                    q = ct * 4 + r
                                    for ri in range(2):
                                        k.op("pe", lambda: P.matmul(psb[4 + r][32 * r:32 * r + 32, ct * W:(ct + 1) * W], CwQ[d][ri][:, q, :],
                                                                    Sb[d][:, ri * 32 + r * 8 + ct, :], start=(ri == 0), stop=(ri == 1),
                                                                    tile_position=(0, 32 * r)),
                                             reads=["CwQ%d%d" % (d, ri), "Sb%d" % d], writes=["psb%d" % (4 + r)], inc=(ct == 7 and ri == 1))
                            for r in range(4):
                                k.op("act", lambda: S.copy(yo[d][32 * r:32 * r + 32, :, :].rearrange("p c w -> p (c w)"), psb[4 + r][32 * r:32 * r + 32, 0:8 * W]),
                                     reads=["psb%d" % (4 + r), "yo%d" % d], writes=["yo%d" % d])
                            k.dma("sp", YF[d].rearrange("(ct p) t -> p ct t", p=128)[:, :, t0:t0 + W], yo[d][:], reads=["yo%d" % d], writes=["YF"])
                    k.barrier()
                    sc.close()
                    if S5_STAGE < 2:
                        return
                    with contextlib.ExitStack() as pg:
                        gw = k.sb(pg, "gw", [128, 8, 2 * D], BF16)
                        for kt in range(8):
                            k.dma("pool", gw[:, kt, :], glu_d[i][kt * 128:(kt + 1) * 128, :], writes=["gw"])
                        yas = [k.sb(pg, "ya%d" % z_, [128, 8, TB], F32) for z_ in range(2)]
                        ybs2 = [k.sb(pg, "yb2%d" % z_, [128, 8, TB], F32) for z_ in range(2)]
                        y2 = k.sb(pg, "y2", [128, 8, TB], F32)
                        y3 = k.sb(pg, "y3", [128, 8, TB], F32)
                        gls = [k.sb(pg, "gl%d" % z_, [128, 8, TB], BF16) for z_ in range(2)]
                        sg = k.sb(pg, "sg", [128, TB], F32)
                        zos = [k.sb(pg, "zo%d" % z_, [128, 8, TB], F32) for z_ in range(2)]
                        YFv = [YF[d_].rearrange("(ct p) t -> p ct t", p=128) for d_ in range(2)]

                        def g_load(blk):
                            z_ = blk % 2
                            t0 = blk * TB
                            k.dma("sp", yas[z_][:], YFv[0][:, :, t0:t0 + TB], reads=["YF"], writes=["ya%d" % z_])
                            k.dma("sp", ybs2[z_][:], YFv[1][:, :, t0:t0 + TB], reads=["YF"], writes=["yb2%d" % z_])

                        def g_pro(blk):
                            z_ = blk % 2
                            t0 = blk * TB
                            ya, yb2, gl = yas[z_], ybs2[z_], gls[z_]
                            ka, kb2, kg = "ya%d" % z_, "yb2%d" % z_, "gl%d" % z_
                            th = [lambda: k.op("dve", lambda: V.tensor_tensor(ya[:], ya[:], yb2[:], ALU.add), reads=[ka, kb2], writes=[ka])]
                            for ct in range(8):
                                th.append(lambda ct=ct: k.op("dve", lambda: V.scalar_tensor_tensor(out=ya[:, ct, :], in0=hT[:, ct, PAD + t0:PAD + t0 + TB], scalar=dsk[:, ct:ct + 1],
                                                                                                  in1=ya[:, ct, :], op0=ALU.mult, op1=ALU.add), reads=[ka, "hT", "dsk"], writes=[ka]))
                            th.append(lambda: k.op("dve", lambda: V.tensor_tensor(y2[:], ya[:], ya[:], ALU.mult), reads=[ka, "y2"], writes=["y2"]))
                            th.append(lambda: k.op("dve", lambda: V.tensor_scalar(y2[:], y2[:], 0.044715, 1.0, ALU.mult, ALU.add), reads=["y2"], writes=["y2"]))
                            th.append(lambda: k.op("dve", lambda: V.tensor_tensor(y2[:], y2[:], ya[:], ALU.mult), reads=["y2", ka], writes=["y2"]))
                            th.append(lambda: k.op("act", lambda: S.activation(y3[:], y2[:], AF.Sigmoid, scale=1.5957691216057308), reads=["y2", "y3"], writes=["y3"]))
                            th.append(lambda: k.op("dve", lambda: V.tensor_tensor(gl[:], y3[:], ya[:], ALU.mult), reads=["y3", ka, kg], writes=[kg]))
                            return th

                        g_load(0)
                        for t_ in g_pro(0):
                            t_()
                        for blk in range(NBLK):
                            z_ = blk % 2
                            t0 = blk * TB
                            gl, zo, kg, kz = gls[z_], zos[z_], "gl%d" % z_, "zo%d" % z_
                            pend = []
                            if blk + 1 < NBLK:
                                g_load(blk + 1)
                                pend = g_pro(blk + 1)
                            for ct in range(8):
                                pa, pb_ = psb[(2 * ct) % 4], psb[(2 * ct + 1) % 4]
                                ka_, kb_ = "psb%d" % ((2 * ct) % 4), "psb%d" % ((2 * ct + 1) % 4)
                                for kt in range(8):
                                    k.op("pe", lambda: P.matmul(pa[:, 0:TB], gw[:, kt, ct * 128:(ct + 1) * 128], gl[:, kt, :], start=(kt == 0), stop=(kt == 7)),
                                         reads=["gw", kg], writes=[ka_], inc=(kt == 7))
                                if pend:
                                    pend.pop(0)()
                                for kt in range(8):
                                    k.op("pe", lambda: P.matmul(pb_[:, 0:TB], gw[:, kt, D + ct * 128:D + (ct + 1) * 128], gl[:, kt, :], start=(kt == 0), stop=(kt == 7)),
                                         reads=["gw", kg], writes=[kb_], inc=(kt == 7))
                                if pend:
                                    pend.pop(0)()
                                k.op("act", lambda: S.activation(sg[:], pb_[:, 0:TB], AF.Sigmoid), reads=[kb_, "sg"], writes=["sg"])
                                k.op("dve", lambda: V.tensor_tensor(zo[:, ct, :], pa[:, 0:TB], sg[:], ALU.mult), reads=[ka_, "sg", kz], writes=[kz])
                            while pend:
                                pend.pop(0)()
                            k.dma("pool", YTv[:, :, t0:t0 + TB], zo[:], reads=[kz], writes=["YT"])
                        k.barrier()

            def even_mixer(l):
                i = l // 2
                NLAT = 2048
                with contextlib.ExitStack() as ph:
                    fT = k.sb(ph, "fT", [128, 4, NT], BF16)
                    QT = k.sb(ph, "QT", [128, 4, NT], BF16)
                    KT = k.sb(ph, "KT", [128, 2, NT], BF16)
                    Vtm = k.sb(ph, "Vtm", [128, NT // 128, 128], BF16)
                    mixT = k.sb(ph, "mixT", [128, 8, NT], BF16)
                    SEall = k.sb(ph, "SEall", [128, 8], F32)
                    SE = k.sb(ph, "SE", [128, 2, 2], F32)
                    sk = even_sink_d[i]
                    k.dma("sp", SEall[:], bass.AP(sk.tensor, sk.offset, [[0, 128], [1, 8]]), writes=["SEall"], allow_slow_non_contiguous=True)
                    k.op("act", lambda: S.activation(SEall[:], SEall[:], AF.Exp), reads=["SEall"], writes=["SEall"])
                    for kh in range(2):
                        for tl in range(2):
                            k.op("dve", lambda: V.tensor_copy(SE[0:64, kh, tl:tl + 1], SEall[0:64, 4 * kh + 2 * tl:4 * kh + 2 * tl + 1]), reads=["SEall", "SE"], writes=["SE"])
                            k.op("dve", lambda: V.tensor_copy(SE[64:128, kh, tl:tl + 1], SEall[64:128, 4 * kh + 2 * tl + 1:4 * kh + 2 * tl + 2]), reads=["SEall", "SE"], writes=["SE"])
                    with contextlib.ExitStack() as pa:
                        hT = k.sb(pa, "hT", [128, 8, NT], BF16)
                        with contextlib.ExitStack() as ph2:
                            prenorm_to_hT(ph2, hT)
                            k.barrier()
                        wb = k.sb(pa, "wb", [128, 8, 1280], BF16)
                        for kt in range(8):
                            k.dma("pool", wb[:, kt, :], w_in_d[i][kt * 128:(kt + 1) * 128, :], writes=["wb"])
                        wsw = k.sb(pa, "wsw", [128, 8, 640], BF16)
                        wv = wb[:, :, 512:1152].rearrange("p k (h two e) -> p k h two e", two=2, e=16)
                        wsv = wsw[:].rearrange("p k (h two e) -> p k h two e", two=2, e=16)
                        for kt in range(8):
                            k.op("pool", lambda: G.tensor_copy(wsv[:, kt, :, 0, :], wv[:, kt, :, 1, :]), reads=["wb", "wsw"], writes=["wsw"])
                            k.op("pool", lambda: G.tensor_copy(wsv[:, kt, :, 1, :], wv[:, kt, :, 0, :]), reads=["wb", "wsw"], writes=["wsw"])
                        wkd = k.sb(pa, "wkd", [128, 8, 2, 128], BF16)
                        wkds = k.sb(pa, "wkds", [128, 8, 2, 128], BF16)
                        for dup in range(2):
                            k.op("pool", lambda: G.tensor_copy(wkd[:, :, :, dup * 64:(dup + 1) * 64], wb[:, :, 1024:1152].rearrange("p k (h d) -> p k h d", d=64)), reads=["wb", "wkd"], writes=["wkd"])
                            k.op("pool", lambda: G.tensor_copy(wkds[:, :, :, dup * 64:(dup + 1) * 64], wsw[:, :, 512:640].rearrange("p k (h d) -> p k h d", d=64)), reads=["wsw", "wkds"], writes=["wkds"])
                        ropc = k.sb(pa, "ropc", [128, NLAT], F32)
                        rops = k.sb(pa, "rops", [128, NLAT], F32)
                        k.dma("sp", ropc[:], ROPC[:, :], reads=["ROP"], writes=["ropc"])
                        k.dma("sp", rops[:], ROPS[:, :], reads=["ROP"], writes=["rops"])
                        t1 = k.sb(pa, "rt1", [128, 512], F32)
                        t2 = k.sb(pa, "rt2", [128, 512], F32)
                        blocks = [(0, 256)] + [(256 + 512 * b, 512) for b in range(4)]
                        nb = 0
                        for (t0, n) in blocks:
                            lat = t0 >= NCTX
                            for g in range(4):
                                pb, pk = psb[nb % 4], "psb%d" % (nb % 4); nb += 1
                                for kt in range(8):
                                    k.op("pe", lambda: P.matmul(pb[:, 0:n], wb[:, kt, g * 128:(g + 1) * 128], hT[:, kt, t0:t0 + n], start=(kt == 0), stop=(kt == 7)),
                                         reads=["wb", "hT"], writes=[pk], inc=(kt == 7))
                                k.op("act", lambda: S.copy(fT[:, g, t0:t0 + n], pb[:, 0:n]), reads=[pk, "fT"], writes=["fT"])
                            for j in range(6):
                                if j < 4:
                                    lw = lambda kt: wb[:, kt, 512 + j * 128:512 + (j + 1) * 128]
                                    lws = lambda kt: wsw[:, kt, j * 128:(j + 1) * 128]
                                    dst = QT[:, j, t0:t0 + n]
                                    dk = "QT"
                                else:
                                    lw = lambda kt: wkd[:, kt, j - 4, :]
                                    lws = lambda kt: wkds[:, kt, j - 4, :]
                                    dst = KT[:, j - 4, t0:t0 + n]
                                    dk = "KT"
                                pb, pk = psb[nb % 4], "psb%d" % (nb % 4); nb += 1
                                for kt in range(8):
                                    k.op("pe", lambda: P.matmul(pb[:, 0:n], lw(kt), hT[:, kt, t0:t0 + n], start=(kt == 0), stop=(kt == 7)),
                                         reads=["wb", "wkd", "hT"], writes=[pk], inc=(kt == 7))
                                if not lat:
                                    k.op("act", lambda: S.copy(dst, pb[:, 0:n]), reads=[pk, dk], writes=[dk])
                                else:
                                    pb2, pk2 = psb[4 + nb % 4], "psb%d" % (4 + nb % 4)
                                    for kt in range(8):
                                        k.op("pe", lambda: P.matmul(pb2[:, 0:n], lws(kt), hT[:, kt, t0:t0 + n], start=(kt == 0), stop=(kt == 7)),
                                             reads=["wsw", "wkds", "hT"], writes=[pk2], inc=(kt == 7))
                                    r0 = t0 - NCTX
                                    k.op("dve", lambda: V.tensor_tensor(t1[:, 0:n], pb[:, 0:n], ropc[:, r0:r0 + n], ALU.mult), reads=[pk, "ropc", "rt1"], writes=["rt1"])
                                    k.op("dve", lambda: V.tensor_tensor(t2[:, 0:n], pb2[:, 0:n], rops[:, r0:r0 + n], ALU.mult), reads=[pk2, "rops", "rt2"], writes=["rt2"])
                                    k.op("pool", lambda: G.tensor_tensor(dst, t1[:, 0:n], t2[:, 0:n], ALU.add), reads=["rt1", "rt2", dk], writes=[dk])
                            for s in range(n // 128):
                                tt = (t0 + s * 128) // 128
                                pb, pk = psb[nb % 4], "psb%d" % (nb % 4); nb += 1
                                for kt in range(8):
                                    k.op("pe", lambda: P.matmul(pb[:, 0:128], hT[:, kt, tt * 128:(tt + 1) * 128], wb[:, kt, 1152:1280], start=(kt == 0), stop=(kt == 7)),
                                         reads=["wb", "hT"], writes=[pk], inc=(kt == 7))
                                k.op("act", lambda: S.copy(Vtm[:, tt, :], pb[:, 0:128]), reads=[pk, "Vtm"], writes=["Vtm"])
                        dump("dbg_fT", fT[:], ["fT"]); dump("dbg_QT", QT[:], ["QT"]); dump("dbg_KT", KT[:], ["KT"]); dump("dbg_Vtm", Vtm[:], ["Vtm"])
                        k.barrier()
                    with contextlib.ExitStack() as pf:
                        Gtm = k.sb(pf, "Gtm", [128, NT // 128, 4, 256], BF16)
                        csc = k.sb(pf, "csc", [128, 256], BF16)
                        k.dma("sp", csc[:, 0:128], CL.rearrange("(t e) c -> t e c", e=16)[:, 0, 0:128], reads=["TAB"], writes=["csc"])
                        k.dma("sp", csc[:, 128:256], SLn.rearrange("(t e) c -> t e c", e=16)[:, 0, 0:128], reads=["TAB"], writes=["csc"])
                        k.op("dve", lambda: V.tensor_scalar(csc[:, 128:256], csc[:, 128:256], -1.0, None, ALU.mult), reads=["csc"], writes=["csc"])
                        sc_lat = float(1.0 / np.sqrt(2048.0 * 128.0))
                        sc_ctx = float(1.0 / np.sqrt(256.0 * 128.0))
                        nb = 0
                        for tt in range(NT // 128):
                            for g in range(4):
                                pb, pk = psb[nb % 4], "psb%d" % (nb % 4); nb += 1
                                k.op("pe", lambda: P.matmul(pb[:, 0:256], fT[:, g, tt * 128:(tt + 1) * 128], csc[:], start=True, stop=True),
                                     reads=["fT", "csc"], writes=[pk])
                                k.op("act", lambda: S.activation(Gtm[:, tt, g, :], pb[:, 0:256], AF.Copy, scale=(sc_ctx if tt < 2 else sc_lat)), reads=[pk, "Gtm"], writes=["Gtm"])
                        cl = k.sb(pf, "cl", [128, 16, 512], BF16)
                        sl = k.sb(pf, "sl", [128, 16, 512], BF16)
                        c8 = CL.rearrange("(t e) c -> t e c", e=8)[:, 0, 0:256].rearrange("(tt p) c -> p tt c", p=128)
                        s8 = SLn.rearrange("(t e) c -> t e c", e=8)[:, 0, 0:256].rearrange("(tt p) c -> p tt c", p=128)
                        k.dma("sp", cl[:, 0:2, 0:256], c8, reads=["TAB"], writes=["cl"])
                        k.dma("sp", sl[:, 0:2, 0:256], s8, reads=["TAB"], writes=["sl"])
                        for g in range(4):
                            pb, pk = psb[4 + g % 4], "psb%d" % (4 + g % 4)
                            n_ = 0
                            for tt in range(2):
                                for (half, tabl, tk) in ((0, cl, "cl"), (1, sl, "sl")):
                                    k.op("pe", lambda: P.matmul(pb[:, 0:256], Gtm[:, tt, g, half * 128:(half + 1) * 128], tabl[:, tt, 0:256], start=(n_ == 0), stop=(n_ == 3)),
                                         reads=["Gtm", tk], writes=[pk], inc=(n_ == 3))
                                    n_ += 1
                            k.op("act", lambda: S.copy(mixT[:, g, 0:256], pb[:, 0:256]), reads=[pk, "mixT"], writes=["mixT"])
                        for pbk in range(4):
                            k.dma("sp", cl[:], CL[:, pbk * 512:(pbk + 1) * 512].rearrange("(tt p) c -> p tt c", p=128), reads=["TAB", "cl"], writes=["cl"])
                            k.dma("sp", sl[:], SLn[:, pbk * 512:(pbk + 1) * 512].rearrange("(tt p) c -> p tt c", p=128), reads=["TAB", "sl"], writes=["sl"])
                            for g in range(4):
                                pb, pk = psb[4 + g % 4], "psb%d" % (4 + g % 4)
                                n_ = 0
                                for tt in range(16):
                                    for (half, tabl, tk) in ((0, cl, "cl"), (1, sl, "sl")):
                                        k.op("pe", lambda: P.matmul(pb[:, 0:512], Gtm[:, 2 + tt, g, half * 128:(half + 1) * 128], tabl[:, tt, :], start=(n_ == 0), stop=(n_ == 31)),
                                             reads=["Gtm", tk], writes=[pk], inc=(n_ == 31))
                                        n_ += 1
                                k.op("act", lambda: S.copy(mixT[:, g, NCTX + pbk * 512:NCTX + (pbk + 1) * 512], pb[:, 0:512]), reads=[pk, "mixT"], writes=["mixT"])
                        k.barrier()
                    with contextlib.ExitStack() as pt:
                        PT = [k.sb(pt, "PT%d" % z, [128, 2, 2, 128], BF16) for z in range(2)]
                        rden = k.sb(pt, "rden", [128, 2, 128], F32)
                        scale = 0.125
                        qblocks = [("c", 0), ("c", 1)] + [("l", n) for n in range(16)]
                        items = []
                        for (kind, n) in qblocks:
                            q0 = n * 128 if kind == "c" else NCTX + n * 128
                            for kh in range(2):
                                chunks = []
                                if kind == "l":
                                    for dlt in (-1, 0, 1):
                                        if 0 <= n + dlt < 16:
                                            chunks.append((NCTX + (n + dlt) * 128, dlt))
                                chunks += [(0, 0), (128, 0)]
                                for ci, (k0, dlt) in enumerate(chunks):
                                    items.append((q0, kh, k0, dlt, ci == 0, ci == len(chunks) - 1))

                        def emit_S(i_):
                            q0, kh, k0, dlt, first, last = items[i_]
                            z = i_ % 2
                            for par in range(2):
                                k.op("pe", lambda: P.matmul(psb[par + 2 * z][:, 0:256].rearrange("p (t q) -> p t q", t=2),
                                                            KT[64 * par:64 * par + 64, kh, k0:k0 + 128],
                                                            QT[64 * par:64 * par + 64, 2 * kh:2 * kh + 2, q0:q0 + 128],
                                                            start=True, stop=True, tile_position=(64 * par, 0)),
                                     reads=["KT", "QT"], writes=["psb%d" % (par + 2 * z)])

                        emit_S(0)
                        for i_ in range(len(items)):
                            q0, kh, k0, dlt, first, last = items[i_]
                            z = i_ % 2
                            if i_ + 1 < len(items):
                                emit_S(i_ + 1)
                            for par in range(2):
                                k.op("act", lambda: S.activation(PT[z][:, par, :, :].rearrange("p t q -> p (t q)"), psb[par + 2 * z][:, 0:256], AF.Exp, scale=scale),
                                     reads=["psb%d" % (par + 2 * z), "PT%d" % z], writes=["PT%d" % z])
                            if dlt != 0:
                                msk = mask_ge if dlt == -1 else mask_le
                                mb = bass.AP(msk[:].tensor, msk[:].offset, [list(msk[:].ap[0]), [0, 4], list(msk[:].ap[1])])
                                k.op("dve", lambda: V.tensor_tensor(PT[z][:].rearrange("p a t q -> p (a t) q"), PT[z][:].rearrange("p a t q -> p (a t) q"), mb, ALU.mult),
                                     reads=["PT%d" % z, "mask_ge", "mask_le"], writes=["PT%d" % z])
                            tt = k0 // 128
                            for par in range(2):
                                k.op("pe", lambda: P.matmul(psb[4 + par][64 * par:64 * par + 64, 0:256], Vtm[:, tt, kh * 64:(kh + 1) * 64],
                                                            PT[z][:, par, :, :].rearrange("p t q -> p (t q)"), start=first, stop=last,
                                                            tile_position=(0, 64 * par)),
                                     reads=["Vtm", "PT%d" % z], writes=["psb%d" % (4 + par)], inc=last)
                                k.op("pe", lambda: P.matmul(psb[6 + par][64 * par:64 * par + 64, 0:256], ones64[:],
                                                            PT[z][:, par, :, :].rearrange("p t q -> p (t q)"), start=first, stop=last,
                                                            tile_position=(0, 64 * par)),
                                     reads=["ones64", "PT%d" % z], writes=["psb%d" % (6 + par)], inc=last)
                            if last:
                                for par in range(2):
                                    lo, hi = 64 * par, 64 * par + 64
                                    seb = bass.AP(SE[:].tensor, SE[lo:hi, kh, :].offset, [list(SE[lo:hi, kh, :].ap[0]), list(SE[lo:hi, kh, :].ap[1]), [0, 128]])
                                    k.op("dve", lambda: V.tensor_tensor(rden[lo:hi, :, :], psb[6 + par][lo:hi, 0:256].rearrange("p (t q) -> p t q", t=2), seb, ALU.add),
                                         reads=["psb%d" % (6 + par), "SE", "rden%d" % par], writes=["rden%d" % par])
                                    k.op("dve", lambda: V.reciprocal(rden[lo:hi, :, :], rden[lo:hi, :, :]), reads=["rden%d" % par], writes=["rden%d" % par])
                                    k.op("dve", lambda: V.tensor_tensor(mixT[lo:hi, 4 + 2 * kh:6 + 2 * kh, q0:q0 + 128], psb[4 + par][lo:hi, 0:256].rearrange("p (t q) -> p t q", t=2),
                                                                        rden[lo:hi, :, :], ALU.mult),
                                         reads=["psb%d" % (4 + par), "rden%d" % par, "mixT"], writes=["mixT"])
                        dump("dbg_mixT", mixT[:], ["mixT"])
                        k.barrier()
                    with contextlib.ExitStack() as po:
                        wo = k.sb(po, "wo", [128, 8, D], BF16)
                        for kt in range(8):
                            k.dma("pool", wo[:, kt, :], w_out_d[i][kt * 128:(kt + 1) * 128, :], writes=["wo"])
                        yo = [k.sb(po, "eyo%d" % z, [128, 8, 256], F32) for z in range(2)]
                        for blk in range(NBLK):
                            t0 = blk * TB
                            z = blk % 2
                            for ct in range(8):
                                pb, pk = psb[ct % 4], "psb%d" % (ct % 4)
                                for mt in range(8):
                                    k.op("pe", lambda: P.matmul(pb[:, 0:TB], wo[:, mt, ct * 128:(ct + 1) * 128], mixT[:, mt, t0:t0 + TB], start=(mt == 0), stop=(mt == 7)),
                                         reads=["wo", "mixT"], writes=[pk], inc=(mt == 7))
                                k.op("act", lambda: S.copy(yo[z][:, ct, :], pb[:, 0:TB]), reads=[pk, "eyo%d" % z], writes=["eyo%d" % z])
                            k.dma("sp", YTv[:, :, t0:t0 + TB], yo[z][:], reads=["eyo%d" % z], writes=["YT"])
                        k.barrier()
            if mixer and l % 2 == 1:
                s5_mixer(l)
            elif mixer:
                even_mixer(l)

            with contextlib.ExitStack() as ph:
                w1b = k.sb(ph, "w1b", [128, 8, DFF], BF16)
                w2b = k.sb(ph, "w2b", [128, 32, D], BF16)
                for kt in range(8):
                    k.dma("pool", w1b[:, kt, :], w1_d[l][kt * 128:(kt + 1) * 128, :], writes=["w1b"])
                for j4 in range(8):
                    k.dma("pool", w2b[:, j4 * 4:(j4 + 1) * 4, :],
                          w2_d[l][j4 * 512:(j4 + 1) * 512, :].rearrange("(j p) n -> p j n", p=128), writes=["w2b"])
                NXB = 3
                xbs = [k.sb(ph, "xb%d" % z, [128, 8, TB], F32) for z in range(NXB)]
                ybs = [k.sb(ph, "yb0", [128, 8, TB], F32)] * 2 if mixer else []
                sq = k.sb(ph, "sq", [128, 8, TB], F32)
                tmp = sq
                rs = k.sb(ph, "rs", [128, TB], F32)
                h2s = [k.sb(ph, "h2%d" % z, [128, 8, TB], BF16) for z in range(2)]
                ob = k.sb(ph, "ob", [128, 8, TB], F32)
                ar = [k.sb(ph, "ar%d" % z, [128, TB], F32) for z in range(2)]
                a2all = k.sb(ph, "a2all", [128, 32, TB], BF16)
                tiles = (sq, rs, psb[3], "psb3")

                def P_load(blk):
                    z = blk % NXB
                    t0 = blk * TB
                    k.dma("sp", xbs[z][:], XTv[:, :, t0:t0 + TB], reads=["XT"], writes=["xb%d" % z])

                def rms_thunks(src3, key_src):
                    pbank, pkey = psb[3], "psb3"
                    th = []
                    if src3 is not None:
                        th.append(lambda: k.op("act", lambda: S.activation(sq[:], src3, AF.Square), reads=[key_src], writes=["sq"]))
                    def mm():
                        for ct in range(8):
                            k.op("pe", lambda: P.matmul(pbank[:, 0:TB], onesm[:], sq[:, ct, :], start=(ct == 0), stop=(ct == 7)),
                                 reads=["sq", "onesm"], writes=[pkey], inc=(ct == 7))
                    th.append(mm)
                    th.append(lambda: k.op("dve", lambda: V.tensor_scalar(rs[:], pbank[:, 0:TB], EPS, None, ALU.add), reads=[pkey], writes=["rs"]))
                    th.append(lambda: k.op("act", lambda: S.activation(rs[:], rs[:], AF.Sqrt), reads=["rs"], writes=["rs"]))
                    th.append(lambda: k.op("dve", lambda: V.reciprocal(rs[:], rs[:]), reads=["rs"], writes=["rs"]))
                    return th

                def P_thunks(blk):
                    z = blk % NXB
                    xb, h2, xk, hk = xbs[z], h2s[blk % 2], "xb%d" % z, "h2%d" % (blk % 2)
                    j = 1 if blk == 0 else 0
                    th = []
                    if mixer:
                        yb, yk = ybs[0], "yb0"
                        t0 = blk * TB
                        th.append(lambda: k.dma("sp", yb[:], YTv[:, :, t0:t0 + TB], reads=["YT"], writes=[yk]))
                        th += rms_thunks(yb[:], yk)
                        th.append(lambda: k.op("dve", lambda: V.tensor_tensor(tmp[:], yb[:], bc_mid(rs[:], 8), ALU.mult), reads=[yk, "rs"], writes=["sq"]))
                        for ct in range(8):
                            th.append(lambda ct=ct: k.op("dve", lambda: V.scalar_tensor_tensor(out=xb[:, ct, :], in0=tmp[:, ct, :], scalar=PRM[:, 2, ct, j:j + 1],
                                                                                              in1=xb[:, ct, :], op0=ALU.mult, op1=ALU.add),
                                                         reads=["sq", xk, "PRM"], writes=[xk]))
                    th += rms_thunks(xb[:], xk)
                    th.append(lambda: k.op("dve", lambda: V.tensor_tensor(tmp[:], xb[:], bc_mid(rs[:], 8), ALU.mult), reads=[xk, "rs"], writes=["sq"]))
                    for ct in range(8):
                        th.append(lambda ct=ct: k.op("act", lambda: S.activation(h2[:, ct, :], tmp[:, ct, :], AF.Identity, bias=PRM[:, 4, ct, j:j + 1],
                                                                                 scale=PRM[:, 3, ct, j:j + 1]), reads=["sq", "PRM"], writes=[hk]))
                    return th

                def E_thunks(blk):
                    z = blk % NXB
                    xb, xk = xbs[z], "xb%d" % z
                    j = 1 if blk == 0 else 0
                    t0 = blk * TB
                    obk = ["ob_%d" % i_ for i_ in range(8)]
                    th = [lambda: k.op("act", lambda: S.activation(sq[:], ob[:], AF.Square), reads=obk, writes=["sq"])]
                    th += rms_thunks(None, "sq")
                    th.append(lambda: k.op("dve", lambda: V.tensor_tensor(tmp[:], ob[:], bc_mid(rs[:], 8), ALU.mult), reads=obk + ["rs"], writes=["sq"]))
                    for ct in range(8):
                        th.append(lambda ct=ct: k.op("dve", lambda: V.scalar_tensor_tensor(out=xb[:, ct, :], in0=tmp[:, ct, :], scalar=PRM[:, 5, ct, j:j + 1],
                                                                                          in1=xb[:, ct, :], op0=ALU.mult, op1=ALU.add),
                                                     reads=["sq", xk, "PRM"], writes=[xk]))
                    th.append(lambda: k.dma("pool", XTv[:, :, t0:t0 + TB], xb[:], reads=[xk], writes=["XT_st"]))
                    return th

                def M1_stage(blk, pending):
                    h2, hk = h2s[blk % 2], "h2%d" % (blk % 2)
                    for jf in range(32):
                        pa = psb[4 + jf % 4]
                        pak = "psb%d" % (4 + jf % 4)
                        for kt in range(8):
                            k.op("pe", lambda: P.matmul(pa[:, 0:TB], w1b[:, kt, jf * 128:(jf + 1) * 128], h2[:, kt, :],
                                                        start=(kt == 0), stop=(kt == 7)),
                                 reads=["w1b", hk], writes=[pak], inc=(kt == 7))
                        k.op("act", lambda: S.activation(ar[jf % 2][:], pa[:, 0:TB], AF.Relu), reads=[pak], writes=["ar%d" % (jf % 2)])
                        k.op("dve", lambda: V.tensor_tensor(a2all[:, jf, :], ar[jf % 2][:], ar[jf % 2][:], ALU.mult),
                             reads=["ar%d" % (jf % 2)], writes=["a2all"])
                        if pending and jf % 2 == 1:
                            pending.pop(0)()

                def M2_stage(blk, pending):
                    for ft in range(8):
                        po = psb[ft % 3]
                        pok = "psb%d" % (ft % 3)
                        for jf in range(32):
                            k.op("pe", lambda: P.matmul(po[:, 0:TB], w2b[:, jf, ft * 128:(ft + 1) * 128], a2all[:, jf, :],
                                                        start=(jf == 0), stop=(jf == 31)),
                                 reads=["w2b", "a2all"], writes=[pok], inc=(jf == 31))
                            if jf % 8 == 7 and pending:
                                pending.pop(0)()
                        k.op("act", lambda: S.copy(ob[:, ft, :], po[:, 0:TB]), reads=[pok], writes=["ob_%d" % ft])

                P_load(0)
                P_load(1)
                for t_ in P_thunks(0):
                    t_()
                for blk in range(NBLK):
                    pend1, pend2 = [], []
                    if blk >= 1:
                        pend1 += E_thunks(blk - 1)
                    if blk + 2 < NBLK:
                        pend1.append(lambda b_=blk + 2: P_load(b_))
                    if blk + 1 < NBLK:
                        pend2 += P_thunks(blk + 1)
                    M1_stage(blk, pend1)
                    pend2 = pend1 + pend2
                    M2_stage(blk, pend2)
                    while pend2:
                        pend2.pop(0)()
                for t_ in E_thunks(NBLK - 1):
                    t_()
                k.barrier()

        with contextlib.ExitStack() as ph:
            xf = [k.sb(ph, "xf%d" % i, [128, 8, 128], F32) for i in range(2)]
            xo = [k.sb(ph, "xo%d" % i, [128, D], F32) for i in range(2)]
            for tt in range(16):
                b = tt % 2
                k.dma("sp", xf[b][:], XTv[:, :, NCTX + tt * 128:NCTX + (tt + 1) * 128], reads=["XT"], writes=["xf%d" % b])
                for half in range(2):
                    pkey = "psb%d" % ((tt * 2 + half) % 8)
                    pb = psb[(tt * 2 + half) % 8]
                    for c4 in range(4):
                        ct = half * 4 + c4
                        k.op("pe", lambda: P.transpose(pb[:, c4 * 128:(c4 + 1) * 128], xf[b][:, ct, :], ident[:]),
                             reads=["xf%d" % b, "ident"], writes=[pkey], inc=(c4 == 3))
                    if half == 0:
                        k.op("act", lambda: S.copy(xo[b][:, 0:512], pb[:]), reads=[pkey], writes=["xo%d_0" % b])
                    else:
                        k.op("dve", lambda: V.tensor_copy(xo[b][:, 512:1024], pb[:]), reads=[pkey], writes=["xo%d_1" % b])
                k.dma("sp", out_d[tt * 128:(tt + 1) * 128, :], xo[b][:], reads=["xo%d_0" % b, "xo%d_1" % b], writes=["out"])
            k.barrier()
        print("ninst", k.ninst, "nwaits", k.nwaits)
    return nc


def make_in_maps(inp):
    gains = np.stack([inp["mix_pre_g"], inp["mix_post_g"], inp["ffn_pre_g"], inp["ffn_post_g"]], 0)
    maps = []
    for b in range(8):
        m = {
            "x": np.ascontiguousarray(inp["x"][b]), "ctx": np.ascontiguousarray(inp["ctx"][b]),
            "cc": np.ascontiguousarray(np.stack([inp["c"][b], inp["c_ctx"]], 0)),
            "mod_w": inp["mod_w"], "mod_b": inp["mod_b"], "gains": np.ascontiguousarray(gains),
            "ffn_w1": inp["ffn_w1"], "ffn_w2": inp["ffn_w2"],
            "ssm_a_re": inp["ssm_a_re"], "ssm_a_im": inp["ssm_a_im"], "ssm_log_dt": inp["ssm_log_dt"],
            "ssm_b_re": inp["ssm_b_re"], "ssm_b_im": inp["ssm_b_im"], "ssm_c_re": inp["ssm_c_re"], "ssm_c_im": inp["ssm_c_im"],
            "ssm_d": inp["ssm_d"], "ssm_glu_w": inp["ssm_glu_w"],
            "even_w_in": inp["even_w_in"], "even_w_out": inp["even_w_out"], "even_sink": inp["even_sink"],
        }
        maps.append(m)
    return maps


def kernel(**inp):
    inp = {k_: np.asarray(v) for k_, v in inp.items()}
    nc = build(mixer=MIXER_ENABLED)
    res = run_bass_kernel_spmd(nc, make_in_maps(inp), core_ids=list(range(8)))
    return np.stack([r["out"] for r in res.results], 0)
```

```python
import contextlib
import numpy as np
import concourse.bass as bass
import concourse.mybir as mybir
from concourse.bass_utils import run_bass_kernel_spmd

F32 = mybir.dt.float32
BF16 = mybir.dt.bfloat16
I32 = mybir.dt.int32
ALU = mybir.AluOpType
AF = mybir.ActivationFunctionType
AX = mybir.AxisListType

S5_STAGE = 2
S5_POOL = False
S5_SUB = 2
S5_NW = None
MIXER_ENABLED = True
SAME_ENGINE_SYNC = {"dve": True, "act": True, "pool": True, "pe": False}
DMA_RING = 8


class KB:
    def __init__(self, nc, es):
        self.nc = nc
        self.es = es
        self.raw = {"pe": nc.tensor, "dve": nc.vector, "act": nc.scalar, "pool": nc.gpsimd, "sp": nc.sync}
        self.sem = {}
        self.cnt = {}
        for e in ("pe", "dve", "act", "pool"):
            self.sem[e] = es.enter_context(nc.semaphore("s_" + e))
            self.cnt[e] = 0
        self.dring = {}
        self.dcnt = {}
        for q in ("sp", "act", "pool"):
            self.dring[q] = [es.enter_context(nc.semaphore("d_%s%d" % (q, i))) for i in range(DMA_RING)]
            self.dcnt[q] = 0
        self.bar_sem = es.enter_context(nc.semaphore("s_bar"))
        self.seen = {e: {} for e in self.raw}
        self.lastw = {}
        self.readers = {}
        self.nwaits = 0
        self.ninst = 0

    def sb(self, es, name, shape, dt):
        self.uid = getattr(self, "uid", 0) + 1
        return es.enter_context(self.nc.sbuf_tensor("%s_u%d" % (name, self.uid), list(shape), dt))

    def ps(self, es, name, shape, dt=F32):
        return es.enter_context(self.nc.psum_tensor(name, list(shape), dt))

    def _collect(self, eng, reads, writes):
        need = {}

        def add(tok):
            if tok is None:
                return
            s, v, src = tok
            if src == eng and (not SAME_ENGINE_SYNC.get(eng, True) or v > self.cnt[eng]):
                return
            if need.get(s, (0,))[0] < v:
                need[s] = (v, src)

        def add_lw(key):
            lw = self.lastw.get(key)
            if isinstance(lw, list):
                for t_ in lw:
                    add(t_)
            else:
                add(lw)

        for r in reads:
            add_lw(r)
        for w in writes:
            add_lw(w)
            for tok in self.readers.get(w, ()):
                add(tok)
        return need

    def _emit_waits(self, eng, need):
        seen = self.seen[eng]
        for s, (v, src) in need.items():
            if seen.get(s, 0) >= v:
                continue
            self.raw[eng].wait_ge(s, v)
            seen[s] = v
            self.nwaits += 1

    def _record(self, tok, reads, writes):
        is_dma = str(tok[2]).startswith("dma_")
        for w in writes:
            prev = self.lastw.get(w)
            if is_dma and prev is not None and not self.readers.get(w):
                plist = prev if isinstance(prev, list) else [prev]
                if all(str(p_[2]).startswith("dma_") for p_ in plist):
                    self.lastw[w] = (plist + [tok])[-2 * DMA_RING:]
                    continue
            self.lastw[w] = tok
            self.readers[w] = []
        for r in reads:
            if r in writes:
                continue
            self.readers.setdefault(r, []).append(tok)

    def op(self, eng, fn, reads=(), writes=(), inc=True):
        need = self._collect(eng, reads, writes)
        self._emit_waits(eng, need)
        inst = fn()
        self.ninst += 1
        if inc:
            self.cnt[eng] += 1
            inst.then_inc(self.sem[eng], 1)
            tok = (self.sem[eng], self.cnt[eng], eng)
        else:
            tok = (self.sem[eng], self.cnt[eng] + 1, eng)
        self._record(tok, reads, writes)
        return inst

    def dma(self, q, out, in_, reads=(), writes=(), **kw):
        need = self._collect(q, reads, writes)
        self._emit_waits(q, need)
        i = self.dcnt[q]
        self.dcnt[q] += 1
        s = self.dring[q][i % DMA_RING]
        v = 16 * (i // DMA_RING + 1)
        inst = self.raw[q].dma_start(out=out, in_=in_, **kw)
        inst.then_inc(s, 16)
        self.ninst += 1
        self._record((s, v, "dma_" + q), reads, writes)
        return inst

    def barrier(self):
        need = {}
        for e in ("pe", "dve", "act", "pool"):
            if self.cnt[e] > 0:
                need[self.sem[e]] = (self.cnt[e], "x")
        for q in ("sp", "act", "pool"):
            n = self.dcnt[q]
            for r in range(DMA_RING):
                cntr = (n - r + DMA_RING - 1) // DMA_RING if n > r else 0
                if cntr > 0:
                    need[self.dring[q][r]] = (16 * cntr, "x")
        for e in ("pe", "dve", "act", "pool", "sp"):
            self._emit_waits(e, need)
        self.nbar = getattr(self, "nbar", 0) + 1
        for e in ("pe", "dve", "act", "pool", "sp"):
            self.raw[e].sem_inc(self.bar_sem, 1)
        for e in ("pe", "dve", "act", "pool", "sp"):
            self.raw[e].wait_ge(self.bar_sem, 5 * self.nbar)
        self.lastw = {}
        self.readers = {}


def bc_mid(ap2, n):
    a = ap2.ap
    return bass.AP(ap2.tensor, ap2.offset, [list(a[0]), [0, n]] + [list(x) for x in a[1:]])


D = 1024
NT = 2304
NCTX = 256
TB = 256
NBLK = NT // TB
DFF = 4096
EPS = 1e-6


def build(nlayers=4, mixer=True, layers=None, debug=False):
    nc = bass.Bass("TRN2", target_bir_lowering=False)
    dt_in = lambda n, s: nc.dram_tensor(n, list(s), F32, kind="ExternalInput").ap()
    x_d = dt_in("x", [2048, D])
    ctx_d = dt_in("ctx", [NCTX, D])
    cc_d = dt_in("cc", [2, D])
    mod_w_d = dt_in("mod_w", [4, D, 6 * D])
    mod_b_d = dt_in("mod_b", [4, 6 * D])
    gains_d = dt_in("gains", [4, 4, D])
    w1_d = dt_in("ffn_w1", [4, D, DFF])
    w2_d = dt_in("ffn_w2", [4, DFF, D])
    w_in_d = dt_in("even_w_in", [2, D, 1280])
    w_out_d = dt_in("even_w_out", [2, D, D])
    even_sink_d = dt_in("even_sink", [2, 8])
    a_re_d = dt_in("ssm_a_re", [2, 2, 64, 64])
    a_im_d = dt_in("ssm_a_im", [2, 2, 64, 64])
    ldt_d = dt_in("ssm_log_dt", [2, 2, 64])
    b_re_d = dt_in("ssm_b_re", [2, 2, 64, 64, 16])
    b_im_d = dt_in("ssm_b_im", [2, 2, 64, 64, 16])
    c_re_d = dt_in("ssm_c_re", [2, 2, 64, 16, 64])
    c_im_d = dt_in("ssm_c_im", [2, 2, 64, 16, 64])
    dsk_d = dt_in("ssm_d", [2, D])
    glu_d = dt_in("ssm_glu_w", [2, D, 2 * D])
    out_d = nc.dram_tensor("out", [2048, D], F32, kind="ExternalOutput").ap()
    YF = nc.dram_tensor("YF", [2, D, NT], F32, kind=("ExternalOutput" if debug else "Internal")).ap()
    XT = nc.dram_tensor("XT", [D, NT], F32).ap()
    YT = nc.dram_tensor("YT", [D, NT], F32, kind=("ExternalOutput" if debug else "Internal")).ap()
    CL = nc.dram_tensor("CLtab", [2048, 2048], BF16, kind=("ExternalOutput" if debug else "Internal")).ap()
    SLn = nc.dram_tensor("SLtab", [2048, 2048], BF16, kind=("ExternalOutput" if debug else "Internal")).ap()
    ROPC = nc.dram_tensor("ROPC", [128, 2048], F32, kind=("ExternalOutput" if debug else "Internal")).ap()
    ROPS = nc.dram_tensor("ROPS", [128, 2048], F32, kind=("ExternalOutput" if debug else "Internal")).ap()
    XTv = XT.rearrange("(ct p) t -> p ct t", p=128)
    YTv = YT.rearrange("(ct p) t -> p ct t", p=128)

    with contextlib.ExitStack() as es:
        k = KB(nc, es)
        V, S, P, G = nc.vector, nc.scalar, nc.tensor, nc.gpsimd

        def dump(name, ap, keys):
            if not debug:
                return
            dt_ = nc.dram_tensor(name, list(ap.shape), ap.dtype, kind="ExternalOutput").ap()
            k.dma("sp", dt_, ap, reads=keys, writes=["dbg_" + name])
        ident = k.sb(es, "ident", [128, 128], F32)
        onesm = k.sb(es, "onesm", [128, 128], F32)
        k.op("pool", lambda: G.memset(ident[:], 0.0), writes=["ident"])
        k.op("pool", lambda: G.affine_select(out=ident[:], in_=ident[:], compare_op=ALU.not_equal, fill=1.0,
                                             base=0, pattern=[[-1, 128]], channel_multiplier=1),
             reads=["ident"], writes=["ident"])
        k.op("pool", lambda: G.memset(onesm[:], 1.0 / D), writes=["onesm"])
        psb = [k.ps(es, "psb%d" % i, [128, 512], F32) for i in range(8)]
        MV = k.sb(es, "MV", [128, 6, 8, 2], F32)
        GN = k.sb(es, "GN", [128, 4, 4, 8], F32)
        SC = k.sb(es, "SC", [128, 8, 2], F32)
        SCb = k.sb(es, "SCb", [128, 8, 2], BF16)
        PRM = k.sb(es, "PRM", [128, 6, 8, 2], F32)
        k.dma("sp", GN[:].rearrange("p a l c -> p (a l) c"),
              gains_d.rearrange("a l (c p) -> p (a l) c", p=128), writes=["GN"], allow_slow_non_contiguous=True)
        for j in range(2):
            k.dma("sp", SC[:, :, j], cc_d[j].rearrange("(c p) -> p c", p=128), writes=["SC"], allow_slow_non_contiguous=True)
        k.op("act", lambda: S.activation(SCb[:], SC[:], AF.Silu), reads=["SC"], writes=["SCb"])


        MAGIC = 12582912.0
        TWO_PI = float(2 * np.pi)
        if mixer and any(l % 2 == 0 for l in (layers if layers is not None else range(nlayers))):
            with contextlib.ExitStack() as ph:
                cidx = k.sb(ph, "cidx", [128, 2048], F32)
                prow = k.sb(ph, "prow", [128, 1], F32)
                tcol = k.sb(ph, "tcol", [128, 1], F32)
                k.op("pool", lambda: G.iota(cidx[:], pattern=[[1, 2048]], base=0, channel_multiplier=0, allow_small_or_imprecise_dtypes=True), writes=["cidx"])
                k.op("pool", lambda: G.iota(prow[:], pattern=[[0, 1]], base=0, channel_multiplier=1, allow_small_or_imprecise_dtypes=True), writes=["prow"])
                uu = [k.sb(ph, "uu%d" % z, [128, 2048], F32) for z in range(2)]
                nn = [k.sb(ph, "nn%d" % z, [128, 2048], F32) for z in range(2)]
                n2 = [k.sb(ph, "nq%d" % z, [128, 2048], F32) for z in range(2)]
                ff = [k.sb(ph, "ff%d" % z, [128, 2048], F32) for z in range(2)]
                tb = [k.sb(ph, "tb%d" % z, [128, 2048], BF16) for z in range(2)]
                tcs = [k.sb(ph, "tcs%d" % z, [128, 1], F32) for z in range(2)]
                SCL = TWO_PI * (1.0 - 2e-6)
                for tt in range(16):
                    tc_ = tcs[tt % 2]
                    k.op("dve", lambda: V.tensor_scalar(tc_[:], prow[:], float(tt * 128), 1.0 / 2048, ALU.add, ALU.mult), reads=["prow", "tcs%d" % (tt % 2)], writes=["tcs%d" % (tt % 2)])
                    for z, (tab, shift, scl) in enumerate(((CL, 0.25, SCL), (SLn, 0.0, -SCL))):
                        k.op("act", lambda: S.activation(uu[z][:], cidx[:], AF.Identity, bias=shift, scale=tc_[:, 0:1]), reads=["cidx", "tcs%d" % (tt % 2), "uu%d" % z], writes=["uu%d" % z])
                        k.op("act", lambda: S.activation(nn[z][:], uu[z][:], AF.Identity, bias=MAGIC, scale=1.0), reads=["uu%d" % z, "nn%d" % z], writes=["nn%d" % z])
                        k.op("act", lambda: S.activation(n2[z][:], nn[z][:], AF.Identity, bias=-MAGIC, scale=1.0), reads=["nn%d" % z, "nq%d" % z], writes=["nq%d" % z])
                        k.op("dve", lambda: V.tensor_tensor(ff[z][:], uu[z][:], n2[z][:], ALU.subtract), reads=["uu%d" % z, "nq%d" % z, "ff%d" % z], writes=["ff%d" % z])
                        k.op("act", lambda: S.activation(tb[z][:], ff[z][:], AF.Sin, scale=scl), reads=["ff%d" % z, "tb%d" % z], writes=["tb%d" % z])
                        k.dma("sp", tab[tt * 128:(tt + 1) * 128, :], tb[z][:], reads=["tb%d" % z], writes=["TAB"])
                k.barrier()
            with contextlib.ExitStack() as ph:
                pidx = k.sb(ph, "rpidx", [128, 1], I32)
                pi2 = k.sb(ph, "rpi2", [128, 1], I32)
                fi = k.sb(ph, "rfi", [128, 1], F32)
                invp = k.sb(ph, "rinvp", [128, 1], F32)
                axs = k.sb(ph, "raxs", [128, 1], F32)
                sgn = k.sb(ph, "rsgn", [128, 1], F32)
                k.op("pool", lambda: G.iota(pidx[:], pattern=[[0, 1]], base=0, channel_multiplier=1), writes=["pidx"])
                k.op("dve", lambda: V.tensor_scalar(pi2[:], pidx[:], 15, None, ALU.bitwise_and), reads=["pidx"], writes=["pi2"])
                k.op("dve", lambda: V.tensor_copy(fi[:], pi2[:]), reads=["pi2"], writes=["fi"])
                k.op("act", lambda: S.activation(invp[:], fi[:], AF.Exp, scale=-float(np.log(10000.0)) / 16.0), reads=["fi"], writes=["invp"])
                k.op("dve", lambda: V.tensor_scalar(pi2[:], pidx[:], 5, 1, ALU.arith_shift_right, ALU.bitwise_and), reads=["pidx", "fi"], writes=["pi2"])
                k.op("dve", lambda: V.tensor_copy(axs[:], pi2[:]), reads=["pi2"], writes=["axs"])
                k.op("dve", lambda: V.tensor_scalar(pi2[:], pidx[:], 4, 1, ALU.arith_shift_right, ALU.bitwise_and), reads=["pidx", "axs"], writes=["pi2"])
                k.op("dve", lambda: V.tensor_copy(sgn[:], pi2[:]), reads=["pi2"], writes=["sgn"])
                k.op("dve", lambda: V.tensor_scalar(sgn[:], sgn[:], 2.0, -1.0, ALU.mult, ALU.add), reads=["sgn"], writes=["sgn"])
                rowp = k.sb(ph, "rowp", [128, 2048], F32)
                colp = k.sb(ph, "colp", [128, 2048], F32)
                ang = k.sb(ph, "rang", [128, 2048], F32)
                u_ = k.sb(ph, "ru", [128, 2048], F32)
                n_ = k.sb(ph, "rn", [128, 2048], F32)
                k.op("pool", lambda: G.iota(rowp[:], pattern=[[1, 32], [0, 64]], base=0, channel_multiplier=0, allow_small_or_imprecise_dtypes=True), writes=["rowp"])
                k.op("pool", lambda: G.iota(colp[:], pattern=[[0, 32], [1, 64]], base=0, channel_multiplier=0, allow_small_or_imprecise_dtypes=True), writes=["colp"])
                k.op("dve", lambda: V.tensor_tensor(colp[:], colp[:], rowp[:], ALU.subtract), reads=["colp", "rowp"], writes=["colp"])
                k.op("dve", lambda: V.scalar_tensor_tensor(out=ang[:], in0=colp[:], scalar=axs[:, 0:1], in1=rowp[:], op0=ALU.mult, op1=ALU.add), reads=["colp", "rowp", "axs"], writes=["ang"])
                k.op("dve", lambda: V.tensor_scalar(ang[:], ang[:], invp[:, 0:1], 1.0 / TWO_PI, ALU.mult, ALU.mult), reads=["ang", "invp"], writes=["ang"])
                for (tab, shift) in ((ROPC, 0.25), (ROPS, 0.0)):
                    k.op("dve", lambda: V.tensor_scalar(u_[:], ang[:], shift, None, ALU.add), reads=["ang", "ru"], writes=["ru"])
                    k.op("dve", lambda: V.tensor_scalar(n_[:], u_[:], MAGIC, None, ALU.add), reads=["ru", "rn"], writes=["rn"])
                    k.op("dve", lambda: V.tensor_scalar(n_[:], n_[:], -MAGIC, None, ALU.add), reads=["rn"], writes=["rn"])
                    k.op("dve", lambda: V.tensor_tensor(u_[:], u_[:], n_[:], ALU.subtract), reads=["rn", "ru"], writes=["ru"])
                    k.op("dve", lambda: V.tensor_scalar(u_[:], u_[:], -0.49999, 0.49999, ALU.max, ALU.min), reads=["ru"], writes=["ru"])
                    k.op("act", lambda: S.activation(n_[:], u_[:], AF.Sin, scale=TWO_PI), reads=["ru", "rn"], writes=["rn"])
                    if shift == 0.0:
                        k.op("dve", lambda: V.tensor_scalar(n_[:], n_[:], sgn[:, 0:1], None, ALU.mult), reads=["rn", "sgn"], writes=["rn"])
                    k.dma("sp", tab[:, :], n_[:], reads=["rn"], writes=["ROP"])
                k.barrier()
        mask_ge = k.sb(es, "mask_ge", [128, 128], BF16)
        mask_le = k.sb(es, "mask_le", [128, 128], BF16)
        ones64 = k.sb(es, "ones64", [128, 64], BF16)
        k.op("pool", lambda: G.memset(mask_ge[:], 1.0), writes=["mask_ge"])
        k.op("pool", lambda: G.affine_select(out=mask_ge[:], in_=mask_ge[:], compare_op=ALU.is_ge, fill=0.0, base=0, pattern=[[-1, 128]], channel_multiplier=1), reads=["mask_ge"], writes=["mask_ge"])
        k.op("pool", lambda: G.memset(mask_le[:], 1.0), writes=["mask_le"])
        k.op("pool", lambda: G.affine_select(out=mask_le[:], in_=mask_le[:], compare_op=ALU.is_ge, fill=0.0, base=0, pattern=[[1, 128]], channel_multiplier=-1), reads=["mask_le"], writes=["mask_le"])
        k.op("pool", lambda: G.memset(ones64[:], 1.0), writes=["ones64"])
        with contextlib.ExitStack() as ph:
            xin = [k.sb(ph, "xin%d" % i, [128, D], F32) for i in range(2)]
            xst = [k.sb(ph, "xst%d" % i, [128, 8, 128], F32) for i in range(2)]
            for tt in range(NT // 128):
                b = tt % 2
                src = ctx_d[tt * 128:(tt + 1) * 128, :] if tt < 2 else x_d[(tt - 2) * 128:(tt - 1) * 128, :]
                k.dma("sp", xin[b][:], src, writes=["xin%d" % b])
                for half in range(2):
                    pb = psb[(tt * 2 + half) % 8]
                    for c4 in range(4):
                        ct = half * 4 + c4
                        k.op("pe", lambda: P.transpose(pb[:, c4 * 128:(c4 + 1) * 128], xin[b][:, ct * 128:(ct + 1) * 128], ident[:]),
                             reads=["xin%d" % b, "ident"], writes=["psb%d" % ((tt * 2 + half) % 8)], inc=(c4 == 3))
                    eng = "act" if half == 0 else "dve"
                    if eng == "act":
                        k.op("act", lambda: S.copy(xst[b][:, half * 4:(half + 1) * 4, :].rearrange("p c t -> p (c t)"), pb[:]),
                             reads=["psb%d" % ((tt * 2 + half) % 8)], writes=["xst%d_%d" % (b, half)])
                    else:
                        k.op("dve", lambda: V.tensor_copy(xst[b][:, half * 4:(half + 1) * 4, :].rearrange("p c t -> p (c t)"), pb[:]),
                             reads=["psb%d" % ((tt * 2 + half) % 8)], writes=["xst%d_%d" % (b, half)])
                k.dma("sp", XTv[:, :, tt * 128:(tt + 1) * 128], xst[b][:], reads=["xst%d_0" % b, "xst%d_1" % b], writes=["XT"])
            k.barrier()

        for l in (layers if layers is not None else range(nlayers)):
            with contextlib.ExitStack() as ph:
                mw = [k.sb(ph, "mw%d" % i, [128, 8, 512], BF16) for i in range(2)]
                mbias = k.sb(ph, "mbias", [128, 48], F32)
                k.dma("sp", mbias[:], mod_b_d[l].rearrange("(c p) -> p c", p=128), writes=["mbias"], allow_slow_non_contiguous=True)
                for ch in range(12):
                    b = ch % 2
                    k.dma("pool", mw[b][:], mod_w_d[l][:, ch * 512:(ch + 1) * 512].rearrange("(kt p) n -> p kt n", p=128),
                          writes=["mw%d" % b])
                    for s4 in range(4):
                        col = ch * 4 + s4
                        pb = psb[col % 8]
                        for kt in range(8):
                            k.op("pe", lambda: P.matmul(pb[:, 0:2], mw[b][:, kt, s4 * 128:(s4 + 1) * 128], SCb[:, kt, :],
                                                        start=(kt == 0), stop=(kt == 7)),
                                 reads=["mw%d" % b, "SCb"], writes=["psb%d" % (col % 8)], inc=(kt == 7))
                        k.op("dve", lambda: V.tensor_scalar(MV[:, col // 8, col % 8, :], pb[:, 0:2], mbias[:, col:col + 1], None, ALU.add),
                             reads=["psb%d" % (col % 8), "mbias"], writes=["MV"])
                for (o, isc, ish, ig, gpre, gpost) in ((0, 1, 0, 2, 0, 1), (3, 4, 3, 5, 2, 3)):
                    for j in range(2):
                        k.op("dve", lambda: V.scalar_tensor_tensor(out=PRM[:, o, :, j], in0=MV[:, isc, :, j], scalar=1.0, in1=GN[:, gpre, l, :],
                                                                   op0=ALU.add, op1=ALU.mult), reads=["MV", "GN"], writes=["PRM"])
                        k.op("dve", lambda: V.tensor_copy(PRM[:, o + 1, :, j], MV[:, ish, :, j]), reads=["MV"], writes=["PRM"])
                        k.op("dve", lambda: V.tensor_tensor(PRM[:, o + 2, :, j], MV[:, ig, :, j], GN[:, gpost, l, :], ALU.mult),
                             reads=["MV", "GN"], writes=["PRM"])
                k.barrier()

            def rms_bc(ph_tiles, src3, key_src, tag):
                sq, rs, pbank, pkey = ph_tiles
                if src3 is not None:
                    k.op("act", lambda: S.activation(sq[:], src3, AF.Square), reads=[key_src], writes=["sq"])
                for ct in range(8):
                    k.op("pe", lambda: P.matmul(pbank[:, 0:TB], onesm[:], sq[:, ct, :], start=(ct == 0), stop=(ct == 7)),
                         reads=["sq", "onesm"], writes=[pkey], inc=(ct == 7))
                k.op("dve", lambda: V.tensor_scalar(rs[:], pbank[:, 0:TB], EPS, None, ALU.add), reads=[pkey], writes=["rs"])
                k.op("act", lambda: S.activation(rs[:], rs[:], AF.Sqrt), reads=["rs"], writes=["rs"])
                k.op("dve", lambda: V.reciprocal(rs[:], rs[:]), reads=["rs"], writes=["rs"])
                return rs


            def prenorm_to_hT(ph, hT, off=0):
                xbb = [k.sb(ph, "pxb%d" % z_, [128, 8, TB], F32) for z_ in range(2)]
                sq = k.sb(ph, "psq", [128, 8, TB], F32)
                tmp = k.sb(ph, "ptmp", [128, 8, TB], F32)
                rs = k.sb(ph, "prs", [128, TB], F32)
                tiles = (sq, rs, psb[6], "psb6")
                k.dma("sp", xbb[0][:], XTv[:, :, 0:TB], reads=["XT"], writes=["pxb0"])
                for blk in range(NBLK):
                    j = 1 if blk == 0 else 0
                    t0 = blk * TB
                    xb, xkk = xbb[blk % 2], "pxb%d" % (blk % 2)
                    if blk + 1 < NBLK:
                        k.dma("sp", xbb[(blk + 1) % 2][:], XTv[:, :, t0 + TB:t0 + 2 * TB], reads=["XT"], writes=["pxb%d" % ((blk + 1) % 2)])
                    rms_bc(tiles, xb[:], xkk, "p")
                    k.op("dve", lambda: V.tensor_tensor(tmp[:], xb[:], bc_mid(rs[:], 8), ALU.mult), reads=[xkk, "rs"], writes=["tmp"])
                    for ct in range(8):
                        k.op("act", lambda: S.activation(hT[:, ct, off + t0:off + t0 + TB], tmp[:, ct, :], AF.Identity, bias=PRM[:, 1, ct, j:j + 1],
                                                         scale=PRM[:, 0, ct, j:j + 1]), reads=["tmp", "PRM"], writes=["hT"])

            def s5_mixer(l):
                i = l // 2
                PI = float(np.pi)
                W = 64
                NW = NT // W
                with contextlib.ExitStack() as ph:
                    T1 = 4
                    PAD = T1 - 1
                    hT = k.sb(ph, "hT", [128, 8, NT + NCTX + 2 * PAD], BF16)
                    k.op("pool", lambda: G.memset(hT[:, :, 0:PAD], 0.0), writes=["hT"])
                    k.op("pool", lambda: G.memset(hT[:, :, PAD + NT + NCTX:], 0.0), reads=["hT"], writes=["hT"])
                    with contextlib.ExitStack() as ph2:
                        prenorm_to_hT(ph2, hT, off=PAD)
                        k.barrier()
                    dsk = k.sb(ph, "dsk", [128, 8], F32)
                    k.dma("sp", dsk[:], dsk_d[i].rearrange("(c p) -> p c", p=128), writes=["dsk"], allow_slow_non_contiguous=True)
                    sc = contextlib.ExitStack()
                    k.op("act", lambda: S.copy(hT[:, :, PAD + NT:PAD + NT + NCTX], hT[:, :, PAD:PAD + NCTX]), reads=["hT"], writes=["hT"])
                    BOFF = PAD + NCTX
                    WinT = [[[k.sb(sc, "WinT%d%d%d" % (d, ri, ti), [128, 8, 128], BF16) for ti in range(T1)] for ri in range(2)] for d in range(2)]
                    CwQ = [[k.sb(sc, "CwQ%d%d" % (d, ri), [128, 32, 32], BF16) for ri in range(2)] for d in range(2)]
                    Gt = [k.sb(sc, "Gt%d" % d, [128, 2, 64], F32) for d in range(2)]
                    with contextlib.ExitStack() as pp:
                        def t32(n):
                            return k.sb(pp, n, [128, 32], F32)
                        twopi = t32("twopi")
                        k.op("pool", lambda: G.memset(twopi[:], 2 * PI), writes=["twopi"])
                        pidx = k.sb(pp, "pidx", [128, 1], I32)
                        modd = k.sb(pp, "modd", [128, 1], F32)
                        mevn = k.sb(pp, "mevn", [128, 1], F32)
                        nodd = k.sb(pp, "nodd", [128, 1], F32)
                        nevn = k.sb(pp, "nevn", [128, 1], F32)
                        k.op("pool", lambda: G.iota(pidx[:], pattern=[[0, 1]], base=0, channel_multiplier=1), writes=["pidx"])
                        k.op("dve", lambda: V.tensor_scalar(pidx[:], pidx[:], 4, 1, ALU.arith_shift_right, ALU.bitwise_and), reads=["pidx"], writes=["pidx"])
                        k.op("dve", lambda: V.tensor_copy(modd[:], pidx[:]), reads=["pidx"], writes=["modd"])
                        k.op("dve", lambda: V.tensor_scalar(mevn[:], modd[:], -1.0, 1.0, ALU.mult, ALU.add), reads=["modd"], writes=["mevn"])
                        k.op("dve", lambda: V.tensor_scalar(nodd[:], modd[:], -1.0, None, ALU.mult), reads=["modd"], writes=["nodd"])
                        k.op("dve", lambda: V.tensor_scalar(nevn[:], mevn[:], -1.0, None, ALU.mult), reads=["mevn"], writes=["nevn"])
                        Br = k.sb(pp, "Br", [128, 32, 32], F32)
                        Bi = k.sb(pp, "Bi", [128, 32, 32], F32)
                        BbR = k.sb(pp, "BbR", [128, 32, 32], F32)
                        BbI = k.sb(pp, "BbI", [128, 32, 32], F32)
                        T1t = k.sb(pp, "T1t", [128, 32, 32], F32)
                        TbR = k.sb(pp, "TbR", [128, 32, 32], F32)
                        TbI = k.sb(pp, "TbI", [128, 32, 32], F32)
                        Cn = k.sb(pp, "Cn", [128, 8, 64], F32)
                        Cblk = k.sb(pp, "Cblk", [128, 8, 128], F32)
                        lr, li, dtt, tq, mag, ang, sa, sinv, cosv, Ar, Ai, am1, n2, kr, ki, u1 = [t32("p%d" % z) for z in range(16)]
                        for d in range(2):
                            def dve(fn, r, w):
                                k.op("dve", fn, reads=r, writes=w)
                            def act(fn, r, w):
                                k.op("act", fn, reads=r, writes=w)
                            k.dma("sp", lr[:], a_re_d[i, d].rearrange("(q a) p -> (a p) q", a=2), writes=["lr"], allow_slow_non_contiguous=True)
                            k.dma("sp", li[:], a_im_d[i, d].rearrange("(q a) p -> (a p) q", a=2), writes=["li"], allow_slow_non_contiguous=True)
                            for g2 in range(2):
                                base = ldt_d[i, d]
                                src = bass.AP(base.tensor, base.offset + g2, [[0, 64], [2, 32]])
                                k.dma("sp", dtt[g2 * 64:(g2 + 1) * 64, :], src, writes=["dtt"], allow_slow_non_contiguous=True)
                            act(lambda: S.activation(dtt[:], dtt[:], AF.Exp), ["dtt"], ["dtt"])
                            dve(lambda: V.tensor_tensor(tq[:], lr[:], dtt[:], ALU.mult), ["lr", "dtt"], ["tq"])
                            act(lambda: S.activation(mag[:], tq[:], AF.Exp), ["tq"], ["mag"])
                            dve(lambda: V.tensor_tensor(ang[:], li[:], dtt[:], ALU.mult), ["li", "dtt"], ["ang"])
                            MAGIC = 12582912.0
                            PIC = 3.1415925
                            for (dst, shift, tag) in ((sinv, 0.0, "s"), (cosv, 0.5 * PI, "c")):
                                dve(lambda: V.tensor_scalar(u1[:], ang[:], shift, None, ALU.add), ["ang", "sa", "u1"], ["u1"])
                                dve(lambda: V.tensor_scalar(sa[:], u1[:], 1.0 / (2 * PI), None, ALU.mult), ["u1", "sa"], ["sa"])
                                dve(lambda: V.tensor_scalar(sa[:], sa[:], MAGIC, None, ALU.add), ["sa"], ["sa"])
                                dve(lambda: V.tensor_scalar(sa[:], sa[:], -MAGIC, None, ALU.add), ["sa"], ["sa"])
                                dve(lambda: V.scalar_tensor_tensor(out=sa[:], in0=sa[:], scalar=-2 * PI, in1=u1[:], op0=ALU.mult, op1=ALU.add), ["sa", "u1"], ["sa"])
                                dve(lambda: V.tensor_scalar(sa[:], sa[:], -PIC, PIC, ALU.max, ALU.min), ["sa"], ["sa"])
                                act(lambda: S.activation(dst[:], sa[:], AF.Sin), ["sa"], ["sinv" if tag == "s" else "cosv"])
                            dve(lambda: V.tensor_tensor(Ar[:], mag[:], cosv[:], ALU.mult), ["mag", "cosv"], ["Ar"])
                            dve(lambda: V.tensor_tensor(Ai[:], mag[:], sinv[:], ALU.mult), ["mag", "sinv"], ["Ai"])
                            gk = "Gt%d" % d
                            pws = [(None, None), (Ar, Ai)]
                            for pi_ in range(2, T1 + 1):
                                pr_ = t32("pwr%d_%d" % (d, pi_)); pim_ = t32("pwi%d_%d" % (d, pi_))
                                qr_, qi_ = pws[pi_ - 1]
                                kq = ["pw%d" % (pi_ - 1), "Ar", "Ai"]
                                dve(lambda: V.tensor_tensor(pr_[:], qr_[:], Ar[:], ALU.mult), kq, ["pw%d" % pi_])
                                dve(lambda: V.tensor_tensor(u1[:], qi_[:], Ai[:], ALU.mult), kq + ["u1"], ["u1"])
                                dve(lambda: V.tensor_tensor(pr_[:], pr_[:], u1[:], ALU.subtract), ["pw%d" % pi_, "u1"], ["pw%d" % pi_])
                                dve(lambda: V.tensor_tensor(pim_[:], qr_[:], Ai[:], ALU.mult), kq + ["pw%d" % pi_], ["pw%d" % pi_])
                                dve(lambda: V.tensor_tensor(u1[:], qi_[:], Ar[:], ALU.mult), kq + ["u1"], ["u1"])
                                dve(lambda: V.tensor_tensor(pim_[:], pim_[:], u1[:], ALU.add), ["pw%d" % pi_, "u1"], ["pw%d" % pi_])
                                pws.append((pr_, pim_))
                            AT_r, AT_i = pws[T1]
                            Arp = AT_r[:].rearrange("p (c r) -> p r c", r=4)
                            Aip = AT_i[:].rearrange("p (c r) -> p r c", r=4)
                            gv = lambda a, lo: Gt[d][:, a, lo:lo + 32].rearrange("p (r c) -> p r c", r=4)
                            dve(lambda: V.tensor_copy(gv(0, 0), Arp), ["pw%d" % T1], [gk])
                            dve(lambda: V.tensor_scalar(gv(0, 32), Aip, -1.0, None, ALU.mult), ["pw%d" % T1, gk], [gk])
                            dve(lambda: V.tensor_copy(gv(1, 0), Aip), ["pw%d" % T1, gk], [gk])
                            dve(lambda: V.tensor_copy(gv(1, 32), Arp), ["pw%d" % T1, gk], [gk])
                            dve(lambda: V.tensor_scalar(am1[:], Ar[:], -1.0, None, ALU.add), ["Ar"], ["am1"])
                            dve(lambda: V.tensor_tensor(n2[:], lr[:], lr[:], ALU.mult), ["lr"], ["n2"])
                            dve(lambda: V.tensor_tensor(u1[:], li[:], li[:], ALU.mult), ["li", "u1"], ["u1"])
                            dve(lambda: V.tensor_tensor(n2[:], n2[:], u1[:], ALU.add), ["n2", "u1"], ["n2"])
                            dve(lambda: V.reciprocal(n2[:], n2[:]), ["n2"], ["n2"])
                            dve(lambda: V.tensor_tensor(kr[:], am1[:], lr[:], ALU.mult), ["am1", "lr"], ["kr"])
                            dve(lambda: V.tensor_tensor(u1[:], Ai[:], li[:], ALU.mult), ["Ai", "li", "n2", "u1"], ["u1"])
                            dve(lambda: V.tensor_tensor(kr[:], kr[:], u1[:], ALU.add), ["kr", "u1"], ["kr"])
                            dve(lambda: V.tensor_tensor(kr[:], kr[:], n2[:], ALU.mult), ["kr", "n2"], ["kr"])
                            dve(lambda: V.tensor_tensor(ki[:], Ai[:], lr[:], ALU.mult), ["Ai", "lr"], ["ki"])
                            dve(lambda: V.tensor_tensor(u1[:], am1[:], li[:], ALU.mult), ["am1", "li", "kr", "u1"], ["u1"])
                            dve(lambda: V.tensor_tensor(ki[:], ki[:], u1[:], ALU.subtract), ["ki", "u1"], ["ki"])
                            dve(lambda: V.tensor_tensor(ki[:], ki[:], n2[:], ALU.mult), ["ki", "n2"], ["ki"])
                            k.op("pool", lambda: G.memset(Br[:], 0.0), reads=["BbR", "BbI"], writes=["Br"])
                            k.op("pool", lambda: G.memset(Bi[:], 0.0), reads=["BbR", "BbI"], writes=["Bi"])
                            for g2 in range(2):
                                for (dst, srcd, key) in ((Br, b_re_d, "Br"), (Bi, b_im_d, "Bi")):
                                    base = srcd[i, d]
                                    src = bass.AP(base.tensor, base.offset + g2 * 1024, [[16, 64], [2048, 32], [1, 16]])
                                    k.dma("sp", dst[g2 * 64:(g2 + 1) * 64, :, g2 * 16:(g2 + 1) * 16], src, reads=[key], writes=[key])
                            def bc_last(a2, n):
                                a = a2.ap
                                return bass.AP(a2.tensor, a2.offset, [list(a[0]), list(a[1]), [0, n]])
                            def cmul(outR, outI, xr, xi, inR, inI, kx, kin, kout):
                                xrb, xib = bc_last(xr[:], 32), bc_last(xi[:], 32)
                                dve(lambda: V.tensor_tensor(outR[:], inR[:], xrb, ALU.mult), kin + kx + kout, kout)
                                dve(lambda: V.tensor_tensor(T1t[:], inI[:], xib, ALU.mult), kin + kx + ["T1t"], ["T1t"])
                                dve(lambda: V.tensor_tensor(outR[:], outR[:], T1t[:], ALU.subtract), kout + ["T1t"], kout)
                                dve(lambda: V.tensor_tensor(outI[:], inI[:], xrb, ALU.mult), kin + kx + kout, kout)
                                dve(lambda: V.tensor_tensor(T1t[:], inR[:], xib, ALU.mult), kin + kx + ["T1t"], ["T1t"])
                                dve(lambda: V.tensor_tensor(outI[:], outI[:], T1t[:], ALU.add), kout + ["T1t"], kout)
                            cmul(BbR, BbI, kr, ki, Br, Bi, ["kr", "ki"], ["Br", "Bi"], ["Bb"])
                            for ti in range(T1):
                                if ti == 0:
                                    srcs = (BbR, BbI)
                                    skey = ["Bb"]
                                else:
                                    cmul(TbR, TbI, pws[ti][0], pws[ti][1], BbR, BbI, ["pw%d" % ti], ["Bb"], ["Tb"])
                                    srcs = (TbR, TbI)
                                    skey = ["Tb"]
                                for ri in range(2):
                                    for half in range(2):
                                        bi_ = (ti * 4 + ri * 2 + half) % 4
                                        pk = "psb%d" % bi_
                                        for c4 in range(4):
                                            ct = half * 4 + c4
                                            k.op("pe", lambda: P.transpose(psb[bi_][:, c4 * 128:(c4 + 1) * 128], srcs[ri][:, 4 * ct:4 * ct + 4, :].rearrange("p a b -> p (a b)"), ident[:]),
                                                 reads=skey + ["ident"], writes=[pk], inc=(c4 == 3))
                                        dstv = WinT[d][ri][ti][:, half * 4:(half + 1) * 4, :].rearrange("p c n -> p (c n)")
                                        if half == 0:
                                            act(lambda: S.copy(dstv, psb[bi_][:]), [pk], ["WinT%d%d" % (d, ri)])
                                        else:
                                            dve(lambda: V.tensor_copy(dstv, psb[bi_][:]), [pk], ["WinT%d%d" % (d, ri)])
                            for ri, srcd, mo, me in ((0, c_re_d, modd, mevn), (1, c_im_d, nodd, nevn)):
                                ck = "CwQ%d%d" % (d, ri)
                                k.dma("sp", Cn[:], srcd[i, d].rearrange("(ct g) c p -> (g c) ct p", g=8), reads=["Cn"], writes=["Cn"])
                                dve(lambda: V.tensor_scalar(Cblk[:, :, 0:64], Cn[:], me[:, 0:1], None, ALU.mult), ["Cn", "mevn", "nevn", "Cblk"], ["Cblk"])
                                dve(lambda: V.tensor_scalar(Cblk[:, :, 64:128], Cn[:], mo[:, 0:1], None, ALU.mult), ["Cn", "modd", "nodd", "Cblk"], ["Cblk"])
                                for half in range(2):
                                    bi_ = 4 + (ri * 2 + half) % 4
                                    pk = "psb%d" % bi_
                                    for c4 in range(4):
                                        ct = half * 4 + c4
                                        k.op("pe", lambda: P.transpose(psb[bi_][:, c4 * 128:(c4 + 1) * 128], Cblk[:, ct, :], ident[:]), reads=["Cblk", "ident"], writes=[pk], inc=(c4 == 3))
                                    act(lambda: S.copy(CwQ[d][ri][:, half * 16:(half + 1) * 16, :].rearrange("p q c -> p (q c)"), psb[bi_][:]), [pk, ck], [ck])
                        k.barrier()
                    if S5_STAGE < 1:
                        sc.close()
                        return
                    NG = W // T1
                    Bw = [[k.sb(sc, "Bw%d_%d" % (d, z_), [128, 64, W], BF16) for z_ in range(2)] for d in range(2)]
                    H = [k.sb(sc, "H%d" % d, [128, 64, W + T1], F32) for d in range(2)]
                    Sb = [k.sb(sc, "Sb%d" % d, [128, 64, W], BF16) for d in range(2)]
                    XY = [k.sb(sc, "XY%d" % d, [128, 2, 64, T1], F32) for d in range(2)]
                    Nn = [k.sb(sc, "Nn%d" % d, [128, 2, 32, T1], F32) for d in range(2)]
                    yo = [k.sb(sc, "yo%d" % d, [128, 8, W], F32) for d in range(2)]
                    k.op("pool", lambda: G.memset(H[0][:], 0.0), writes=["H0"])
                    k.op("pool", lambda: G.memset(H[1][:], 0.0), writes=["H1"])

                    def bc4(ap3):
                        a_ = ap3.ap
                        return bass.AP(ap3.tensor, ap3.offset, [list(a_[0]), [0, 2], list(a_[1]), list(a_[2])])

                    def gbc(d):
                        a_ = Gt[d][:].ap
                        return bass.AP(Gt[d][:].tensor, Gt[d][:].offset, [list(a_[0]), list(a_[1]), list(a_[2]), [0, T1]])

                    NWR = NW if S5_NW is None else S5_NW
                    CE = ("dve", "pool") if S5_POOL else ("dve", "dve")
                    CV = (V, G) if S5_POOL else (V, V)

                    def emit_bu(step_w):
                        zb = step_w % 2
                        wins = (step_w, NW - 1 - step_w)
                        for d in range(2):
                            p0 = wins[d] * W
                            bk = "Bw%d_%d" % (d, zb)
                            for half in range(2):
                                for c4 in range(4):
                                    ct = half * 4 + c4
                                    for ri in range(2):
                                        for r in range(4):
                                            slot = c4 * 2 + ri
                                            last = (c4 == 3 and ri == 1)
                                            for ti in range(T1):
                                                c0 = (PAD + p0 - ti) if d == 0 else (BOFF + p0 + ti)
                                                k.op("pe", lambda: P.matmul(psb[r][:, slot * W:(slot + 1) * W],
                                                                            WinT[d][ri][ti][32 * r:32 * r + 32, ct, :],
                                                                            hT[32 * r:32 * r + 32, ct, c0:c0 + W],
                                                                            start=(ti == 0), stop=(ti == T1 - 1), tile_position=(32 * r, 0)),
                                                     reads=["hT", "WinT%d%d" % (d, ri)], writes=["psb%d" % r], inc=(last and r == 3 and ti == T1 - 1))
                                for r in range(4):
                                    for ri in range(2):
                                        src = psb[r][:].rearrange("p (c i w) -> p c i w", i=2, w=W)[:, :, ri, :]
                                        lo = ri * 32 + r * 8 + half * 4
                                        k.op("act", lambda: S.copy(Bw[d][zb][:, lo:lo + 4, :], src), reads=["psb%d" % r, bk], writes=[bk])

                    emit_bu(0)
                    for step_w in range(NWR):
                        zb = step_w % 2
                        wins = (step_w, NW - 1 - step_w)
                        if step_w + 1 < NWR:
                            emit_bu(step_w + 1)
                        for g in range(NG if S5_SUB >= 1 else 0):
                            rd = (g * T1, W - g * T1)
                            wr = (T1 + g * T1, W - (g + 1) * T1)
                            bj = (g * T1, W - (g + 1) * T1)
                            for d in range(2):
                                k.op(CE[d], lambda: CV[d].tensor_tensor(XY[d][:], bc4(H[d][:, :, rd[d]:rd[d] + T1]), gbc(d), ALU.mult),
                                     reads=["H%d" % d, "Gt%d" % d], writes=["XY%d" % d])
                            for d in range(2):
                                k.op(CE[d], lambda: CV[d].tensor_tensor(Nn[d][:], XY[d][:, :, 0:32, :], XY[d][:, :, 32:64, :], ALU.add),
                                     reads=["XY%d" % d], writes=["Nn%d" % d])
                            for d in range(2):
                                k.op(CE[d], lambda: CV[d].tensor_tensor(H[d][:, :, wr[d]:wr[d] + T1], Nn[d][:].rearrange("p a q t -> p (a q) t"),
                                                                        Bw[d][zb][:, :, bj[d]:bj[d] + T1], ALU.add),
                                     reads=["Nn%d" % d, "Bw%d_%d" % (d, zb)], writes=["H%d" % d])
                        for d in range(2 if S5_SUB >= 2 else 0):
                            p0 = wins[d] * W
                            if d == 0:
                                t0 = p0
                                k.op("act", lambda: S.copy(Sb[0][:], H[0][:, :, T1:W + T1]), reads=["H0"], writes=["Sb0"])
                                k.op("dve", lambda: V.tensor_copy(H[0][:, :, 0:T1], H[0][:, :, W:W + T1]), reads=["H0"], writes=["H0"])
                            else:
                                t0 = (NCTX + p0) if p0 < 2048 else (p0 - 2048)
                                k.op("act", lambda: S.copy(Sb[1][:], H[1][:, :, 0:W]), reads=["H1"], writes=["Sb1"])
                                k.op(CE[1], lambda: CV[1].tensor_copy(H[1][:, :, W:W + T1], H[1][:, :, 0:T1]), reads=["H1"], writes=["H1"])
                            for ct in range(8):
                                for r in range(4):
                                    q = ct * 4 + r
                                    for ri in range(2):
                                        k.op("pe", lambda: P.matmul(psb[4 + r][32 * r:32 * r + 32, ct * W:(ct + 1) * W], CwQ[d][ri][:, q, :],
                                                                    Sb[d][:, ri * 32 + r * 8 + ct, :], start=(ri == 0), stop=(ri == 1),
                                                                    tile_position=(0, 32 * r)),
                                             reads=["CwQ%d%d" % (d, ri), "Sb%d" % d], writes=["psb%d" % (4 + r)], inc=(ct == 7 and ri == 1))
                            for r in range(4):
                                k.op("act", lambda: S.copy(yo[d][32 * r:32 * r + 32, :, :].rearrange("p c w -> p (c w)"), psb[4 + r][32 * r:32 * r + 32, 0:8 * W]),
                                     reads=["psb%d" % (4 + r), "yo%d" % d], writes=["yo%d" % d])
                            k.dma("sp", YF[d].rearrange("(ct p) t -> p ct t", p=128)[:, :, t0:t0 + W], yo[d][:], reads=["yo%d" % d], writes=["YF"])
                    k.barrier()
                    sc.close()
                    if S5_STAGE < 2:
                        return
                    with contextlib.ExitStack() as pg:
                        gw = k.sb(pg, "gw", [128, 8, 2 * D], BF16)
                        for kt in range(8):
                            k.dma("pool", gw[:, kt, :], glu_d[i][kt * 128:(kt + 1) * 128, :], writes=["gw"])
                        yas = [k.sb(pg, "ya%d" % z_, [128, 8, TB], F32) for z_ in range(2)]
                        ybs2 = [k.sb(pg, "yb2%d" % z_, [128, 8, TB], F32) for z_ in range(2)]
                        y2 = k.sb(pg, "y2", [128, 8, TB], F32)
                        y3 = k.sb(pg, "y3", [128, 8, TB], F32)
                        gls = [k.sb(pg, "gl%d" % z_, [128, 8, TB], BF16) for z_ in range(2)]
                        sg = k.sb(pg, "sg", [128, TB], F32)
                        zos = [k.sb(pg, "zo%d" % z_, [128, 8, TB], F32) for z_ in range(2)]
                        YFv = [YF[d_].rearrange("(ct p) t -> p ct t", p=128) for d_ in range(2)]

                        def g_load(blk):
                            z_ = blk % 2
                            t0 = blk * TB
                            k.dma("sp", yas[z_][:], YFv[0][:, :, t0:t0 + TB], reads=["YF"], writes=["ya%d" % z_])
                            k.dma("sp", ybs2[z_][:], YFv[1][:, :, t0:t0 + TB], reads=["YF"], writes=["yb2%d" % z_])

                        def g_pro(blk):
                            z_ = blk % 2
                            t0 = blk * TB
                            ya, yb2, gl = yas[z_], ybs2[z_], gls[z_]
                            ka, kb2, kg = "ya%d" % z_, "yb2%d" % z_, "gl%d" % z_
                            th = [lambda: k.op("dve", lambda: V.tensor_tensor(ya[:], ya[:], yb2[:], ALU.add), reads=[ka, kb2], writes=[ka])]
                            for ct in range(8):
                                th.append(lambda ct=ct: k.op("dve", lambda: V.scalar_tensor_tensor(out=ya[:, ct, :], in0=hT[:, ct, PAD + t0:PAD + t0 + TB], scalar=dsk[:, ct:ct + 1],
                                                                                                  in1=ya[:, ct, :], op0=ALU.mult, op1=ALU.add), reads=[ka, "hT", "dsk"], writes=[ka]))
                            th.append(lambda: k.op("dve", lambda: V.tensor_tensor(y2[:], ya[:], ya[:], ALU.mult), reads=[ka, "y2"], writes=["y2"]))
                            th.append(lambda: k.op("dve", lambda: V.tensor_scalar(y2[:], y2[:], 0.044715, 1.0, ALU.mult, ALU.add), reads=["y2"], writes=["y2"]))
                            th.append(lambda: k.op("dve", lambda: V.tensor_tensor(y2[:], y2[:], ya[:], ALU.mult), reads=["y2", ka], writes=["y2"]))
                            th.append(lambda: k.op("act", lambda: S.activation(y3[:], y2[:], AF.Sigmoid, scale=1.5957691216057308), reads=["y2", "y3"], writes=["y3"]))
                            th.append(lambda: k.op("dve", lambda: V.tensor_tensor(gl[:], y3[:], ya[:], ALU.mult), reads=["y3", ka, kg], writes=[kg]))
                            return th

                        g_load(0)
                        for t_ in g_pro(0):
                            t_()
                        for blk in range(NBLK):
                            z_ = blk % 2
                            t0 = blk * TB
                            gl, zo, kg, kz = gls[z_], zos[z_], "gl%d" % z_, "zo%d" % z_
                            pend = []
                            if blk + 1 < NBLK:
                                g_load(blk + 1)
                                pend = g_pro(blk + 1)
                            for ct in range(8):
                                pa, pb_ = psb[(2 * ct) % 4], psb[(2 * ct + 1) % 4]
                                ka_, kb_ = "psb%d" % ((2 * ct) % 4), "psb%d" % ((2 * ct + 1) % 4)
                                for kt in range(8):
                                    k.op("pe", lambda: P.matmul(pa[:, 0:TB], gw[:, kt, ct * 128:(ct + 1) * 128], gl[:, kt, :], start=(kt == 0), stop=(kt == 7)),
                                         reads=["gw", kg], writes=[ka_], inc=(kt == 7))
                                if pend:
                                    pend.pop(0)()
                                for kt in range(8):
                                    k.op("pe", lambda: P.matmul(pb_[:, 0:TB], gw[:, kt, D + ct * 128:D + (ct + 1) * 128], gl[:, kt, :], start=(kt == 0), stop=(kt == 7)),
                                         reads=["gw", kg], writes=[kb_], inc=(kt == 7))
                                if pend:
                                    pend.pop(0)()
                                k.op("act", lambda: S.activation(sg[:], pb_[:, 0:TB], AF.Sigmoid), reads=[kb_, "sg"], writes=["sg"])
                                k.op("dve", lambda: V.tensor_tensor(zo[:, ct, :], pa[:, 0:TB], sg[:], ALU.mult), reads=[ka_, "sg", kz], writes=[kz])
                            while pend:
                                pend.pop(0)()
                            k.dma("pool", YTv[:, :, t0:t0 + TB], zo[:], reads=[kz], writes=["YT"])
                        k.barrier()

            def even_mixer(l):
                i = l // 2
                NLAT = 2048
                with contextlib.ExitStack() as ph:
                    fT = k.sb(ph, "fT", [128, 4, NT], BF16)
                    QT = k.sb(ph, "QT", [128, 4, NT], BF16)
                    KT = k.sb(ph, "KT", [128, 2, NT], BF16)
                    Vtm = k.sb(ph, "Vtm", [128, NT // 128, 128], BF16)
                    mixT = k.sb(ph, "mixT", [128, 8, NT], BF16)
                    SEall = k.sb(ph, "SEall", [128, 8], F32)
                    SE = k.sb(ph, "SE", [128, 2, 2], F32)
                    sk = even_sink_d[i]
                    k.dma("sp", SEall[:], bass.AP(sk.tensor, sk.offset, [[0, 128], [1, 8]]), writes=["SEall"], allow_slow_non_contiguous=True)
                    k.op("act", lambda: S.activation(SEall[:], SEall[:], AF.Exp), reads=["SEall"], writes=["SEall"])
                    for kh in range(2):
                        for tl in range(2):
                            k.op("dve", lambda: V.tensor_copy(SE[0:64, kh, tl:tl + 1], SEall[0:64, 4 * kh + 2 * tl:4 * kh + 2 * tl + 1]), reads=["SEall", "SE"], writes=["SE"])
                            k.op("dve", lambda: V.tensor_copy(SE[64:128, kh, tl:tl + 1], SEall[64:128, 4 * kh + 2 * tl + 1:4 * kh + 2 * tl + 2]), reads=["SEall", "SE"], writes=["SE"])
                    with contextlib.ExitStack() as pa:
                        hT = k.sb(pa, "hT", [128, 8, NT], BF16)
                        with contextlib.ExitStack() as ph2:
                            prenorm_to_hT(ph2, hT)
                            k.barrier()
                        wb = k.sb(pa, "wb", [128, 8, 1280], BF16)
                        for kt in range(8):
                            k.dma("pool", wb[:, kt, :], w_in_d[i][kt * 128:(kt + 1) * 128, :], writes=["wb"])
                        wsw = k.sb(pa, "wsw", [128, 8, 640], BF16)
                        wv = wb[:, :, 512:1152].rearrange("p k (h two e) -> p k h two e", two=2, e=16)
                        wsv = wsw[:].rearrange("p k (h two e) -> p k h two e", two=2, e=16)
                        for kt in range(8):
                            k.op("pool", lambda: G.tensor_copy(wsv[:, kt, :, 0, :], wv[:, kt, :, 1, :]), reads=["wb", "wsw"], writes=["wsw"])
                            k.op("pool", lambda: G.tensor_copy(wsv[:, kt, :, 1, :], wv[:, kt, :, 0, :]), reads=["wb", "wsw"], writes=["wsw"])
                        wkd = k.sb(pa, "wkd", [128, 8, 2, 128], BF16)
                        wkds = k.sb(pa, "wkds", [128, 8, 2, 128], BF16)
                        for dup in range(2):
                            k.op("pool", lambda: G.tensor_copy(wkd[:, :, :, dup * 64:(dup + 1) * 64], wb[:, :, 1024:1152].rearrange("p k (h d) -> p k h d", d=64)), reads=["wb", "wkd"], writes=["wkd"])
                            k.op("pool", lambda: G.tensor_copy(wkds[:, :, :, dup * 64:(dup + 1) * 64], wsw[:, :, 512:640].rearrange("p k (h d) -> p k h d", d=64)), reads=["wsw", "wkds"], writes=["wkds"])
                        ropc = k.sb(pa, "ropc", [128, NLAT], F32)
                        rops = k.sb(pa, "rops", [128, NLAT], F32)
                        k.dma("sp", ropc[:], ROPC[:, :], reads=["ROP"], writes=["ropc"])
                        k.dma("sp", rops[:], ROPS[:, :], reads=["ROP"], writes=["rops"])
                        t1 = k.sb(pa, "rt1", [128, 512], F32)
                        t2 = k.sb(pa, "rt2", [128, 512], F32)
                        blocks = [(0, 256)] + [(256 + 512 * b, 512) for b in range(4)]
                        nb = 0
                        for (t0, n) in blocks:
                            lat = t0 >= NCTX
                            for g in range(4):
                                pb, pk = psb[nb % 4], "psb%d" % (nb % 4); nb += 1
                                for kt in range(8):
                                    k.op("pe", lambda: P.matmul(pb[:, 0:n], wb[:, kt, g * 128:(g + 1) * 128], hT[:, kt, t0:t0 + n], start=(kt == 0), stop=(kt == 7)),
                                         reads=["wb", "hT"], writes=[pk], inc=(kt == 7))
                                k.op("act", lambda: S.copy(fT[:, g, t0:t0 + n], pb[:, 0:n]), reads=[pk, "fT"], writes=["fT"])
                            for j in range(6):
                                if j < 4:
                                    lw = lambda kt: wb[:, kt, 512 + j * 128:512 + (j + 1) * 128]
                                    lws = lambda kt: wsw[:, kt, j * 128:(j + 1) * 128]
                                    dst = QT[:, j, t0:t0 + n]
                                    dk = "QT"
                                else:
                                    lw = lambda kt: wkd[:, kt, j - 4, :]
                                    lws = lambda kt: wkds[:, kt, j - 4, :]
                                    dst = KT[:, j - 4, t0:t0 + n]
                                    dk = "KT"
                                pb, pk = psb[nb % 4], "psb%d" % (nb % 4); nb += 1
                                for kt in range(8):
                                    k.op("pe", lambda: P.matmul(pb[:, 0:n], lw(kt), hT[:, kt, t0:t0 + n], start=(kt == 0), stop=(kt == 7)),
                                         reads=["wb", "wkd", "hT"], writes=[pk], inc=(kt == 7))
                                if not lat:
                                    k.op("act", lambda: S.copy(dst, pb[:, 0:n]), reads=[pk, dk], writes=[dk])
                                else:
                                    pb2, pk2 = psb[4 + nb % 4], "psb%d" % (4 + nb % 4)
                                    for kt in range(8):
                                        k.op("pe", lambda: P.matmul(pb2[:, 0:n], lws(kt), hT[:, kt, t0:t0 + n], start=(kt == 0), stop=(kt == 7)),
                                             reads=["wsw", "wkds", "hT"], writes=[pk2], inc=(kt == 7))
                                    r0 = t0 - NCTX
                                    k.op("dve", lambda: V.tensor_tensor(t1[:, 0:n], pb[:, 0:n], ropc[:, r0:r0 + n], ALU.mult), reads=[pk, "ropc", "rt1"], writes=["rt1"])
                                    k.op("dve", lambda: V.tensor_tensor(t2[:, 0:n], pb2[:, 0:n], rops[:, r0:r0 + n], ALU.mult), reads=[pk2, "rops", "rt2"], writes=["rt2"])
                                    k.op("pool", lambda: G.tensor_tensor(dst, t1[:, 0:n], t2[:, 0:n], ALU.add), reads=["rt1", "rt2", dk], writes=[dk])
                            for s in range(n // 128):
                                tt = (t0 + s * 128) // 128
                                pb, pk = psb[nb % 4], "psb%d" % (nb % 4); nb += 1
                                for kt in range(8):
                                    k.op("pe", lambda: P.matmul(pb[:, 0:128], hT[:, kt, tt * 128:(tt + 1) * 128], wb[:, kt, 1152:1280], start=(kt == 0), stop=(kt == 7)),
                                         reads=["wb", "hT"], writes=[pk], inc=(kt == 7))
                                k.op("act", lambda: S.copy(Vtm[:, tt, :], pb[:, 0:128]), reads=[pk, "Vtm"], writes=["Vtm"])
                        dump("dbg_fT", fT[:], ["fT"]); dump("dbg_QT", QT[:], ["QT"]); dump("dbg_KT", KT[:], ["KT"]); dump("dbg_Vtm", Vtm[:], ["Vtm"])
                        k.barrier()
                    with contextlib.ExitStack() as pf:
                        Gtm = k.sb(pf, "Gtm", [128, NT // 128, 4, 256], BF16)
                        csc = k.sb(pf, "csc", [128, 256], BF16)
                        k.dma("sp", csc[:, 0:128], CL.rearrange("(t e) c -> t e c", e=16)[:, 0, 0:128], reads=["TAB"], writes=["csc"])
                        k.dma("sp", csc[:, 128:256], SLn.rearrange("(t e) c -> t e c", e=16)[:, 0, 0:128], reads=["TAB"], writes=["csc"])
                        k.op("dve", lambda: V.tensor_scalar(csc[:, 128:256], csc[:, 128:256], -1.0, None, ALU.mult), reads=["csc"], writes=["csc"])
                        sc_lat = float(1.0 / np.sqrt(2048.0 * 128.0))
                        sc_ctx = float(1.0 / np.sqrt(256.0 * 128.0))
                        nb = 0
                        for tt in range(NT // 128):
                            for g in range(4):
                                pb, pk = psb[nb % 4], "psb%d" % (nb % 4); nb += 1
                                k.op("pe", lambda: P.matmul(pb[:, 0:256], fT[:, g, tt * 128:(tt + 1) * 128], csc[:], start=True, stop=True),
                                     reads=["fT", "csc"], writes=[pk])
                                k.op("act", lambda: S.activation(Gtm[:, tt, g, :], pb[:, 0:256], AF.Copy, scale=(sc_ctx if tt < 2 else sc_lat)), reads=[pk, "Gtm"], writes=["Gtm"])
                        cl = k.sb(pf, "cl", [128, 16, 512], BF16)
                        sl = k.sb(pf, "sl", [128, 16, 512], BF16)
                        c8 = CL.rearrange("(t e) c -> t e c", e=8)[:, 0, 0:256].rearrange("(tt p) c -> p tt c", p=128)
                        s8 = SLn.rearrange("(t e) c -> t e c", e=8)[:, 0, 0:256].rearrange("(tt p) c -> p tt c", p=128)
                        k.dma("sp", cl[:, 0:2, 0:256], c8, reads=["TAB"], writes=["cl"])
                        k.dma("sp", sl[:, 0:2, 0:256], s8, reads=["TAB"], writes=["sl"])
                        for g in range(4):
                            pb, pk = psb[4 + g % 4], "psb%d" % (4 + g % 4)
                            n_ = 0
                            for tt in range(2):
                                for (half, tabl, tk) in ((0, cl, "cl"), (1, sl, "sl")):
                                    k.op("pe", lambda: P.matmul(pb[:, 0:256], Gtm[:, tt, g, half * 128:(half + 1) * 128], tabl[:, tt, 0:256], start=(n_ == 0), stop=(n_ == 3)),
                                         reads=["Gtm", tk], writes=[pk], inc=(n_ == 3))
                                    n_ += 1
                            k.op("act", lambda: S.copy(mixT[:, g, 0:256], pb[:, 0:256]), reads=[pk, "mixT"], writes=["mixT"])
                        for pbk in range(4):
                            k.dma("sp", cl[:], CL[:, pbk * 512:(pbk + 1) * 512].rearrange("(tt p) c -> p tt c", p=128), reads=["TAB", "cl"], writes=["cl"])
                            k.dma("sp", sl[:], SLn[:, pbk * 512:(pbk + 1) * 512].rearrange("(tt p) c -> p tt c", p=128), reads=["TAB", "sl"], writes=["sl"])
                            for g in range(4):
                                pb, pk = psb[4 + g % 4], "psb%d" % (4 + g % 4)
                                n_ = 0
                                for tt in range(16):
                                    for (half, tabl, tk) in ((0, cl, "cl"), (1, sl, "sl")):
                                        k.op("pe", lambda: P.matmul(pb[:, 0:512], Gtm[:, 2 + tt, g, half * 128:(half + 1) * 128], tabl[:, tt, :], start=(n_ == 0), stop=(n_ == 31)),
                                             reads=["Gtm", tk], writes=[pk], inc=(n_ == 31))
                                        n_ += 1
                                k.op("act", lambda: S.copy(mixT[:, g, NCTX + pbk * 512:NCTX + (pbk + 1) * 512], pb[:, 0:512]), reads=[pk, "mixT"], writes=["mixT"])
                        k.barrier()
                    with contextlib.ExitStack() as pt:
                        PT = [k.sb(pt, "PT%d" % z, [128, 2, 2, 128], BF16) for z in range(2)]
                        rden = k.sb(pt, "rden", [128, 2, 128], F32)
                        scale = 0.125
                        qblocks = [("c", 0), ("c", 1)] + [("l", n) for n in range(16)]
                        items = []
                        for (kind, n) in qblocks:
                            q0 = n * 128 if kind == "c" else NCTX + n * 128
                            for kh in range(2):
                                chunks = []
                                if kind == "l":
                                    for dlt in (-1, 0, 1):
                                        if 0 <= n + dlt < 16:
                                            chunks.append((NCTX + (n + dlt) * 128, dlt))
                                chunks += [(0, 0), (128, 0)]
                                for ci, (k0, dlt) in enumerate(chunks):
                                    items.append((q0, kh, k0, dlt, ci == 0, ci == len(chunks) - 1))

                        def emit_S(i_):
                            q0, kh, k0, dlt, first, last = items[i_]
                            z = i_ % 2
                            for par in range(2):
                                k.op("pe", lambda: P.matmul(psb[par + 2 * z][:, 0:256].rearrange("p (t q) -> p t q", t=2),
                                                            KT[64 * par:64 * par + 64, kh, k0:k0 + 128],
                                                            QT[64 * par:64 * par + 64, 2 * kh:2 * kh + 2, q0:q0 + 128],
                                                            start=True, stop=True, tile_position=(64 * par, 0)),
                                     reads=["KT", "QT"], writes=["psb%d" % (par + 2 * z)])

                        emit_S(0)
                        for i_ in range(len(items)):
                            q0, kh, k0, dlt, first, last = items[i_]
                            z = i_ % 2
                            if i_ + 1 < len(items):
                                emit_S(i_ + 1)
                            for par in range(2):
                                k.op("act", lambda: S.activation(PT[z][:, par, :, :].rearrange("p t q -> p (t q)"), psb[par + 2 * z][:, 0:256], AF.Exp, scale=scale),
                                     reads=["psb%d" % (par + 2 * z), "PT%d" % z], writes=["PT%d" % z])
                            if dlt != 0:
                                msk = mask_ge if dlt == -1 else mask_le
                                mb = bass.AP(msk[:].tensor, msk[:].offset, [list(msk[:].ap[0]), [0, 4], list(msk[:].ap[1])])
                                k.op("dve", lambda: V.tensor_tensor(PT[z][:].rearrange("p a t q -> p (a t) q"), PT[z][:].rearrange("p a t q -> p (a t) q"), mb, ALU.mult),
                                     reads=["PT%d" % z, "mask_ge", "mask_le"], writes=["PT%d" % z])
                            tt = k0 // 128
                            for par in range(2):
                                k.op("pe", lambda: P.matmul(psb[4 + par][64 * par:64 * par + 64, 0:256], Vtm[:, tt, kh * 64:(kh + 1) * 64],
                                                            PT[z][:, par, :, :].rearrange("p t q -> p (t q)"), start=first, stop=last,
                                                            tile_position=(0, 64 * par)),
                                     reads=["Vtm", "PT%d" % z], writes=["psb%d" % (4 + par)], inc=last)
                                k.op("pe", lambda: P.matmul(psb[6 + par][64 * par:64 * par + 64, 0:256], ones64[:],
                                                            PT[z][:, par, :, :].rearrange("p t q -> p (t q)"), start=first, stop=last,
                                                            tile_position=(0, 64 * par)),
                                     reads=["ones64", "PT%d" % z], writes=["psb%d" % (6 + par)], inc=last)
                            if last:
                                for par in range(2):
                                    lo, hi = 64 * par, 64 * par + 64
                                    seb = bass.AP(SE[:].tensor, SE[lo:hi, kh, :].offset, [list(SE[lo:hi, kh, :].ap[0]), list(SE[lo:hi, kh, :].ap[1]), [0, 128]])
                                    k.op("dve", lambda: V.tensor_tensor(rden[lo:hi, :, :], psb[6 + par][lo:hi, 0:256].rearrange("p (t q) -> p t q", t=2), seb, ALU.add),
                                         reads=["psb%d" % (6 + par), "SE", "rden%d" % par], writes=["rden%d" % par])
                                    k.op("dve", lambda: V.reciprocal(rden[lo:hi, :, :], rden[lo:hi, :, :]), reads=["rden%d" % par], writes=["rden%d" % par])
                                    k.op("dve", lambda: V.tensor_tensor(mixT[lo:hi, 4 + 2 * kh:6 + 2 * kh, q0:q0 + 128], psb[4 + par][lo:hi, 0:256].rearrange("p (t q) -> p t q", t=2),
                                                                        rden[lo:hi, :, :], ALU.mult),
                                         reads=["psb%d" % (4 + par), "rden%d" % par, "mixT"], writes=["mixT"])
                        dump("dbg_mixT", mixT[:], ["mixT"])
                        k.barrier()
                    with contextlib.ExitStack() as po:
                        wo = k.sb(po, "wo", [128, 8, D], BF16)
                        for kt in range(8):
                            k.dma("pool", wo[:, kt, :], w_out_d[i][kt * 128:(kt + 1) * 128, :], writes=["wo"])
                        yo = [k.sb(po, "eyo%d" % z, [128, 8, 256], F32) for z in range(2)]
                        for blk in range(NBLK):
                            t0 = blk * TB
                            z = blk % 2
                            for ct in range(8):
                                pb, pk = psb[ct % 4], "psb%d" % (ct % 4)
                                for mt in range(8):
                                    k.op("pe", lambda: P.matmul(pb[:, 0:TB], wo[:, mt, ct * 128:(ct + 1) * 128], mixT[:, mt, t0:t0 + TB], start=(mt == 0), stop=(mt == 7)),
                                         reads=["wo", "mixT"], writes=[pk], inc=(mt == 7))
                                k.op("act", lambda: S.copy(yo[z][:, ct, :], pb[:, 0:TB]), reads=[pk, "eyo%d" % z], writes=["eyo%d" % z])
                            k.dma("sp", YTv[:, :, t0:t0 + TB], yo[z][:], reads=["eyo%d" % z], writes=["YT"])
                        k.barrier()
            if mixer and l % 2 == 1:
                s5_mixer(l)
            elif mixer:
                even_mixer(l)

            with contextlib.ExitStack() as ph:
                w1b = k.sb(ph, "w1b", [128, 8, DFF], BF16)
                w2b = k.sb(ph, "w2b", [128, 32, D], BF16)
                for kt in range(8):
                    k.dma("pool", w1b[:, kt, :], w1_d[l][kt * 128:(kt + 1) * 128, :], writes=["w1b"])
                for j4 in range(8):
                    k.dma("pool", w2b[:, j4 * 4:(j4 + 1) * 4, :],
                          w2_d[l][j4 * 512:(j4 + 1) * 512, :].rearrange("(j p) n -> p j n", p=128), writes=["w2b"])
                NXB = 3
                xbs = [k.sb(ph, "xb%d" % z, [128, 8, TB], F32) for z in range(NXB)]
                ybs = [k.sb(ph, "yb0", [128, 8, TB], F32)] * 2 if mixer else []
                sq = k.sb(ph, "sq", [128, 8, TB], F32)
                tmp = sq
                rs = k.sb(ph, "rs", [128, TB], F32)
                h2s = [k.sb(ph, "h2%d" % z, [128, 8, TB], BF16) for z in range(2)]
                ob = k.sb(ph, "ob", [128, 8, TB], F32)
                ar = [k.sb(ph, "ar%d" % z, [128, TB], F32) for z in range(2)]
                a2all = k.sb(ph, "a2all", [128, 32, TB], BF16)
                tiles = (sq, rs, psb[3], "psb3")

                def P_load(blk):
                    z = blk % NXB
                    t0 = blk * TB
                    k.dma("sp", xbs[z][:], XTv[:, :, t0:t0 + TB], reads=["XT"], writes=["xb%d" % z])

                def rms_thunks(src3, key_src):
                    pbank, pkey = psb[3], "psb3"
                    th = []
                    if src3 is not None:
                        th.append(lambda: k.op("act", lambda: S.activation(sq[:], src3, AF.Square), reads=[key_src], writes=["sq"]))
                    def mm():
                        for ct in range(8):
                            k.op("pe", lambda: P.matmul(pbank[:, 0:TB], onesm[:], sq[:, ct, :], start=(ct == 0), stop=(ct == 7)),
                                 reads=["sq", "onesm"], writes=[pkey], inc=(ct == 7))
                    th.append(mm)
                    th.append(lambda: k.op("dve", lambda: V.tensor_scalar(rs[:], pbank[:, 0:TB], EPS, None, ALU.add), reads=[pkey], writes=["rs"]))
                    th.append(lambda: k.op("act", lambda: S.activation(rs[:], rs[:], AF.Sqrt), reads=["rs"], writes=["rs"]))
                    th.append(lambda: k.op("dve", lambda: V.reciprocal(rs[:], rs[:]), reads=["rs"], writes=["rs"]))
                    return th

                def P_thunks(blk):
                    z = blk % NXB
                    xb, h2, xk, hk = xbs[z], h2s[blk % 2], "xb%d" % z, "h2%d" % (blk % 2)
                    j = 1 if blk == 0 else 0
                    th = []
                    if mixer:
                        yb, yk = ybs[0], "yb0"
                        t0 = blk * TB
                        th.append(lambda: k.dma("sp", yb[:], YTv[:, :, t0:t0 + TB], reads=["YT"], writes=[yk]))
                        th += rms_thunks(yb[:], yk)
                        th.append(lambda: k.op("dve", lambda: V.tensor_tensor(tmp[:], yb[:], bc_mid(rs[:], 8), ALU.mult), reads=[yk, "rs"], writes=["sq"]))
                        for ct in range(8):
                            th.append(lambda ct=ct: k.op("dve", lambda: V.scalar_tensor_tensor(out=xb[:, ct, :], in0=tmp[:, ct, :], scalar=PRM[:, 2, ct, j:j + 1],
                                                                                              in1=xb[:, ct, :], op0=ALU.mult, op1=ALU.add),
                                                         reads=["sq", xk, "PRM"], writes=[xk]))
                    th += rms_thunks(xb[:], xk)
                    th.append(lambda: k.op("dve", lambda: V.tensor_tensor(tmp[:], xb[:], bc_mid(rs[:], 8), ALU.mult), reads=[xk, "rs"], writes=["sq"]))
                    for ct in range(8):
                        th.append(lambda ct=ct: k.op("act", lambda: S.activation(h2[:, ct, :], tmp[:, ct, :], AF.Identity, bias=PRM[:, 4, ct, j:j + 1],
                                                                                 scale=PRM[:, 3, ct, j:j + 1]), reads=["sq", "PRM"], writes=[hk]))
                    return th

                def E_thunks(blk):
                    z = blk % NXB
                    xb, xk = xbs[z], "xb%d" % z
                    j = 1 if blk == 0 else 0
                    t0 = blk * TB
                    obk = ["ob_%d" % i_ for i_ in range(8)]
                    th = [lambda: k.op("act", lambda: S.activation(sq[:], ob[:], AF.Square), reads=obk, writes=["sq"])]
                    th += rms_thunks(None, "sq")
                    th.append(lambda: k.op("dve", lambda: V.tensor_tensor(tmp[:], ob[:], bc_mid(rs[:], 8), ALU.mult), reads=obk + ["rs"], writes=["sq"]))
                    for ct in range(8):
                        th.append(lambda ct=ct: k.op("dve", lambda: V.scalar_tensor_tensor(out=xb[:, ct, :], in0=tmp[:, ct, :], scalar=PRM[:, 5, ct, j:j + 1],
                                                                                          in1=xb[:, ct, :], op0=ALU.mult, op1=ALU.add),
                                                     reads=["sq", xk, "PRM"], writes=[xk]))
                    th.append(lambda: k.dma("pool", XTv[:, :, t0:t0 + TB], xb[:], reads=[xk], writes=["XT_st"]))
                    return th

                def M1_stage(blk, pending):
                    h2, hk = h2s[blk % 2], "h2%d" % (blk % 2)
                    for jf in range(32):
                        pa = psb[4 + jf % 4]
                        pak = "psb%d" % (4 + jf % 4)
                        for kt in range(8):
                            k.op("pe", lambda: P.matmul(pa[:, 0:TB], w1b[:, kt, jf * 128:(jf + 1) * 128], h2[:, kt, :],
                                                        start=(kt == 0), stop=(kt == 7)),
                                 reads=["w1b", hk], writes=[pak], inc=(kt == 7))
                        k.op("act", lambda: S.activation(ar[jf % 2][:], pa[:, 0:TB], AF.Relu), reads=[pak], writes=["ar%d" % (jf % 2)])
                        k.op("dve", lambda: V.tensor_tensor(a2all[:, jf, :], ar[jf % 2][:], ar[jf % 2][:], ALU.mult),
                             reads=["ar%d" % (jf % 2)], writes=["a2all"])
                        if pending and jf % 2 == 1:
                            pending.pop(0)()

                def M2_stage(blk, pending):
                    for ft in range(8):
                        po = psb[ft % 3]
                        pok = "psb%d" % (ft % 3)
                        for jf in range(32):
                            k.op("pe", lambda: P.matmul(po[:, 0:TB], w2b[:, jf, ft * 128:(ft + 1) * 128], a2all[:, jf, :],
                                                        start=(jf == 0), stop=(jf == 31)),
                                 reads=["w2b", "a2all"], writes=[pok], inc=(jf == 31))
                            if jf % 8 == 7 and pending:
                                pending.pop(0)()
                        k.op("act", lambda: S.copy(ob[:, ft, :], po[:, 0:TB]), reads=[pok], writes=["ob_%d" % ft])

                P_load(0)
                P_load(1)
                for t_ in P_thunks(0):
                    t_()
                for blk in range(NBLK):
                    pend1, pend2 = [], []
                    if blk >= 1:
                        pend1 += E_thunks(blk - 1)
                    if blk + 2 < NBLK:
                        pend1.append(lambda b_=blk + 2: P_load(b_))
                    if blk + 1 < NBLK:
                        pend2 += P_thunks(blk + 1)
                    M1_stage(blk, pend1)
                    pend2 = pend1 + pend2
                    M2_stage(blk, pend2)
                    while pend2:
                        pend2.pop(0)()
                for t_ in E_thunks(NBLK - 1):
                    t_()
                k.barrier()

        with contextlib.ExitStack() as ph:
            xf = [k.sb(ph, "xf%d" % i, [128, 8, 128], F32) for i in range(2)]
            xo = [k.sb(ph, "xo%d" % i, [128, D], F32) for i in range(2)]
            for tt in range(16):
                b = tt % 2
                k.dma("sp", xf[b][:], XTv[:, :, NCTX + tt * 128:NCTX + (tt + 1) * 128], reads=["XT"], writes=["xf%d" % b])
                for half in range(2):
                    pkey = "psb%d" % ((tt * 2 + half) % 8)
                    pb = psb[(tt * 2 + half) % 8]
                    for c4 in range(4):
                        ct = half * 4 + c4
                        k.op("pe", lambda: P.transpose(pb[:, c4 * 128:(c4 + 1) * 128], xf[b][:, ct, :], ident[:]),
                             reads=["xf%d" % b, "ident"], writes=[pkey], inc=(c4 == 3))
                    if half == 0:
                        k.op("act", lambda: S.copy(xo[b][:, 0:512], pb[:]), reads=[pkey], writes=["xo%d_0" % b])
                    else:
                        k.op("dve", lambda: V.tensor_copy(xo[b][:, 512:1024], pb[:]), reads=[pkey], writes=["xo%d_1" % b])
                k.dma("sp", out_d[tt * 128:(tt + 1) * 128, :], xo[b][:], reads=["xo%d_0" % b, "xo%d_1" % b], writes=["out"])
            k.barrier()
        print("ninst", k.ninst, "nwaits", k.nwaits)
    return nc


def make_in_maps(inp):
    gains = np.stack([inp["mix_pre_g"], inp["mix_post_g"], inp["ffn_pre_g"], inp["ffn_post_g"]], 0)
    maps = []
    for b in range(8):
        m = {
            "x": np.ascontiguousarray(inp["x"][b]), "ctx": np.ascontiguousarray(inp["ctx"][b]),
            "cc": np.ascontiguousarray(np.stack([inp["c"][b], inp["c_ctx"]], 0)),
            "mod_w": inp["mod_w"], "mod_b": inp["mod_b"], "gains": np.ascontiguousarray(gains),
            "ffn_w1": inp["ffn_w1"], "ffn_w2": inp["ffn_w2"],
            "ssm_a_re": inp["ssm_a_re"], "ssm_a_im": inp["ssm_a_im"], "ssm_log_dt": inp["ssm_log_dt"],
            "ssm_b_re": inp["ssm_b_re"], "ssm_b_im": inp["ssm_b_im"], "ssm_c_re": inp["ssm_c_re"], "ssm_c_im": inp["ssm_c_im"],
            "ssm_d": inp["ssm_d"], "ssm_glu_w": inp["ssm_glu_w"],
            "even_w_in": inp["even_w_in"], "even_w_out": inp["even_w_out"], "even_sink": inp["even_sink"],
        }
        maps.append(m)
    return maps


def kernel(**inp):
    inp = {k_: np.asarray(v) for k_, v in inp.items()}
    nc = build(mixer=MIXER_ENABLED)
    res = run_bass_kernel_spmd(nc, make_in_maps(inp), core_ids=list(range(8)))
    return np.stack([r["out"] for r in res.results], 0)
```
